# Optimizing a Trainium2 kernel written in Bass

```python
import jax, jax.numpy as jnp
from jax import lax
import numpy as np

D_MODEL = 1024
BATCH = 4
SEQ = 8192
DEPTH = 2

HEAD_DIM = 64
A_Q_HEADS = 8
A_KV_HEADS = 2
A_BLOCK = 128
WINDOW = 128
ROPE_THETA = 10000.0
B_HEADS = 8
GRID_W = 64
NA_MAX_KH = 8
NA_KW = 16
NA_QBLOCK_W = 16
NA_SPAN_W = 2 * NA_KW
A_Q_DIM = A_Q_HEADS * HEAD_DIM
A_KV_DIM = A_KV_HEADS * HEAD_DIM
B_DIM = B_HEADS * HEAD_DIM
ATTN_IN = A_Q_DIM + 2 * A_KV_DIM + 3 * B_DIM
ATTN_OUT = A_Q_DIM + B_DIM
SSM_D_INNER = 2 * D_MODEL
SSM_HEAD_DIM = 64
SSM_HEADS = SSM_D_INNER // SSM_HEAD_DIM
SSM_GROUPS = 8
SSM_STATE = 128
SSM_CONV = 5
SSM_CHUNK = 128
SSM_CONV_DIM = SSM_D_INNER + 2 * SSM_GROUPS * SSM_STATE
SSM_IN = SSM_D_INNER + SSM_CONV_DIM + 2 * SSM_HEADS
D_FF = 4 * D_MODEL
DN_ALPHA = (2 * DEPTH) ** 0.25
DN_BETA = (8 * DEPTH) ** -0.25
LN_EPS = 1e-5
RMS_EPS = 1e-5
NEG_INF = -1e30
N_ATTN_LAYERS = (DEPTH + 1) // 2
N_SSM_LAYERS = DEPTH // 2

kernel_name = "hybrid_window_natten_ssd_encoder"


def layer_norm(x, g, b):
    xf = x.astype(jnp.float32)
    mu = jnp.mean(xf, -1, keepdims=True)
    var = jnp.mean(jnp.square(xf - mu), -1, keepdims=True)
    return ((xf - mu) * lax.rsqrt(var + LN_EPS)).astype(x.dtype) * g + b


def rope(x, pos):
    half = HEAD_DIM // 2
    inv = ROPE_THETA ** (-jnp.arange(half, dtype=jnp.float32) / half)
    ang = pos.astype(jnp.float32)[:, None] * inv[None, :]
    cos = jnp.cos(ang)[None, :, None, :].astype(x.dtype)
    sin = jnp.sin(ang)[None, :, None, :].astype(x.dtype)
    x1, x2 = x[..., :half], x[..., half:]
    return jnp.concatenate([x1 * cos - x2 * sin, x2 * cos + x1 * sin], axis=-1)


def windowed_gqa_sink(q, k, v, sink):
    bsz, t = q.shape[:2]
    nb = t // A_BLOCK
    rep = A_Q_HEADS // A_KV_HEADS
    qb = q.reshape(bsz, nb, A_BLOCK, A_KV_HEADS, rep, HEAD_DIM)

    def band(u):
        ub = u.reshape(bsz, nb, A_BLOCK, A_KV_HEADS, HEAD_DIM)
        up = jnp.pad(ub, ((0, 0), (1, 1), (0, 0), (0, 0), (0, 0)))
        return jnp.concatenate([up[:, :-2], up[:, 1:-1], up[:, 2:]], axis=2)

    kb, vb = band(k), band(v)
    s = jnp.einsum('bnqgrd,bnkgd->bngrqk', qb, kb).astype(jnp.float32) * (HEAD_DIM ** -0.5)
    qi = np.arange(A_BLOCK)[:, None] + A_BLOCK
    ki = np.arange(3 * A_BLOCK)[None, :]
    kabs = (np.arange(nb)[:, None, None] - 1) * A_BLOCK + ki[None]
    valid = (np.abs(qi - ki) <= WINDOW)[None] & (kabs >= 0) & (kabs < t)
    s = jnp.where(jnp.asarray(valid)[None, :, None, None], s, NEG_INF)
    sink_l = jnp.broadcast_to(sink.astype(jnp.float32).reshape(A_KV_HEADS, rep)[None, None, :, :, None, None],
                              s.shape[:-1] + (1,))
    p = jax.nn.softmax(jnp.concatenate([s, sink_l], axis=-1), axis=-1)[..., :-1]
    o = jnp.einsum('bngrqk,bnkgd->bnqgrd', p.astype(v.dtype), vb)
    return o.reshape(bsz, t, A_Q_DIM)


def _na_static():
    ncb = GRID_W // NA_QBLOCK_W
    starts = [min(max(j * NA_QBLOCK_W - NA_KW // 2, 0), GRID_W - NA_SPAN_W) for j in range(ncb)]
    c = np.arange(GRID_W).reshape(ncb, NA_QBLOCK_W)
    cs = np.clip(c - NA_KW // 2, 0, GRID_W - NA_KW)
    kc = np.array(starts)[:, None] + np.arange(NA_SPAN_W)[None, :]
    mask = (kc[:, None, :] >= cs[:, :, None]) & (kc[:, None, :] < cs[:, :, None] + NA_KW)
    dc = np.clip(kc[:, None, :] - c[:, :, None] + NA_KW - 1, 0, 2 * NA_KW - 2)
    return starts, mask, dc


def neighborhood_attention(q, k, v, rpb):
    bsz, t = q.shape[:2]
    rows = t // GRID_W
    kh = min(NA_MAX_KH, rows)
    ncb = GRID_W // NA_QBLOCK_W
    starts, mask, dc = _na_static()
    qg = q.reshape(bsz, rows, ncb, NA_QBLOCK_W, B_HEADS, HEAD_DIM)
    kg = k.reshape(bsz, rows, GRID_W, B_HEADS, HEAD_DIM)
    vg = v.reshape(bsz, rows, GRID_W, B_HEADS, HEAD_DIM)
    mask_j = jnp.asarray(mask)[:, :, None, :]
    scale = HEAD_DIM ** -0.5

    def row_step(r):
        rs = jnp.clip(r - kh // 2, 0, rows - kh)
        k_rows = lax.dynamic_slice_in_dim(kg, rs, kh, axis=1)
        v_rows = lax.dynamic_slice_in_dim(vg, rs, kh, axis=1)
        k_blk = jnp.stack([k_rows[:, :, s0:s0 + NA_SPAN_W] for s0 in starts], axis=1)
        v_blk = jnp.stack([v_rows[:, :, s0:s0 + NA_SPAN_W] for s0 in starts], axis=1)
        q_r = lax.dynamic_index_in_dim(qg, r, axis=1, keepdims=False)
        s = jnp.einsum('bjqhd,bjakhd->bhjqak', q_r, k_blk).astype(jnp.float32) * scale
        dr = rs + jnp.arange(kh) - r + NA_MAX_KH - 1
        bias = rpb[:, dr][:, :, dc].transpose(0, 2, 3, 1, 4)
        s = jnp.where(mask_j, s + bias.astype(jnp.float32), NEG_INF)
        p = jax.nn.softmax(s.reshape(s.shape[:-2] + (kh * NA_SPAN_W,)), axis=-1).reshape(s.shape)
        o = jnp.einsum('bhjqak,bjakhd->bjqhd', p.astype(v.dtype), v_blk)
        return o.reshape(bsz, GRID_W, B_DIM)

    out = lax.map(row_step, jnp.arange(rows))
    return out.transpose(1, 0, 2, 3).reshape(bsz, t, B_DIM)


def attention_mixer(x, w_in, sink, rpb, w_out):
    bsz, t, _ = x.shape
    h = x @ w_in
    qa, ka, va, qb, kb, vb = jnp.split(
        h, [A_Q_DIM, A_Q_DIM + A_KV_DIM, A_Q_DIM + 2 * A_KV_DIM,
            A_Q_DIM + 2 * A_KV_DIM + B_DIM, A_Q_DIM + 2 * A_KV_DIM + 2 * B_DIM], axis=-1)
    pos = jnp.arange(t)
    qa = rope(qa.reshape(bsz, t, A_Q_HEADS, HEAD_DIM), pos)
    ka = rope(ka.reshape(bsz, t, A_KV_HEADS, HEAD_DIM), pos)
    va = va.reshape(bsz, t, A_KV_HEADS, HEAD_DIM)
    oa = windowed_gqa_sink(qa, ka, va, sink)
    shp = (bsz, t, B_HEADS, HEAD_DIM)
    ob = neighborhood_attention(qb.reshape(shp), kb.reshape(shp), vb.reshape(shp), rpb)
    return jnp.concatenate([oa, ob], axis=-1) @ w_out


def depthwise_conv_centred(u, w, b):
    c = u.shape[-1]
    pad = SSM_CONV // 2
    out = lax.conv_general_dilated(u, w[:, None, :].astype(u.dtype), window_strides=(1,),
                                   padding=[(pad, pad)], dimension_numbers=('NWC', 'WIO', 'NWC'),
                                   feature_group_count=c)
    return out + b


def ssd_chunked(x, dt, a_neg, bm, cm):
    bsz, t = x.shape[:2]
    nc, L, G, R = t // SSM_CHUNK, SSM_CHUNK, SSM_GROUPS, SSM_HEADS // SSM_GROUPS
    x = x.astype(jnp.float32)
    dt = dt.astype(jnp.float32)
    xc = (x * dt[..., None]).reshape(bsz, nc, L, G, R, SSM_HEAD_DIM)
    a = (dt * a_neg.astype(jnp.float32)).reshape(bsz, nc, L, G, R).transpose(0, 1, 3, 4, 2)
    acum = jnp.cumsum(a, axis=-1)
    bc = bm.astype(jnp.float32).reshape(bsz, nc, L, G, SSM_STATE)
    cc = cm.astype(jnp.float32).reshape(bsz, nc, L, G, SSM_STATE)
    tri = jnp.tril(jnp.ones((L, L), dtype=bool))
    lmat = jnp.exp(jnp.where(tri, acum[..., :, None] - acum[..., None, :], -jnp.inf))
    cb = jnp.einsum('bclgn,bcsgn->bcgls', cc, bc)
    y_diag = jnp.einsum('bcgrls,bcsgrp->bclgrp', cb[:, :, :, None] * lmat, xc)
    decay_states = jnp.exp(acum[..., -1:] - acum)
    states = jnp.einsum('bcsgn,bcgrs,bcsgrp->bcgrpn', bc, decay_states, xc)
    chunk_decay = jnp.exp(acum[..., -1])

    def step(h, inp):
        st, dec = inp
        return h * dec[..., None, None] + st, h

    h0 = jnp.zeros((bsz, G, R, SSM_HEAD_DIM, SSM_STATE), jnp.float32)
    _, prev = lax.scan(step, h0, (jnp.moveaxis(states, 1, 0), jnp.moveaxis(chunk_decay, 1, 0)))
    prev = jnp.moveaxis(prev, 0, 1)
    y_off = jnp.einsum('bclgn,bcgrpn,bcgrl->bclgrp', cc, prev, jnp.exp(acum))
    return (y_diag + y_off).reshape(bsz, t, SSM_HEADS, SSM_HEAD_DIM)


def mamba2_bidir_mixer(x, w_in, conv_w, conv_b, dt_bias, a_log, d_skip, norm_w, w_out):
    bsz, t, _ = x.shape
    zxbcdt = x @ w_in
    z, xbc, dt = jnp.split(zxbcdt, [SSM_D_INNER, SSM_D_INNER + SSM_CONV_DIM], axis=-1)
    xbc = jax.nn.silu(depthwise_conv_centred(xbc, conv_w, conv_b))
    xs, bm, cm = jnp.split(xbc, [SSM_D_INNER, SSM_D_INNER + SSM_GROUPS * SSM_STATE], axis=-1)
    xs = xs.reshape(bsz, t, SSM_HEADS, SSM_HEAD_DIM)
    bm = bm.reshape(bsz, t, SSM_GROUPS, SSM_STATE)
    cm = cm.reshape(bsz, t, SSM_GROUPS, SSM_STATE)
    dt = jax.nn.softplus((dt + dt_bias).astype(jnp.float32))
    dt_f, dt_b = dt[..., :SSM_HEADS], dt[..., SSM_HEADS:]
    a_neg = -jnp.exp(a_log.astype(jnp.float32))
    y_f = ssd_chunked(xs, dt_f, a_neg[0], bm, cm)
    flip = lambda u: jnp.flip(u, axis=1)
    y_b = flip(ssd_chunked(flip(xs), flip(dt_b), a_neg[1], flip(bm), flip(cm)))
    y = (y_f + y_b).astype(x.dtype) + xs * d_skip[:, None]
    y = y.reshape(bsz, t, SSM_D_INNER) * jax.nn.silu(z)
    yg = y.astype(jnp.float32).reshape(bsz, t, SSM_GROUPS, SSM_D_INNER // SSM_GROUPS)
    yg = yg * lax.rsqrt(jnp.mean(jnp.square(yg), -1, keepdims=True) + RMS_EPS)
    y = yg.reshape(bsz, t, SSM_D_INNER).astype(x.dtype) * norm_w
    return y @ w_out


def squared_relu_mlp(x, w1, w2):
    return jnp.square(jax.nn.relu(x @ w1)) @ w2


def setup_inputs(seed: int = 0) -> dict:
    key = jax.random.key(seed)
    ks = jax.random.split(key, 20)
    f32 = jnp.float32
    nrm = lambda k, shp, sc: jax.random.normal(k, shp, f32) * sc
    dt0 = jnp.exp(jax.random.uniform(ks[9], (N_SSM_LAYERS, 2 * SSM_HEADS), f32,
                                     jnp.log(1e-3), jnp.log(1e-1)))
    return {
        "x": nrm(ks[0], (BATCH, SEQ, D_MODEL), 1.0),
        "attn_w_in": nrm(ks[1], (N_ATTN_LAYERS, D_MODEL, ATTN_IN), D_MODEL ** -0.5),
        "attn_sink": nrm(ks[2], (N_ATTN_LAYERS, A_Q_HEADS), 1.0),
        "attn_rpb": nrm(ks[3], (N_ATTN_LAYERS, B_HEADS, 2 * NA_MAX_KH - 1, 2 * NA_KW - 1), 0.1),
        "attn_w_out": nrm(ks[4], (N_ATTN_LAYERS, ATTN_OUT, D_MODEL), ATTN_OUT ** -0.5 * DN_BETA),
        "ssm_w_in": nrm(ks[5], (N_SSM_LAYERS, D_MODEL, SSM_IN), D_MODEL ** -0.5),
        "ssm_conv_w": nrm(ks[6], (N_SSM_LAYERS, SSM_CONV, SSM_CONV_DIM), SSM_CONV ** -0.5),
        "ssm_conv_b": nrm(ks[7], (N_SSM_LAYERS, SSM_CONV_DIM), 0.02),
        "ssm_dt_bias": dt0 + jnp.log(-jnp.expm1(-dt0)),
        "ssm_A_log": jnp.log(jax.random.uniform(ks[10], (N_SSM_LAYERS, 2, SSM_HEADS), f32, 1.0, 16.0)),
        "ssm_D": 1.0 + nrm(ks[11], (N_SSM_LAYERS, SSM_HEADS), 0.05),
        "ssm_norm_w": 1.0 + nrm(ks[12], (N_SSM_LAYERS, SSM_D_INNER), 0.05),
        "ssm_w_out": nrm(ks[13], (N_SSM_LAYERS, SSM_D_INNER, D_MODEL), SSM_D_INNER ** -0.5 * DN_BETA),
        "mlp_w1": nrm(ks[14], (DEPTH, D_MODEL, D_FF), D_MODEL ** -0.5),
        "mlp_w2": nrm(ks[15], (DEPTH, D_FF, D_MODEL), D_FF ** -0.5 * DN_BETA),
        "ln1_g": 1.0 + nrm(ks[16], (DEPTH, D_MODEL), 0.05),
        "ln1_b": nrm(ks[17], (DEPTH, D_MODEL), 0.02),
        "ln2_g": 1.0 + nrm(ks[18], (DEPTH, D_MODEL), 0.05),
        "ln2_b": nrm(ks[19], (DEPTH, D_MODEL), 0.02),
    }


def reference(x, attn_w_in, attn_sink, attn_rpb, attn_w_out, ssm_w_in, ssm_conv_w, ssm_conv_b,
              ssm_dt_bias, ssm_A_log, ssm_D, ssm_norm_w, ssm_w_out, mlp_w1, mlp_w2,
              ln1_g, ln1_b, ln2_g, ln2_b):
    for i in range(DEPTH):
        j = i // 2
        if i % 2 == 0:
            mix = attention_mixer(x, attn_w_in[j], attn_sink[j], attn_rpb[j], attn_w_out[j])
        else:
            mix = mamba2_bidir_mixer(x, ssm_w_in[j], ssm_conv_w[j], ssm_conv_b[j], ssm_dt_bias[j],
                                     ssm_A_log[j], ssm_D[j], ssm_norm_w[j], ssm_w_out[j])
        x = layer_norm(DN_ALPHA * x + mix, ln1_g[i], ln1_b[i])
        x = layer_norm(DN_ALPHA * x + squared_relu_mlp(x, mlp_w1[i], mlp_w2[i]), ln2_g[i], ln2_b[i])
    return x
```

```python
import contextlib
import numpy as np
import concourse.bass as bass
import concourse.mybir as mybir
from concourse.bass_utils import run_bass_kernel_spmd

F32 = mybir.dt.float32
BF16 = mybir.dt.bfloat16
ALU = mybir.AluOpType
AF = mybir.ActivationFunctionType
AX = mybir.AxisListType

D = 1024
ALPHA = 4.0 ** 0.25
NEGB = -30000.0


class T:
    __slots__ = ("name", "w", "r")

    def __init__(self, name=""):
        self.name = name
        self.w = None
        self.r = []


class Op:
    __slots__ = ("eng", "emit", "deps", "sig", "dma", "cnt", "dsem", "dval")

    def __init__(self, eng, emit, dma):
        self.eng = eng
        self.emit = emit
        self.deps = []
        self.sig = False
        self.dma = dma
        self.cnt = 0
        self.dsem = -1
        self.dval = 0


class Sched:
    ENGS = ("pe", "act", "dve", "pool", "sp")
    ENGOBJ = {"pe": "tensor", "act": "scalar", "dve": "vector", "pool": "gpsimd", "sp": "sync"}

    def __init__(self, nc, st, n_dma_sems=14):
        self.nc = nc
        self.ops = {e: [] for e in self.ENGS}
        self.n_dma_sems = n_dma_sems
        self.csem = {e: st.enter_context(nc.semaphore("c_" + e)) for e in self.ENGS}
        self.dsems = {e: [st.enter_context(nc.semaphore("d_%s%d" % (e, i))) for i in range(n_dma_sems)]
                      for e in ("sp", "pool")}
        self.phsem = st.enter_context(nc.semaphore("phase"))
        self.ccount = {e: 0 for e in self.ENGS}
        self.dcount = {e: 0 for e in ("sp", "pool")}
        self.duses = {e: [0] * n_dma_sems for e in ("sp", "pool")}
        self.nflush = 0

    def add(self, eng, emit, reads=(), writes=(), dma=False):
        op = Op(eng, emit, dma)
        deps = op.deps
        for t in reads:
            w = t.w
            if w is not None and (w.eng != eng or w.dma or eng != "pe"):
                deps.append(w)
        for t in writes:
            w = t.w
            if w is not None and (w.eng != eng or w.dma or dma):
                deps.append(w)
            for r in t.r:
                if r.eng != eng or r.dma or dma:
                    deps.append(r)
        for t in reads:
            t.r.append(op)
        for t in writes:
            t.w = op
            t.r = []
        for d in deps:
            d.sig = True
        self.ops[eng].append(op)
        return op

    def dma(self, eng, out, in_, reads=(), writes=()):
        return self.add(eng, lambda e: e.dma_start(out=out, in_=in_), reads, writes, dma=True)

    def flush(self):
        nc = self.nc
        import os
        if os.environ.get("KDBG"):
            print("flush", self.nflush, "sbuf remaining", nc.sbuf_bytes_remaining, {e: len(v) for e, v in self.ops.items()})
        csem, dsems = self.csem, self.dsems
        last_cnt = {}
        for e in self.ENGS:
            comp = [op for op in self.ops[e] if not op.dma]
            if comp:
                comp[-1].sig = True
            for op in self.ops[e]:
                if op.dma:
                    k = self.dcount[e]
                    self.dcount[e] += 1
                    op.dsem = k % self.n_dma_sems
                    self.duses[e][op.dsem] += 1
                    op.dval = 16 * self.duses[e][op.dsem]
                elif op.sig:
                    self.ccount[e] += 1
                    op.cnt = self.ccount[e]
            last_cnt[e] = self.ccount[e]
        nfl = self.nflush
        ops_all = self.ops

        def run(ename, eobj):
            if nfl > 0:
                eobj.wait_ge(self.phsem, 5 * nfl)
            known = {}
            last_dma = {}
            for op in ops_all[ename]:
                need = {}
                for d in op.deps:
                    if d.dma:
                        key = ("d", d.eng, d.dsem)
                        val = d.dval
                    else:
                        key = ("c", d.eng)
                        val = d.cnt
                    if need.get(key, 0) < val:
                        need[key] = val
                if op.dma:
                    key = ("d", ename, op.dsem)
                    val = op.dval - 16
                    if val > 0 and need.get(key, 0) < val:
                        need[key] = val
                for key, val in need.items():
                    if val <= 0 or known.get(key, 0) >= val:
                        continue
                    known[key] = val
                    sem = csem[key[1]] if key[0] == "c" else dsems[key[1]][key[2]]
                    eobj.wait_ge(sem, val)
                ins = op.emit(eobj)
                if op.dma:
                    ins.then_inc(dsems[ename][op.dsem], 16)
                    last_dma[op.dsem] = op.dval
                elif op.sig:
                    ins.then_inc(csem[ename], 1)
            for k, v in last_dma.items():
                if known.get(("d", ename, k), 0) < v:
                    eobj.wait_ge(dsems[ename][k], v)
            if last_cnt[ename] > 0 and known.get(("c", ename), 0) < last_cnt[ename]:
                eobj.wait_ge(csem[ename], last_cnt[ename])
            eobj.sem_inc(self.phsem, 1)

        with nc.Block() as block:
            for ename in self.ENGS:
                getattr(block, self.ENGOBJ[ename])(lambda eobj, ename=ename: run(ename, eobj))
        self.ops = {e: [] for e in self.ENGS}
        self.nflush += 1


class Ctx:
    _inst = [0]

    def __init__(self, nc, st):
        self.nc = nc
        self.st = st
        self.n = 0
        Ctx._inst[0] += 1
        self.pfx = "c%d_" % Ctx._inst[0]

    def sb(self, shape, dt, name=None):
        self.n += 1
        return self.st.enter_context(self.nc.sbuf_tensor(self.pfx + "s_" + (name or "sb%d" % self.n), list(shape), dt))

    def ps(self, shape, dt, name=None):
        self.n += 1
        return self.st.enter_context(self.nc.psum_tensor(self.pfx + "p_" + (name or "ps%d" % self.n), list(shape), dt))


def _na_types(NT):
    return {"int": [-2, -1, 0, 1, 2], "t0": [0, 1, 2, 3, 3], "t1": [-1, 0, 1, 2, 2],
            "tm2": [-2, -1, 0, 1, 1], "tm1": [-3, -2, -1, 0, 0]}


def _na_type_of(t, NT):
    if t == 0:
        return "t0"
    if t == 1:
        return "t1"
    if t == NT - 2:
        return "tm2"
    if t == NT - 1:
        return "tm1"
    return "int"


def _na_bias_tables(rpb, NT):
    rows = 2 * NT
    types = _na_types(NT)
    rep_t = {"int": min(2, NT - 1), "t0": 0, "t1": 1, "tm2": NT - 2, "tm1": NT - 1}
    out = {}
    q = np.arange(128)
    k = np.arange(128)
    for name, offs in types.items():
        t = rep_t[name]
        tab = np.full((128, 8, 5, 128), NEGB, np.float32)
        r = 2 * t + q // 64
        c = q % 64
        rs = np.clip(r - 4, 0, rows - 8)
        cs = np.clip(c - 8, 0, 64 - 16)
        seen = set()
        for j, off in enumerate(offs):
            if off in seen:
                continue
            seen.add(off)
            kt = t + off
            if kt < 0 or kt >= NT:
                continue
            kr = 2 * kt + k // 64
            kc = k % 64
            valid = ((kr[:, None] >= rs[None, :]) & (kr[:, None] < rs[None, :] + 8) &
                     (kc[:, None] >= cs[None, :]) & (kc[:, None] < cs[None, :] + 16))
            dr = np.clip(kr[:, None] - r[None, :] + 7, 0, 14)
            dc = np.clip(kc[:, None] - c[None, :] + 15, 0, 30)
            g = rpb[:, dr, dc]
            g = np.where(valid[None], g, NEGB)
            tab[:, :, j, :] = g.transpose(1, 0, 2)
        out[name] = tab
    return out


def _rope_tables(T):
    half = 32
    inv = (10000.0 ** (-np.arange(half, dtype=np.float32) / half)).astype(np.float32)
    pos = np.arange(T, dtype=np.float32)
    ang = pos[None, :] * inv[:, None]
    cos = np.cos(ang).astype(np.float32)
    sin = np.sin(ang).astype(np.float32)
    cosT = np.concatenate([cos, cos, cos, cos], 0)
    sinS = np.concatenate([-sin, sin, -sin, sin], 0)
    return np.ascontiguousarray(cosT), np.ascontiguousarray(sinS)


def _attn_w_layout(w_in):
    qa = w_in[:, 0:512].reshape(1024, 8, 64)
    ka = w_in[:, 512:640].reshape(1024, 2, 64)
    va = w_in[:, 640:768]
    qb = w_in[:, 768:1280]
    kb = w_in[:, 1280:1792]
    vb = w_in[:, 1792:2304]
    order = [0, 4, 1, 5, 2, 6, 3, 7]
    qa2 = qa[:, order, :]
    sw = lambda a: np.concatenate([a[..., 32:], a[..., :32]], -1)
    cols = [qa2.reshape(1024, 512), sw(qa2).reshape(1024, 512), ka.reshape(1024, 128),
            sw(ka).reshape(1024, 128), qb, kb, va, vb]
    return np.ascontiguousarray(np.concatenate(cols, 1))


def build(Tlen, upto="all"):
    NT = Tlen // 128
    nc = bass.Bass("TRN2", target_bir_lowering=False)

    def din(name, shape):
        return nc.dram_tensor(name, list(shape), F32, kind="ExternalInput").ap()

    x_d = din("x", [Tlen, D])
    awin_d = din("awin", [D, 2944])
    awout_d = din("awout", [D, D])
    sink_d = din("sink", [1, 8])
    nbt_d = din("nbt", [5, 128, 8 * 5 * 128])
    cos_d = din("cosT", [128, Tlen])
    sin_d = din("sinS", [128, Tlen])
    ma_d = din("maskA", [128, 2 * 128])
    lng_d = din("lng", [4, D])
    lnb_d = din("lnb", [4, D])
    ident_d = din("ident", [128, 128])
    w1_d = [din("w1_%d" % i, [D, 4096]) for i in range(2)]
    w2_d = [din("w2_%d" % i, [4096, D]) for i in range(2)]
    swin_d = din("swin", [D, 6208])
    cw_d = din("cw", [128, 160])
    cb_d = din("cb", [128, 32])
    dtb_d = din("dtb", [1, 64])
    alog_d = din("alog", [1, 64])
    dsk_d = din("dsk", [1, 32])
    normw_d = din("normwT", [128, 16])
    swout_d = din("swout", [2048, D])
    tri_d = din("tri", [128, 4 * 128])
    stages = ["a", "m0", "c1", "c2", "d", "m1"]
    if upto != "all":
        stages = stages[:stages.index(upto) + 1]
    out_d = nc.dram_tensor("out", [Tlen, D], F32, kind="ExternalOutput").ap()

    def scratch(name, shape, dt=F32):
        return nc.dram_tensor(name, list(shape), dt, kind="Internal").ap()

    def act_out(stage, name):
        return out_d if stages[-1] == stage else scratch(name, [Tlen, D])

    x1_d = act_out("a", "x1")
    x2_d = act_out("m0", "x2")
    x3_d = act_out("d", "x3")
    xtok_d = scratch("xtok", [Tlen, 2048], BF16)
    btok_d = scratch("btok", [Tlen, 1024], BF16)
    bT_d = scratch("bT", [1024, Tlen], BF16)
    cT_d = scratch("cT", [1024, Tlen], BF16)
    zs_d = scratch("zs", [Tlen, 2048], BF16)
    dt_d = scratch("dtv", [Tlen, 64], F32)
    yf_d = scratch("yf", [Tlen, 2048], F32)
    acf_d = scratch("acf", [Tlen, 32], F32)
    acb_d = scratch("acb", [Tlen, 32], F32)
    acfT_d = scratch("acfT", [NT, 32 * 128], F32)
    acbT_d = scratch("acbT", [NT, 32 * 128], F32)
    SS = dict(xtok_d=xtok_d, btok_d=btok_d, bT_d=bT_d, cT_d=cT_d, zs_d=zs_d, dt_d=dt_d, yf_d=yf_d, tri_d=tri_d,
              alog_d=alog_d, dsk_d=dsk_d, acf_d=acf_d, acb_d=acb_d, acfT_d=acfT_d, acbT_d=acbT_d)

    with contextlib.ExitStack() as st0:
        S = Sched(nc, st0)
        C0 = Ctx(nc, st0)
        ident = C0.sb([128, 128], BF16, "ident")
        t_ident = T("ident")
        S.dma("pool", ident[:], ident_d, writes=[t_ident])
        eps_t = C0.sb([128, 1], F32, "eps")
        t_eps = T("eps")
        S.add("pool", lambda e: e.memset(eps_t[:], 1e-5), [], [t_eps])
        K = dict(ident=ident, t_ident=t_ident, eps_t=eps_t, t_eps=t_eps, lng_d=lng_d, lnb_d=lnb_d)

        with contextlib.ExitStack() as st:
            phase_a(nc, S, Ctx(nc, st), T_=Tlen, x_d=x_d, awin_d=awin_d, awout_d=awout_d, sink_d=sink_d, nbt_d=nbt_d,
                    cos_d=cos_d, sin_d=sin_d, ma_d=ma_d, x1_d=x1_d, **K)
            S.flush()
        if "m0" in stages:
            with contextlib.ExitStack() as st:
                phase_mlp(nc, S, Ctx(nc, st), T_=Tlen, xin_d=x1_d, xout_d=x2_d, w1_d=w1_d[0], w2_d=w2_d[0], ln_row=1, **K)
                S.flush()
        if "c1" in stages:
            with contextlib.ExitStack() as st:
                phase_c1(nc, S, Ctx(nc, st), T_=Tlen, xin_d=x2_d, swin_d=swin_d, cw_d=cw_d, cb_d=cb_d, dtb_d=dtb_d, **SS, **K)
                S.flush()
        if "c2" in stages:
            with contextlib.ExitStack() as st:
                phase_ssd(nc, S, Ctx(nc, st), T_=Tlen, bwd=False, **SS, **K)
                S.flush()
        if "d" in stages:
            with contextlib.ExitStack() as st:
                phase_ssd(nc, S, Ctx(nc, st), T_=Tlen, bwd=True, **SS, **K)
                S.flush()
            with contextlib.ExitStack() as st:
                phase_e(nc, S, Ctx(nc, st), T_=Tlen, yt_d=yf_d, zs_d=zs_d, normw_d=normw_d, swout_d=swout_d, xres_d=x2_d, xout_d=x3_d, ln_row=2, **K)
                S.flush()
        if "m1" in stages:
            with contextlib.ExitStack() as st:
                phase_mlp(nc, S, Ctx(nc, st), T_=Tlen, xin_d=x3_d, xout_d=out_d, w1_d=w1_d[1], w2_d=w2_d[1], ln_row=3, **K)
                S.flush()
    return nc


def phase_a(nc, S, C, T_, x_d, awin_d, awout_d, sink_d, nbt_d, cos_d, sin_d, ma_d, lng_d, lnb_d, ident, t_ident,
            eps_t, t_eps, x1_d):
    T_tok = T_
    NT = T_tok // 128
    NG = NT // 4
    SCALE = 0.125
    w_in = C.sb([128, 8, 2944], BF16, "a_win")
    t_win = T("a_win")
    wv = awin_d.rearrange("(kt p) n -> p kt n", p=128)
    for kt in range(8):
        S.dma("pool", w_in[:, kt, :], wv[:, kt, :], writes=[t_win])
    w_out = C.sb([128, 8, D], BF16, "a_wout")
    t_wout = T("a_wout")
    S.dma("pool", w_out[:], awout_d.rearrange("(kt p) n -> p kt n", p=128), writes=[t_wout])
    g_bc = C.sb([128, D], F32, "a_g")
    b_bc = C.sb([128, D], F32, "a_b")
    t_gb = T("a_gb")
    S.dma("sp", g_bc[:], lng_d[0:1, :].partition_broadcast(128), writes=[t_gb])
    S.dma("sp", b_bc[:], lnb_d[0:1, :].partition_broadcast(128), writes=[t_gb])
    maskA = C.sb([128, 2, 128], BF16, "maskA")
    t_maskA = T("maskA")
    S.dma("pool", maskA[:], ma_d.rearrange("p (a q) -> p a q", a=2), writes=[t_maskA])
    esink = C.sb([128, 8], F32, "esink")
    t_esink = T("esink")
    S.dma("sp", esink[:], sink_d.partition_broadcast(128), writes=[t_esink])
    S.add("act", lambda e: e.activation(out=esink[:], in_=esink[:], func=AF.Exp), [t_esink], [t_esink])

    EB = C.sb([128, 8, 5, 128], BF16, "EB")
    t_EB = T("EB")
    stage = [C.sb([128, 640], F32, "nbstage%d" % i) for i in range(2)]
    t_stage = [T("nbstage%d" % i) for i in range(2)]
    type_idx = {"int": 0, "t0": 1, "t1": 2, "tm2": 3, "tm1": 4}
    cur_type = [None]

    def load_type(name):
        if cur_type[0] == name:
            return
        cur_type[0] = name
        ti = type_idx[name]
        for hh in range(8):
            sl = hh % 2
            S.dma("sp", stage[sl][:], nbt_d[ti, :, hh * 640:(hh + 1) * 640], writes=[t_stage[sl]])
            S.add("act", lambda e, hh=hh, sl=sl: e.activation(out=EB[:, hh, :, :].rearrange("p j q -> p (j q)"), in_=stage[sl][:],
                                                                func=AF.Exp), [t_stage[sl]], [t_EB])

    xf = [C.sb([128, D], F32, "xf%d" % i) for i in range(2)]
    t_xf = [T("xf%d" % i) for i in range(2)]
    xb = [C.sb([128, D], BF16, "xb%d" % i) for i in range(2)]
    t_xb = [T("xb%d" % i) for i in range(2)]
    xT = [C.sb([128, 8, 512], BF16, "xT%d" % i) for i in range(2)]
    t_xT = [[T("xT%d_%d" % (i, j)) for j in range(4)] for i in range(2)]
    qaT = [C.sb([128, 4, 512], BF16, "qaT%d" % i) for i in range(2)]
    t_qaT = [[T("qaT%d_%d" % (i, c)) for c in range(4)] for i in range(2)]
    qbT = [C.sb([128, 4, 512], BF16, "qbT%d" % i) for i in range(2)]
    t_qbT = [[T("qbT%d_%d" % (i, c)) for c in range(4)] for i in range(2)]
    kaT = C.sb([128, 12 * 128], BF16, "kaT")
    kbT = C.sb([128, 4, 12 * 128], BF16, "kbT")
    t_kaT = [T("kaT%d" % i) for i in range(3)]
    t_kbT = [[T("kbT%d_%d" % (i, c)) for c in range(4)] for i in range(3)]
    va = C.sb([128, 12, 2, 65], BF16, "va")
    vb = C.sb([128, 12, 8, 65], BF16, "vb")
    t_v = [T("v%d" % i) for i in range(12)]
    S.add("pool", lambda e: e.memset(va[:], 1.0), [], t_v)
    S.add("pool", lambda e: e.memset(vb[:], 1.0), [], t_v)
    cs_t = C.sb([128, 2, 512], F32, "cossin")
    t_cs = T("cossin")
    r1 = C.sb([128, 512], F32, "rope1")
    r2 = C.sb([128, 512], F32, "rope2")
    t_r1, t_r2 = T("r1"), T("r2")
    esA = [C.sb([128, 3, 512], BF16, "esA%d" % i) for i in range(2)]
    t_esA = [[T("esA%d_%d" % (i, j)) for j in range(3)] for i in range(2)]
    esB = [C.sb([128, 5, 128], BF16, "esB%d" % i) for i in range(2)]
    t_esB = [T("esB%d" % i) for i in range(2)]
    o_tok = C.sb([128, D], BF16, "o_tok")
    t_otok = T("o_tok")
    den = C.sb([128, 16], F32, "den")
    t_den = T("den")
    oT = C.sb([128, 8, 128], BF16, "oT")
    t_oT = T("oT")
    xres = [C.sb([128, D], F32, "xres%d" % i) for i in range(2)]
    t_xres = [T("xres%d" % i) for i in range(2)]
    lnb = {"h": (C.sb([128, D], F32, "ln_h"), T("ln_h")), "st": (C.sb([128, 12], F32, "ln_st"), T("ln_st")),
           "mv": (C.sb([128, 4], F32, "ln_mv"), T("ln_mv"))}
    xo = [C.sb([128, D], F32, "xo%d" % i) for i in range(2)]
    t_xo = [T("xo%d" % i) for i in range(2)]

    pP = [C.ps([128, 512], F32, "pP%d" % i) for i in range(2)]
    t_pP = [T("pP%d" % i) for i in range(2)]
    pS = [C.ps([128, 512], F32, "pS%d" % i) for i in range(4)]
    t_pS = [T("pS%d" % i) for i in range(4)]
    pO = [C.ps([128, 512], F32, "pO%d" % i) for i in range(2)]
    t_pO = [T("pO%d" % i) for i in range(2)]
    pcount = [0]

    def next_pP():
        i = pcount[0] % 2
        pcount[0] += 1
        return pP[i], t_pP[i]

    COL = {"qa": 0, "qas": 512, "ka": 1024, "kas": 1152, "qb": 1280, "kb": 1792, "va": 2304, "vb": 2432}

    def project_group(g):
        gb = g % 2
        rg = g % 3
        for i in range(4):
            t = 4 * g + i
            s = t % 2
            S.dma("sp", xf[s][:], x_d[t * 128:(t + 1) * 128, :], writes=[t_xf[s]])
            S.add("dve", lambda e, s=s: e.tensor_copy(out=xb[s][:], in_=xf[s][:]), [t_xf[s]], [t_xb[s]])
            pp, t_pp = next_pP()
            ppb = pp[:].bitcast(BF16)
            for kt in range(8):
                S.add("pe", lambda e, s=s, kt=kt, ppb=ppb: e.transpose(ppb[:, kt * 128:(kt + 1) * 128], xb[s][:, kt * 128:(kt + 1) * 128], ident[:]),
                      [t_xb[s], t_ident], [t_pp])
            S.add("act", lambda e, i=i, gb=gb, ppb=ppb: e.copy(out=xT[gb][:, :, i * 128:(i + 1) * 128],
                                                               in_=ppb.rearrange("p (k t) -> p k t", k=8)),
                  [t_pp], [t_xT[gb][i]])
        S.dma("sp", cs_t[:, 0, :], cos_d[:, g * 512:(g + 1) * 512], writes=[t_cs])
        S.dma("sp", cs_t[:, 1, :], sin_d[:, g * 512:(g + 1) * 512], writes=[t_cs])

        def fm_proj(col):
            pp, t_pp = next_pP()
            for kt in range(8):
                S.add("pe", lambda e, kt=kt, col=col, pp=pp: e.matmul(pp[:], lhsT=w_in[:, kt, col:col + 128], rhs=xT[gb][:, kt, :],
                                                                     start=(kt == 0), stop=(kt == 7)),
                      [t_win] + t_xT[gb], [t_pp])
            return pp, t_pp

        def rope_half(col, which, rdst, t_rdst):
            pp, t_pp = fm_proj(col)
            S.add("dve", lambda e, pp=pp: e.tensor_tensor(out=rdst[:], in0=pp[:], in1=cs_t[:, which, :], op=ALU.mult), [t_pp, t_cs], [t_rdst])

        def rope_add(dst, t_dst):
            S.add("pool", lambda e: e.tensor_tensor(out=dst, in0=r1[:], in1=r2[:], op=ALU.add), [t_r1, t_r2], [t_dst])

        rope_half(COL["ka"], 0, r1, t_r1)
        rope_half(COL["kas"], 1, r2, t_r2)
        rope_add(kaT[:, rg * 512:(rg + 1) * 512], t_kaT[rg])
        for c in range(4):
            pp, t_pp = fm_proj(COL["kb"] + c * 128)
            S.add("act", lambda e, pp=pp, c=c: e.copy(out=kbT[:, c, rg * 512:(rg + 1) * 512], in_=pp[:]), [t_pp], [t_kbT[rg][c]])
        for i in range(4):
            t = 4 * g + i
            slot = t % 12
            pp, t_pp = next_pP()
            for kt in range(8):
                S.add("pe", lambda e, kt=kt, pp=pp, i=i: e.matmul(pp[:], lhsT=xT[gb][:, kt, i * 128:(i + 1) * 128],
                                                                 rhs=w_in[:, kt, COL["vb"]:COL["vb"] + 512],
                                                                 start=(kt == 0), stop=(kt == 7)),
                      [t_win, t_xT[gb][i]], [t_pp])
            S.add("act", lambda e, pp=pp, slot=slot: e.copy(out=vb[:, slot, :, 0:64], in_=pp[:].rearrange("p (h d) -> p h d", h=8)),
                  [t_pp], [t_v[slot]])
            pp, t_pp = next_pP()
            for kt in range(8):
                S.add("pe", lambda e, kt=kt, pp=pp, i=i: e.matmul(pp[:, 0:128], lhsT=xT[gb][:, kt, i * 128:(i + 1) * 128],
                                                                 rhs=w_in[:, kt, COL["va"]:COL["va"] + 128],
                                                                 start=(kt == 0), stop=(kt == 7)),
                      [t_win, t_xT[gb][i]], [t_pp])
            S.add("act", lambda e, pp=pp, slot=slot: e.copy(out=va[:, slot, :, 0:64], in_=pp[:, 0:128].rearrange("p (h d) -> p h d", h=2)),
                  [t_pp], [t_v[slot]])
        yield "K"
        for c in range(4):
            rope_half(COL["qa"] + c * 128, 0, r1, t_r1)
            yield None
            rope_half(COL["qas"] + c * 128, 1, r2, t_r2)
            rope_add(qaT[gb][:, c, :], t_qaT[gb][c])
            yield None
        for c in range(4):
            pp, t_pp = fm_proj(COL["qb"] + c * 128)
            S.add("act", lambda e, pp=pp, c=c: e.copy(out=qbT[gb][:, c, :], in_=pp[:]), [t_pp], [t_qbT[gb][c]])
            yield None

    es_cnt = [0, 0]
    ps_rr = [0]

    def attend_tile(t, pump):
        g = t // 4
        gb = g % 2
        i = t % 4
        qsl = slice(i * 128, (i + 1) * 128)
        js = [j for j in (-1, 0, 1) if 0 <= t + j < NT]

        ebs = []
        for gi in range(2):
            eb = es_cnt[0] % 2
            es_cnt[0] += 1
            ebs.append(eb)
            for j in js:
                tk = t + j
                rgk = (tk // 4) % 3
                kof = (tk % 12) * 128
                bi = ps_rr[0] % 4
                ps_rr[0] += 1
                ps, t_ps = pS[bi], t_pS[bi]
                S.add("pe", lambda e, ps=ps, kof=kof, gi=gi: e.matmul(ps[:].rearrange("p (c q) -> p c q", c=4),
                                                                      lhsT=kaT[gi * 64:(gi + 1) * 64, kof:kof + 128],
                                                                      rhs=qaT[gb][gi * 64:(gi + 1) * 64, :, qsl], start=True, stop=True),
                      [t_kaT[rgk]] + t_qaT[gb], [t_ps])
                S.add("act", lambda e, ps=ps, eb=eb, j=j: e.activation(out=esA[eb][:, j + 1, :], in_=ps[:], func=AF.Exp, scale=SCALE),
                      [t_ps], [t_esA[eb][j + 1]])
                if j != 0:
                    mi = 0 if j == -1 else 1
                    S.add("dve", lambda e, eb=eb, j=j, mi=mi: e.tensor_tensor(out=esA[eb][:, j + 1, :].rearrange("p (c q) -> p c q", c=4),
                                                                              in0=esA[eb][:, j + 1, :].rearrange("p (c q) -> p c q", c=4),
                                                                              in1=maskA[:, mi:mi + 1, :].to_broadcast([128, 4, 128]), op=ALU.mult),
                          [t_esA[eb][j + 1], t_maskA], [t_esA[eb][j + 1]])
            pump()
        for gi in range(2):
            eb = ebs[gi]
            for c in range(4):
                for n, j in enumerate(js):
                    slot = (t + j) % 12
                    S.add("pe", lambda e, c=c, j=j, slot=slot, gi=gi, eb=eb, n=n, nj=len(js): e.matmul(pO[gi][:, c * 65:(c + 1) * 65],
                                                                                                    lhsT=esA[eb][:, j + 1, c * 128:(c + 1) * 128],
                                                                                                    rhs=va[:, slot, gi, :], start=(n == 0), stop=(n == nj - 1)),
                          [t_esA[eb][j + 1], t_v[slot]], [t_pO[gi]])
        for gi in range(2):
            ov = pO[gi][:, 0:260].rearrange("p (c e) -> p c e", c=4)
            S.add("dve", lambda e, gi=gi, ov=ov: e.tensor_tensor(out=den[:, gi * 4:(gi + 1) * 4], in0=ov[:, :, 64], in1=esink[:, gi * 4:(gi + 1) * 4], op=ALU.add),
                  [t_pO[gi], t_esink], [t_den])
            S.add("dve", lambda e, gi=gi: e.reciprocal(out=den[:, gi * 4:(gi + 1) * 4], in_=den[:, gi * 4:(gi + 1) * 4]), [t_den], [t_den])
            S.add("dve", lambda e, gi=gi, ov=ov: e.tensor_tensor(out=o_tok[:, gi * 256:(gi + 1) * 256].rearrange("p (c d) -> p c d", c=4), in0=ov[:, :, 0:64],
                                                                 in1=den[:, gi * 4:(gi + 1) * 4].unsqueeze(2).to_broadcast([128, 4, 64]), op=ALU.mult),
                  [t_pO[gi], t_den], [t_otok])

        typ = _na_type_of(t, NT)
        load_type(typ)
        offs = _na_types(NT)[typ]

        def b_scores(hh):
            c, pb = hh // 2, (hh % 2) * 64
            sb_i = es_cnt[1] % 2
            es_cnt[1] += 1
            pa, t_pa = pS[2 * sb_i], t_pS[2 * sb_i]
            pb2, t_pb2 = pS[2 * sb_i + 1], t_pS[2 * sb_i + 1]
            for j, off in enumerate(offs):
                tk = t + off
                rgk = (tk // 4) % 3
                kof = (tk % 12) * 128
                dst, t_dst = (pa[:, j * 128:(j + 1) * 128], t_pa) if j < 4 else (pb2[:, 0:128], t_pb2)
                S.add("pe", lambda e, dst=dst, kof=kof, c=c, pb=pb: e.matmul(dst, lhsT=kbT[pb:pb + 64, c, kof:kof + 128],
                                                                             rhs=qbT[gb][pb:pb + 64, c, qsl], start=True, stop=True),
                      [t_kbT[rgk][c], t_qbT[gb][c]], [t_dst])
            S.add("act", lambda e, pa=pa, sb_i=sb_i: e.activation(out=esB[sb_i][:, 0:4, :].rearrange("p j q -> p (j q)"), in_=pa[:], func=AF.Exp, scale=SCALE),
                  [t_pa], [t_esB[sb_i]])
            S.add("act", lambda e, pb2=pb2, sb_i=sb_i: e.activation(out=esB[sb_i][:, 4, :], in_=pb2[:, 0:128], func=AF.Exp, scale=SCALE),
                  [t_pb2], [t_esB[sb_i]])
            S.add("dve", lambda e, sb_i=sb_i, hh=hh: e.tensor_tensor(out=esB[sb_i][:], in0=esB[sb_i][:], in1=EB[:, hh, :, :], op=ALU.mult),
                  [t_esB[sb_i], t_EB], [t_esB[sb_i]])
            return sb_i

        def b_pv(hh, sb_i):
            ob = hh // 4
            for j, off in enumerate(offs):
                slot = (t + off) % 12
                S.add("pe", lambda e, j=j, slot=slot, hh=hh, sb_i=sb_i, ob=ob: e.matmul(pO[ob][:, (hh % 4) * 65:(hh % 4 + 1) * 65],
                                                                                       lhsT=esB[sb_i][:, j, :], rhs=vb[:, slot, hh, :],
                                                                                       start=(j == 0), stop=(j == 4)),
                      [t_esB[sb_i], t_v[slot]], [t_pO[ob]])

        prev = None
        for hh in range(8):
            sb_i = b_scores(hh)
            pump()
            if prev is not None:
                b_pv(*prev)
            prev = (hh, sb_i)
        b_pv(*prev)
        for ob in range(2):
            ov = pO[ob][:, 0:260].rearrange("p (c e) -> p c e", c=4)
            S.add("dve", lambda e, ob=ob, ov=ov: e.reciprocal(out=den[:, 8 + ob * 4:8 + (ob + 1) * 4], in_=ov[:, :, 64]), [t_pO[ob]], [t_den])
            S.add("dve", lambda e, ob=ob, ov=ov: e.tensor_tensor(out=o_tok[:, 512 + ob * 256:512 + (ob + 1) * 256].rearrange("p (c d) -> p c d", c=4), in0=ov[:, :, 0:64],
                                                                 in1=den[:, 8 + ob * 4:8 + (ob + 1) * 4].unsqueeze(2).to_broadcast([128, 4, 64]), op=ALU.mult),
                  [t_pO[ob], t_den], [t_otok])
        pp, t_pp = next_pP()
        ppb = pp[:].bitcast(BF16)
        for kt in range(8):
            S.add("pe", lambda e, kt=kt, ppb=ppb: e.transpose(ppb[:, kt * 128:(kt + 1) * 128], o_tok[:, kt * 128:(kt + 1) * 128], ident[:]),
                  [t_otok, t_ident], [t_pp])
        S.add("act", lambda e, ppb=ppb: e.copy(out=oT[:], in_=ppb.rearrange("p (k t) -> p k t", k=8)), [t_pp], [t_oT])
        xr = t % 2
        S.dma("sp", xres[xr][:], x_d[t * 128:(t + 1) * 128, :], writes=[t_xres[xr]])
        pump()
        halves = []
        for hf in range(2):
            pp, t_pp = next_pP()
            for kt in range(8):
                S.add("pe", lambda e, kt=kt, pp=pp, hf=hf: e.matmul(pp[:], lhsT=oT[:, kt, :], rhs=w_out[:, kt, hf * 512:(hf + 1) * 512],
                                                                   start=(kt == 0), stop=(kt == 7)),
                      [t_oT, t_wout], [t_pp])
            halves.append((pp, t_pp))
        h, t_h = lnb["h"]
        for hf, (pp, t_pp) in enumerate(halves):
            S.add("dve", lambda e, pp=pp, hf=hf, xr=xr: e.scalar_tensor_tensor(out=h[:, hf * 512:(hf + 1) * 512], in0=xres[xr][:, hf * 512:(hf + 1) * 512],
                                                                              scalar=ALPHA, in1=pp[:], op0=ALU.mult, op1=ALU.add),
                  [t_xres[xr], t_pp], [t_h])
        ln_core(S, lnb, g_bc[:], b_bc[:], t_gb, eps_t, t_eps, xo[xr][:], t_xo[xr])
        S.dma("pool", x1_d[t * 128:(t + 1) * 128, :], xo[xr][:], reads=[t_xo[xr]])

    for g in range(NG + 1):
        gen = project_group(g) if g < NG else iter(())
        for u in gen:
            if u == "K":
                break

        def pump(gen=gen):
            next(gen, None)

        if g >= 1:
            for t in range(4 * (g - 1), 4 * g):
                attend_tile(t, pump)
        for u in gen:
            pass


def phase_c1(nc, S, C, T_, xin_d, swin_d, cw_d, cb_d, dtb_d, xtok_d, btok_d, bT_d, cT_d, zs_d, dt_d,
             ident, t_ident, tri_d=None, alog_d=None, acf_d=None, acb_d=None, acfT_d=None, acbT_d=None, **_):
    NT = T_ // 128
    NG2 = NT // 2
    w_in = C.sb([128, 8, 6208], BF16, "c_win")
    NBLK = 13
    t_w = [T() for i in range(NBLK)]
    wv = swin_d.rearrange("(kt p) n -> p kt n", p=128)
    order = [4, 5, 6, 7, 8, 9, 10, 11, 12, 0, 1, 2, 3]
    for bi in order:
        c0, c1 = bi * 512, min(6208, (bi + 1) * 512)
        S.dma("pool", w_in[:, :, c0:c1], wv[:, :, c0:c1], writes=[t_w[bi]])

    def wdeps(c0, c1):
        return [t_w[b] for b in range(c0 // 512, (c1 - 1) // 512 + 1)]

    cw = C.sb([128, 160], F32, "c_cw")
    cb = C.sb([128, 32], F32, "c_cb")
    t_cw = T()
    S.dma("sp", cw[:], cw_d, writes=[t_cw])
    S.dma("sp", cb[:], cb_d, writes=[t_cw])
    dtb = C.sb([128, 64], F32, "c_dtb")
    S.dma("sp", dtb[:], dtb_d.partition_broadcast(128), writes=[t_cw])
    one_t = C.sb([128, 1], F32, "c_one")
    S.add("pool", lambda e: e.memset(one_t[:], 1.0), [], [t_cw])
    cmask = C.sb([128, 4, 128], BF16, "c_masks")
    S.dma("pool", cmask[:], tri_d.rearrange("p (a q) -> p a q", a=4), writes=[t_cw])
    aneg = C.sb([128, 64], F32, "c_aneg")
    t_an = T()
    S.dma("sp", aneg[:], alog_d.partition_broadcast(128), writes=[t_an])
    S.add("act", lambda e: e.activation(out=aneg[:], in_=aneg[:], func=AF.Exp), [t_an], [t_an])
    S.add("dve", lambda e: e.tensor_scalar(out=aneg[:], in0=aneg[:], scalar1=-1.0, scalar2=None, op0=ALU.mult), [t_an], [t_an])
    av = C.sb([128, 64], F32, "c_av")
    av_hi = C.sb([128, 64], BF16, "c_avhi")
    av_lo = C.sb([128, 64], BF16, "c_avlo")
    acs = C.sb([128, 64], F32, "c_acs")
    t_av = T()
    t_acs = T()
    acsT = C.sb([32, 256], F32, "c_acsT")
    t_acsT = T()
    acc = [C.sb([128, 256], F32, "c_acc%d" % i) for i in range(4)]
    t_acc = [T() for i in range(4)]
    xb = [C.sb([128, D], BF16, "c_xb%d" % i) for i in range(2)]
    t_xb = [T(), T()]
    xT = [C.sb([128, 8, 260], BF16, "c_xT%d" % i) for i in range(2)]
    t_xT = [[T(), T(), T()] for i in range(2)]
    u = [C.sb([128, 260], BF16, "c_u%d" % i) for i in range(2)]
    t_u = [T(), T()]
    xsT = C.sb([128, 16, 256], BF16, "c_xsT")
    t_xsT = [T() for i in range(16)]
    BT = C.sb([128, 8, 256], BF16, "c_BT")
    t_BT = [T() for i in range(8)]
    CT = C.sb([128, 8, 256], BF16, "c_CT")
    t_CT = [T() for i in range(8)]
    xtok = C.sb([128, 2048], BF16, "c_xtok")
    t_xtok = T()
    btok = C.sb([128, 1024], BF16, "c_btok")
    t_btok = T()
    zs = C.sb([128, 2048], BF16, "c_zs")
    t_zs = T()
    dv = C.sb([128, 4, 64], F32, "c_dv")
    t_dv = T()
    pT = [C.ps([128, 512], F32, "c_pT%d" % i) for i in range(2)]
    t_pT = [T(), T()]
    pU = [C.ps([128, 512], F32, "c_pU%d" % i) for i in range(4)]
    t_pU = [T() for i in range(4)]
    pZ = [C.ps([128, 512], F32, "c_pZ%d" % i) for i in range(2)]
    t_pZ = [T(), T()]
    cnt = {"T": 0, "U": 0, "C": 0, "Z": 0, "x": 0, "A": 0}

    def nxt(key, arr, tarr):
        i = cnt[key] % len(arr)
        cnt[key] += 1
        return arr[i], tarr[i]

    bTv = bT_d.rearrange("(g n) t -> n g t", n=128)
    cTv = cT_d.rearrange("(g n) t -> n g t", n=128)

    for g2 in range(NG2):
        gb = g2 % 2
        t0 = 256 * g2
        pieces = [(t0 - 2, 128, 0), (t0 + 126, 128, 128), (t0 + 254, 4, 256)]
        for pi_, (r0, nr, c0) in enumerate(pieces):
            s_ = cnt["x"] % 2
            cnt["x"] += 1
            lo, hi = max(r0, 0), min(r0 + nr, T_)
            if lo > r0 or hi < r0 + nr:
                S.add("pool", lambda e, s_=s_: e.memset(xb[s_][:], 0.0), [], [t_xb[s_]])
            S.dma("pool", xb[s_][lo - r0:hi - r0, :], xin_d[lo:hi, :], writes=[t_xb[s_]])
            pp, t_pp = nxt("T", pT, t_pT)
            ppb = pp[:].bitcast(BF16)
            for kt in range(8):
                S.add("pe", lambda e, s_=s_, kt=kt, ppb=ppb, nr=nr: e.transpose(ppb[:, kt * 128:kt * 128 + nr], xb[s_][0:nr, kt * 128:(kt + 1) * 128], ident[0:nr, 0:nr]),
                      [t_xb[s_], t_ident], [t_pp])
            S.add("dve", lambda e, gb=gb, ppb=ppb, nr=nr, c0=c0: e.tensor_copy(out=xT[gb][:, :, c0:c0 + nr], in_=ppb.rearrange("p (k t) -> p k t", k=8)[:, :, 0:nr]),
                  [t_pp], [t_xT[gb][pi_]])
        for c in range(32):
            col = 2048 + c * 128
            pu, t_pu = nxt("U", pU, t_pU)
            for kt in range(8):
                S.add("pe", lambda e, kt=kt, col=col, pu=pu, gb=gb: e.matmul(pu[:, 0:260], lhsT=w_in[:, kt, col:col + 128], rhs=xT[gb][:, kt, :],
                                                                            start=(kt == 0), stop=(kt == 7)),
                      wdeps(col, col + 128) + t_xT[gb], [t_pu])
            if c < 16:
                dst, t_dst = xsT[:, c, :], t_xsT[c]
            elif c < 24:
                dst, t_dst = BT[:, c - 16, :], t_BT[c - 16]
            else:
                dst, t_dst = CT[:, c - 24, :], t_CT[c - 24]
            ai = cnt["A"] % 4
            cnt["A"] += 1
            ui = c % 2
            S.add("act", lambda e, ui=ui, pu=pu: e.copy(out=u[ui][:], in_=pu[:, 0:260]), [t_pu], [t_u[ui]])
            S.add("dve", lambda e, ui=ui, ai=ai, c=c: e.tensor_scalar(out=acc[ai][:], in0=u[ui][:, 0:256], scalar1=cw[:, c:c + 1], scalar2=None, op0=ALU.mult),
                  [t_u[ui], t_cw], [t_acc[ai]])
            for k in range(1, 5):
                S.add("dve", lambda e, ui=ui, ai=ai, c=c, k=k: e.scalar_tensor_tensor(out=acc[ai][:], in0=u[ui][:, k:k + 256], scalar=cw[:, k * 32 + c:k * 32 + c + 1],
                                                                                     in1=acc[ai][:], op0=ALU.mult, op1=ALU.add),
                      [t_u[ui], t_cw, t_acc[ai]], [t_acc[ai]])
            S.add("act", lambda e, ai=ai, dst=dst, c=c: e.activation(out=dst, in_=acc[ai][:], func=AF.Silu, bias=cb[:, c:c + 1], scale=1.0),
                  [t_acc[ai], t_cw], [t_dst])
        S.dma("sp", bTv[:, :, t0:t0 + 256], BT[:], reads=t_BT)
        S.dma("sp", cTv[:, :, t0:t0 + 256], CT[:], reads=t_CT)
        for i in range(2):
            t = 2 * g2 + i
            for qz in range(4):
                pz, t_pz = nxt("Z", pZ, t_pZ)
                for kt in range(8):
                    S.add("pe", lambda e, kt=kt, qz=qz, pz=pz, gb=gb, i=i: e.matmul(pz[:], lhsT=xT[gb][:, kt, 2 + i * 128:2 + (i + 1) * 128],
                                                                                   rhs=w_in[:, kt, qz * 512:(qz + 1) * 512], start=(kt == 0), stop=(kt == 7)),
                          wdeps(qz * 512, (qz + 1) * 512) + t_xT[gb], [t_pz])
                S.add("act", lambda e, pz=pz, qz=qz: e.activation(out=zs[:, qz * 512:(qz + 1) * 512], in_=pz[:], func=AF.Silu), [t_pz], [t_zs])
            S.dma("sp", zs_d[t * 128:(t + 1) * 128, :], zs[:], reads=[t_zs])
            for hf in range(2):
                pp, t_pp = nxt("T", pT, t_pT)
                ppb = pp[:].bitcast(BF16)
                for cc in range(8):
                    c = hf * 8 + cc
                    S.add("pe", lambda e, c=c, cc=cc, ppb=ppb, i=i: e.transpose(ppb[:, cc * 128:(cc + 1) * 128], xsT[:, c, i * 128:(i + 1) * 128], ident[:]),
                          [t_xsT[c], t_ident], [t_pp])
                S.add("dve", lambda e, ppb=ppb, hf=hf: e.tensor_copy(out=xtok[:, hf * 1024:(hf + 1) * 1024], in_=ppb), [t_pp], [t_xtok])
            S.dma("sp", xtok_d[t * 128:(t + 1) * 128, :], xtok[:], reads=[t_xtok])
            pp, t_pp = nxt("T", pT, t_pT)
            ppb = pp[:].bitcast(BF16)
            for gg in range(8):
                S.add("pe", lambda e, gg=gg, ppb=ppb, i=i: e.transpose(ppb[:, gg * 128:(gg + 1) * 128], BT[:, gg, i * 128:(i + 1) * 128], ident[:]),
                      [t_BT[gg], t_ident], [t_pp])
            S.add("dve", lambda e, ppb=ppb: e.tensor_copy(out=btok[:], in_=ppb), [t_pp], [t_btok])
            S.dma("sp", btok_d[t * 128:(t + 1) * 128, :], btok[:], reads=[t_btok])
        for i in range(2):
            t = 2 * g2 + i
            pz, t_pz = nxt("Z", pZ, t_pZ)
            for kt in range(8):
                S.add("pe", lambda e, kt=kt, pz=pz, gb=gb, i=i: e.matmul(pz[:, 0:64], lhsT=xT[gb][:, kt, 2 + i * 128:2 + (i + 1) * 128],
                                                                        rhs=w_in[:, kt, 6144:6208], start=(kt == 0), stop=(kt == 7)),
                      wdeps(6144, 6208) + t_xT[gb], [t_pz])
            S.add("dve", lambda e, pz=pz: e.tensor_tensor(out=dv[:, 0, :], in0=pz[:, 0:64], in1=dtb[:], op=ALU.add), [t_pz, t_cw], [t_dv])
            S.add("act", lambda e: e.activation(out=dv[:, 1, :], in_=dv[:, 0, :], func=AF.Abs), [t_dv], [t_dv])
            S.add("act", lambda e: e.activation(out=dv[:, 1, :], in_=dv[:, 1, :], func=AF.Exp, scale=-1.0), [t_dv], [t_dv])
            S.add("act", lambda e: e.activation(out=dv[:, 1, :], in_=dv[:, 1, :], func=AF.Ln, bias=one_t[:], scale=1.0), [t_dv, t_cw], [t_dv])
            S.add("dve", lambda e: e.tensor_scalar(out=dv[:, 2, :], in0=dv[:, 0, :], scalar1=0.0, scalar2=None, op0=ALU.max), [t_dv], [t_dv])
            S.add("dve", lambda e: e.tensor_tensor(out=dv[:, 3, :], in0=dv[:, 2, :], in1=dv[:, 1, :], op=ALU.add), [t_dv], [t_dv])
            S.dma("sp", dt_d[t * 128:(t + 1) * 128, :], dv[:, 3, :], reads=[t_dv])
            S.add("dve", lambda e: e.tensor_tensor(out=av[:], in0=dv[:, 3, :], in1=aneg[:], op=ALU.mult), [t_dv, t_an], [t_av])
            S.add("dve", lambda e: e.tensor_copy(out=av_hi[:], in_=av[:]), [t_av], [t_av])
            S.add("dve", lambda e: e.tensor_tensor(out=av_lo[:], in0=av[:], in1=av_hi[:], op=ALU.subtract), [t_av], [t_av])
            pz, t_pz = nxt("Z", pZ, t_pZ)
            for ci, mi in ((0, 0), (1, 2)):
                S.add("pe", lambda e, ci=ci, mi=mi, pz=pz: e.matmul(pz[:, ci * 32:(ci + 1) * 32], lhsT=cmask[:, mi, :], rhs=av_hi[:, ci * 32:(ci + 1) * 32], start=True, stop=False),
                      [t_av, t_cw], [t_pz])
                S.add("pe", lambda e, ci=ci, mi=mi, pz=pz: e.matmul(pz[:, ci * 32:(ci + 1) * 32], lhsT=cmask[:, mi, :], rhs=av_lo[:, ci * 32:(ci + 1) * 32], start=False, stop=True),
                      [t_av, t_cw], [t_pz])
            S.add("dve", lambda e, pz=pz: e.tensor_copy(out=acs[:], in_=pz[:, 0:64]), [t_pz], [t_acs])
            S.dma("sp", acf_d[t * 128:(t + 1) * 128, :], acs[:, 0:32], reads=[t_acs])
            S.dma("sp", acb_d[t * 128:(t + 1) * 128, :], acs[:, 32:64], reads=[t_acs])
            pz, t_pz = nxt("Z", pZ, t_pZ)
            for ci, mi in ((0, 0), (1, 2)):
                S.add("pe", lambda e, ci=ci, mi=mi, pz=pz: e.matmul(pz[0:32, ci * 128:(ci + 1) * 128], lhsT=av_hi[:, ci * 32:(ci + 1) * 32], rhs=cmask[:, mi, :], start=True, stop=False),
                      [t_av, t_cw], [t_pz])
                S.add("pe", lambda e, ci=ci, mi=mi, pz=pz: e.matmul(pz[0:32, ci * 128:(ci + 1) * 128], lhsT=av_lo[:, ci * 32:(ci + 1) * 32], rhs=cmask[:, mi, :], start=False, stop=True),
                      [t_av, t_cw], [t_pz])
            S.add("dve", lambda e, pz=pz: e.tensor_copy(out=acsT[:], in_=pz[0:32, 0:256]), [t_pz], [t_acsT])
            S.dma("sp", acfT_d[t, :].rearrange("(h l) -> h l", h=32), acsT[:, 0:128], reads=[t_acsT])
            S.dma("sp", acbT_d[t, :].rearrange("(h l) -> h l", h=32), acsT[:, 128:256], reads=[t_acsT])


def phase_ssd(nc, S, C, T_, bwd, xtok_d, btok_d, bT_d, cT_d, dt_d, yf_d, tri_d, alog_d, dsk_d, acf_d, acb_d,
              acfT_d=None, acbT_d=None,
              zs_d=None, normw_d=None, swout_d=None, xres_d=None, xout_d=None, ln_row=2,
              ident=None, t_ident=None, eps_t=None, t_eps=None, lng_d=None, lnb_d=None, **_):
    NT = T_ // 128
    masks = C.sb([128, 4, 128], BF16, "s_masks")
    t_c = T()
    S.dma("pool", masks[:], tri_d.rearrange("p (a q) -> p a q", a=4), writes=[t_c])
    M1 = masks[:, 2, :] if bwd else masks[:, 0, :]
    M2 = masks[:, 3, :] if bwd else masks[:, 1, :]
    ones_m = C.sb([128, 128], BF16, "s_ones")
    S.add("pool", lambda e: e.memset(ones_m[:], 1.0), [], [t_c])
    zero_t = C.sb([128, 1], F32, "s_zero")
    S.add("pool", lambda e: e.memset(zero_t[:], 0.0), [], [t_c])
    aneg = C.sb([128, 32], F32, "s_aneg")
    h0 = 32 if bwd else 0
    S.dma("sp", aneg[:], alog_d[:, h0:h0 + 32].partition_broadcast(128), writes=[t_c])
    S.add("act", lambda e: e.activation(out=aneg[:], in_=aneg[:], func=AF.Exp), [t_c], [t_c])
    S.add("dve", lambda e: e.tensor_scalar(out=aneg[:], in0=aneg[:], scalar1=-1.0, scalar2=None, op0=ALU.mult), [t_c], [t_c])
    if not bwd:
        dsk = C.sb([128, 32], F32, "s_dsk")
        S.dma("sp", dsk[:], dsk_d.partition_broadcast(128), writes=[t_c])
        sk = [C.sb([128, 2048], F32, "s_sk%d" % i) for i in range(2)]
        t_sk = [T(), T()]
    else:
        yf = [C.sb([128, 2048], F32, "s_yf%d" % i) for i in range(2)]
        t_yf = [T(), T()]
    R2 = range(2)
    xtok = [C.sb([128, 32, 64], BF16, "s_xtok%d" % i) for i in R2]
    t_xtok = [T() for i in R2]
    btok = [C.sb([128, 1024], BF16, "s_btok%d" % i) for i in range(3)]
    t_btok = [T() for i in range(3)]
    BT = [C.sb([128, 8, 128], BF16, "s_BT%d" % i) for i in R2]
    t_BT = [T() for i in R2]
    CT = [C.sb([128, 8, 128], BF16, "s_CT%d" % i) for i in R2]
    t_CT = [T() for i in R2]
    dtt = [C.sb([128, 64], F32, "s_dt%d" % i) for i in R2]
    t_dt = [T() for i in R2]
    acT_d = acbT_d if bwd else acfT_d
    ac_d = acb_d if bwd else acf_d
    Rb = [C.sb([128, 32, 128], F32, "s_Rb%d" % i) for i in R2]
    t_Rb = [T() for i in R2]
    act_ = [C.sb([128, 32], F32, "s_act%d" % i) for i in R2]
    t_act = [T() for i in R2]
    a_f = [C.sb([128, 32], F32, "s_a%d" % i) for i in R2]
    a_hi = [C.sb([128, 32], BF16, "s_ahi%d" % i) for i in R2]
    a_lo = [C.sb([128, 32], BF16, "s_alo%d" % i) for i in R2]
    t_a = [T() for i in R2]
    esm = [C.sb([128, 96], F32, "s_esm%d" % i) for i in R2]
    t_esm = [T() for i in R2]
    GTm = [C.sb([128, 8, 128], BF16, "s_GTm%d" % i) for i in R2]
    t_GTm = [T() for i in R2]
    Dm = [C.sb([128, 32, 128], BF16, "s_Dm%d" % i) for i in R2]
    t_Dm = [[T() for q in range(8)] for i in R2]
    xdt = [C.sb([128, 32, 64], BF16, "s_xdt%d" % i) for i in R2]
    t_xdt = [T() for i in R2]
    xdd = [C.sb([128, 32, 64], BF16, "s_xdd%d" % i) for i in R2]
    t_xdd = [T() for i in R2]
    Xc = C.sb([128, 16, 128], F32, "s_Xc")
    t_Xc = [T() for i in range(4)]
    CE = [C.sb([128, 32, 128], BF16, "s_CE%d" % i) for i in R2]
    t_CE = [[T() for q in range(8)] for i in R2]
    hst = C.sb([128, 2048], F32, "s_h")
    t_h = [T() for i in range(4)]
    hbf = C.sb([128, 2048], BF16, "s_hbf")
    t_hbf = [T() for i in range(4)]
    S.add("pool", lambda e: e.memset(hst[:], 0.0), [], t_h)
    S.add("pool", lambda e: e.memset(hbf[:], 0.0), [], t_hbf)
    p_sm = C.ps([128, 512], F32, "s_psm")
    t_psm = T()
    pG = [C.ps([128, 512], F32, "s_pG%d" % i) for i in R2]
    t_pG = [T(), T()]
    pY = [C.ps([128, 512], F32, "s_pY%d" % i) for i in range(4)]
    t_pY = [T() for i in range(4)]
    pSt = C.ps([128, 512], F32, "s_pSt")
    t_pSt = T()
    bTv = bT_d.rearrange("(g n) t -> n g t", n=128)
    cTv = cT_d.rearrange("(g n) t -> n g t", n=128)
    seq = list(range(NT - 1, -1, -1)) if bwd else list(range(NT))

    def stage_P(it):
        t = seq[it]
        b_ = it % 2
        rows = slice(t * 128, (t + 1) * 128)
        b3 = it % 3
        S.dma("sp", dtt[b_][:], dt_d[rows, :], writes=[t_dt[b_]])
        S.dma("sp", act_[b_][:], ac_d[rows, :], writes=[t_act[b_]])
        S.dma("sp", BT[b_][:], bTv[:, :, rows], writes=[t_BT[b_]])
        S.dma("sp", CT[b_][:], cTv[:, :, rows], writes=[t_CT[b_]])
        S.dma("sp", Rb[b_][:].rearrange("p h l -> p (h l)"), acT_d[t, :].partition_broadcast(128), writes=[t_Rb[b_]])
        S.dma("sp", xtok[b_][:].rearrange("p h d -> p (h d)"), xtok_d[rows, :], writes=[t_xtok[b_]])
        S.dma("sp", btok[b3][:], btok_d[rows, :], writes=[t_btok[b3]])
        dth = dtt[b_][:, h0:h0 + 32]
        S.add("dve", lambda e: e.tensor_tensor(out=a_f[b_][:], in0=dth, in1=aneg[:], op=ALU.mult), [t_dt[b_], t_c], [t_a[b_]])
        S.add("dve", lambda e: e.tensor_copy(out=a_hi[b_][:], in_=a_f[b_][:]), [t_a[b_]], [t_a[b_]])
        S.add("dve", lambda e: e.tensor_tensor(out=a_lo[b_][:], in0=a_f[b_][:], in1=a_hi[b_][:], op=ALU.subtract), [t_a[b_]], [t_a[b_]])
        for ci, lm in enumerate((M1, M2, ones_m[:])):
            S.add("pe", lambda e, ci=ci, lm=lm: e.matmul(p_sm[:, ci * 32:(ci + 1) * 32], lhsT=lm, rhs=a_hi[b_][:], start=True, stop=False), [t_a[b_], t_c], [t_psm])
            S.add("pe", lambda e, ci=ci, lm=lm: e.matmul(p_sm[:, ci * 32:(ci + 1) * 32], lhsT=lm, rhs=a_lo[b_][:], start=False, stop=True), [t_a[b_], t_c], [t_psm])
        S.add("act", lambda e: e.activation(out=esm[b_][:], in_=p_sm[:, 0:96], func=AF.Exp), [t_psm], [t_esm[b_]])
        for g in range(8):
            S.add("pe", lambda e, g=g: e.matmul(pG[g // 4][:, (g % 4) * 128:(g % 4 + 1) * 128], lhsT=BT[b_][:, g, :], rhs=CT[b_][:, g, :], start=True, stop=True),
                  [t_BT[b_], t_CT[b_]], [t_pG[g // 4]])
        for k in range(2):
            S.add("dve", lambda e, k=k: e.tensor_tensor(out=GTm[b_][:, k * 4:(k + 1) * 4, :], in0=pG[k][:].rearrange("p (g l) -> p g l", g=4),
                                                        in1=M1.unsqueeze(1).to_broadcast([128, 4, 128]), op=ALU.mult),
                  [t_pG[k], t_c], [t_GTm[b_]])

        def x_relu(q):
            xq = q % 4
            for hh in range(4):
                h = 4 * q + hh
                S.add("act", lambda e, h=h, hh=hh, xq=xq: e.activation(out=Xc[:, 4 * xq + hh, :], in_=Rb[b_][:, h, :], func=AF.Relu,
                                                                      bias=act_[b_][:, h:h + 1], scale=-1.0),
                      [t_Rb[b_], t_act[b_]], [t_Xc[xq]])

        def x_exp(q):
            xq = q % 4
            S.add("act", lambda e, q=q, xq=xq: e.activation(out=Dm[b_][:, 4 * q:4 * q + 4, :].rearrange("p h l -> p (h l)"),
                                                            in_=Xc[:, 4 * xq:4 * xq + 4, :].rearrange("p h l -> p (h l)"), func=AF.Exp, scale=-1.0),
                  [t_Xc[xq]], [t_Dm[b_][q]])

        def ce_op(q):
            S.add("act", lambda e, q=q: e.activation(out=CE[b_][:, 4 * q:4 * q + 4, :].rearrange("p h l -> p (h l)"),
                                                     in_=Rb[b_][:, 4 * q:4 * q + 4, :].rearrange("p h l -> p (h l)"), func=AF.Exp), [t_Rb[b_]], [t_CE[b_][q]])
            S.add("dve", lambda e, q=q: e.tensor_tensor(out=CE[b_][:, 4 * q:4 * q + 4, :], in0=CE[b_][:, 4 * q:4 * q + 4, :],
                                                        in1=CT[b_][:, q:q + 1, :].to_broadcast([128, 4, 128]), op=ALU.mult),
                  [t_CE[b_][q], t_CT[b_]], [t_CE[b_][q]])

        def m_op(q):
            S.add("dve", lambda e, q=q: e.tensor_tensor(out=Dm[b_][:, 4 * q:4 * q + 4, :], in0=Dm[b_][:, 4 * q:4 * q + 4, :],
                                                        in1=GTm[b_][:, q:q + 1, :].to_broadcast([128, 4, 128]), op=ALU.mult),
                  [t_Dm[b_][q], t_GTm[b_]], [t_Dm[b_][q]])

        S.add("dve", lambda e: e.tensor_tensor(out=xdt[b_][:], in0=xtok[b_][:], in1=dth.unsqueeze(2).to_broadcast([128, 32, 64]), op=ALU.mult),
              [t_xtok[b_], t_dt[b_]], [t_xdt[b_]])
        if not bwd:
            S.add("pool", lambda e: e.tensor_tensor(out=sk[b_][:].rearrange("p (h d) -> p h d", h=32), in0=xtok[b_][:],
                                                    in1=dsk[:].unsqueeze(2).to_broadcast([128, 32, 64]), op=ALU.mult),
                  [t_xtok[b_], t_c], [t_sk[b_]])
        x_relu(0)
        x_relu(1)
        x_exp(0)
        S.add("dve", lambda e: e.tensor_tensor(out=xdd[b_][:], in0=xdt[b_][:], in1=esm[b_][:, 32:64].unsqueeze(2).to_broadcast([128, 32, 64]), op=ALU.mult),
              [t_xdt[b_], t_esm[b_]], [t_xdd[b_]])
        for q in range(2, 8):
            x_relu(q)
            x_exp(q - 1)
            m_op(q - 2)
            ce_op(q - 2)
        x_exp(7)
        m_op(6)
        ce_op(6)
        m_op(7)
        ce_op(7)

    def stage_Q(it):
        t = seq[it]
        b_ = it % 2
        b3 = it % 3
        rows = slice(t * 128, (t + 1) * 128)
        for h in range(32):
            yo = pY[h // 8][:, (h % 8) * 64:(h % 8 + 1) * 64]
            S.add("pe", lambda e, h=h, yo=yo: e.matmul(yo, lhsT=Dm[b_][:, h, :], rhs=xdt[b_][:, h, :], start=True, stop=False),
                  [t_Dm[b_][h // 4], t_xdt[b_]], [t_pY[h // 8]])
            S.add("pe", lambda e, h=h, yo=yo: e.matmul(yo, lhsT=CE[b_][:, h, :], rhs=hbf[:, h * 64:(h + 1) * 64], start=False, stop=True),
                  [t_CE[b_][h // 4], t_hbf[h // 8]], [t_pY[h // 8]])
        for gp in range(4):
            for k in range(2):
                g = 2 * gp + k
                S.add("pe", lambda e, g=g, k=k: e.matmul(pSt[:, k * 256:(k + 1) * 256], lhsT=btok[b3][:, g * 128:(g + 1) * 128],
                                                         rhs=xdd[b_][:, 4 * g:4 * g + 4, :], start=True, stop=True),
                      [t_btok[b3], t_xdd[b_]], [t_pSt])
            S.add("pool", lambda e, gp=gp: e.tensor_tensor(out=hst[:, gp * 512:(gp + 1) * 512].rearrange("p (h d) -> p h d", h=8),
                                                           in0=hst[:, gp * 512:(gp + 1) * 512].rearrange("p (h d) -> p h d", h=8),
                                                           in1=esm[b_][:, 64 + gp * 8:64 + gp * 8 + 8].unsqueeze(2).to_broadcast([128, 8, 64]), op=ALU.mult),
                  [t_h[gp], t_esm[b_]], [t_h[gp]])
            S.add("dve", lambda e, gp=gp: e.tensor_tensor(out=hst[:, gp * 512:(gp + 1) * 512], in0=hst[:, gp * 512:(gp + 1) * 512], in1=pSt[:], op=ALU.add),
                  [t_h[gp], t_pSt], [t_h[gp]])
            S.add("act", lambda e, gp=gp: e.copy(out=hbf[:, gp * 512:(gp + 1) * 512], in_=hst[:, gp * 512:(gp + 1) * 512]), [t_h[gp]], [t_hbf[gp]])
        if not bwd:
            for k in range(4):
                S.add("dve", lambda e, k=k: e.tensor_tensor(out=sk[b_][:, k * 512:(k + 1) * 512], in0=sk[b_][:, k * 512:(k + 1) * 512], in1=pY[k][:], op=ALU.add),
                      [t_sk[b_], t_pY[k]], [t_sk[b_]])
            S.dma("pool", yf_d[rows, :], sk[b_][:], reads=[t_sk[b_]])
        else:
            S.dma("sp", yf[b_][:], yf_d[rows, :], writes=[t_yf[b_]])
            for k in range(4):
                S.add("dve", lambda e, k=k: e.tensor_tensor(out=yf[b_][:, k * 512:(k + 1) * 512], in0=yf[b_][:, k * 512:(k + 1) * 512], in1=pY[k][:], op=ALU.add),
                      [t_yf[b_], t_pY[k]], [t_yf[b_]])
            S.dma("pool", yf_d[rows, :], yf[b_][:], reads=[t_yf[b_]])

    for it in range(NT + 1):
        if it < NT:
            stage_P(it)
        if it >= 1:
            stage_Q(it - 1)


def phase_e(nc, S, C, T_, yt_d, zs_d, normw_d, swout_d, xres_d, xout_d, ln_row, ident, t_ident, eps_t, t_eps, lng_d, lnb_d, **_):
    NT = T_ // 128
    w_out = C.sb([128, 16, D], BF16, "e_wout")
    w_tmp = C.sb([128, 16, D], F32, "e_wtmp")
    nw = C.sb([128, 16], F32, "e_nw")
    t_wout = T()
    t_wtmp = T()
    S.dma("sp", w_tmp[:], swout_d.rearrange("(kt p) n -> p kt n", p=128), writes=[t_wtmp])
    S.dma("sp", nw[:], normw_d, writes=[t_wtmp])
    for kt in range(16):
        S.add("dve" if kt % 2 == 0 else "pool", lambda e, kt=kt: e.tensor_scalar(out=w_out[:, kt, :], in0=w_tmp[:, kt, :], scalar1=nw[:, kt:kt + 1], scalar2=None, op0=ALU.mult),
              [t_wtmp], [t_wout])
    g_bc = C.sb([128, D], F32, "e_g")
    b_bc = C.sb([128, D], F32, "e_b")
    t_gb = T()
    S.dma("sp", g_bc[:], lng_d[ln_row:ln_row + 1, :].partition_broadcast(128), writes=[t_gb])
    S.dma("sp", b_bc[:], lnb_d[ln_row:ln_row + 1, :].partition_broadcast(128), writes=[t_gb])
    R2 = range(2)
    yt = [C.sb([128, 2048], F32, "e_yt%d" % i) for i in R2]
    t_yt = [T() for i in R2]
    zst = [C.sb([128, 2048], BF16, "e_zs%d" % i) for i in R2]
    t_zs = [T() for i in R2]
    ss = [C.sb([128, 16], F32, "e_ss%d" % i) for i in R2]
    t_ss = [T() for i in R2]
    junk = C.sb([128, 256], F32, "e_junk")
    t_junk = T()
    ynw = [C.sb([128, 2048], BF16, "e_ynw%d" % i) for i in R2]
    t_ynw = [T() for i in R2]
    yT = [C.sb([128, 16, 128], BF16, "e_yT%d" % i) for i in R2]
    t_yT = [T() for i in R2]
    xres = [C.sb([128, D], F32, "e_xres%d" % i) for i in R2]
    t_xres = [T() for i in R2]
    lnbs = [{"h": (C.sb([128, D], F32, "e_ln_h%d" % i), T()), "st": (C.sb([128, 12], F32, "e_ln_st%d" % i), T()),
             "mv": (C.sb([128, 4], F32, "e_ln_mv%d" % i), T())} for i in R2]
    pT = [C.ps([128, 512], F32, "e_pT%d" % i) for i in range(4)]
    t_pT = [T() for i in range(4)]
    pO = [C.ps([128, 512], F32, "e_pO%d" % i) for i in range(4)]
    t_pO = [T() for i in range(4)]

    def E1(t):
        b_ = t % 2
        rows = slice(t * 128, (t + 1) * 128)
        S.dma("sp", yt[b_][:], yt_d[rows, :], writes=[t_yt[b_]])
        S.dma("sp", zst[b_][:], zs_d[rows, :], writes=[t_zs[b_]])
        S.dma("sp", xres[b_][:], xres_d[rows, :], writes=[t_xres[b_]])
        S.add("pool", lambda e: e.tensor_tensor(out=yt[b_][:], in0=yt[b_][:], in1=zst[b_][:], op=ALU.mult), [t_yt[b_], t_zs[b_]], [t_yt[b_]])
        S.add("pool", lambda e: e.memset(ss[b_][:], 0.0), [], [t_ss[b_]])
        for g in range(8):
            S.add("act", lambda e, g=g: e.activation(out=junk[:], in_=yt[b_][:, g * 256:(g + 1) * 256], func=AF.Square, accum_out=ss[b_][:, g:g + 1]),
                  [t_yt[b_], t_ss[b_]], [t_junk, t_ss[b_]])
        S.add("act", lambda e: e.activation(out=ss[b_][:, 8:16], in_=ss[b_][:, 0:8], func=AF.Ln, bias=eps_t[:], scale=1.0 / 256.0), [t_ss[b_], t_eps], [t_ss[b_]])
        S.add("act", lambda e: e.activation(out=ss[b_][:, 8:16], in_=ss[b_][:, 8:16], func=AF.Exp, scale=-0.5), [t_ss[b_]], [t_ss[b_]])

    def E2(t):
        b_ = t % 2
        S.add("dve", lambda e: e.tensor_tensor(out=ynw[b_][:].rearrange("p (g c) -> p g c", g=8), in0=yt[b_][:].rearrange("p (g c) -> p g c", g=8),
                                               in1=ss[b_][:, 8:16].unsqueeze(2).to_broadcast([128, 8, 256]), op=ALU.mult), [t_yt[b_], t_ss[b_]], [t_ynw[b_]])
        for hf in range(2):
            pp, t_pp = pT[2 * b_ + hf], t_pT[2 * b_ + hf]
            ppb = pp[:].bitcast(BF16)
            for cc in range(8):
                kt = hf * 8 + cc
                S.add("pe", lambda e, kt=kt, cc=cc, ppb=ppb: e.transpose(ppb[:, cc * 128:(cc + 1) * 128], ynw[b_][:, kt * 128:(kt + 1) * 128], ident[:]),
                      [t_ynw[b_], t_ident], [t_pp])
            S.add("act", lambda e, ppb=ppb, hf=hf: e.copy(out=yT[b_][:, hf * 8:(hf + 1) * 8, :], in_=ppb.rearrange("p (k t) -> p k t", k=8)), [t_pp], [t_yT[b_]])
        for hf in range(2):
            po, t_po = pO[2 * b_ + hf], t_pO[2 * b_ + hf]
            for kt in range(16):
                S.add("pe", lambda e, kt=kt, hf=hf, po=po: e.matmul(po[:], lhsT=yT[b_][:, kt, :], rhs=w_out[:, kt, hf * 512:(hf + 1) * 512], start=(kt == 0), stop=(kt == 15)),
                      [t_yT[b_], t_wout], [t_po])

    def E3(t):
        b_ = t % 2
        rows = slice(t * 128, (t + 1) * 128)
        lnb = lnbs[b_]
        h_, t_h_ = lnb["h"]
        for hf in range(2):
            po, t_po = pO[2 * b_ + hf], t_pO[2 * b_ + hf]
            S.add("dve", lambda e, hf=hf, po=po: e.scalar_tensor_tensor(out=h_[:, hf * 512:(hf + 1) * 512], in0=xres[b_][:, hf * 512:(hf + 1) * 512],
                                                                       scalar=ALPHA, in1=po[:], op0=ALU.mult, op1=ALU.add),
                  [t_xres[b_], t_po], [t_h_])
        ln_core(S, lnb, g_bc[:], b_bc[:], t_gb, eps_t, t_eps, h_[:], t_h_)
        S.dma("pool", xout_d[rows, :], h_[:], reads=[t_h_])

    for it in range(NT + 2):
        if it >= 2:
            E3(it - 2)
        if 1 <= it <= NT:
            E2(it - 1)
        if it < NT:
            E1(it)


def phase_mlp(nc, S, C, T_, xin_d, xout_d, w1_d, w2_d, ln_row, ident, t_ident, eps_t, t_eps, lng_d, lnb_d, **_):
    NT = T_ // 128
    NG2 = NT // 2
    w1 = C.sb([128, 8, 4096], BF16, "m_w1")
    w2 = C.sb([128, 32, D], BF16, "m_w2")
    t_w1 = [T("m_w1_%d" % i) for i in range(8)]
    t_w2 = [T("m_w2_%d" % i) for i in range(8)]
    w1v = w1_d.rearrange("(kt p) n -> p kt n", p=128)
    w2v = w2_d.rearrange("(f p) n -> p f n", p=128)
    for i in range(8):
        S.dma("pool", w1[:, :, i * 512:(i + 1) * 512], w1v[:, :, i * 512:(i + 1) * 512], writes=[t_w1[i]])
    for i in range(8):
        S.dma("pool", w2[:, i * 4:(i + 1) * 4, :], w2v[:, i * 4:(i + 1) * 4, :], writes=[t_w2[i]])
    g_bc = C.sb([128, D], F32, "m_g")
    b_bc = C.sb([128, D], F32, "m_b")
    t_gb = T("m_gb")
    S.dma("sp", g_bc[:], lng_d[ln_row:ln_row + 1, :].partition_broadcast(128), writes=[t_gb])
    S.dma("sp", b_bc[:], lnb_d[ln_row:ln_row + 1, :].partition_broadcast(128), writes=[t_gb])
    xf = [C.sb([128, D], F32, "m_xf%d" % i) for i in range(2)]
    t_xf = [T() for i in range(2)]
    xb = [C.sb([128, D], BF16, "m_xb%d" % i) for i in range(2)]
    t_xb = [T() for i in range(2)]
    xT = [C.sb([128, 8, 256], BF16, "m_xT%d" % i) for i in range(2)]
    t_xT = [[T(), T()] for i in range(2)]
    h1T = C.sb([128, 32, 256], BF16, "m_h1T")
    t_h1 = [T() for i in range(32)]
    rl = [C.sb([128, 256], F32, "m_rl%d" % i) for i in range(2)]
    t_rl = [T(), T()]
    xres = [C.sb([128, D], F32, "m_xres%d" % i) for i in range(2)]
    t_xres = [T(), T()]
    lnb = {"h": (C.sb([128, D], F32, "m_ln_h"), T()), "st": (C.sb([128, 12], F32, "m_ln_st"), T()),
           "mv": (C.sb([128, 4], F32, "m_ln_mv"), T())}
    xo = [C.sb([128, D], F32, "m_xo%d" % i) for i in range(2)]
    t_xo = [T(), T()]
    pT = [C.ps([128, 512], F32, "m_pT%d" % i) for i in range(2)]
    t_pT = [T(), T()]
    pH = [C.ps([128, 512], F32, "m_pH%d" % i) for i in range(2)]
    t_pH = [T(), T()]
    pY = [C.ps([128, 512], F32, "m_pY%d" % i) for i in range(4)]
    t_pY = [T() for i in range(4)]
    cnt = [0, 0, 0]
    for g in range(NG2):
        gb = g % 2
        for i in range(2):
            t = 2 * g + i
            s_ = t % 2
            S.dma("sp", xf[s_][:], xin_d[t * 128:(t + 1) * 128, :], writes=[t_xf[s_]])
            S.add("dve", lambda e, s_=s_: e.tensor_copy(out=xb[s_][:], in_=xf[s_][:]), [t_xf[s_]], [t_xb[s_]])
            pi = cnt[0] % 2
            cnt[0] += 1
            ppb = pT[pi][:].bitcast(BF16)
            for kt in range(8):
                S.add("pe", lambda e, s_=s_, kt=kt, ppb=ppb: e.transpose(ppb[:, kt * 128:(kt + 1) * 128], xb[s_][:, kt * 128:(kt + 1) * 128], ident[:]),
                      [t_xb[s_], t_ident], [t_pT[pi]])
            S.add("act", lambda e, i=i, gb=gb, ppb=ppb: e.copy(out=xT[gb][:, :, i * 128:(i + 1) * 128], in_=ppb.rearrange("p (k t) -> p k t", k=8)),
                  [t_pT[pi]], [t_xT[gb][i]])
        for f in range(32):
            hi = cnt[1] % 2
            cnt[1] += 1
            for kt in range(8):
                S.add("pe", lambda e, kt=kt, f=f, hi=hi, gb=gb: e.matmul(pH[hi][:, 0:256], lhsT=w1[:, kt, f * 128:(f + 1) * 128], rhs=xT[gb][:, kt, :],
                                                                        start=(kt == 0), stop=(kt == 7)),
                      [t_w1[f // 4]] + t_xT[gb], [t_pH[hi]])
            S.add("act", lambda e, hi=hi: e.activation(out=rl[hi][:], in_=pH[hi][:, 0:256], func=AF.Relu), [t_pH[hi]], [t_rl[hi]])
            S.add("dve" if f % 2 == 0 else "pool", lambda e, hi=hi, f=f: e.tensor_tensor(out=h1T[:, f, :], in0=rl[hi][:], in1=rl[hi][:], op=ALU.mult),
                  [t_rl[hi]], [t_h1[f]])
        for i in range(2):
            t = 2 * g + i
            xr = t % 2
            S.dma("sp", xres[xr][:], xin_d[t * 128:(t + 1) * 128, :], writes=[t_xres[xr]])
            yb = (cnt[2] % 2) * 2
            cnt[2] += 1
            for hf in range(2):
                for f in range(32):
                    S.add("pe", lambda e, f=f, hf=hf, i=i, yb=yb: e.matmul(pY[yb + hf][:], lhsT=h1T[:, f, i * 128:(i + 1) * 128], rhs=w2[:, f, hf * 512:(hf + 1) * 512],
                                                                          start=(f == 0), stop=(f == 31)),
                          [t_h1[f], t_w2[f // 4]], [t_pY[yb + hf]])
            h, t_h = lnb["h"]
            for hf in range(2):
                S.add("dve", lambda e, hf=hf, xr=xr, yb=yb: e.scalar_tensor_tensor(out=h[:, hf * 512:(hf + 1) * 512], in0=xres[xr][:, hf * 512:(hf + 1) * 512],
                                                                                  scalar=ALPHA, in1=pY[yb + hf][:], op0=ALU.mult, op1=ALU.add),
                      [t_xres[xr], t_pY[yb + hf]], [t_h])
            ln_core(S, lnb, g_bc[:], b_bc[:], t_gb, eps_t, t_eps, xo[xr][:], t_xo[xr])
            S.dma("pool", xout_d[t * 128:(t + 1) * 128, :], xo[xr][:], reads=[t_xo[xr]])


def ln_core(S, bufs, g_bc, b_bc, t_gb, eps_t, t_eps, out_sb, t_out):
    h, t_h = bufs["h"]
    st6, t_st = bufs["st"]
    mv, t_mv = bufs["mv"]
    S.add("dve", lambda e: e.bn_stats(out=st6[:, 0:6], in_=h[:, 0:512]), [t_h], [t_st])
    S.add("dve", lambda e: e.bn_stats(out=st6[:, 6:12], in_=h[:, 512:1024]), [t_h], [t_st])
    S.add("dve", lambda e: e.bn_aggr(out=mv[:, 0:2], in_=st6[:, 0:12]), [t_st], [t_mv])
    S.add("act", lambda e: e.activation(out=mv[:, 2:3], in_=mv[:, 1:2], func=AF.Ln, bias=eps_t[:], scale=1.0), [t_mv, t_eps], [t_mv])
    S.add("act", lambda e: e.activation(out=mv[:, 3:4], in_=mv[:, 2:3], func=AF.Exp, scale=-0.5), [t_mv], [t_mv])
    S.add("dve", lambda e: e.tensor_scalar(out=h[:], in0=h[:], scalar1=mv[:, 0:1], scalar2=mv[:, 3:4],
                                           op0=ALU.subtract, op1=ALU.mult), [t_h, t_mv], [t_h])
    S.add("pool", lambda e: e.tensor_tensor(out=h[:], in0=h[:], in1=g_bc, op=ALU.mult), [t_h, t_gb], [t_h])
    S.add("pool", lambda e: e.tensor_tensor(out=out_sb, in0=h[:], in1=b_bc, op=ALU.add), [t_h, t_gb], [t_out])


def host_inputs(T, x, attn_w_in, attn_sink, attn_rpb, attn_w_out, ln1_g, ln1_b, ln2_g, ln2_b, mlp_w1, mlp_w2,
                ssm_w_in, ssm_conv_w, ssm_conv_b, ssm_dt_bias, ssm_A_log, ssm_D, ssm_norm_w, ssm_w_out, **_):
    NT = T // 128
    cosT, sinS = _rope_tables(T)
    tabs = _na_bias_tables(np.asarray(attn_rpb[0]), NT)
    nbt = np.stack([tabs[k].reshape(128, -1) for k in ("int", "t0", "t1", "tm2", "tm1")], 0)
    kq = np.arange(128)
    maskA = np.concatenate([(kq[:, None] >= kq[None, :]), (kq[:, None] <= kq[None, :])], 1).astype(np.float32)
    common = {
        "awin": _attn_w_layout(np.asarray(attn_w_in[0])),
        "awout": np.ascontiguousarray(attn_w_out[0]),
        "sink": np.ascontiguousarray(attn_sink[0:1]),
        "nbt": np.ascontiguousarray(nbt),
        "cosT": cosT, "sinS": sinS, "maskA": np.ascontiguousarray(maskA),
        "lng": np.ascontiguousarray(np.stack([ln1_g[0], ln2_g[0], ln1_g[1], ln2_g[1]])),
        "lnb": np.ascontiguousarray(np.stack([ln1_b[0], ln2_b[0], ln1_b[1], ln2_b[1]])),
        "ident": np.eye(128, dtype=np.float32),
        "swin": np.ascontiguousarray(ssm_w_in[0]),
        "cw": np.ascontiguousarray(np.asarray(ssm_conv_w[0]).reshape(5, 32, 128).transpose(2, 0, 1).reshape(128, 160)),
        "cb": np.ascontiguousarray(np.asarray(ssm_conv_b[0]).reshape(32, 128).T),
        "dtb": np.ascontiguousarray(np.asarray(ssm_dt_bias[0]).reshape(1, 64)),
        "alog": np.ascontiguousarray(np.asarray(ssm_A_log[0]).reshape(1, 64)),
        "dsk": np.ascontiguousarray(np.asarray(ssm_D[0]).reshape(1, 32)),
        "normwT": np.ascontiguousarray(np.asarray(ssm_norm_w[0]).reshape(16, 128).T),
        "swout": np.ascontiguousarray(ssm_w_out[0]),
        "tri": np.ascontiguousarray(np.concatenate([kq[:, None] <= kq[None, :], kq[:, None] > kq[None, :],
                                                    kq[:, None] >= kq[None, :], kq[:, None] < kq[None, :]], 1).astype(np.float32)),
        "w1_0": np.ascontiguousarray(mlp_w1[0]), "w1_1": np.ascontiguousarray(mlp_w1[1]),
        "w2_0": np.ascontiguousarray(mlp_w2[0]), "w2_1": np.ascontiguousarray(mlp_w2[1]),
    }
    return common


def kernel(**inputs):
    x = np.asarray(inputs["x"])
    B, T, _ = x.shape
    common = host_inputs(T, **inputs)
    nc = build(T)
    in_maps = []
    for c in range(8):
        m = dict(common)
        m["x"] = np.ascontiguousarray(x[c % B])
        in_maps.append(m)
    res = run_bass_kernel_spmd(nc, in_maps, core_ids=list(range(8)))
    out = np.stack([res.results[b]["out"] for b in range(B)], 0)
    return out.astype(np.float32)
```

```python
import contextlib
import numpy as np
import concourse.bass as bass
import concourse.mybir as mybir
from concourse.bass_utils import run_bass_kernel_spmd

F32 = mybir.dt.float32
BF16 = mybir.dt.bfloat16
ALU = mybir.AluOpType
AF = mybir.ActivationFunctionType
AX = mybir.AxisListType

D = 1024
ALPHA = 4.0 ** 0.25
NEGB = -30000.0


class T:
    __slots__ = ("name", "w", "r")

    def __init__(self, name=""):
        self.name = name
        self.w = None
        self.r = []


class Op:
    __slots__ = ("eng", "emit", "deps", "sig", "dma", "cnt", "dsem", "dval")

    def __init__(self, eng, emit, dma):
        self.eng = eng
        self.emit = emit
        self.deps = []
        self.sig = False
        self.dma = dma
        self.cnt = 0
        self.dsem = -1
        self.dval = 0


class Sched:
    ENGS = ("pe", "act", "dve", "pool", "sp")
    ENGOBJ = {"pe": "tensor", "act": "scalar", "dve": "vector", "pool": "gpsimd", "sp": "sync"}

    def __init__(self, nc, st, n_dma_sems=14):
        self.nc = nc
        self.ops = {e: [] for e in self.ENGS}
        self.n_dma_sems = n_dma_sems
        self.csem = {e: st.enter_context(nc.semaphore("c_" + e)) for e in self.ENGS}
        self.dsems = {e: [st.enter_context(nc.semaphore("d_%s%d" % (e, i))) for i in range(n_dma_sems)]
                      for e in ("sp", "pool")}
        self.phsem = st.enter_context(nc.semaphore("phase"))
        self.ccount = {e: 0 for e in self.ENGS}
        self.dcount = {e: 0 for e in ("sp", "pool")}
        self.duses = {e: [0] * n_dma_sems for e in ("sp", "pool")}
        self.nflush = 0

    def add(self, eng, emit, reads=(), writes=(), dma=False):
        op = Op(eng, emit, dma)
        deps = op.deps
        for t in reads:
            w = t.w
            if w is not None and (w.eng != eng or w.dma or eng != "pe"):
                deps.append(w)
        for t in writes:
            w = t.w
            if w is not None and (w.eng != eng or w.dma or dma):
                deps.append(w)
            for r in t.r:
                if r.eng != eng or r.dma or dma:
                    deps.append(r)
        for t in reads:
            t.r.append(op)
        for t in writes:
            t.w = op
            t.r = []
        for d in deps:
            d.sig = True
        self.ops[eng].append(op)
        return op

    def dma(self, eng, out, in_, reads=(), writes=()):
        return self.add(eng, lambda e: e.dma_start(out=out, in_=in_), reads, writes, dma=True)

    def flush(self):
        nc = self.nc
        import os
        if os.environ.get("KDBG"):
            print("flush", self.nflush, "sbuf remaining", nc.sbuf_bytes_remaining, {e: len(v) for e, v in self.ops.items()})
        csem, dsems = self.csem, self.dsems
        last_cnt = {}
        for e in self.ENGS:
            comp = [op for op in self.ops[e] if not op.dma]
            if comp:
                comp[-1].sig = True
            for op in self.ops[e]:
                if op.dma:
                    k = self.dcount[e]
                    self.dcount[e] += 1
                    op.dsem = k % self.n_dma_sems
                    self.duses[e][op.dsem] += 1
                    op.dval = 16 * self.duses[e][op.dsem]
                elif op.sig:
                    self.ccount[e] += 1
                    op.cnt = self.ccount[e]
            last_cnt[e] = self.ccount[e]
        nfl = self.nflush
        ops_all = self.ops

        def run(ename, eobj):
            if nfl > 0:
                eobj.wait_ge(self.phsem, 5 * nfl)
            known = {}
            last_dma = {}
            for op in ops_all[ename]:
                need = {}
                for d in op.deps:
                    if d.dma:
                        key = ("d", d.eng, d.dsem)
                        val = d.dval
                    else:
                        key = ("c", d.eng)
                        val = d.cnt
                    if need.get(key, 0) < val:
                        need[key] = val
                if op.dma:
                    key = ("d", ename, op.dsem)
                    val = op.dval - 16
                    if val > 0 and need.get(key, 0) < val:
                        need[key] = val
                for key, val in need.items():
                    if val <= 0 or known.get(key, 0) >= val:
                        continue
                    known[key] = val
                    sem = csem[key[1]] if key[0] == "c" else dsems[key[1]][key[2]]
                    eobj.wait_ge(sem, val)
                ins = op.emit(eobj)
                if op.dma:
                    ins.then_inc(dsems[ename][op.dsem], 16)
                    last_dma[op.dsem] = op.dval
                elif op.sig:
                    ins.then_inc(csem[ename], 1)
            for k, v in last_dma.items():
                if known.get(("d", ename, k), 0) < v:
                    eobj.wait_ge(dsems[ename][k], v)
            if last_cnt[ename] > 0 and known.get(("c", ename), 0) < last_cnt[ename]:
                eobj.wait_ge(csem[ename], last_cnt[ename])
            eobj.sem_inc(self.phsem, 1)

        with nc.Block() as block:
            for ename in self.ENGS:
                getattr(block, self.ENGOBJ[ename])(lambda eobj, ename=ename: run(ename, eobj))
        self.ops = {e: [] for e in self.ENGS}
        self.nflush += 1


class Ctx:
    _inst = [0]

    def __init__(self, nc, st):
        self.nc = nc
        self.st = st
        self.n = 0
        Ctx._inst[0] += 1
        self.pfx = "c%d_" % Ctx._inst[0]

    def sb(self, shape, dt, name=None):
        self.n += 1
        return self.st.enter_context(self.nc.sbuf_tensor(self.pfx + "s_" + (name or "sb%d" % self.n), list(shape), dt))

    def ps(self, shape, dt, name=None):
        self.n += 1
        return self.st.enter_context(self.nc.psum_tensor(self.pfx + "p_" + (name or "ps%d" % self.n), list(shape), dt))


def _na_types(NT):
    return {"int": [-2, -1, 0, 1, 2], "t0": [0, 1, 2, 3, 3], "t1": [-1, 0, 1, 2, 2],
            "tm2": [-2, -1, 0, 1, 1], "tm1": [-3, -2, -1, 0, 0]}


def _na_type_of(t, NT):
    if t == 0:
        return "t0"
    if t == 1:
        return "t1"
    if t == NT - 2:
        return "tm2"
    if t == NT - 1:
        return "tm1"
    return "int"


def _na_bias_tables(rpb, NT):
    rows = 2 * NT
    types = _na_types(NT)
    rep_t = {"int": min(2, NT - 1), "t0": 0, "t1": 1, "tm2": NT - 2, "tm1": NT - 1}
    out = {}
    q = np.arange(128)
    k = np.arange(128)
    for name, offs in types.items():
        t = rep_t[name]
        tab = np.full((128, 8, 5, 128), NEGB, np.float32)
        r = 2 * t + q // 64
        c = q % 64
        rs = np.clip(r - 4, 0, rows - 8)
        cs = np.clip(c - 8, 0, 64 - 16)
        seen = set()
        for j, off in enumerate(offs):
            if off in seen:
                continue
            seen.add(off)
            kt = t + off
            if kt < 0 or kt >= NT:
                continue
            kr = 2 * kt + k // 64
            kc = k % 64
            valid = ((kr[:, None] >= rs[None, :]) & (kr[:, None] < rs[None, :] + 8) &
                     (kc[:, None] >= cs[None, :]) & (kc[:, None] < cs[None, :] + 16))
            dr = np.clip(kr[:, None] - r[None, :] + 7, 0, 14)
            dc = np.clip(kc[:, None] - c[None, :] + 15, 0, 30)
            g = rpb[:, dr, dc]
            g = np.where(valid[None], g, NEGB)
            tab[:, :, j, :] = g.transpose(1, 0, 2)
        out[name] = tab
    return out


def _rope_tables(T):
    half = 32
    inv = (10000.0 ** (-np.arange(half, dtype=np.float32) / half)).astype(np.float32)
    pos = np.arange(T, dtype=np.float32)
    ang = pos[None, :] * inv[:, None]
    cos = np.cos(ang).astype(np.float32)
    sin = np.sin(ang).astype(np.float32)
    cosT = np.concatenate([cos, cos, cos, cos], 0)
    sinS = np.concatenate([-sin, sin, -sin, sin], 0)
    return np.ascontiguousarray(cosT), np.ascontiguousarray(sinS)


def _attn_w_layout(w_in):
    qa = w_in[:, 0:512].reshape(1024, 8, 64)
    ka = w_in[:, 512:640].reshape(1024, 2, 64)
    va = w_in[:, 640:768]
    qb = w_in[:, 768:1280]
    kb = w_in[:, 1280:1792]
    vb = w_in[:, 1792:2304]
    order = [0, 4, 1, 5, 2, 6, 3, 7]
    qa2 = qa[:, order, :]
    sw = lambda a: np.concatenate([a[..., 32:], a[..., :32]], -1)
    cols = [qa2.reshape(1024, 512), sw(qa2).reshape(1024, 512), ka.reshape(1024, 128),
            sw(ka).reshape(1024, 128), qb, kb, va, vb]
    return np.ascontiguousarray(np.concatenate(cols, 1))


def build(Tlen, upto="all"):
    NT = Tlen // 128
    nc = bass.Bass("TRN2", target_bir_lowering=False)

    def din(name, shape):
        return nc.dram_tensor(name, list(shape), F32, kind="ExternalInput").ap()

    x_d = din("x", [Tlen, D])
    awin_d = din("awin", [D, 2944])
    awout_d = din("awout", [D, D])
    sink_d = din("sink", [1, 8])
    nbt_d = din("nbt", [5, 128, 8 * 5 * 128])
    cos_d = din("cosT", [128, Tlen])
    sin_d = din("sinS", [128, Tlen])
    ma_d = din("maskA", [128, 2 * 128])
    lng_d = din("lng", [4, D])
    lnb_d = din("lnb", [4, D])
    ident_d = din("ident", [128, 128])
    w1_d = [din("w1_%d" % i, [D, 4096]) for i in range(2)]
    w2_d = [din("w2_%d" % i, [4096, D]) for i in range(2)]
    swin_d = din("swin", [D, 6208])
    cw_d = din("cw", [128, 160])
    cb_d = din("cb", [128, 32])
    dtb_d = din("dtb", [1, 64])
    alog_d = din("alog", [1, 64])
    dsk_d = din("dsk", [1, 32])
    normw_d = din("normwT", [128, 16])
    swout_d = din("swout", [2048, D])
    tri_d = din("tri", [128, 4 * 128])
    stages = ["a", "m0", "c1", "c2", "d", "m1"]
    if upto != "all":
        stages = stages[:stages.index(upto) + 1]
    out_d = nc.dram_tensor("out", [Tlen, D], F32, kind="ExternalOutput").ap()

    def scratch(name, shape, dt=F32):
        return nc.dram_tensor(name, list(shape), dt, kind="Internal").ap()

    def act_out(stage, name):
        return out_d if stages[-1] == stage else scratch(name, [Tlen, D])

    x1_d = act_out("a", "x1")
    x2_d = act_out("m0", "x2")
    x3_d = act_out("d", "x3")
    xtok_d = scratch("xtok", [Tlen, 2048], BF16)
    btok_d = scratch("btok", [Tlen, 1024], BF16)
    bT_d = scratch("bT", [1024, Tlen], BF16)
    cT_d = scratch("cT", [1024, Tlen], BF16)
    zs_d = scratch("zs", [Tlen, 2048], BF16)
    dt_d = scratch("dtv", [Tlen, 64], F32)
    yf_d = scratch("yf", [Tlen, 2048], F32)
    acf_d = scratch("acf", [Tlen, 32], F32)
    acb_d = scratch("acb", [Tlen, 32], F32)
    acfT_d = scratch("acfT", [NT, 32 * 128], F32)
    acbT_d = scratch("acbT", [NT, 32 * 128], F32)
    SS = dict(xtok_d=xtok_d, btok_d=btok_d, bT_d=bT_d, cT_d=cT_d, zs_d=zs_d, dt_d=dt_d, yf_d=yf_d, tri_d=tri_d,
              alog_d=alog_d, dsk_d=dsk_d, acf_d=acf_d, acb_d=acb_d, acfT_d=acfT_d, acbT_d=acbT_d)

    with contextlib.ExitStack() as st0:
        S = Sched(nc, st0)
        C0 = Ctx(nc, st0)
        ident = C0.sb([128, 128], BF16, "ident")
        t_ident = T("ident")
        S.dma("pool", ident[:], ident_d, writes=[t_ident])
        eps_t = C0.sb([128, 1], F32, "eps")
        t_eps = T("eps")
        S.add("pool", lambda e: e.memset(eps_t[:], 1e-5), [], [t_eps])
        K = dict(ident=ident, t_ident=t_ident, eps_t=eps_t, t_eps=t_eps, lng_d=lng_d, lnb_d=lnb_d)

        with contextlib.ExitStack() as st:
            phase_a(nc, S, Ctx(nc, st), T_=Tlen, x_d=x_d, awin_d=awin_d, awout_d=awout_d, sink_d=sink_d, nbt_d=nbt_d,
                    cos_d=cos_d, sin_d=sin_d, ma_d=ma_d, x1_d=x1_d, **K)
            S.flush()
        if "m0" in stages:
            with contextlib.ExitStack() as st:
                phase_mlp(nc, S, Ctx(nc, st), T_=Tlen, xin_d=x1_d, xout_d=x2_d, w1_d=w1_d[0], w2_d=w2_d[0], ln_row=1, **K)
                S.flush()
        if "c1" in stages:
            with contextlib.ExitStack() as st:
                phase_c1(nc, S, Ctx(nc, st), T_=Tlen, xin_d=x2_d, swin_d=swin_d, cw_d=cw_d, cb_d=cb_d, dtb_d=dtb_d, **SS, **K)
                S.flush()
        if "c2" in stages:
            with contextlib.ExitStack() as st:
                phase_ssd(nc, S, Ctx(nc, st), T_=Tlen, bwd=False, **SS, **K)
                S.flush()
        if "d" in stages:
            with contextlib.ExitStack() as st:
                phase_ssd(nc, S, Ctx(nc, st), T_=Tlen, bwd=True, **SS, **K)
                S.flush()
            with contextlib.ExitStack() as st:
                phase_e(nc, S, Ctx(nc, st), T_=Tlen, yt_d=yf_d, zs_d=zs_d, normw_d=normw_d, swout_d=swout_d, xres_d=x2_d, xout_d=x3_d, ln_row=2, **K)
                S.flush()
        if "m1" in stages:
            with contextlib.ExitStack() as st:
                phase_mlp(nc, S, Ctx(nc, st), T_=Tlen, xin_d=x3_d, xout_d=out_d, w1_d=w1_d[1], w2_d=w2_d[1], ln_row=3, **K)
                S.flush()
    return nc


def phase_a(nc, S, C, T_, x_d, awin_d, awout_d, sink_d, nbt_d, cos_d, sin_d, ma_d, lng_d, lnb_d, ident, t_ident,
            eps_t, t_eps, x1_d):
    T_tok = T_
    NT = T_tok // 128
    NG = NT // 4
    SCALE = 0.125
    w_in = C.sb([128, 8, 2944], BF16, "a_win")
    t_win = T("a_win")
    wv = awin_d.rearrange("(kt p) n -> p kt n", p=128)
    for kt in range(8):
        S.dma("pool", w_in[:, kt, :], wv[:, kt, :], writes=[t_win])
    w_out = C.sb([128, 8, D], BF16, "a_wout")
    t_wout = T("a_wout")
    S.dma("pool", w_out[:], awout_d.rearrange("(kt p) n -> p kt n", p=128), writes=[t_wout])
    g_bc = C.sb([128, D], F32, "a_g")
    b_bc = C.sb([128, D], F32, "a_b")
    t_gb = T("a_gb")
    S.dma("sp", g_bc[:], lng_d[0:1, :].partition_broadcast(128), writes=[t_gb])
    S.dma("sp", b_bc[:], lnb_d[0:1, :].partition_broadcast(128), writes=[t_gb])
    maskA = C.sb([128, 2, 128], BF16, "maskA")
    t_maskA = T("maskA")
    S.dma("pool", maskA[:], ma_d.rearrange("p (a q) -> p a q", a=2), writes=[t_maskA])
    esink = C.sb([128, 8], F32, "esink")
    t_esink = T("esink")
    S.dma("sp", esink[:], sink_d.partition_broadcast(128), writes=[t_esink])
    S.add("act", lambda e: e.activation(out=esink[:], in_=esink[:], func=AF.Exp), [t_esink], [t_esink])

    EB = C.sb([128, 8, 5, 128], BF16, "EB")
    t_EB = T("EB")
    stage = [C.sb([128, 640], F32, "nbstage%d" % i) for i in range(2)]
    t_stage = [T("nbstage%d" % i) for i in range(2)]
    type_idx = {"int": 0, "t0": 1, "t1": 2, "tm2": 3, "tm1": 4}
    cur_type = [None]

    def load_type(name):
        if cur_type[0] == name:
            return
        cur_type[0] = name
        ti = type_idx[name]
        for hh in range(8):
            sl = hh % 2
            S.dma("sp", stage[sl][:], nbt_d[ti, :, hh * 640:(hh + 1) * 640], writes=[t_stage[sl]])
            S.add("act", lambda e, hh=hh, sl=sl: e.activation(out=EB[:, hh, :, :].rearrange("p j q -> p (j q)"), in_=stage[sl][:],
                                                                func=AF.Exp), [t_stage[sl]], [t_EB])

    xf = [C.sb([128, D], F32, "xf%d" % i) for i in range(2)]
    t_xf = [T("xf%d" % i) for i in range(2)]
    xb = [C.sb([128, D], BF16, "xb%d" % i) for i in range(2)]
    t_xb = [T("xb%d" % i) for i in range(2)]
    xT = [C.sb([128, 8, 512], BF16, "xT%d" % i) for i in range(2)]
    t_xT = [[T("xT%d_%d" % (i, j)) for j in range(4)] for i in range(2)]
    qaT = [C.sb([128, 4, 512], BF16, "qaT%d" % i) for i in range(2)]
    t_qaT = [[T("qaT%d_%d" % (i, c)) for c in range(4)] for i in range(2)]
    qbT = [C.sb([128, 4, 512], BF16, "qbT%d" % i) for i in range(2)]
    t_qbT = [[T("qbT%d_%d" % (i, c)) for c in range(4)] for i in range(2)]
    kaT = C.sb([128, 12 * 128], BF16, "kaT")
    kbT = C.sb([128, 4, 12 * 128], BF16, "kbT")
    t_kaT = [T("kaT%d" % i) for i in range(3)]
    t_kbT = [[T("kbT%d_%d" % (i, c)) for c in range(4)] for i in range(3)]
    va = C.sb([128, 12, 2, 65], BF16, "va")
    vb = C.sb([128, 12, 8, 65], BF16, "vb")
    t_v = [T("v%d" % i) for i in range(12)]
    S.add("pool", lambda e: e.memset(va[:], 1.0), [], t_v)
    S.add("pool", lambda e: e.memset(vb[:], 1.0), [], t_v)
    cs_t = C.sb([128, 2, 512], F32, "cossin")
    t_cs = T("cossin")
    r1 = C.sb([128, 512], F32, "rope1")
    r2 = C.sb([128, 512], F32, "rope2")
    t_r1, t_r2 = T("r1"), T("r2")
    esA = [C.sb([128, 3, 512], BF16, "esA%d" % i) for i in range(2)]
    t_esA = [[T("esA%d_%d" % (i, j)) for j in range(3)] for i in range(2)]
    esB = [C.sb([128, 5, 128], BF16, "esB%d" % i) for i in range(2)]
    t_esB = [T("esB%d" % i) for i in range(2)]
    o_tok = C.sb([128, D], BF16, "o_tok")
    t_otok = T("o_tok")
    den = C.sb([128, 16], F32, "den")
    t_den = T("den")
    oT = C.sb([128, 8, 128], BF16, "oT")
    t_oT = T("oT")
    xres = [C.sb([128, D], F32, "xres%d" % i) for i in range(2)]
    t_xres = [T("xres%d" % i) for i in range(2)]
    lnb = {"h": (C.sb([128, D], F32, "ln_h"), T("ln_h")), "st": (C.sb([128, 12], F32, "ln_st"), T("ln_st")),
           "mv": (C.sb([128, 4], F32, "ln_mv"), T("ln_mv"))}
    xo = [C.sb([128, D], F32, "xo%d" % i) for i in range(2)]
    t_xo = [T("xo%d" % i) for i in range(2)]

    pP = [C.ps([128, 512], F32, "pP%d" % i) for i in range(2)]
    t_pP = [T("pP%d" % i) for i in range(2)]
    pS = [C.ps([128, 512], F32, "pS%d" % i) for i in range(4)]
    t_pS = [T("pS%d" % i) for i in range(4)]
    pO = [C.ps([128, 512], F32, "pO%d" % i) for i in range(2)]
    t_pO = [T("pO%d" % i) for i in range(2)]
    pcount = [0]

    def next_pP():
        i = pcount[0] % 2
        pcount[0] += 1
        return pP[i], t_pP[i]

    COL = {"qa": 0, "qas": 512, "ka": 1024, "kas": 1152, "qb": 1280, "kb": 1792, "va": 2304, "vb": 2432}

    def project_group(g):
        gb = g % 2
        rg = g % 3
        for i in range(4):
            t = 4 * g + i
            s = t % 2
            S.dma("sp", xf[s][:], x_d[t * 128:(t + 1) * 128, :], writes=[t_xf[s]])
            S.add("dve", lambda e, s=s: e.tensor_copy(out=xb[s][:], in_=xf[s][:]), [t_xf[s]], [t_xb[s]])
            pp, t_pp = next_pP()
            ppb = pp[:].bitcast(BF16)
            for kt in range(8):
                S.add("pe", lambda e, s=s, kt=kt, ppb=ppb: e.transpose(ppb[:, kt * 128:(kt + 1) * 128], xb[s][:, kt * 128:(kt + 1) * 128], ident[:]),
                      [t_xb[s], t_ident], [t_pp])
            S.add("act", lambda e, i=i, gb=gb, ppb=ppb: e.copy(out=xT[gb][:, :, i * 128:(i + 1) * 128],
                                                               in_=ppb.rearrange("p (k t) -> p k t", k=8)),
                  [t_pp], [t_xT[gb][i]])
        S.dma("sp", cs_t[:, 0, :], cos_d[:, g * 512:(g + 1) * 512], writes=[t_cs])
        S.dma("sp", cs_t[:, 1, :], sin_d[:, g * 512:(g + 1) * 512], writes=[t_cs])

        def fm_proj(col):
            pp, t_pp = next_pP()
            for kt in range(8):
                S.add("pe", lambda e, kt=kt, col=col, pp=pp: e.matmul(pp[:], lhsT=w_in[:, kt, col:col + 128], rhs=xT[gb][:, kt, :],
                                                                     start=(kt == 0), stop=(kt == 7)),
                      [t_win] + t_xT[gb], [t_pp])
            return pp, t_pp

        def rope_half(col, which, rdst, t_rdst):
            pp, t_pp = fm_proj(col)
            S.add("dve", lambda e, pp=pp: e.tensor_tensor(out=rdst[:], in0=pp[:], in1=cs_t[:, which, :], op=ALU.mult), [t_pp, t_cs], [t_rdst])

        def rope_add(dst, t_dst):
            S.add("pool", lambda e: e.tensor_tensor(out=dst, in0=r1[:], in1=r2[:], op=ALU.add), [t_r1, t_r2], [t_dst])

        rope_half(COL["ka"], 0, r1, t_r1)
        rope_half(COL["kas"], 1, r2, t_r2)
        rope_add(kaT[:, rg * 512:(rg + 1) * 512], t_kaT[rg])
        for c in range(4):
            pp, t_pp = fm_proj(COL["kb"] + c * 128)
            S.add("act", lambda e, pp=pp, c=c: e.copy(out=kbT[:, c, rg * 512:(rg + 1) * 512], in_=pp[:]), [t_pp], [t_kbT[rg][c]])
        for i in range(4):
            t = 4 * g + i
            slot = t % 12
            pp, t_pp = next_pP()
            for kt in range(8):
                S.add("pe", lambda e, kt=kt, pp=pp, i=i: e.matmul(pp[:], lhsT=xT[gb][:, kt, i * 128:(i + 1) * 128],
                                                                 rhs=w_in[:, kt, COL["vb"]:COL["vb"] + 512],
                                                                 start=(kt == 0), stop=(kt == 7)),
                      [t_win, t_xT[gb][i]], [t_pp])
            S.add("act", lambda e, pp=pp, slot=slot: e.copy(out=vb[:, slot, :, 0:64], in_=pp[:].rearrange("p (h d) -> p h d", h=8)),
                  [t_pp], [t_v[slot]])
            pp, t_pp = next_pP()
            for kt in range(8):
                S.add("pe", lambda e, kt=kt, pp=pp, i=i: e.matmul(pp[:, 0:128], lhsT=xT[gb][:, kt, i * 128:(i + 1) * 128],
                                                                 rhs=w_in[:, kt, COL["va"]:COL["va"] + 128],
                                                                 start=(kt == 0), stop=(kt == 7)),
                      [t_win, t_xT[gb][i]], [t_pp])
            S.add("act", lambda e, pp=pp, slot=slot: e.copy(out=va[:, slot, :, 0:64], in_=pp[:, 0:128].rearrange("p (h d) -> p h d", h=2)),
                  [t_pp], [t_v[slot]])
        yield "K"
        for c in range(4):
            rope_half(COL["qa"] + c * 128, 0, r1, t_r1)
            yield None
            rope_half(COL["qas"] + c * 128, 1, r2, t_r2)
            rope_add(qaT[gb][:, c, :], t_qaT[gb][c])
            yield None
        for c in range(4):
            pp, t_pp = fm_proj(COL["qb"] + c * 128)
            S.add("act", lambda e, pp=pp, c=c: e.copy(out=qbT[gb][:, c, :], in_=pp[:]), [t_pp], [t_qbT[gb][c]])
            yield None

    es_cnt = [0, 0]
    ps_rr = [0]

    def attend_tile(t, pump):
        g = t // 4
        gb = g % 2
        i = t % 4
        qsl = slice(i * 128, (i + 1) * 128)
        js = [j for j in (-1, 0, 1) if 0 <= t + j < NT]

        ebs = []
        for gi in range(2):
            eb = es_cnt[0] % 2
            es_cnt[0] += 1
            ebs.append(eb)
            for j in js:
                tk = t + j
                rgk = (tk // 4) % 3
                kof = (tk % 12) * 128
                bi = ps_rr[0] % 4
                ps_rr[0] += 1
                ps, t_ps = pS[bi], t_pS[bi]
                S.add("pe", lambda e, ps=ps, kof=kof, gi=gi: e.matmul(ps[:].rearrange("p (c q) -> p c q", c=4),
                                                                      lhsT=kaT[gi * 64:(gi + 1) * 64, kof:kof + 128],
                                                                      rhs=qaT[gb][gi * 64:(gi + 1) * 64, :, qsl], start=True, stop=True),
                      [t_kaT[rgk]] + t_qaT[gb], [t_ps])
                S.add("act", lambda e, ps=ps, eb=eb, j=j: e.activation(out=esA[eb][:, j + 1, :], in_=ps[:], func=AF.Exp, scale=SCALE),
                      [t_ps], [t_esA[eb][j + 1]])
                if j != 0:
                    mi = 0 if j == -1 else 1
                    S.add("dve", lambda e, eb=eb, j=j, mi=mi: e.tensor_tensor(out=esA[eb][:, j + 1, :].rearrange("p (c q) -> p c q", c=4),
                                                                              in0=esA[eb][:, j + 1, :].rearrange("p (c q) -> p c q", c=4),
                                                                              in1=maskA[:, mi:mi + 1, :].to_broadcast([128, 4, 128]), op=ALU.mult),
                          [t_esA[eb][j + 1], t_maskA], [t_esA[eb][j + 1]])
            pump()
        for gi in range(2):
            eb = ebs[gi]
            for c in range(4):
                for n, j in enumerate(js):
                    slot = (t + j) % 12
                    S.add("pe", lambda e, c=c, j=j, slot=slot, gi=gi, eb=eb, n=n, nj=len(js): e.matmul(pO[gi][:, c * 65:(c + 1) * 65],
                                                                                                    lhsT=esA[eb][:, j + 1, c * 128:(c + 1) * 128],
                                                                                                    rhs=va[:, slot, gi, :], start=(n == 0), stop=(n == nj - 1)),
                          [t_esA[eb][j + 1], t_v[slot]], [t_pO[gi]])
        for gi in range(2):
            ov = pO[gi][:, 0:260].rearrange("p (c e) -> p c e", c=4)
            S.add("dve", lambda e, gi=gi, ov=ov: e.tensor_tensor(out=den[:, gi * 4:(gi + 1) * 4], in0=ov[:, :, 64], in1=esink[:, gi * 4:(gi + 1) * 4], op=ALU.add),
                  [t_pO[gi], t_esink], [t_den])
            S.add("dve", lambda e, gi=gi: e.reciprocal(out=den[:, gi * 4:(gi + 1) * 4], in_=den[:, gi * 4:(gi + 1) * 4]), [t_den], [t_den])
            S.add("dve", lambda e, gi=gi, ov=ov: e.tensor_tensor(out=o_tok[:, gi * 256:(gi + 1) * 256].rearrange("p (c d) -> p c d", c=4), in0=ov[:, :, 0:64],
                                                                 in1=den[:, gi * 4:(gi + 1) * 4].unsqueeze(2).to_broadcast([128, 4, 64]), op=ALU.mult),
                  [t_pO[gi], t_den], [t_otok])

        typ = _na_type_of(t, NT)
        load_type(typ)
        offs = _na_types(NT)[typ]

        def b_scores(hh):
            c, pb = hh // 2, (hh % 2) * 64
            sb_i = es_cnt[1] % 2
            es_cnt[1] += 1
            pa, t_pa = pS[2 * sb_i], t_pS[2 * sb_i]
            pb2, t_pb2 = pS[2 * sb_i + 1], t_pS[2 * sb_i + 1]
            for j, off in enumerate(offs):
                tk = t + off
                rgk = (tk // 4) % 3
                kof = (tk % 12) * 128
                dst, t_dst = (pa[:, j * 128:(j + 1) * 128], t_pa) if j < 4 else (pb2[:, 0:128], t_pb2)
                S.add("pe", lambda e, dst=dst, kof=kof, c=c, pb=pb: e.matmul(dst, lhsT=kbT[pb:pb + 64, c, kof:kof + 128],
                                                                             rhs=qbT[gb][pb:pb + 64, c, qsl], start=True, stop=True),
                      [t_kbT[rgk][c], t_qbT[gb][c]], [t_dst])
            S.add("act", lambda e, pa=pa, sb_i=sb_i: e.activation(out=esB[sb_i][:, 0:4, :].rearrange("p j q -> p (j q)"), in_=pa[:], func=AF.Exp, scale=SCALE),
                  [t_pa], [t_esB[sb_i]])
            S.add("act", lambda e, pb2=pb2, sb_i=sb_i: e.activation(out=esB[sb_i][:, 4, :], in_=pb2[:, 0:128], func=AF.Exp, scale=SCALE),
                  [t_pb2], [t_esB[sb_i]])
            S.add("dve", lambda e, sb_i=sb_i, hh=hh: e.tensor_tensor(out=esB[sb_i][:], in0=esB[sb_i][:], in1=EB[:, hh, :, :], op=ALU.mult),
                  [t_esB[sb_i], t_EB], [t_esB[sb_i]])
            return sb_i

        def b_pv(hh, sb_i):
            ob = hh // 4
            for j, off in enumerate(offs):
                slot = (t + off) % 12
                S.add("pe", lambda e, j=j, slot=slot, hh=hh, sb_i=sb_i, ob=ob: e.matmul(pO[ob][:, (hh % 4) * 65:(hh % 4 + 1) * 65],
                                                                                       lhsT=esB[sb_i][:, j, :], rhs=vb[:, slot, hh, :],
                                                                                       start=(j == 0), stop=(j == 4)),
                      [t_esB[sb_i], t_v[slot]], [t_pO[ob]])

        prev = None
        for hh in range(8):
            sb_i = b_scores(hh)
            pump()
            if prev is not None:
                b_pv(*prev)
            prev = (hh, sb_i)
        b_pv(*prev)
        for ob in range(2):
            ov = pO[ob][:, 0:260].rearrange("p (c e) -> p c e", c=4)
            S.add("dve", lambda e, ob=ob, ov=ov: e.reciprocal(out=den[:, 8 + ob * 4:8 + (ob + 1) * 4], in_=ov[:, :, 64]), [t_pO[ob]], [t_den])
            S.add("dve", lambda e, ob=ob, ov=ov: e.tensor_tensor(out=o_tok[:, 512 + ob * 256:512 + (ob + 1) * 256].rearrange("p (c d) -> p c d", c=4), in0=ov[:, :, 0:64],
                                                                 in1=den[:, 8 + ob * 4:8 + (ob + 1) * 4].unsqueeze(2).to_broadcast([128, 4, 64]), op=ALU.mult),
                  [t_pO[ob], t_den], [t_otok])
        pp, t_pp = next_pP()
        ppb = pp[:].bitcast(BF16)
        for kt in range(8):
            S.add("pe", lambda e, kt=kt, ppb=ppb: e.transpose(ppb[:, kt * 128:(kt + 1) * 128], o_tok[:, kt * 128:(kt + 1) * 128], ident[:]),
                  [t_otok, t_ident], [t_pp])
        S.add("act", lambda e, ppb=ppb: e.copy(out=oT[:], in_=ppb.rearrange("p (k t) -> p k t", k=8)), [t_pp], [t_oT])
        xr = t % 2
        S.dma("sp", xres[xr][:], x_d[t * 128:(t + 1) * 128, :], writes=[t_xres[xr]])
        pump()
        halves = []
        for hf in range(2):
            pp, t_pp = next_pP()
            for kt in range(8):
                S.add("pe", lambda e, kt=kt, pp=pp, hf=hf: e.matmul(pp[:], lhsT=oT[:, kt, :], rhs=w_out[:, kt, hf * 512:(hf + 1) * 512],
                                                                   start=(kt == 0), stop=(kt == 7)),
                      [t_oT, t_wout], [t_pp])
            halves.append((pp, t_pp))
        h, t_h = lnb["h"]
        for hf, (pp, t_pp) in enumerate(halves):
            S.add("dve", lambda e, pp=pp, hf=hf, xr=xr: e.scalar_tensor_tensor(out=h[:, hf * 512:(hf + 1) * 512], in0=xres[xr][:, hf * 512:(hf + 1) * 512],
                                                                              scalar=ALPHA, in1=pp[:], op0=ALU.mult, op1=ALU.add),
                  [t_xres[xr], t_pp], [t_h])
        ln_core(S, lnb, g_bc[:], b_bc[:], t_gb, eps_t, t_eps, xo[xr][:], t_xo[xr])
        S.dma("pool", x1_d[t * 128:(t + 1) * 128, :], xo[xr][:], reads=[t_xo[xr]])

    for g in range(NG + 1):
        gen = project_group(g) if g < NG else iter(())
        for u in gen:
            if u == "K":
                break

        def pump(gen=gen):
            next(gen, None)

        if g >= 1:
            for t in range(4 * (g - 1), 4 * g):
                attend_tile(t, pump)
        for u in gen:
            pass


def phase_c1(nc, S, C, T_, xin_d, swin_d, cw_d, cb_d, dtb_d, xtok_d, btok_d, bT_d, cT_d, zs_d, dt_d,
             ident, t_ident, tri_d=None, alog_d=None, acf_d=None, acb_d=None, acfT_d=None, acbT_d=None, **_):
    NT = T_ // 128
    NG2 = NT // 2
    w_in = C.sb([128, 8, 6208], BF16, "c_win")
    NBLK = 13
    t_w = [T() for i in range(NBLK)]
    wv = swin_d.rearrange("(kt p) n -> p kt n", p=128)
    order = [4, 5, 6, 7, 8, 9, 10, 11, 12, 0, 1, 2, 3]
    for bi in order:
        c0, c1 = bi * 512, min(6208, (bi + 1) * 512)
        S.dma("pool", w_in[:, :, c0:c1], wv[:, :, c0:c1], writes=[t_w[bi]])

    def wdeps(c0, c1):
        return [t_w[b] for b in range(c0 // 512, (c1 - 1) // 512 + 1)]

    cw = C.sb([128, 160], F32, "c_cw")
    cb = C.sb([128, 32], F32, "c_cb")
    t_cw = T()
    S.dma("sp", cw[:], cw_d, writes=[t_cw])
    S.dma("sp", cb[:], cb_d, writes=[t_cw])
    dtb = C.sb([128, 64], F32, "c_dtb")
    S.dma("sp", dtb[:], dtb_d.partition_broadcast(128), writes=[t_cw])
    one_t = C.sb([128, 1], F32, "c_one")
    S.add("pool", lambda e: e.memset(one_t[:], 1.0), [], [t_cw])
    cmask = C.sb([128, 4, 128], BF16, "c_masks")
    S.dma("pool", cmask[:], tri_d.rearrange("p (a q) -> p a q", a=4), writes=[t_cw])
    aneg = C.sb([128, 64], F32, "c_aneg")
    t_an = T()
    S.dma("sp", aneg[:], alog_d.partition_broadcast(128), writes=[t_an])
    S.add("act", lambda e: e.activation(out=aneg[:], in_=aneg[:], func=AF.Exp), [t_an], [t_an])
    S.add("dve", lambda e: e.tensor_scalar(out=aneg[:], in0=aneg[:], scalar1=-1.0, scalar2=None, op0=ALU.mult), [t_an], [t_an])
    av = C.sb([128, 64], F32, "c_av")
    av_hi = C.sb([128, 64], BF16, "c_avhi")
    av_lo = C.sb([128, 64], BF16, "c_avlo")
    acs = C.sb([128, 64], F32, "c_acs")
    t_av = T()
    t_acs = T()
    acsT = C.sb([32, 256], F32, "c_acsT")
    t_acsT = T()
    diag = C.sb([128, 32, 5, 128], BF16, "c_diag")
    t_diag = [T() for i in range(32)]
    for c in range(32):
        for k in range(5):
            S.add("dve" if (c + k) % 2 == 0 else "pool",
                  lambda e, c=c, k=k: e.tensor_scalar(out=diag[:, c, k, :], in0=ident[:], scalar1=cw[:, k * 32 + c:k * 32 + c + 1], scalar2=None, op0=ALU.mult),
                  [t_ident, t_cw], [t_diag[c]])
    xb = [C.sb([128, D], BF16, "c_xb%d" % i) for i in range(2)]
    t_xb = [T(), T()]
    xT = [C.sb([128, 8, 260], BF16, "c_xT%d" % i) for i in range(2)]
    t_xT = [[T(), T(), T()] for i in range(2)]
    u = [C.sb([128, 260], BF16, "c_u%d" % i) for i in range(2)]
    t_u = [T(), T()]
    xsT = C.sb([128, 16, 256], BF16, "c_xsT")
    t_xsT = [T() for i in range(16)]
    BT = C.sb([128, 8, 256], BF16, "c_BT")
    t_BT = [T() for i in range(8)]
    CT = C.sb([128, 8, 256], BF16, "c_CT")
    t_CT = [T() for i in range(8)]
    xtok = C.sb([128, 2048], BF16, "c_xtok")
    t_xtok = T()
    btok = C.sb([128, 1024], BF16, "c_btok")
    t_btok = T()
    zs = C.sb([128, 2048], BF16, "c_zs")
    t_zs = T()
    dv = C.sb([128, 4, 64], F32, "c_dv")
    t_dv = T()
    pT = [C.ps([128, 512], F32, "c_pT%d" % i) for i in range(2)]
    t_pT = [T(), T()]
    pU = [C.ps([128, 512], F32, "c_pU%d" % i) for i in range(2)]
    t_pU = [T(), T()]
    pC = [C.ps([128, 512], F32, "c_pC%d" % i) for i in range(2)]
    t_pC = [T(), T()]
    pZ = [C.ps([128, 512], F32, "c_pZ%d" % i) for i in range(2)]
    t_pZ = [T(), T()]
    cnt = {"T": 0, "U": 0, "C": 0, "Z": 0, "x": 0}

    def nxt(key, arr, tarr):
        i = cnt[key] % 2
        cnt[key] += 1
        return arr[i], tarr[i]

    bTv = bT_d.rearrange("(g n) t -> n g t", n=128)
    cTv = cT_d.rearrange("(g n) t -> n g t", n=128)

    for g2 in range(NG2):
        gb = g2 % 2
        t0 = 256 * g2
        pieces = [(t0 - 2, 128, 0), (t0 + 126, 128, 128), (t0 + 254, 4, 256)]
        for pi_, (r0, nr, c0) in enumerate(pieces):
            s_ = cnt["x"] % 2
            cnt["x"] += 1
            lo, hi = max(r0, 0), min(r0 + nr, T_)
            if lo > r0 or hi < r0 + nr:
                S.add("pool", lambda e, s_=s_: e.memset(xb[s_][:], 0.0), [], [t_xb[s_]])
            S.dma("pool", xb[s_][lo - r0:hi - r0, :], xin_d[lo:hi, :], writes=[t_xb[s_]])
            pp, t_pp = nxt("T", pT, t_pT)
            ppb = pp[:].bitcast(BF16)
            for kt in range(8):
                S.add("pe", lambda e, s_=s_, kt=kt, ppb=ppb, nr=nr: e.transpose(ppb[:, kt * 128:kt * 128 + nr], xb[s_][0:nr, kt * 128:(kt + 1) * 128], ident[0:nr, 0:nr]),
                      [t_xb[s_], t_ident], [t_pp])
            S.add("dve", lambda e, gb=gb, ppb=ppb, nr=nr, c0=c0: e.tensor_copy(out=xT[gb][:, :, c0:c0 + nr], in_=ppb.rearrange("p (k t) -> p k t", k=8)[:, :, 0:nr]),
                  [t_pp], [t_xT[gb][pi_]])
        for c in range(32):
            col = 2048 + c * 128
            pu, t_pu = nxt("U", pU, t_pU)
            for kt in range(8):
                S.add("pe", lambda e, kt=kt, col=col, pu=pu, gb=gb: e.matmul(pu[:, 0:260], lhsT=w_in[:, kt, col:col + 128], rhs=xT[gb][:, kt, :],
                                                                            start=(kt == 0), stop=(kt == 7)),
                      wdeps(col, col + 128) + t_xT[gb], [t_pu])
            ui = c % 2
            S.add("dve", lambda e, ui=ui, pu=pu: e.tensor_copy(out=u[ui][:], in_=pu[:, 0:260]), [t_pu], [t_u[ui]])
            pc, t_pc = nxt("C", pC, t_pC)
            for k in range(5):
                S.add("pe", lambda e, k=k, c=c, ui=ui, pc=pc: e.matmul(pc[:, 0:256], lhsT=diag[:, c, k, :], rhs=u[ui][:, k:k + 256],
                                                                      start=(k == 0), stop=(k == 4)),
                      [t_diag[c], t_u[ui]], [t_pc])
            if c < 16:
                dst, t_dst = xsT[:, c, :], t_xsT[c]
            elif c < 24:
                dst, t_dst = BT[:, c - 16, :], t_BT[c - 16]
            else:
                dst, t_dst = CT[:, c - 24, :], t_CT[c - 24]
            S.add("act", lambda e, pc=pc, dst=dst, c=c: e.activation(out=dst, in_=pc[:, 0:256], func=AF.Silu, bias=cb[:, c:c + 1], scale=1.0),
                  [t_pc, t_cw], [t_dst])
        S.dma("sp", bTv[:, :, t0:t0 + 256], BT[:], reads=t_BT)
        S.dma("sp", cTv[:, :, t0:t0 + 256], CT[:], reads=t_CT)
        for i in range(2):
            t = 2 * g2 + i
            for qz in range(4):
                pz, t_pz = nxt("Z", pZ, t_pZ)
                for kt in range(8):
                    S.add("pe", lambda e, kt=kt, qz=qz, pz=pz, gb=gb, i=i: e.matmul(pz[:], lhsT=xT[gb][:, kt, 2 + i * 128:2 + (i + 1) * 128],
                                                                                   rhs=w_in[:, kt, qz * 512:(qz + 1) * 512], start=(kt == 0), stop=(kt == 7)),
                          wdeps(qz * 512, (qz + 1) * 512) + t_xT[gb], [t_pz])
                S.add("act", lambda e, pz=pz, qz=qz: e.activation(out=zs[:, qz * 512:(qz + 1) * 512], in_=pz[:], func=AF.Silu), [t_pz], [t_zs])
            S.dma("sp", zs_d[t * 128:(t + 1) * 128, :], zs[:], reads=[t_zs])
            for hf in range(2):
                pp, t_pp = nxt("T", pT, t_pT)
                ppb = pp[:].bitcast(BF16)
                for cc in range(8):
                    c = hf * 8 + cc
                    S.add("pe", lambda e, c=c, cc=cc, ppb=ppb, i=i: e.transpose(ppb[:, cc * 128:(cc + 1) * 128], xsT[:, c, i * 128:(i + 1) * 128], ident[:]),
                          [t_xsT[c], t_ident], [t_pp])
                S.add("dve", lambda e, ppb=ppb, hf=hf: e.tensor_copy(out=xtok[:, hf * 1024:(hf + 1) * 1024], in_=ppb), [t_pp], [t_xtok])
            S.dma("sp", xtok_d[t * 128:(t + 1) * 128, :], xtok[:], reads=[t_xtok])
            pp, t_pp = nxt("T", pT, t_pT)
            ppb = pp[:].bitcast(BF16)
            for gg in range(8):
                S.add("pe", lambda e, gg=gg, ppb=ppb, i=i: e.transpose(ppb[:, gg * 128:(gg + 1) * 128], BT[:, gg, i * 128:(i + 1) * 128], ident[:]),
                      [t_BT[gg], t_ident], [t_pp])
            S.add("dve", lambda e, ppb=ppb: e.tensor_copy(out=btok[:], in_=ppb), [t_pp], [t_btok])
            S.dma("sp", btok_d[t * 128:(t + 1) * 128, :], btok[:], reads=[t_btok])
        for i in range(2):
            t = 2 * g2 + i
            pz, t_pz = nxt("Z", pZ, t_pZ)
            for kt in range(8):
                S.add("pe", lambda e, kt=kt, pz=pz, gb=gb, i=i: e.matmul(pz[:, 0:64], lhsT=xT[gb][:, kt, 2 + i * 128:2 + (i + 1) * 128],
                                                                        rhs=w_in[:, kt, 6144:6208], start=(kt == 0), stop=(kt == 7)),
                      wdeps(6144, 6208) + t_xT[gb], [t_pz])
            S.add("dve", lambda e, pz=pz: e.tensor_tensor(out=dv[:, 0, :], in0=pz[:, 0:64], in1=dtb[:], op=ALU.add), [t_pz, t_cw], [t_dv])
            S.add("act", lambda e: e.activation(out=dv[:, 1, :], in_=dv[:, 0, :], func=AF.Abs), [t_dv], [t_dv])
            S.add("act", lambda e: e.activation(out=dv[:, 1, :], in_=dv[:, 1, :], func=AF.Exp, scale=-1.0), [t_dv], [t_dv])
            S.add("act", lambda e: e.activation(out=dv[:, 1, :], in_=dv[:, 1, :], func=AF.Ln, bias=one_t[:], scale=1.0), [t_dv, t_cw], [t_dv])
            S.add("dve", lambda e: e.tensor_scalar(out=dv[:, 2, :], in0=dv[:, 0, :], scalar1=0.0, scalar2=None, op0=ALU.max), [t_dv], [t_dv])
            S.add("dve", lambda e: e.tensor_tensor(out=dv[:, 3, :], in0=dv[:, 2, :], in1=dv[:, 1, :], op=ALU.add), [t_dv], [t_dv])
            S.dma("sp", dt_d[t * 128:(t + 1) * 128, :], dv[:, 3, :], reads=[t_dv])
            S.add("dve", lambda e: e.tensor_tensor(out=av[:], in0=dv[:, 3, :], in1=aneg[:], op=ALU.mult), [t_dv, t_an], [t_av])
            S.add("dve", lambda e: e.tensor_copy(out=av_hi[:], in_=av[:]), [t_av], [t_av])
            S.add("dve", lambda e: e.tensor_tensor(out=av_lo[:], in0=av[:], in1=av_hi[:], op=ALU.subtract), [t_av], [t_av])
            pz, t_pz = nxt("Z", pZ, t_pZ)
            for ci, mi in ((0, 0), (1, 2)):
                S.add("pe", lambda e, ci=ci, mi=mi, pz=pz: e.matmul(pz[:, ci * 32:(ci + 1) * 32], lhsT=cmask[:, mi, :], rhs=av_hi[:, ci * 32:(ci + 1) * 32], start=True, stop=False),
                      [t_av, t_cw], [t_pz])
                S.add("pe", lambda e, ci=ci, mi=mi, pz=pz: e.matmul(pz[:, ci * 32:(ci + 1) * 32], lhsT=cmask[:, mi, :], rhs=av_lo[:, ci * 32:(ci + 1) * 32], start=False, stop=True),
                      [t_av, t_cw], [t_pz])
            S.add("dve", lambda e, pz=pz: e.tensor_copy(out=acs[:], in_=pz[:, 0:64]), [t_pz], [t_acs])
            S.dma("sp", acf_d[t * 128:(t + 1) * 128, :], acs[:, 0:32], reads=[t_acs])
            S.dma("sp", acb_d[t * 128:(t + 1) * 128, :], acs[:, 32:64], reads=[t_acs])
            pz, t_pz = nxt("Z", pZ, t_pZ)
            for ci, mi in ((0, 0), (1, 2)):
                S.add("pe", lambda e, ci=ci, mi=mi, pz=pz: e.matmul(pz[0:32, ci * 128:(ci + 1) * 128], lhsT=av_hi[:, ci * 32:(ci + 1) * 32], rhs=cmask[:, mi, :], start=True, stop=False),
                      [t_av, t_cw], [t_pz])
                S.add("pe", lambda e, ci=ci, mi=mi, pz=pz: e.matmul(pz[0:32, ci * 128:(ci + 1) * 128], lhsT=av_lo[:, ci * 32:(ci + 1) * 32], rhs=cmask[:, mi, :], start=False, stop=True),
                      [t_av, t_cw], [t_pz])
            S.add("dve", lambda e, pz=pz: e.tensor_copy(out=acsT[:], in_=pz[0:32, 0:256]), [t_pz], [t_acsT])
            S.dma("sp", acfT_d[t, :].rearrange("(h l) -> h l", h=32), acsT[:, 0:128], reads=[t_acsT])
            S.dma("sp", acbT_d[t, :].rearrange("(h l) -> h l", h=32), acsT[:, 128:256], reads=[t_acsT])


def phase_ssd(nc, S, C, T_, bwd, xtok_d, btok_d, bT_d, cT_d, dt_d, yf_d, tri_d, alog_d, dsk_d, acf_d, acb_d,
              acfT_d=None, acbT_d=None,
              zs_d=None, normw_d=None, swout_d=None, xres_d=None, xout_d=None, ln_row=2,
              ident=None, t_ident=None, eps_t=None, t_eps=None, lng_d=None, lnb_d=None, **_):
    NT = T_ // 128
    masks = C.sb([128, 4, 128], BF16, "s_masks")
    t_c = T()
    S.dma("pool", masks[:], tri_d.rearrange("p (a q) -> p a q", a=4), writes=[t_c])
    M1 = masks[:, 2, :] if bwd else masks[:, 0, :]
    M2 = masks[:, 3, :] if bwd else masks[:, 1, :]
    ones_m = C.sb([128, 128], BF16, "s_ones")
    S.add("pool", lambda e: e.memset(ones_m[:], 1.0), [], [t_c])
    zero_t = C.sb([128, 1], F32, "s_zero")
    S.add("pool", lambda e: e.memset(zero_t[:], 0.0), [], [t_c])
    aneg = C.sb([128, 32], F32, "s_aneg")
    h0 = 32 if bwd else 0
    S.dma("sp", aneg[:], alog_d[:, h0:h0 + 32].partition_broadcast(128), writes=[t_c])
    S.add("act", lambda e: e.activation(out=aneg[:], in_=aneg[:], func=AF.Exp), [t_c], [t_c])
    S.add("dve", lambda e: e.tensor_scalar(out=aneg[:], in0=aneg[:], scalar1=-1.0, scalar2=None, op0=ALU.mult), [t_c], [t_c])
    if not bwd:
        dsk = C.sb([128, 32], F32, "s_dsk")
        S.dma("sp", dsk[:], dsk_d.partition_broadcast(128), writes=[t_c])
        sk = [C.sb([128, 2048], F32, "s_sk%d" % i) for i in range(2)]
        t_sk = [T(), T()]
    else:
        yf = [C.sb([128, 2048], F32, "s_yf%d" % i) for i in range(2)]
        t_yf = [T(), T()]
    R2 = range(2)
    xtok = [C.sb([128, 32, 64], BF16, "s_xtok%d" % i) for i in R2]
    t_xtok = [T() for i in R2]
    btok = [C.sb([128, 1024], BF16, "s_btok%d" % i) for i in range(3)]
    t_btok = [T() for i in range(3)]
    BT = [C.sb([128, 8, 128], BF16, "s_BT%d" % i) for i in R2]
    t_BT = [T() for i in R2]
    CT = [C.sb([128, 8, 128], BF16, "s_CT%d" % i) for i in R2]
    t_CT = [T() for i in R2]
    dtt = [C.sb([128, 64], F32, "s_dt%d" % i) for i in R2]
    t_dt = [T() for i in R2]
    acT_d = acbT_d if bwd else acfT_d
    ac_d = acb_d if bwd else acf_d
    Rb = [C.sb([128, 32, 128], F32, "s_Rb%d" % i) for i in R2]
    t_Rb = [T() for i in R2]
    act_ = [C.sb([128, 32], F32, "s_act%d" % i) for i in R2]
    t_act = [T() for i in R2]
    a_f = [C.sb([128, 32], F32, "s_a%d" % i) for i in R2]
    a_hi = [C.sb([128, 32], BF16, "s_ahi%d" % i) for i in R2]
    a_lo = [C.sb([128, 32], BF16, "s_alo%d" % i) for i in R2]
    t_a = [T() for i in R2]
    esm = [C.sb([128, 96], F32, "s_esm%d" % i) for i in R2]
    t_esm = [T() for i in R2]
    GTm = [C.sb([128, 8, 128], BF16, "s_GTm%d" % i) for i in R2]
    t_GTm = [T() for i in R2]
    Dm = [C.sb([128, 32, 128], BF16, "s_Dm%d" % i) for i in R2]
    t_Dm = [[T() for q in range(8)] for i in R2]
    xdt = [C.sb([128, 32, 64], BF16, "s_xdt%d" % i) for i in R2]
    t_xdt = [T() for i in R2]
    xdd = [C.sb([128, 32, 64], BF16, "s_xdd%d" % i) for i in R2]
    t_xdd = [T() for i in R2]
    Xc = C.sb([128, 16, 128], F32, "s_Xc")
    t_Xc = [T() for i in range(4)]
    CE = [C.sb([128, 32, 128], BF16, "s_CE%d" % i) for i in R2]
    t_CE = [[T() for q in range(8)] for i in R2]
    hst = C.sb([128, 2048], F32, "s_h")
    t_h = [T() for i in range(4)]
    hbf = C.sb([128, 2048], BF16, "s_hbf")
    t_hbf = [T() for i in range(4)]
    S.add("pool", lambda e: e.memset(hst[:], 0.0), [], t_h)
    S.add("pool", lambda e: e.memset(hbf[:], 0.0), [], t_hbf)
    p_sm = C.ps([128, 512], F32, "s_psm")
    t_psm = T()
    pG = [C.ps([128, 512], F32, "s_pG%d" % i) for i in R2]
    t_pG = [T(), T()]
    pY = [C.ps([128, 512], F32, "s_pY%d" % i) for i in range(4)]
    t_pY = [T() for i in range(4)]
    pSt = C.ps([128, 512], F32, "s_pSt")
    t_pSt = T()
    bTv = bT_d.rearrange("(g n) t -> n g t", n=128)
    cTv = cT_d.rearrange("(g n) t -> n g t", n=128)
    seq = list(range(NT - 1, -1, -1)) if bwd else list(range(NT))

    def stage_P(it):
        t = seq[it]
        b_ = it % 2
        rows = slice(t * 128, (t + 1) * 128)
        b3 = it % 3
        S.dma("sp", dtt[b_][:], dt_d[rows, :], writes=[t_dt[b_]])
        S.dma("sp", act_[b_][:], ac_d[rows, :], writes=[t_act[b_]])
        S.dma("sp", BT[b_][:], bTv[:, :, rows], writes=[t_BT[b_]])
        S.dma("sp", CT[b_][:], cTv[:, :, rows], writes=[t_CT[b_]])
        S.dma("sp", Rb[b_][:].rearrange("p h l -> p (h l)"), acT_d[t, :].partition_broadcast(128), writes=[t_Rb[b_]])
        S.dma("sp", xtok[b_][:].rearrange("p h d -> p (h d)"), xtok_d[rows, :], writes=[t_xtok[b_]])
        S.dma("sp", btok[b3][:], btok_d[rows, :], writes=[t_btok[b3]])
        dth = dtt[b_][:, h0:h0 + 32]
        S.add("dve", lambda e: e.tensor_tensor(out=a_f[b_][:], in0=dth, in1=aneg[:], op=ALU.mult), [t_dt[b_], t_c], [t_a[b_]])
        S.add("dve", lambda e: e.tensor_copy(out=a_hi[b_][:], in_=a_f[b_][:]), [t_a[b_]], [t_a[b_]])
        S.add("dve", lambda e: e.tensor_tensor(out=a_lo[b_][:], in0=a_f[b_][:], in1=a_hi[b_][:], op=ALU.subtract), [t_a[b_]], [t_a[b_]])
        for ci, lm in enumerate((M1, M2, ones_m[:])):
            S.add("pe", lambda e, ci=ci, lm=lm: e.matmul(p_sm[:, ci * 32:(ci + 1) * 32], lhsT=lm, rhs=a_hi[b_][:], start=True, stop=False), [t_a[b_], t_c], [t_psm])
            S.add("pe", lambda e, ci=ci, lm=lm: e.matmul(p_sm[:, ci * 32:(ci + 1) * 32], lhsT=lm, rhs=a_lo[b_][:], start=False, stop=True), [t_a[b_], t_c], [t_psm])
        S.add("act", lambda e: e.activation(out=esm[b_][:], in_=p_sm[:, 0:96], func=AF.Exp), [t_psm], [t_esm[b_]])
        for g in range(8):
            S.add("pe", lambda e, g=g: e.matmul(pG[g // 4][:, (g % 4) * 128:(g % 4 + 1) * 128], lhsT=BT[b_][:, g, :], rhs=CT[b_][:, g, :], start=True, stop=True),
                  [t_BT[b_], t_CT[b_]], [t_pG[g // 4]])
        for k in range(2):
            S.add("dve", lambda e, k=k: e.tensor_tensor(out=GTm[b_][:, k * 4:(k + 1) * 4, :], in0=pG[k][:].rearrange("p (g l) -> p g l", g=4),
                                                        in1=M1.unsqueeze(1).to_broadcast([128, 4, 128]), op=ALU.mult),
                  [t_pG[k], t_c], [t_GTm[b_]])

        def x_relu(q):
            xq = q % 4
            for hh in range(4):
                h = 4 * q + hh
                S.add("act", lambda e, h=h, hh=hh, xq=xq: e.activation(out=Xc[:, 4 * xq + hh, :], in_=Rb[b_][:, h, :], func=AF.Relu,
                                                                      bias=act_[b_][:, h:h + 1], scale=-1.0),
                      [t_Rb[b_], t_act[b_]], [t_Xc[xq]])

        def x_exp(q):
            xq = q % 4
            S.add("act", lambda e, q=q, xq=xq: e.activation(out=Dm[b_][:, 4 * q:4 * q + 4, :].rearrange("p h l -> p (h l)"),
                                                            in_=Xc[:, 4 * xq:4 * xq + 4, :].rearrange("p h l -> p (h l)"), func=AF.Exp, scale=-1.0),
                  [t_Xc[xq]], [t_Dm[b_][q]])

        def ce_op(q):
            S.add("act", lambda e, q=q: e.activation(out=CE[b_][:, 4 * q:4 * q + 4, :].rearrange("p h l -> p (h l)"),
                                                     in_=Rb[b_][:, 4 * q:4 * q + 4, :].rearrange("p h l -> p (h l)"), func=AF.Exp), [t_Rb[b_]], [t_CE[b_][q]])
            S.add("dve", lambda e, q=q: e.tensor_tensor(out=CE[b_][:, 4 * q:4 * q + 4, :], in0=CE[b_][:, 4 * q:4 * q + 4, :],
                                                        in1=CT[b_][:, q:q + 1, :].to_broadcast([128, 4, 128]), op=ALU.mult),
                  [t_CE[b_][q], t_CT[b_]], [t_CE[b_][q]])

        def m_op(q):
            S.add("dve", lambda e, q=q: e.tensor_tensor(out=Dm[b_][:, 4 * q:4 * q + 4, :], in0=Dm[b_][:, 4 * q:4 * q + 4, :],
                                                        in1=GTm[b_][:, q:q + 1, :].to_broadcast([128, 4, 128]), op=ALU.mult),
                  [t_Dm[b_][q], t_GTm[b_]], [t_Dm[b_][q]])

        S.add("dve", lambda e: e.tensor_tensor(out=xdt[b_][:], in0=xtok[b_][:], in1=dth.unsqueeze(2).to_broadcast([128, 32, 64]), op=ALU.mult),
              [t_xtok[b_], t_dt[b_]], [t_xdt[b_]])
        if not bwd:
            S.add("dve", lambda e: e.tensor_tensor(out=sk[b_][:].rearrange("p (h d) -> p h d", h=32), in0=xtok[b_][:],
                                                    in1=dsk[:].unsqueeze(2).to_broadcast([128, 32, 64]), op=ALU.mult),
                  [t_xtok[b_], t_c], [t_sk[b_]])
        x_relu(0)
        x_relu(1)
        x_exp(0)
        S.add("dve", lambda e: e.tensor_tensor(out=xdd[b_][:], in0=xdt[b_][:], in1=esm[b_][:, 32:64].unsqueeze(2).to_broadcast([128, 32, 64]), op=ALU.mult),
              [t_xdt[b_], t_esm[b_]], [t_xdd[b_]])
        for q in range(2, 8):
            x_relu(q)
            x_exp(q - 1)
            m_op(q - 2)
            ce_op(q - 2)
        x_exp(7)
        m_op(6)
        ce_op(6)
        m_op(7)
        ce_op(7)

    def stage_Q(it):
        t = seq[it]
        b_ = it % 2
        b3 = it % 3
        rows = slice(t * 128, (t + 1) * 128)
        for h in range(32):
            yo = pY[h // 8][:, (h % 8) * 64:(h % 8 + 1) * 64]
            S.add("pe", lambda e, h=h, yo=yo: e.matmul(yo, lhsT=Dm[b_][:, h, :], rhs=xdt[b_][:, h, :], start=True, stop=False),
                  [t_Dm[b_][h // 4], t_xdt[b_]], [t_pY[h // 8]])
            S.add("pe", lambda e, h=h, yo=yo: e.matmul(yo, lhsT=CE[b_][:, h, :], rhs=hbf[:, h * 64:(h + 1) * 64], start=False, stop=True),
                  [t_CE[b_][h // 4], t_hbf[h // 8]], [t_pY[h // 8]])
        for gp in range(4):
            for k in range(2):
                g = 2 * gp + k
                S.add("pe", lambda e, g=g, k=k: e.matmul(pSt[:, k * 256:(k + 1) * 256], lhsT=btok[b3][:, g * 128:(g + 1) * 128],
                                                         rhs=xdd[b_][:, 4 * g:4 * g + 4, :], start=True, stop=True),
                      [t_btok[b3], t_xdd[b_]], [t_pSt])
            S.add("dve", lambda e, gp=gp: e.tensor_tensor(out=hst[:, gp * 512:(gp + 1) * 512].rearrange("p (h d) -> p h d", h=8),
                                                           in0=hst[:, gp * 512:(gp + 1) * 512].rearrange("p (h d) -> p h d", h=8),
                                                           in1=esm[b_][:, 64 + gp * 8:64 + gp * 8 + 8].unsqueeze(2).to_broadcast([128, 8, 64]), op=ALU.mult),
                  [t_h[gp], t_esm[b_]], [t_h[gp]])
            S.add("dve", lambda e, gp=gp: e.tensor_tensor(out=hst[:, gp * 512:(gp + 1) * 512], in0=hst[:, gp * 512:(gp + 1) * 512], in1=pSt[:], op=ALU.add),
                  [t_h[gp], t_pSt], [t_h[gp]])
            S.add("act", lambda e, gp=gp: e.copy(out=hbf[:, gp * 512:(gp + 1) * 512], in_=hst[:, gp * 512:(gp + 1) * 512]), [t_h[gp]], [t_hbf[gp]])
        if not bwd:
            for k in range(4):
                S.add("dve", lambda e, k=k: e.tensor_tensor(out=sk[b_][:, k * 512:(k + 1) * 512], in0=sk[b_][:, k * 512:(k + 1) * 512], in1=pY[k][:], op=ALU.add),
                      [t_sk[b_], t_pY[k]], [t_sk[b_]])
            S.dma("pool", yf_d[rows, :], sk[b_][:], reads=[t_sk[b_]])
        else:
            S.dma("sp", yf[b_][:], yf_d[rows, :], writes=[t_yf[b_]])
            for k in range(4):
                S.add("dve", lambda e, k=k: e.tensor_tensor(out=yf[b_][:, k * 512:(k + 1) * 512], in0=yf[b_][:, k * 512:(k + 1) * 512], in1=pY[k][:], op=ALU.add),
                      [t_yf[b_], t_pY[k]], [t_yf[b_]])
            S.dma("pool", yf_d[rows, :], yf[b_][:], reads=[t_yf[b_]])

    for it in range(NT + 1):
        if it < NT:
            stage_P(it)
        if it >= 1:
            stage_Q(it - 1)


def phase_e(nc, S, C, T_, yt_d, zs_d, normw_d, swout_d, xres_d, xout_d, ln_row, ident, t_ident, eps_t, t_eps, lng_d, lnb_d, **_):
    NT = T_ // 128
    w_out = C.sb([128, 16, D], BF16, "e_wout")
    w_tmp = C.sb([128, 16, D], F32, "e_wtmp")
    nw = C.sb([128, 16], F32, "e_nw")
    t_wout = T()
    t_wtmp = T()
    S.dma("sp", w_tmp[:], swout_d.rearrange("(kt p) n -> p kt n", p=128), writes=[t_wtmp])
    S.dma("sp", nw[:], normw_d, writes=[t_wtmp])
    for kt in range(16):
        S.add("dve" if kt % 2 == 0 else "pool", lambda e, kt=kt: e.tensor_scalar(out=w_out[:, kt, :], in0=w_tmp[:, kt, :], scalar1=nw[:, kt:kt + 1], scalar2=None, op0=ALU.mult),
              [t_wtmp], [t_wout])
    g_bc = C.sb([128, D], F32, "e_g")
    b_bc = C.sb([128, D], F32, "e_b")
    t_gb = T()
    S.dma("sp", g_bc[:], lng_d[ln_row:ln_row + 1, :].partition_broadcast(128), writes=[t_gb])
    S.dma("sp", b_bc[:], lnb_d[ln_row:ln_row + 1, :].partition_broadcast(128), writes=[t_gb])
    R2 = range(2)
    yt = [C.sb([128, 2048], F32, "e_yt%d" % i) for i in R2]
    t_yt = [T() for i in R2]
    zst = [C.sb([128, 2048], BF16, "e_zs%d" % i) for i in R2]
    t_zs = [T() for i in R2]
    ss = [C.sb([128, 16], F32, "e_ss%d" % i) for i in R2]
    t_ss = [T() for i in R2]
    junk = C.sb([128, 256], F32, "e_junk")
    t_junk = T()
    ynw = [C.sb([128, 2048], BF16, "e_ynw%d" % i) for i in R2]
    t_ynw = [T() for i in R2]
    yT = [C.sb([128, 16, 128], BF16, "e_yT%d" % i) for i in R2]
    t_yT = [T() for i in R2]
    xres = [C.sb([128, D], F32, "e_xres%d" % i) for i in R2]
    t_xres = [T() for i in R2]
    lnbs = [{"h": (C.sb([128, D], F32, "e_ln_h%d" % i), T()), "st": (C.sb([128, 12], F32, "e_ln_st%d" % i), T()),
             "mv": (C.sb([128, 4], F32, "e_ln_mv%d" % i), T())} for i in R2]
    pT = [C.ps([128, 512], F32, "e_pT%d" % i) for i in range(4)]
    t_pT = [T() for i in range(4)]
    pO = [C.ps([128, 512], F32, "e_pO%d" % i) for i in range(4)]
    t_pO = [T() for i in range(4)]

    def E1(t):
        b_ = t % 2
        rows = slice(t * 128, (t + 1) * 128)
        S.dma("sp", yt[b_][:], yt_d[rows, :], writes=[t_yt[b_]])
        S.dma("sp", zst[b_][:], zs_d[rows, :], writes=[t_zs[b_]])
        S.dma("sp", xres[b_][:], xres_d[rows, :], writes=[t_xres[b_]])
        S.add("dve", lambda e: e.tensor_tensor(out=yt[b_][:], in0=yt[b_][:], in1=zst[b_][:], op=ALU.mult), [t_yt[b_], t_zs[b_]], [t_yt[b_]])
        S.add("pool", lambda e: e.memset(ss[b_][:], 0.0), [], [t_ss[b_]])
        for g in range(8):
            S.add("act", lambda e, g=g: e.activation(out=junk[:], in_=yt[b_][:, g * 256:(g + 1) * 256], func=AF.Square, accum_out=ss[b_][:, g:g + 1]),
                  [t_yt[b_], t_ss[b_]], [t_junk, t_ss[b_]])
        S.add("act", lambda e: e.activation(out=ss[b_][:, 8:16], in_=ss[b_][:, 0:8], func=AF.Ln, bias=eps_t[:], scale=1.0 / 256.0), [t_ss[b_], t_eps], [t_ss[b_]])
        S.add("act", lambda e: e.activation(out=ss[b_][:, 8:16], in_=ss[b_][:, 8:16], func=AF.Exp, scale=-0.5), [t_ss[b_]], [t_ss[b_]])

    def E2(t):
        b_ = t % 2
        S.add("dve", lambda e: e.tensor_tensor(out=ynw[b_][:].rearrange("p (g c) -> p g c", g=8), in0=yt[b_][:].rearrange("p (g c) -> p g c", g=8),
                                               in1=ss[b_][:, 8:16].unsqueeze(2).to_broadcast([128, 8, 256]), op=ALU.mult), [t_yt[b_], t_ss[b_]], [t_ynw[b_]])
        for hf in range(2):
            pp, t_pp = pT[2 * b_ + hf], t_pT[2 * b_ + hf]
            ppb = pp[:].bitcast(BF16)
            for cc in range(8):
                kt = hf * 8 + cc
                S.add("pe", lambda e, kt=kt, cc=cc, ppb=ppb: e.transpose(ppb[:, cc * 128:(cc + 1) * 128], ynw[b_][:, kt * 128:(kt + 1) * 128], ident[:]),
                      [t_ynw[b_], t_ident], [t_pp])
            S.add("act", lambda e, ppb=ppb, hf=hf: e.copy(out=yT[b_][:, hf * 8:(hf + 1) * 8, :], in_=ppb.rearrange("p (k t) -> p k t", k=8)), [t_pp], [t_yT[b_]])
        for hf in range(2):
            po, t_po = pO[2 * b_ + hf], t_pO[2 * b_ + hf]
            for kt in range(16):
                S.add("pe", lambda e, kt=kt, hf=hf, po=po: e.matmul(po[:], lhsT=yT[b_][:, kt, :], rhs=w_out[:, kt, hf * 512:(hf + 1) * 512], start=(kt == 0), stop=(kt == 15)),
                      [t_yT[b_], t_wout], [t_po])

    def E3(t):
        b_ = t % 2
        rows = slice(t * 128, (t + 1) * 128)
        lnb = lnbs[b_]
        h_, t_h_ = lnb["h"]
        for hf in range(2):
            po, t_po = pO[2 * b_ + hf], t_pO[2 * b_ + hf]
            S.add("dve", lambda e, hf=hf, po=po: e.scalar_tensor_tensor(out=h_[:, hf * 512:(hf + 1) * 512], in0=xres[b_][:, hf * 512:(hf + 1) * 512],
                                                                       scalar=ALPHA, in1=po[:], op0=ALU.mult, op1=ALU.add),
                  [t_xres[b_], t_po], [t_h_])
        ln_core(S, lnb, g_bc[:], b_bc[:], t_gb, eps_t, t_eps, h_[:], t_h_)
        S.dma("pool", xout_d[rows, :], h_[:], reads=[t_h_])

    for it in range(NT + 2):
        if it >= 2:
            E3(it - 2)
        if 1 <= it <= NT:
            E2(it - 1)
        if it < NT:
            E1(it)


def phase_mlp(nc, S, C, T_, xin_d, xout_d, w1_d, w2_d, ln_row, ident, t_ident, eps_t, t_eps, lng_d, lnb_d, **_):
    NT = T_ // 128
    NG2 = NT // 2
    w1 = C.sb([128, 8, 4096], BF16, "m_w1")
    w2 = C.sb([128, 32, D], BF16, "m_w2")
    t_w1 = [T("m_w1_%d" % i) for i in range(8)]
    t_w2 = [T("m_w2_%d" % i) for i in range(8)]
    w1v = w1_d.rearrange("(kt p) n -> p kt n", p=128)
    w2v = w2_d.rearrange("(f p) n -> p f n", p=128)
    for i in range(8):
        S.dma("pool", w1[:, :, i * 512:(i + 1) * 512], w1v[:, :, i * 512:(i + 1) * 512], writes=[t_w1[i]])
    for i in range(8):
        S.dma("pool", w2[:, i * 4:(i + 1) * 4, :], w2v[:, i * 4:(i + 1) * 4, :], writes=[t_w2[i]])
    g_bc = C.sb([128, D], F32, "m_g")
    b_bc = C.sb([128, D], F32, "m_b")
    t_gb = T("m_gb")
    S.dma("sp", g_bc[:], lng_d[ln_row:ln_row + 1, :].partition_broadcast(128), writes=[t_gb])
    S.dma("sp", b_bc[:], lnb_d[ln_row:ln_row + 1, :].partition_broadcast(128), writes=[t_gb])
    xf = [C.sb([128, D], F32, "m_xf%d" % i) for i in range(2)]
    t_xf = [T() for i in range(2)]
    xb = [C.sb([128, D], BF16, "m_xb%d" % i) for i in range(2)]
    t_xb = [T() for i in range(2)]
    xT = [C.sb([128, 8, 256], BF16, "m_xT%d" % i) for i in range(2)]
    t_xT = [[T(), T()] for i in range(2)]
    h1T = C.sb([128, 32, 256], BF16, "m_h1T")
    t_h1 = [T() for i in range(32)]
    rl = [C.sb([128, 256], F32, "m_rl%d" % i) for i in range(2)]
    t_rl = [T(), T()]
    xres = [C.sb([128, D], F32, "m_xres%d" % i) for i in range(2)]
    t_xres = [T(), T()]
    lnb = {"h": (C.sb([128, D], F32, "m_ln_h"), T()), "st": (C.sb([128, 12], F32, "m_ln_st"), T()),
           "mv": (C.sb([128, 4], F32, "m_ln_mv"), T())}
    xo = [C.sb([128, D], F32, "m_xo%d" % i) for i in range(2)]
    t_xo = [T(), T()]
    pT = [C.ps([128, 512], F32, "m_pT%d" % i) for i in range(2)]
    t_pT = [T(), T()]
    pH = [C.ps([128, 512], F32, "m_pH%d" % i) for i in range(2)]
    t_pH = [T(), T()]
    pY = [C.ps([128, 512], F32, "m_pY%d" % i) for i in range(4)]
    t_pY = [T() for i in range(4)]
    cnt = [0, 0, 0]
    for g in range(NG2):
        gb = g % 2
        for i in range(2):
            t = 2 * g + i
            s_ = t % 2
            S.dma("sp", xf[s_][:], xin_d[t * 128:(t + 1) * 128, :], writes=[t_xf[s_]])
            S.add("dve", lambda e, s_=s_: e.tensor_copy(out=xb[s_][:], in_=xf[s_][:]), [t_xf[s_]], [t_xb[s_]])
            pi = cnt[0] % 2
            cnt[0] += 1
            ppb = pT[pi][:].bitcast(BF16)
            for kt in range(8):
                S.add("pe", lambda e, s_=s_, kt=kt, ppb=ppb: e.transpose(ppb[:, kt * 128:(kt + 1) * 128], xb[s_][:, kt * 128:(kt + 1) * 128], ident[:]),
                      [t_xb[s_], t_ident], [t_pT[pi]])
            S.add("act", lambda e, i=i, gb=gb, ppb=ppb: e.copy(out=xT[gb][:, :, i * 128:(i + 1) * 128], in_=ppb.rearrange("p (k t) -> p k t", k=8)),
                  [t_pT[pi]], [t_xT[gb][i]])
        for f in range(32):
            hi = cnt[1] % 2
            cnt[1] += 1
            for kt in range(8):
                S.add("pe", lambda e, kt=kt, f=f, hi=hi, gb=gb: e.matmul(pH[hi][:, 0:256], lhsT=w1[:, kt, f * 128:(f + 1) * 128], rhs=xT[gb][:, kt, :],
                                                                        start=(kt == 0), stop=(kt == 7)),
                      [t_w1[f // 4]] + t_xT[gb], [t_pH[hi]])
            S.add("act", lambda e, hi=hi: e.activation(out=rl[hi][:], in_=pH[hi][:, 0:256], func=AF.Relu), [t_pH[hi]], [t_rl[hi]])
            S.add("dve" if f % 2 == 0 else "pool", lambda e, hi=hi, f=f: e.tensor_tensor(out=h1T[:, f, :], in0=rl[hi][:], in1=rl[hi][:], op=ALU.mult),
                  [t_rl[hi]], [t_h1[f]])
        for i in range(2):
            t = 2 * g + i
            xr = t % 2
            S.dma("sp", xres[xr][:], xin_d[t * 128:(t + 1) * 128, :], writes=[t_xres[xr]])
            yb = (cnt[2] % 2) * 2
            cnt[2] += 1
            for hf in range(2):
                for f in range(32):
                    S.add("pe", lambda e, f=f, hf=hf, i=i, yb=yb: e.matmul(pY[yb + hf][:], lhsT=h1T[:, f, i * 128:(i + 1) * 128], rhs=w2[:, f, hf * 512:(hf + 1) * 512],
                                                                          start=(f == 0), stop=(f == 31)),
                          [t_h1[f], t_w2[f // 4]], [t_pY[yb + hf]])
            h, t_h = lnb["h"]
            for hf in range(2):
                S.add("dve", lambda e, hf=hf, xr=xr, yb=yb: e.scalar_tensor_tensor(out=h[:, hf * 512:(hf + 1) * 512], in0=xres[xr][:, hf * 512:(hf + 1) * 512],
                                                                                  scalar=ALPHA, in1=pY[yb + hf][:], op0=ALU.mult, op1=ALU.add),
                      [t_xres[xr], t_pY[yb + hf]], [t_h])
            ln_core(S, lnb, g_bc[:], b_bc[:], t_gb, eps_t, t_eps, xo[xr][:], t_xo[xr])
            S.dma("pool", xout_d[t * 128:(t + 1) * 128, :], xo[xr][:], reads=[t_xo[xr]])


def ln_core(S, bufs, g_bc, b_bc, t_gb, eps_t, t_eps, out_sb, t_out):
    h, t_h = bufs["h"]
    st6, t_st = bufs["st"]
    mv, t_mv = bufs["mv"]
    S.add("dve", lambda e: e.bn_stats(out=st6[:, 0:6], in_=h[:, 0:512]), [t_h], [t_st])
    S.add("dve", lambda e: e.bn_stats(out=st6[:, 6:12], in_=h[:, 512:1024]), [t_h], [t_st])
    S.add("dve", lambda e: e.bn_aggr(out=mv[:, 0:2], in_=st6[:, 0:12]), [t_st], [t_mv])
    S.add("act", lambda e: e.activation(out=mv[:, 2:3], in_=mv[:, 1:2], func=AF.Ln, bias=eps_t[:], scale=1.0), [t_mv, t_eps], [t_mv])
    S.add("act", lambda e: e.activation(out=mv[:, 3:4], in_=mv[:, 2:3], func=AF.Exp, scale=-0.5), [t_mv], [t_mv])
    S.add("dve", lambda e: e.tensor_scalar(out=h[:], in0=h[:], scalar1=mv[:, 0:1], scalar2=mv[:, 3:4],
                                           op0=ALU.subtract, op1=ALU.mult), [t_h, t_mv], [t_h])
    S.add("dve", lambda e: e.tensor_tensor(out=h[:], in0=h[:], in1=g_bc, op=ALU.mult), [t_h, t_gb], [t_h])
    S.add("dve", lambda e: e.tensor_tensor(out=out_sb, in0=h[:], in1=b_bc, op=ALU.add), [t_h, t_gb], [t_out])


def host_inputs(T, x, attn_w_in, attn_sink, attn_rpb, attn_w_out, ln1_g, ln1_b, ln2_g, ln2_b, mlp_w1, mlp_w2,
                ssm_w_in, ssm_conv_w, ssm_conv_b, ssm_dt_bias, ssm_A_log, ssm_D, ssm_norm_w, ssm_w_out, **_):
    NT = T // 128
    cosT, sinS = _rope_tables(T)
    tabs = _na_bias_tables(np.asarray(attn_rpb[0]), NT)
    nbt = np.stack([tabs[k].reshape(128, -1) for k in ("int", "t0", "t1", "tm2", "tm1")], 0)
    kq = np.arange(128)
    maskA = np.concatenate([(kq[:, None] >= kq[None, :]), (kq[:, None] <= kq[None, :])], 1).astype(np.float32)
    common = {
        "awin": _attn_w_layout(np.asarray(attn_w_in[0])),
        "awout": np.ascontiguousarray(attn_w_out[0]),
        "sink": np.ascontiguousarray(attn_sink[0:1]),
        "nbt": np.ascontiguousarray(nbt),
        "cosT": cosT, "sinS": sinS, "maskA": np.ascontiguousarray(maskA),
        "lng": np.ascontiguousarray(np.stack([ln1_g[0], ln2_g[0], ln1_g[1], ln2_g[1]])),
        "lnb": np.ascontiguousarray(np.stack([ln1_b[0], ln2_b[0], ln1_b[1], ln2_b[1]])),
        "ident": np.eye(128, dtype=np.float32),
        "swin": np.ascontiguousarray(ssm_w_in[0]),
        "cw": np.ascontiguousarray(np.asarray(ssm_conv_w[0]).reshape(5, 32, 128).transpose(2, 0, 1).reshape(128, 160)),
        "cb": np.ascontiguousarray(np.asarray(ssm_conv_b[0]).reshape(32, 128).T),
        "dtb": np.ascontiguousarray(np.asarray(ssm_dt_bias[0]).reshape(1, 64)),
        "alog": np.ascontiguousarray(np.asarray(ssm_A_log[0]).reshape(1, 64)),
        "dsk": np.ascontiguousarray(np.asarray(ssm_D[0]).reshape(1, 32)),
        "normwT": np.ascontiguousarray(np.asarray(ssm_norm_w[0]).reshape(16, 128).T),
        "swout": np.ascontiguousarray(ssm_w_out[0]),
        "tri": np.ascontiguousarray(np.concatenate([kq[:, None] <= kq[None, :], kq[:, None] > kq[None, :],
                                                    kq[:, None] >= kq[None, :], kq[:, None] < kq[None, :]], 1).astype(np.float32)),
        "w1_0": np.ascontiguousarray(mlp_w1[0]), "w1_1": np.ascontiguousarray(mlp_w1[1]),
        "w2_0": np.ascontiguousarray(mlp_w2[0]), "w2_1": np.ascontiguousarray(mlp_w2[1]),
    }
    return common


def kernel(**inputs):
    x = np.asarray(inputs["x"])
    B, T, _ = x.shape
    common = host_inputs(T, **inputs)
    nc = build(T)
    in_maps = []
    for c in range(8):
        m = dict(common)
        m["x"] = np.ascontiguousarray(x[c % B])
        in_maps.append(m)
    res = run_bass_kernel_spmd(nc, in_maps, core_ids=list(range(8)))
    out = np.stack([res.results[b]["out"] for b in range(B)], 0)
    return out.astype(np.float32)
```

```python
import contextlib
import numpy as np
import concourse.bass as bass
import concourse.mybir as mybir
from concourse.bass_utils import run_bass_kernel_spmd

F32 = mybir.dt.float32
BF16 = mybir.dt.bfloat16
ALU = mybir.AluOpType
AF = mybir.ActivationFunctionType
AX = mybir.AxisListType

D = 1024
ALPHA = 4.0 ** 0.25
NEGB = -30000.0


class T:
    __slots__ = ("name", "w", "r")

    def __init__(self, name=""):
        self.name = name
        self.w = None
        self.r = []


class Op:
    __slots__ = ("eng", "emit", "deps", "sig", "dma", "cnt", "dsem", "dval")

    def __init__(self, eng, emit, dma):
        self.eng = eng
        self.emit = emit
        self.deps = []
        self.sig = False
        self.dma = dma
        self.cnt = 0
        self.dsem = -1
        self.dval = 0


class Sched:
    ENGS = ("pe", "act", "dve", "pool", "sp")
    ENGOBJ = {"pe": "tensor", "act": "scalar", "dve": "vector", "pool": "gpsimd", "sp": "sync"}

    def __init__(self, nc, st, n_dma_sems=14):
        self.nc = nc
        self.ops = {e: [] for e in self.ENGS}
        self.n_dma_sems = n_dma_sems
        self.csem = {e: st.enter_context(nc.semaphore("c_" + e)) for e in self.ENGS}
        self.dsems = {e: [st.enter_context(nc.semaphore("d_%s%d" % (e, i))) for i in range(n_dma_sems)]
                      for e in ("sp", "pool")}
        self.phsem = st.enter_context(nc.semaphore("phase"))
        self.ccount = {e: 0 for e in self.ENGS}
        self.dcount = {e: 0 for e in ("sp", "pool")}
        self.duses = {e: [0] * n_dma_sems for e in ("sp", "pool")}
        self.nflush = 0

    def add(self, eng, emit, reads=(), writes=(), dma=False):
        op = Op(eng, emit, dma)
        deps = op.deps
        for t in reads:
            w = t.w
            if w is not None and (w.eng != eng or w.dma or eng != "pe"):
                deps.append(w)
        for t in writes:
            w = t.w
            if w is not None and (w.eng != eng or w.dma or dma):
                deps.append(w)
            for r in t.r:
                if r.eng != eng or r.dma or dma:
                    deps.append(r)
        for t in reads:
            t.r.append(op)
        for t in writes:
            t.w = op
            t.r = []
        for d in deps:
            d.sig = True
        self.ops[eng].append(op)
        return op

    def dma(self, eng, out, in_, reads=(), writes=()):
        return self.add(eng, lambda e: e.dma_start(out=out, in_=in_), reads, writes, dma=True)

    def flush(self):
        nc = self.nc
        import os
        if os.environ.get("KDBG"):
            print("flush", self.nflush, "sbuf remaining", nc.sbuf_bytes_remaining, {e: len(v) for e, v in self.ops.items()})
        csem, dsems = self.csem, self.dsems
        last_cnt = {}
        for e in self.ENGS:
            comp = [op for op in self.ops[e] if not op.dma]
            if comp:
                comp[-1].sig = True
            for op in self.ops[e]:
                if op.dma:
                    k = self.dcount[e]
                    self.dcount[e] += 1
                    op.dsem = k % self.n_dma_sems
                    self.duses[e][op.dsem] += 1
                    op.dval = 16 * self.duses[e][op.dsem]
                elif op.sig:
                    self.ccount[e] += 1
                    op.cnt = self.ccount[e]
            last_cnt[e] = self.ccount[e]
        nfl = self.nflush
        ops_all = self.ops

        def run(ename, eobj):
            if nfl > 0:
                eobj.wait_ge(self.phsem, 5 * nfl)
            known = {}
            last_dma = {}
            for op in ops_all[ename]:
                need = {}
                for d in op.deps:
                    if d.dma:
                        key = ("d", d.eng, d.dsem)
                        val = d.dval
                    else:
                        key = ("c", d.eng)
                        val = d.cnt
                    if need.get(key, 0) < val:
                        need[key] = val
                if op.dma:
                    key = ("d", ename, op.dsem)
                    val = op.dval - 16
                    if val > 0 and need.get(key, 0) < val:
                        need[key] = val
                for key, val in need.items():
                    if val <= 0 or known.get(key, 0) >= val:
                        continue
                    known[key] = val
                    sem = csem[key[1]] if key[0] == "c" else dsems[key[1]][key[2]]
                    eobj.wait_ge(sem, val)
                ins = op.emit(eobj)
                if op.dma:
                    ins.then_inc(dsems[ename][op.dsem], 16)
                    last_dma[op.dsem] = op.dval
                elif op.sig:
                    ins.then_inc(csem[ename], 1)
            for k, v in last_dma.items():
                if known.get(("d", ename, k), 0) < v:
                    eobj.wait_ge(dsems[ename][k], v)
            if last_cnt[ename] > 0 and known.get(("c", ename), 0) < last_cnt[ename]:
                eobj.wait_ge(csem[ename], last_cnt[ename])
            eobj.sem_inc(self.phsem, 1)

        with nc.Block() as block:
            for ename in self.ENGS:
                getattr(block, self.ENGOBJ[ename])(lambda eobj, ename=ename: run(ename, eobj))
        self.ops = {e: [] for e in self.ENGS}
        self.nflush += 1


class Ctx:
    _inst = [0]

    def __init__(self, nc, st):
        self.nc = nc
        self.st = st
        self.n = 0
        Ctx._inst[0] += 1
        self.pfx = "c%d_" % Ctx._inst[0]

    def sb(self, shape, dt, name=None):
        self.n += 1
        return self.st.enter_context(self.nc.sbuf_tensor(self.pfx + "s_" + (name or "sb%d" % self.n), list(shape), dt))

    def ps(self, shape, dt, name=None):
        self.n += 1
        return self.st.enter_context(self.nc.psum_tensor(self.pfx + "p_" + (name or "ps%d" % self.n), list(shape), dt))


def _na_types(NT):
    return {"int": [-2, -1, 0, 1, 2], "t0": [0, 1, 2, 3, 3], "t1": [-1, 0, 1, 2, 2],
            "tm2": [-2, -1, 0, 1, 1], "tm1": [-3, -2, -1, 0, 0]}


def _na_type_of(t, NT):
    if t == 0:
        return "t0"
    if t == 1:
        return "t1"
    if t == NT - 2:
        return "tm2"
    if t == NT - 1:
        return "tm1"
    return "int"


def _na_bias_tables(rpb, NT):
    rows = 2 * NT
    types = _na_types(NT)
    rep_t = {"int": min(2, NT - 1), "t0": 0, "t1": 1, "tm2": NT - 2, "tm1": NT - 1}
    out = {}
    q = np.arange(128)
    k = np.arange(128)
    for name, offs in types.items():
        t = rep_t[name]
        tab = np.full((128, 8, 5, 128), NEGB, np.float32)
        r = 2 * t + q // 64
        c = q % 64
        rs = np.clip(r - 4, 0, rows - 8)
        cs = np.clip(c - 8, 0, 64 - 16)
        seen = set()
        for j, off in enumerate(offs):
            if off in seen:
                continue
            seen.add(off)
            kt = t + off
            if kt < 0 or kt >= NT:
                continue
            kr = 2 * kt + k // 64
            kc = k % 64
            valid = ((kr[:, None] >= rs[None, :]) & (kr[:, None] < rs[None, :] + 8) &
                     (kc[:, None] >= cs[None, :]) & (kc[:, None] < cs[None, :] + 16))
            dr = np.clip(kr[:, None] - r[None, :] + 7, 0, 14)
            dc = np.clip(kc[:, None] - c[None, :] + 15, 0, 30)
            g = rpb[:, dr, dc]
            g = np.where(valid[None], g, NEGB)
            tab[:, :, j, :] = g.transpose(1, 0, 2)
        out[name] = tab
    return out


def _rope_tables(T):
    half = 32
    inv = (10000.0 ** (-np.arange(half, dtype=np.float32) / half)).astype(np.float32)
    pos = np.arange(T, dtype=np.float32)
    ang = pos[None, :] * inv[:, None]
    cos = np.cos(ang).astype(np.float32)
    sin = np.sin(ang).astype(np.float32)
    cosT = np.concatenate([cos, cos, cos, cos], 0)
    sinS = np.concatenate([-sin, sin, -sin, sin], 0)
    return np.ascontiguousarray(cosT), np.ascontiguousarray(sinS)


def _attn_w_layout(w_in):
    qa = w_in[:, 0:512].reshape(1024, 8, 64)
    ka = w_in[:, 512:640].reshape(1024, 2, 64)
    va = w_in[:, 640:768]
    qb = w_in[:, 768:1280]
    kb = w_in[:, 1280:1792]
    vb = w_in[:, 1792:2304]
    order = [0, 4, 1, 5, 2, 6, 3, 7]
    qa2 = qa[:, order, :]
    sw = lambda a: np.concatenate([a[..., 32:], a[..., :32]], -1)
    cols = [qa2.reshape(1024, 512), sw(qa2).reshape(1024, 512), ka.reshape(1024, 128),
            sw(ka).reshape(1024, 128), qb, kb, va, vb]
    return np.ascontiguousarray(np.concatenate(cols, 1))


def build(Tlen, upto="all"):
    NT = Tlen // 128
    nc = bass.Bass("TRN2", target_bir_lowering=False)

    def din(name, shape):
        return nc.dram_tensor(name, list(shape), F32, kind="ExternalInput").ap()

    x_d = din("x", [Tlen, D])
    awin_d = din("awin", [D, 2944])
    awout_d = din("awout", [D, D])
    sink_d = din("sink", [1, 8])
    nbt_d = din("nbt", [5, 128, 8 * 5 * 128])
    cos_d = din("cosT", [128, Tlen])
    sin_d = din("sinS", [128, Tlen])
    ma_d = din("maskA", [128, 2 * 128])
    lng_d = din("lng", [4, D])
    lnb_d = din("lnb", [4, D])
    ident_d = din("ident", [128, 128])
    w1_d = [din("w1_%d" % i, [D, 4096]) for i in range(2)]
    w2_d = [din("w2_%d" % i, [4096, D]) for i in range(2)]
    swin_d = din("swin", [D, 6208])
    cw_d = din("cw", [128, 160])
    cb_d = din("cb", [128, 32])
    dtb_d = din("dtb", [1, 64])
    alog_d = din("alog", [1, 64])
    dsk_d = din("dsk", [1, 32])
    normw_d = din("normwT", [128, 16])
    swout_d = din("swout", [2048, D])
    tri_d = din("tri", [128, 4 * 128])
    stages = ["a", "m0", "c1", "c2", "d", "m1"]
    if upto != "all":
        stages = stages[:stages.index(upto) + 1]
    out_d = nc.dram_tensor("out", [Tlen, D], F32, kind="ExternalOutput").ap()

    def scratch(name, shape, dt=F32):
        return nc.dram_tensor(name, list(shape), dt, kind="Internal").ap()

    def act_out(stage, name):
        return out_d if stages[-1] == stage else scratch(name, [Tlen, D])

    x1_d = act_out("a", "x1")
    x2_d = act_out("m0", "x2")
    x3_d = act_out("d", "x3")
    xtok_d = scratch("xtok", [Tlen, 2048], BF16)
    btok_d = scratch("btok", [Tlen, 1024], BF16)
    bT_d = scratch("bT", [1024, Tlen], BF16)
    cT_d = scratch("cT", [1024, Tlen], BF16)
    zs_d = scratch("zs", [Tlen, 2048], BF16)
    dt_d = scratch("dtv", [Tlen, 64], F32)
    yf_d = scratch("yf", [Tlen, 2048], F32)
    acf_d = scratch("acf", [Tlen, 32], F32)
    acb_d = scratch("acb", [Tlen, 32], F32)
    acfT_d = scratch("acfT", [NT, 32 * 128], F32)
    acbT_d = scratch("acbT", [NT, 32 * 128], F32)
    SS = dict(xtok_d=xtok_d, btok_d=btok_d, bT_d=bT_d, cT_d=cT_d, zs_d=zs_d, dt_d=dt_d, yf_d=yf_d, tri_d=tri_d,
              alog_d=alog_d, dsk_d=dsk_d, acf_d=acf_d, acb_d=acb_d, acfT_d=acfT_d, acbT_d=acbT_d)

    with contextlib.ExitStack() as st0:
        S = Sched(nc, st0)
        C0 = Ctx(nc, st0)
        ident = C0.sb([128, 128], BF16, "ident")
        t_ident = T("ident")
        S.dma("pool", ident[:], ident_d, writes=[t_ident])
        eps_t = C0.sb([128, 1], F32, "eps")
        t_eps = T("eps")
        S.add("pool", lambda e: e.memset(eps_t[:], 1e-5), [], [t_eps])
        K = dict(ident=ident, t_ident=t_ident, eps_t=eps_t, t_eps=t_eps, lng_d=lng_d, lnb_d=lnb_d)

        with contextlib.ExitStack() as st:
            phase_a(nc, S, Ctx(nc, st), T_=Tlen, x_d=x_d, awin_d=awin_d, awout_d=awout_d, sink_d=sink_d, nbt_d=nbt_d,
                    cos_d=cos_d, sin_d=sin_d, ma_d=ma_d, x1_d=x1_d, **K)
            S.flush()
        if "m0" in stages:
            with contextlib.ExitStack() as st:
                phase_mlp(nc, S, Ctx(nc, st), T_=Tlen, xin_d=x1_d, xout_d=x2_d, w1_d=w1_d[0], w2_d=w2_d[0], ln_row=1, **K)
                S.flush()
        if "c1" in stages:
            with contextlib.ExitStack() as st:
                phase_c1(nc, S, Ctx(nc, st), T_=Tlen, xin_d=x2_d, swin_d=swin_d, cw_d=cw_d, cb_d=cb_d, dtb_d=dtb_d, **SS, **K)
                S.flush()
        if "c2" in stages:
            with contextlib.ExitStack() as st:
                phase_ssd(nc, S, Ctx(nc, st), T_=Tlen, bwd=False, **SS, **K)
                S.flush()
        if "d" in stages:
            with contextlib.ExitStack() as st:
                phase_ssd(nc, S, Ctx(nc, st), T_=Tlen, bwd=True, **SS, **K)
                S.flush()
            with contextlib.ExitStack() as st:
                phase_e(nc, S, Ctx(nc, st), T_=Tlen, yt_d=yf_d, zs_d=zs_d, normw_d=normw_d, swout_d=swout_d, xres_d=x2_d, xout_d=x3_d, ln_row=2, **K)
                S.flush()
        if "m1" in stages:
            with contextlib.ExitStack() as st:
                phase_mlp(nc, S, Ctx(nc, st), T_=Tlen, xin_d=x3_d, xout_d=out_d, w1_d=w1_d[1], w2_d=w2_d[1], ln_row=3, **K)
                S.flush()
    return nc


def phase_a(nc, S, C, T_, x_d, awin_d, awout_d, sink_d, nbt_d, cos_d, sin_d, ma_d, lng_d, lnb_d, ident, t_ident,
            eps_t, t_eps, x1_d):
    T_tok = T_
    NT = T_tok // 128
    NG = NT // 4
    SCALE = 0.125
    w_in = C.sb([128, 8, 2944], BF16, "a_win")
    t_win = T("a_win")
    wv = awin_d.rearrange("(kt p) n -> p kt n", p=128)
    for kt in range(8):
        S.dma("pool", w_in[:, kt, :], wv[:, kt, :], writes=[t_win])
    w_out = C.sb([128, 8, D], BF16, "a_wout")
    t_wout = T("a_wout")
    S.dma("pool", w_out[:], awout_d.rearrange("(kt p) n -> p kt n", p=128), writes=[t_wout])
    g_bc = C.sb([128, D], F32, "a_g")
    b_bc = C.sb([128, D], F32, "a_b")
    t_gb = T("a_gb")
    S.dma("sp", g_bc[:], lng_d[0:1, :].partition_broadcast(128), writes=[t_gb])
    S.dma("sp", b_bc[:], lnb_d[0:1, :].partition_broadcast(128), writes=[t_gb])
    maskA = C.sb([128, 2, 128], BF16, "maskA")
    t_maskA = T("maskA")
    S.dma("pool", maskA[:], ma_d.rearrange("p (a q) -> p a q", a=2), writes=[t_maskA])
    esink = C.sb([128, 8], F32, "esink")
    t_esink = T("esink")
    S.dma("sp", esink[:], sink_d.partition_broadcast(128), writes=[t_esink])
    S.add("act", lambda e: e.activation(out=esink[:], in_=esink[:], func=AF.Exp), [t_esink], [t_esink])

    EB = C.sb([128, 8, 5, 128], BF16, "EB")
    t_EB = T("EB")
    stage = [C.sb([128, 640], F32, "nbstage%d" % i) for i in range(2)]
    t_stage = [T("nbstage%d" % i) for i in range(2)]
    type_idx = {"int": 0, "t0": 1, "t1": 2, "tm2": 3, "tm1": 4}
    cur_type = [None]

    def load_type(name):
        if cur_type[0] == name:
            return
        cur_type[0] = name
        ti = type_idx[name]
        for hh in range(8):
            sl = hh % 2
            S.dma("sp", stage[sl][:], nbt_d[ti, :, hh * 640:(hh + 1) * 640], writes=[t_stage[sl]])
            S.add("act", lambda e, hh=hh, sl=sl: e.activation(out=EB[:, hh, :, :].rearrange("p j q -> p (j q)"), in_=stage[sl][:],
                                                                func=AF.Exp), [t_stage[sl]], [t_EB])

    xf = [C.sb([128, D], F32, "xf%d" % i) for i in range(2)]
    t_xf = [T("xf%d" % i) for i in range(2)]
    xb = [C.sb([128, D], BF16, "xb%d" % i) for i in range(2)]
    t_xb = [T("xb%d" % i) for i in range(2)]
    xT = [C.sb([128, 8, 512], BF16, "xT%d" % i) for i in range(2)]
    t_xT = [[T("xT%d_%d" % (i, j)) for j in range(4)] for i in range(2)]
    qaT = [C.sb([128, 4, 512], BF16, "qaT%d" % i) for i in range(2)]
    t_qaT = [[T("qaT%d_%d" % (i, c)) for c in range(4)] for i in range(2)]
    qbT = [C.sb([128, 4, 512], BF16, "qbT%d" % i) for i in range(2)]
    t_qbT = [[T("qbT%d_%d" % (i, c)) for c in range(4)] for i in range(2)]
    kaT = C.sb([128, 12 * 128], BF16, "kaT")
    kbT = C.sb([128, 4, 12 * 128], BF16, "kbT")
    t_kaT = [T("kaT%d" % i) for i in range(3)]
    t_kbT = [[T("kbT%d_%d" % (i, c)) for c in range(4)] for i in range(3)]
    va = C.sb([128, 12, 2, 65], BF16, "va")
    vb = C.sb([128, 12, 8, 65], BF16, "vb")
    t_v = [T("v%d" % i) for i in range(12)]
    S.add("pool", lambda e: e.memset(va[:], 1.0), [], t_v)
    S.add("pool", lambda e: e.memset(vb[:], 1.0), [], t_v)
    cs_t = C.sb([128, 2, 512], F32, "cossin")
    t_cs = T("cossin")
    r1 = C.sb([128, 512], F32, "rope1")
    r2 = C.sb([128, 512], F32, "rope2")
    t_r1, t_r2 = T("r1"), T("r2")
    esA = [C.sb([128, 3, 512], BF16, "esA%d" % i) for i in range(2)]
    t_esA = [[T("esA%d_%d" % (i, j)) for j in range(3)] for i in range(2)]
    esB = [C.sb([128, 5, 128], BF16, "esB%d" % i) for i in range(2)]
    t_esB = [T("esB%d" % i) for i in range(2)]
    o_tok = C.sb([128, D], BF16, "o_tok")
    t_otok = T("o_tok")
    den = C.sb([128, 16], F32, "den")
    t_den = T("den")
    oT = C.sb([128, 8, 128], BF16, "oT")
    t_oT = T("oT")
    xres = [C.sb([128, D], F32, "xres%d" % i) for i in range(2)]
    t_xres = [T("xres%d" % i) for i in range(2)]
    lnb = {"h": (C.sb([128, D], F32, "ln_h"), T("ln_h")), "st": (C.sb([128, 12], F32, "ln_st"), T("ln_st")),
           "mv": (C.sb([128, 4], F32, "ln_mv"), T("ln_mv"))}
    xo = [C.sb([128, D], F32, "xo%d" % i) for i in range(2)]
    t_xo = [T("xo%d" % i) for i in range(2)]

    pP = [C.ps([128, 512], F32, "pP%d" % i) for i in range(2)]
    t_pP = [T("pP%d" % i) for i in range(2)]
    pS = [C.ps([128, 512], F32, "pS%d" % i) for i in range(4)]
    t_pS = [T("pS%d" % i) for i in range(4)]
    pO = [C.ps([128, 512], F32, "pO%d" % i) for i in range(2)]
    t_pO = [T("pO%d" % i) for i in range(2)]
    pcount = [0]

    def next_pP():
        i = pcount[0] % 2
        pcount[0] += 1
        return pP[i], t_pP[i]

    COL = {"qa": 0, "qas": 512, "ka": 1024, "kas": 1152, "qb": 1280, "kb": 1792, "va": 2304, "vb": 2432}

    def project_group(g):
        gb = g % 2
        rg = g % 3
        for i in range(4):
            t = 4 * g + i
            s = t % 2
            S.dma("sp", xf[s][:], x_d[t * 128:(t + 1) * 128, :], writes=[t_xf[s]])
            S.add("dve", lambda e, s=s: e.tensor_copy(out=xb[s][:], in_=xf[s][:]), [t_xf[s]], [t_xb[s]])
            pp, t_pp = next_pP()
            ppb = pp[:].bitcast(BF16)
            for kt in range(8):
                S.add("pe", lambda e, s=s, kt=kt, ppb=ppb: e.transpose(ppb[:, kt * 128:(kt + 1) * 128], xb[s][:, kt * 128:(kt + 1) * 128], ident[:]),
                      [t_xb[s], t_ident], [t_pp])
            S.add("act", lambda e, i=i, gb=gb, ppb=ppb: e.copy(out=xT[gb][:, :, i * 128:(i + 1) * 128],
                                                               in_=ppb.rearrange("p (k t) -> p k t", k=8)),
                  [t_pp], [t_xT[gb][i]])
        S.dma("sp", cs_t[:, 0, :], cos_d[:, g * 512:(g + 1) * 512], writes=[t_cs])
        S.dma("sp", cs_t[:, 1, :], sin_d[:, g * 512:(g + 1) * 512], writes=[t_cs])

        def fm_proj(col):
            pp, t_pp = next_pP()
            for kt in range(8):
                S.add("pe", lambda e, kt=kt, col=col, pp=pp: e.matmul(pp[:], lhsT=w_in[:, kt, col:col + 128], rhs=xT[gb][:, kt, :],
                                                                     start=(kt == 0), stop=(kt == 7)),
                      [t_win] + t_xT[gb], [t_pp])
            return pp, t_pp

        def rope_half(col, which, rdst, t_rdst):
            pp, t_pp = fm_proj(col)
            S.add("dve", lambda e, pp=pp: e.tensor_tensor(out=rdst[:], in0=pp[:], in1=cs_t[:, which, :], op=ALU.mult), [t_pp, t_cs], [t_rdst])

        def rope_add(dst, t_dst):
            S.add("pool", lambda e: e.tensor_tensor(out=dst, in0=r1[:], in1=r2[:], op=ALU.add), [t_r1, t_r2], [t_dst])

        rope_half(COL["ka"], 0, r1, t_r1)
        rope_half(COL["kas"], 1, r2, t_r2)
        rope_add(kaT[:, rg * 512:(rg + 1) * 512], t_kaT[rg])
        for c in range(4):
            pp, t_pp = fm_proj(COL["kb"] + c * 128)
            S.add("act", lambda e, pp=pp, c=c: e.copy(out=kbT[:, c, rg * 512:(rg + 1) * 512], in_=pp[:]), [t_pp], [t_kbT[rg][c]])
        for i in range(4):
            t = 4 * g + i
            slot = t % 12
            pp, t_pp = next_pP()
            for kt in range(8):
                S.add("pe", lambda e, kt=kt, pp=pp, i=i: e.matmul(pp[:], lhsT=xT[gb][:, kt, i * 128:(i + 1) * 128],
                                                                 rhs=w_in[:, kt, COL["vb"]:COL["vb"] + 512],
                                                                 start=(kt == 0), stop=(kt == 7)),
                      [t_win, t_xT[gb][i]], [t_pp])
            S.add("act", lambda e, pp=pp, slot=slot: e.copy(out=vb[:, slot, :, 0:64], in_=pp[:].rearrange("p (h d) -> p h d", h=8)),
                  [t_pp], [t_v[slot]])
            pp, t_pp = next_pP()
            for kt in range(8):
                S.add("pe", lambda e, kt=kt, pp=pp, i=i: e.matmul(pp[:, 0:128], lhsT=xT[gb][:, kt, i * 128:(i + 1) * 128],
                                                                 rhs=w_in[:, kt, COL["va"]:COL["va"] + 128],
                                                                 start=(kt == 0), stop=(kt == 7)),
                      [t_win, t_xT[gb][i]], [t_pp])
            S.add("act", lambda e, pp=pp, slot=slot: e.copy(out=va[:, slot, :, 0:64], in_=pp[:, 0:128].rearrange("p (h d) -> p h d", h=2)),
                  [t_pp], [t_v[slot]])
        yield "K"
        for c in range(4):
            rope_half(COL["qa"] + c * 128, 0, r1, t_r1)
            yield None
            rope_half(COL["qas"] + c * 128, 1, r2, t_r2)
            rope_add(qaT[gb][:, c, :], t_qaT[gb][c])
            yield None
        for c in range(4):
            pp, t_pp = fm_proj(COL["qb"] + c * 128)
            S.add("act", lambda e, pp=pp, c=c: e.copy(out=qbT[gb][:, c, :], in_=pp[:]), [t_pp], [t_qbT[gb][c]])
            yield None

    es_cnt = [0, 0]
    ps_rr = [0]

    def attend_tile(t, pump):
        g = t // 4
        gb = g % 2
        i = t % 4
        qsl = slice(i * 128, (i + 1) * 128)
        js = [j for j in (-1, 0, 1) if 0 <= t + j < NT]

        ebs = []
        for gi in range(2):
            eb = es_cnt[0] % 2
            es_cnt[0] += 1
            ebs.append(eb)
            for j in js:
                tk = t + j
                rgk = (tk // 4) % 3
                kof = (tk % 12) * 128
                bi = ps_rr[0] % 4
                ps_rr[0] += 1
                ps, t_ps = pS[bi], t_pS[bi]
                S.add("pe", lambda e, ps=ps, kof=kof, gi=gi: e.matmul(ps[:].rearrange("p (c q) -> p c q", c=4),
                                                                      lhsT=kaT[gi * 64:(gi + 1) * 64, kof:kof + 128],
                                                                      rhs=qaT[gb][gi * 64:(gi + 1) * 64, :, qsl], start=True, stop=True),
                      [t_kaT[rgk]] + t_qaT[gb], [t_ps])
                S.add("act", lambda e, ps=ps, eb=eb, j=j: e.activation(out=esA[eb][:, j + 1, :], in_=ps[:], func=AF.Exp, scale=SCALE),
                      [t_ps], [t_esA[eb][j + 1]])
                if j != 0:
                    mi = 0 if j == -1 else 1
                    S.add("dve", lambda e, eb=eb, j=j, mi=mi: e.tensor_tensor(out=esA[eb][:, j + 1, :].rearrange("p (c q) -> p c q", c=4),
                                                                              in0=esA[eb][:, j + 1, :].rearrange("p (c q) -> p c q", c=4),
                                                                              in1=maskA[:, mi:mi + 1, :].to_broadcast([128, 4, 128]), op=ALU.mult),
                          [t_esA[eb][j + 1], t_maskA], [t_esA[eb][j + 1]])
            pump()
        for gi in range(2):
            eb = ebs[gi]
            for c in range(4):
                for n, j in enumerate(js):
                    slot = (t + j) % 12
                    S.add("pe", lambda e, c=c, j=j, slot=slot, gi=gi, eb=eb, n=n, nj=len(js): e.matmul(pO[gi][:, c * 65:(c + 1) * 65],
                                                                                                    lhsT=esA[eb][:, j + 1, c * 128:(c + 1) * 128],
                                                                                                    rhs=va[:, slot, gi, :], start=(n == 0), stop=(n == nj - 1)),
                          [t_esA[eb][j + 1], t_v[slot]], [t_pO[gi]])
        for gi in range(2):
            ov = pO[gi][:, 0:260].rearrange("p (c e) -> p c e", c=4)
            S.add("dve", lambda e, gi=gi, ov=ov: e.tensor_tensor(out=den[:, gi * 4:(gi + 1) * 4], in0=ov[:, :, 64], in1=esink[:, gi * 4:(gi + 1) * 4], op=ALU.add),
                  [t_pO[gi], t_esink], [t_den])
            S.add("dve", lambda e, gi=gi: e.reciprocal(out=den[:, gi * 4:(gi + 1) * 4], in_=den[:, gi * 4:(gi + 1) * 4]), [t_den], [t_den])
            S.add("dve", lambda e, gi=gi, ov=ov: e.tensor_tensor(out=o_tok[:, gi * 256:(gi + 1) * 256].rearrange("p (c d) -> p c d", c=4), in0=ov[:, :, 0:64],
                                                                 in1=den[:, gi * 4:(gi + 1) * 4].unsqueeze(2).to_broadcast([128, 4, 64]), op=ALU.mult),
                  [t_pO[gi], t_den], [t_otok])

        typ = _na_type_of(t, NT)
        load_type(typ)
        offs = _na_types(NT)[typ]

        def b_scores(hh):
            c, pb = hh // 2, (hh % 2) * 64
            sb_i = es_cnt[1] % 2
            es_cnt[1] += 1
            pa, t_pa = pS[2 * sb_i], t_pS[2 * sb_i]
            pb2, t_pb2 = pS[2 * sb_i + 1], t_pS[2 * sb_i + 1]
            for j, off in enumerate(offs):
                tk = t + off
                rgk = (tk // 4) % 3
                kof = (tk % 12) * 128
                dst, t_dst = (pa[:, j * 128:(j + 1) * 128], t_pa) if j < 4 else (pb2[:, 0:128], t_pb2)
                S.add("pe", lambda e, dst=dst, kof=kof, c=c, pb=pb: e.matmul(dst, lhsT=kbT[pb:pb + 64, c, kof:kof + 128],
                                                                             rhs=qbT[gb][pb:pb + 64, c, qsl], start=True, stop=True),
                      [t_kbT[rgk][c], t_qbT[gb][c]], [t_dst])
            S.add("act", lambda e, pa=pa, sb_i=sb_i: e.activation(out=esB[sb_i][:, 0:4, :].rearrange("p j q -> p (j q)"), in_=pa[:], func=AF.Exp, scale=SCALE),
                  [t_pa], [t_esB[sb_i]])
            S.add("act", lambda e, pb2=pb2, sb_i=sb_i: e.activation(out=esB[sb_i][:, 4, :], in_=pb2[:, 0:128], func=AF.Exp, scale=SCALE),
                  [t_pb2], [t_esB[sb_i]])
            S.add("dve", lambda e, sb_i=sb_i, hh=hh: e.tensor_tensor(out=esB[sb_i][:], in0=esB[sb_i][:], in1=EB[:, hh, :, :], op=ALU.mult),
                  [t_esB[sb_i], t_EB], [t_esB[sb_i]])
            return sb_i

        def b_pv(hh, sb_i):
            ob = hh // 4
            for j, off in enumerate(offs):
                slot = (t + off) % 12
                S.add("pe", lambda e, j=j, slot=slot, hh=hh, sb_i=sb_i, ob=ob: e.matmul(pO[ob][:, (hh % 4) * 65:(hh % 4 + 1) * 65],
                                                                                       lhsT=esB[sb_i][:, j, :], rhs=vb[:, slot, hh, :],
                                                                                       start=(j == 0), stop=(j == 4)),
                      [t_esB[sb_i], t_v[slot]], [t_pO[ob]])

        prev = None
        for hh in range(8):
            sb_i = b_scores(hh)
            pump()
            if prev is not None:
                b_pv(*prev)
            prev = (hh, sb_i)
        b_pv(*prev)
        for ob in range(2):
            ov = pO[ob][:, 0:260].rearrange("p (c e) -> p c e", c=4)
            S.add("dve", lambda e, ob=ob, ov=ov: e.reciprocal(out=den[:, 8 + ob * 4:8 + (ob + 1) * 4], in_=ov[:, :, 64]), [t_pO[ob]], [t_den])
            S.add("dve", lambda e, ob=ob, ov=ov: e.tensor_tensor(out=o_tok[:, 512 + ob * 256:512 + (ob + 1) * 256].rearrange("p (c d) -> p c d", c=4), in0=ov[:, :, 0:64],
                                                                 in1=den[:, 8 + ob * 4:8 + (ob + 1) * 4].unsqueeze(2).to_broadcast([128, 4, 64]), op=ALU.mult),
                  [t_pO[ob], t_den], [t_otok])
        pp, t_pp = next_pP()
        ppb = pp[:].bitcast(BF16)
        for kt in range(8):
            S.add("pe", lambda e, kt=kt, ppb=ppb: e.transpose(ppb[:, kt * 128:(kt + 1) * 128], o_tok[:, kt * 128:(kt + 1) * 128], ident[:]),
                  [t_otok, t_ident], [t_pp])
        S.add("act", lambda e, ppb=ppb: e.copy(out=oT[:], in_=ppb.rearrange("p (k t) -> p k t", k=8)), [t_pp], [t_oT])
        xr = t % 2
        S.dma("sp", xres[xr][:], x_d[t * 128:(t + 1) * 128, :], writes=[t_xres[xr]])
        pump()
        halves = []
        for hf in range(2):
            pp, t_pp = next_pP()
            for kt in range(8):
                S.add("pe", lambda e, kt=kt, pp=pp, hf=hf: e.matmul(pp[:], lhsT=oT[:, kt, :], rhs=w_out[:, kt, hf * 512:(hf + 1) * 512],
                                                                   start=(kt == 0), stop=(kt == 7)),
                      [t_oT, t_wout], [t_pp])
            halves.append((pp, t_pp))
        h, t_h = lnb["h"]
        for hf, (pp, t_pp) in enumerate(halves):
            S.add("dve", lambda e, pp=pp, hf=hf, xr=xr: e.scalar_tensor_tensor(out=h[:, hf * 512:(hf + 1) * 512], in0=xres[xr][:, hf * 512:(hf + 1) * 512],
                                                                              scalar=ALPHA, in1=pp[:], op0=ALU.mult, op1=ALU.add),
                  [t_xres[xr], t_pp], [t_h])
        ln_core(S, lnb, g_bc[:], b_bc[:], t_gb, eps_t, t_eps, xo[xr][:], t_xo[xr])
        S.dma("pool", x1_d[t * 128:(t + 1) * 128, :], xo[xr][:], reads=[t_xo[xr]])

    for g in range(NG + 1):
        gen = project_group(g) if g < NG else iter(())
        for u in gen:
            if u == "K":
                break

        def pump(gen=gen):
            next(gen, None)

        if g >= 1:
            for t in range(4 * (g - 1), 4 * g):
                attend_tile(t, pump)
        for u in gen:
            pass


def phase_c1(nc, S, C, T_, xin_d, swin_d, cw_d, cb_d, dtb_d, xtok_d, btok_d, bT_d, cT_d, zs_d, dt_d,
             ident, t_ident, tri_d=None, alog_d=None, acf_d=None, acb_d=None, acfT_d=None, acbT_d=None, **_):
    NT = T_ // 128
    NG2 = NT // 2
    w_in = C.sb([128, 8, 6208], BF16, "c_win")
    NBLK = 13
    t_w = [T() for i in range(NBLK)]
    wv = swin_d.rearrange("(kt p) n -> p kt n", p=128)
    order = [4, 5, 6, 7, 8, 9, 10, 11, 12, 0, 1, 2, 3]
    for bi in order:
        c0, c1 = bi * 512, min(6208, (bi + 1) * 512)
        S.dma("pool", w_in[:, :, c0:c1], wv[:, :, c0:c1], writes=[t_w[bi]])

    def wdeps(c0, c1):
        return [t_w[b] for b in range(c0 // 512, (c1 - 1) // 512 + 1)]

    cw = C.sb([128, 160], F32, "c_cw")
    cb = C.sb([128, 32], F32, "c_cb")
    t_cw = T()
    S.dma("sp", cw[:], cw_d, writes=[t_cw])
    S.dma("sp", cb[:], cb_d, writes=[t_cw])
    dtb = C.sb([128, 64], F32, "c_dtb")
    S.dma("sp", dtb[:], dtb_d.partition_broadcast(128), writes=[t_cw])
    one_t = C.sb([128, 1], F32, "c_one")
    S.add("pool", lambda e: e.memset(one_t[:], 1.0), [], [t_cw])
    cmask = C.sb([128, 4, 128], BF16, "c_masks")
    S.dma("pool", cmask[:], tri_d.rearrange("p (a q) -> p a q", a=4), writes=[t_cw])
    aneg = C.sb([128, 64], F32, "c_aneg")
    t_an = T()
    S.dma("sp", aneg[:], alog_d.partition_broadcast(128), writes=[t_an])
    S.add("act", lambda e: e.activation(out=aneg[:], in_=aneg[:], func=AF.Exp), [t_an], [t_an])
    S.add("dve", lambda e: e.tensor_scalar(out=aneg[:], in0=aneg[:], scalar1=-1.0, scalar2=None, op0=ALU.mult), [t_an], [t_an])
    av = C.sb([128, 64], F32, "c_av")
    av_hi = C.sb([128, 64], BF16, "c_avhi")
    av_lo = C.sb([128, 64], BF16, "c_avlo")
    acs = C.sb([128, 64], F32, "c_acs")
    t_av = T()
    t_acs = T()
    acsT = C.sb([32, 256], F32, "c_acsT")
    t_acsT = T()
    diag = C.sb([128, 32, 5, 128], BF16, "c_diag")
    t_diag = [T() for i in range(32)]
    for c in range(32):
        for k in range(5):
            S.add("dve" if (c + k) % 2 == 0 else "pool",
                  lambda e, c=c, k=k: e.tensor_scalar(out=diag[:, c, k, :], in0=ident[:], scalar1=cw[:, k * 32 + c:k * 32 + c + 1], scalar2=None, op0=ALU.mult),
                  [t_ident, t_cw], [t_diag[c]])
    xb = [C.sb([128, D], BF16, "c_xb%d" % i) for i in range(2)]
    t_xb = [T(), T()]
    xT = [C.sb([128, 8, 260], BF16, "c_xT%d" % i) for i in range(2)]
    t_xT = [[T(), T(), T()] for i in range(2)]
    u = [C.sb([128, 260], BF16, "c_u%d" % i) for i in range(2)]
    t_u = [T(), T()]
    xsT = C.sb([128, 16, 256], BF16, "c_xsT")
    t_xsT = [T() for i in range(16)]
    BT = C.sb([128, 8, 256], BF16, "c_BT")
    t_BT = [T() for i in range(8)]
    CT = C.sb([128, 8, 256], BF16, "c_CT")
    t_CT = [T() for i in range(8)]
    xtok = C.sb([128, 2048], BF16, "c_xtok")
    t_xtok = T()
    btok = C.sb([128, 1024], BF16, "c_btok")
    t_btok = T()
    zs = C.sb([128, 2048], BF16, "c_zs")
    t_zs = T()
    dv = C.sb([128, 4, 64], F32, "c_dv")
    t_dv = T()
    pT = [C.ps([128, 512], F32, "c_pT%d" % i) for i in range(2)]
    t_pT = [T(), T()]
    pU = [C.ps([128, 512], F32, "c_pU%d" % i) for i in range(2)]
    t_pU = [T(), T()]
    pC = [C.ps([128, 512], F32, "c_pC%d" % i) for i in range(2)]
    t_pC = [T(), T()]
    pZ = [C.ps([128, 512], F32, "c_pZ%d" % i) for i in range(2)]
    t_pZ = [T(), T()]
    cnt = {"T": 0, "U": 0, "C": 0, "Z": 0, "x": 0}

    def nxt(key, arr, tarr):
        i = cnt[key] % 2
        cnt[key] += 1
        return arr[i], tarr[i]

    bTv = bT_d.rearrange("(g n) t -> n g t", n=128)
    cTv = cT_d.rearrange("(g n) t -> n g t", n=128)

    for g2 in range(NG2):
        gb = g2 % 2
        t0 = 256 * g2
        pieces = [(t0 - 2, 128, 0), (t0 + 126, 128, 128), (t0 + 254, 4, 256)]
        for pi_, (r0, nr, c0) in enumerate(pieces):
            s_ = cnt["x"] % 2
            cnt["x"] += 1
            lo, hi = max(r0, 0), min(r0 + nr, T_)
            if lo > r0 or hi < r0 + nr:
                S.add("pool", lambda e, s_=s_: e.memset(xb[s_][:], 0.0), [], [t_xb[s_]])
            S.dma("pool", xb[s_][lo - r0:hi - r0, :], xin_d[lo:hi, :], writes=[t_xb[s_]])
            pp, t_pp = nxt("T", pT, t_pT)
            ppb = pp[:].bitcast(BF16)
            for kt in range(8):
                S.add("pe", lambda e, s_=s_, kt=kt, ppb=ppb, nr=nr: e.transpose(ppb[:, kt * 128:kt * 128 + nr], xb[s_][0:nr, kt * 128:(kt + 1) * 128], ident[0:nr, 0:nr]),
                      [t_xb[s_], t_ident], [t_pp])
            S.add("dve", lambda e, gb=gb, ppb=ppb, nr=nr, c0=c0: e.tensor_copy(out=xT[gb][:, :, c0:c0 + nr], in_=ppb.rearrange("p (k t) -> p k t", k=8)[:, :, 0:nr]),
                  [t_pp], [t_xT[gb][pi_]])
        for c in range(32):
            col = 2048 + c * 128
            pu, t_pu = nxt("U", pU, t_pU)
            for kt in range(8):
                S.add("pe", lambda e, kt=kt, col=col, pu=pu, gb=gb: e.matmul(pu[:, 0:260], lhsT=w_in[:, kt, col:col + 128], rhs=xT[gb][:, kt, :],
                                                                            start=(kt == 0), stop=(kt == 7)),
                      wdeps(col, col + 128) + t_xT[gb], [t_pu])
            ui = c % 2
            S.add("dve", lambda e, ui=ui, pu=pu: e.tensor_copy(out=u[ui][:], in_=pu[:, 0:260]), [t_pu], [t_u[ui]])
            pc, t_pc = nxt("C", pC, t_pC)
            for k in range(5):
                S.add("pe", lambda e, k=k, c=c, ui=ui, pc=pc: e.matmul(pc[:, 0:256], lhsT=diag[:, c, k, :], rhs=u[ui][:, k:k + 256],
                                                                      start=(k == 0), stop=(k == 4)),
                      [t_diag[c], t_u[ui]], [t_pc])
            if c < 16:
                dst, t_dst = xsT[:, c, :], t_xsT[c]
            elif c < 24:
                dst, t_dst = BT[:, c - 16, :], t_BT[c - 16]
            else:
                dst, t_dst = CT[:, c - 24, :], t_CT[c - 24]
            S.add("act", lambda e, pc=pc, dst=dst, c=c: e.activation(out=dst, in_=pc[:, 0:256], func=AF.Silu, bias=cb[:, c:c + 1], scale=1.0),
                  [t_pc, t_cw], [t_dst])
        S.dma("sp", bTv[:, :, t0:t0 + 256], BT[:], reads=t_BT)
        S.dma("sp", cTv[:, :, t0:t0 + 256], CT[:], reads=t_CT)
        for i in range(2):
            t = 2 * g2 + i
            for qz in range(4):
                pz, t_pz = nxt("Z", pZ, t_pZ)
                for kt in range(8):
                    S.add("pe", lambda e, kt=kt, qz=qz, pz=pz, gb=gb, i=i: e.matmul(pz[:], lhsT=xT[gb][:, kt, 2 + i * 128:2 + (i + 1) * 128],
                                                                                   rhs=w_in[:, kt, qz * 512:(qz + 1) * 512], start=(kt == 0), stop=(kt == 7)),
                          wdeps(qz * 512, (qz + 1) * 512) + t_xT[gb], [t_pz])
                S.add("act", lambda e, pz=pz, qz=qz: e.activation(out=zs[:, qz * 512:(qz + 1) * 512], in_=pz[:], func=AF.Silu), [t_pz], [t_zs])
            S.dma("sp", zs_d[t * 128:(t + 1) * 128, :], zs[:], reads=[t_zs])
            for hf in range(2):
                pp, t_pp = nxt("T", pT, t_pT)
                ppb = pp[:].bitcast(BF16)
                for cc in range(8):
                    c = hf * 8 + cc
                    S.add("pe", lambda e, c=c, cc=cc, ppb=ppb, i=i: e.transpose(ppb[:, cc * 128:(cc + 1) * 128], xsT[:, c, i * 128:(i + 1) * 128], ident[:]),
                          [t_xsT[c], t_ident], [t_pp])
                S.add("dve", lambda e, ppb=ppb, hf=hf: e.tensor_copy(out=xtok[:, hf * 1024:(hf + 1) * 1024], in_=ppb), [t_pp], [t_xtok])
            S.dma("sp", xtok_d[t * 128:(t + 1) * 128, :], xtok[:], reads=[t_xtok])
            pp, t_pp = nxt("T", pT, t_pT)
            ppb = pp[:].bitcast(BF16)
            for gg in range(8):
                S.add("pe", lambda e, gg=gg, ppb=ppb, i=i: e.transpose(ppb[:, gg * 128:(gg + 1) * 128], BT[:, gg, i * 128:(i + 1) * 128], ident[:]),
                      [t_BT[gg], t_ident], [t_pp])
            S.add("dve", lambda e, ppb=ppb: e.tensor_copy(out=btok[:], in_=ppb), [t_pp], [t_btok])
            S.dma("sp", btok_d[t * 128:(t + 1) * 128, :], btok[:], reads=[t_btok])
        for i in range(2):
            t = 2 * g2 + i
            pz, t_pz = nxt("Z", pZ, t_pZ)
            for kt in range(8):
                S.add("pe", lambda e, kt=kt, pz=pz, gb=gb, i=i: e.matmul(pz[:, 0:64], lhsT=xT[gb][:, kt, 2 + i * 128:2 + (i + 1) * 128],
                                                                        rhs=w_in[:, kt, 6144:6208], start=(kt == 0), stop=(kt == 7)),
                      wdeps(6144, 6208) + t_xT[gb], [t_pz])
            S.add("dve", lambda e, pz=pz: e.tensor_tensor(out=dv[:, 0, :], in0=pz[:, 0:64], in1=dtb[:], op=ALU.add), [t_pz, t_cw], [t_dv])
            S.add("act", lambda e: e.activation(out=dv[:, 1, :], in_=dv[:, 0, :], func=AF.Abs), [t_dv], [t_dv])
            S.add("act", lambda e: e.activation(out=dv[:, 1, :], in_=dv[:, 1, :], func=AF.Exp, scale=-1.0), [t_dv], [t_dv])
            S.add("act", lambda e: e.activation(out=dv[:, 1, :], in_=dv[:, 1, :], func=AF.Ln, bias=one_t[:], scale=1.0), [t_dv, t_cw], [t_dv])
            S.add("dve", lambda e: e.tensor_scalar(out=dv[:, 2, :], in0=dv[:, 0, :], scalar1=0.0, scalar2=None, op0=ALU.max), [t_dv], [t_dv])
            S.add("dve", lambda e: e.tensor_tensor(out=dv[:, 3, :], in0=dv[:, 2, :], in1=dv[:, 1, :], op=ALU.add), [t_dv], [t_dv])
            S.dma("sp", dt_d[t * 128:(t + 1) * 128, :], dv[:, 3, :], reads=[t_dv])
            S.add("dve", lambda e: e.tensor_tensor(out=av[:], in0=dv[:, 3, :], in1=aneg[:], op=ALU.mult), [t_dv, t_an], [t_av])
            S.add("dve", lambda e: e.tensor_copy(out=av_hi[:], in_=av[:]), [t_av], [t_av])
            S.add("dve", lambda e: e.tensor_tensor(out=av_lo[:], in0=av[:], in1=av_hi[:], op=ALU.subtract), [t_av], [t_av])
            pz, t_pz = nxt("Z", pZ, t_pZ)
            for ci, mi in ((0, 0), (1, 2)):
                S.add("pe", lambda e, ci=ci, mi=mi, pz=pz: e.matmul(pz[:, ci * 32:(ci + 1) * 32], lhsT=cmask[:, mi, :], rhs=av_hi[:, ci * 32:(ci + 1) * 32], start=True, stop=False),
                      [t_av, t_cw], [t_pz])
                S.add("pe", lambda e, ci=ci, mi=mi, pz=pz: e.matmul(pz[:, ci * 32:(ci + 1) * 32], lhsT=cmask[:, mi, :], rhs=av_lo[:, ci * 32:(ci + 1) * 32], start=False, stop=True),
                      [t_av, t_cw], [t_pz])
            S.add("dve", lambda e, pz=pz: e.tensor_copy(out=acs[:], in_=pz[:, 0:64]), [t_pz], [t_acs])
            S.dma("sp", acf_d[t * 128:(t + 1) * 128, :], acs[:, 0:32], reads=[t_acs])
            S.dma("sp", acb_d[t * 128:(t + 1) * 128, :], acs[:, 32:64], reads=[t_acs])
            pz, t_pz = nxt("Z", pZ, t_pZ)
            for ci, mi in ((0, 0), (1, 2)):
                S.add("pe", lambda e, ci=ci, mi=mi, pz=pz: e.matmul(pz[0:32, ci * 128:(ci + 1) * 128], lhsT=av_hi[:, ci * 32:(ci + 1) * 32], rhs=cmask[:, mi, :], start=True, stop=False),
                      [t_av, t_cw], [t_pz])
                S.add("pe", lambda e, ci=ci, mi=mi, pz=pz: e.matmul(pz[0:32, ci * 128:(ci + 1) * 128], lhsT=av_lo[:, ci * 32:(ci + 1) * 32], rhs=cmask[:, mi, :], start=False, stop=True),
                      [t_av, t_cw], [t_pz])
            S.add("dve", lambda e, pz=pz: e.tensor_copy(out=acsT[:], in_=pz[0:32, 0:256]), [t_pz], [t_acsT])
            S.dma("sp", acfT_d[t, :].rearrange("(h l) -> h l", h=32), acsT[:, 0:128], reads=[t_acsT])
            S.dma("sp", acbT_d[t, :].rearrange("(h l) -> h l", h=32), acsT[:, 128:256], reads=[t_acsT])


def phase_ssd(nc, S, C, T_, bwd, xtok_d, btok_d, bT_d, cT_d, dt_d, yf_d, tri_d, alog_d, dsk_d, acf_d, acb_d,
              acfT_d=None, acbT_d=None,
              zs_d=None, normw_d=None, swout_d=None, xres_d=None, xout_d=None, ln_row=2,
              ident=None, t_ident=None, eps_t=None, t_eps=None, lng_d=None, lnb_d=None, **_):
    NT = T_ // 128
    masks = C.sb([128, 4, 128], BF16, "s_masks")
    t_c = T()
    S.dma("pool", masks[:], tri_d.rearrange("p (a q) -> p a q", a=4), writes=[t_c])
    M1 = masks[:, 2, :] if bwd else masks[:, 0, :]
    M2 = masks[:, 3, :] if bwd else masks[:, 1, :]
    ones_m = C.sb([128, 128], BF16, "s_ones")
    S.add("pool", lambda e: e.memset(ones_m[:], 1.0), [], [t_c])
    zero_t = C.sb([128, 1], F32, "s_zero")
    S.add("pool", lambda e: e.memset(zero_t[:], 0.0), [], [t_c])
    aneg = C.sb([128, 32], F32, "s_aneg")
    h0 = 32 if bwd else 0
    S.dma("sp", aneg[:], alog_d[:, h0:h0 + 32].partition_broadcast(128), writes=[t_c])
    S.add("act", lambda e: e.activation(out=aneg[:], in_=aneg[:], func=AF.Exp), [t_c], [t_c])
    S.add("dve", lambda e: e.tensor_scalar(out=aneg[:], in0=aneg[:], scalar1=-1.0, scalar2=None, op0=ALU.mult), [t_c], [t_c])
    if not bwd:
        dsk = C.sb([128, 32], F32, "s_dsk")
        S.dma("sp", dsk[:], dsk_d.partition_broadcast(128), writes=[t_c])
        sk = [C.sb([128, 2048], F32, "s_sk%d" % i) for i in range(2)]
        t_sk = [T(), T()]
    else:
        yf = [C.sb([128, 2048], F32, "s_yf%d" % i) for i in range(2)]
        t_yf = [T(), T()]
    R2 = range(2)
    xtok = [C.sb([128, 32, 64], BF16, "s_xtok%d" % i) for i in R2]
    t_xtok = [T() for i in R2]
    btok = [C.sb([128, 1024], BF16, "s_btok%d" % i) for i in range(3)]
    t_btok = [T() for i in range(3)]
    BT = [C.sb([128, 8, 128], BF16, "s_BT%d" % i) for i in R2]
    t_BT = [T() for i in R2]
    CT = [C.sb([128, 8, 128], BF16, "s_CT%d" % i) for i in R2]
    t_CT = [T() for i in R2]
    dtt = [C.sb([128, 64], F32, "s_dt%d" % i) for i in R2]
    t_dt = [T() for i in R2]
    acT_d = acbT_d if bwd else acfT_d
    ac_d = acb_d if bwd else acf_d
    Rb = [C.sb([128, 32, 128], F32, "s_Rb%d" % i) for i in R2]
    t_Rb = [T() for i in R2]
    act_ = [C.sb([128, 32], F32, "s_act%d" % i) for i in R2]
    t_act = [T() for i in R2]
    a_f = [C.sb([128, 32], F32, "s_a%d" % i) for i in R2]
    a_hi = [C.sb([128, 32], BF16, "s_ahi%d" % i) for i in R2]
    a_lo = [C.sb([128, 32], BF16, "s_alo%d" % i) for i in R2]
    t_a = [T() for i in R2]
    esm = [C.sb([128, 96], F32, "s_esm%d" % i) for i in R2]
    t_esm = [T() for i in R2]
    GTm = [C.sb([128, 8, 128], BF16, "s_GTm%d" % i) for i in R2]
    t_GTm = [T() for i in R2]
    Dm = [C.sb([128, 32, 128], BF16, "s_Dm%d" % i) for i in R2]
    t_Dm = [[T() for q in range(8)] for i in R2]
    xdt = [C.sb([128, 32, 64], BF16, "s_xdt%d" % i) for i in R2]
    t_xdt = [T() for i in R2]
    xdd = [C.sb([128, 32, 64], BF16, "s_xdd%d" % i) for i in R2]
    t_xdd = [T() for i in R2]
    Xc = C.sb([128, 16, 128], F32, "s_Xc")
    t_Xc = [T() for i in range(4)]
    CE = [C.sb([128, 32, 128], BF16, "s_CE%d" % i) for i in R2]
    t_CE = [[T() for q in range(8)] for i in R2]
    hst = C.sb([128, 2048], F32, "s_h")
    t_h = [T() for i in range(4)]
    hbf = C.sb([128, 2048], BF16, "s_hbf")
    t_hbf = [T() for i in range(4)]
    S.add("pool", lambda e: e.memset(hst[:], 0.0), [], t_h)
    S.add("pool", lambda e: e.memset(hbf[:], 0.0), [], t_hbf)
    p_sm = C.ps([128, 512], F32, "s_psm")
    t_psm = T()
    pG = [C.ps([128, 512], F32, "s_pG%d" % i) for i in R2]
    t_pG = [T(), T()]
    pY = [C.ps([128, 512], F32, "s_pY%d" % i) for i in range(4)]
    t_pY = [T() for i in range(4)]
    pSt = C.ps([128, 512], F32, "s_pSt")
    t_pSt = T()
    bTv = bT_d.rearrange("(g n) t -> n g t", n=128)
    cTv = cT_d.rearrange("(g n) t -> n g t", n=128)
    seq = list(range(NT - 1, -1, -1)) if bwd else list(range(NT))

    def stage_P(it):
        t = seq[it]
        b_ = it % 2
        rows = slice(t * 128, (t + 1) * 128)
        b3 = it % 3
        S.dma("sp", dtt[b_][:], dt_d[rows, :], writes=[t_dt[b_]])
        S.dma("sp", act_[b_][:], ac_d[rows, :], writes=[t_act[b_]])
        S.dma("sp", BT[b_][:], bTv[:, :, rows], writes=[t_BT[b_]])
        S.dma("sp", CT[b_][:], cTv[:, :, rows], writes=[t_CT[b_]])
        S.dma("sp", Rb[b_][:].rearrange("p h l -> p (h l)"), acT_d[t, :].partition_broadcast(128), writes=[t_Rb[b_]])
        S.dma("sp", xtok[b_][:].rearrange("p h d -> p (h d)"), xtok_d[rows, :], writes=[t_xtok[b_]])
        S.dma("sp", btok[b3][:], btok_d[rows, :], writes=[t_btok[b3]])
        dth = dtt[b_][:, h0:h0 + 32]
        S.add("dve", lambda e: e.tensor_tensor(out=a_f[b_][:], in0=dth, in1=aneg[:], op=ALU.mult), [t_dt[b_], t_c], [t_a[b_]])
        S.add("dve", lambda e: e.tensor_copy(out=a_hi[b_][:], in_=a_f[b_][:]), [t_a[b_]], [t_a[b_]])
        S.add("dve", lambda e: e.tensor_tensor(out=a_lo[b_][:], in0=a_f[b_][:], in1=a_hi[b_][:], op=ALU.subtract), [t_a[b_]], [t_a[b_]])
        for ci, lm in enumerate((M1, M2, ones_m[:])):
            S.add("pe", lambda e, ci=ci, lm=lm: e.matmul(p_sm[:, ci * 32:(ci + 1) * 32], lhsT=lm, rhs=a_hi[b_][:], start=True, stop=False), [t_a[b_], t_c], [t_psm])
            S.add("pe", lambda e, ci=ci, lm=lm: e.matmul(p_sm[:, ci * 32:(ci + 1) * 32], lhsT=lm, rhs=a_lo[b_][:], start=False, stop=True), [t_a[b_], t_c], [t_psm])
        S.add("act", lambda e: e.activation(out=esm[b_][:], in_=p_sm[:, 0:96], func=AF.Exp), [t_psm], [t_esm[b_]])
        for g in range(8):
            S.add("pe", lambda e, g=g: e.matmul(pG[g // 4][:, (g % 4) * 128:(g % 4 + 1) * 128], lhsT=BT[b_][:, g, :], rhs=CT[b_][:, g, :], start=True, stop=True),
                  [t_BT[b_], t_CT[b_]], [t_pG[g // 4]])
        for k in range(2):
            S.add("dve", lambda e, k=k: e.tensor_tensor(out=GTm[b_][:, k * 4:(k + 1) * 4, :], in0=pG[k][:].rearrange("p (g l) -> p g l", g=4),
                                                        in1=M1.unsqueeze(1).to_broadcast([128, 4, 128]), op=ALU.mult),
                  [t_pG[k], t_c], [t_GTm[b_]])

        def x_relu(q):
            xq = q % 4
            for hh in range(4):
                h = 4 * q + hh
                S.add("act", lambda e, h=h, hh=hh, xq=xq: e.activation(out=Xc[:, 4 * xq + hh, :], in_=Rb[b_][:, h, :], func=AF.Relu,
                                                                      bias=act_[b_][:, h:h + 1], scale=-1.0),
                      [t_Rb[b_], t_act[b_]], [t_Xc[xq]])

        def x_exp(q):
            xq = q % 4
            S.add("act", lambda e, q=q, xq=xq: e.activation(out=Dm[b_][:, 4 * q:4 * q + 4, :].rearrange("p h l -> p (h l)"),
                                                            in_=Xc[:, 4 * xq:4 * xq + 4, :].rearrange("p h l -> p (h l)"), func=AF.Exp, scale=-1.0),
                  [t_Xc[xq]], [t_Dm[b_][q]])

        def ce_op(q):
            S.add("act", lambda e, q=q: e.activation(out=CE[b_][:, 4 * q:4 * q + 4, :].rearrange("p h l -> p (h l)"),
                                                     in_=Rb[b_][:, 4 * q:4 * q + 4, :].rearrange("p h l -> p (h l)"), func=AF.Exp), [t_Rb[b_]], [t_CE[b_][q]])
            S.add("dve", lambda e, q=q: e.tensor_tensor(out=CE[b_][:, 4 * q:4 * q + 4, :], in0=CE[b_][:, 4 * q:4 * q + 4, :],
                                                        in1=CT[b_][:, q:q + 1, :].to_broadcast([128, 4, 128]), op=ALU.mult),
                  [t_CE[b_][q], t_CT[b_]], [t_CE[b_][q]])

        def m_op(q):
            S.add("dve", lambda e, q=q: e.tensor_tensor(out=Dm[b_][:, 4 * q:4 * q + 4, :], in0=Dm[b_][:, 4 * q:4 * q + 4, :],
                                                        in1=GTm[b_][:, q:q + 1, :].to_broadcast([128, 4, 128]), op=ALU.mult),
                  [t_Dm[b_][q], t_GTm[b_]], [t_Dm[b_][q]])

        S.add("dve", lambda e: e.tensor_tensor(out=xdt[b_][:], in0=xtok[b_][:], in1=dth.unsqueeze(2).to_broadcast([128, 32, 64]), op=ALU.mult),
              [t_xtok[b_], t_dt[b_]], [t_xdt[b_]])
        if not bwd:
            S.add("dve", lambda e: e.tensor_tensor(out=sk[b_][:].rearrange("p (h d) -> p h d", h=32), in0=xtok[b_][:],
                                                    in1=dsk[:].unsqueeze(2).to_broadcast([128, 32, 64]), op=ALU.mult),
                  [t_xtok[b_], t_c], [t_sk[b_]])
        x_relu(0)
        x_relu(1)
        x_exp(0)
        S.add("dve", lambda e: e.tensor_tensor(out=xdd[b_][:], in0=xdt[b_][:], in1=esm[b_][:, 32:64].unsqueeze(2).to_broadcast([128, 32, 64]), op=ALU.mult),
              [t_xdt[b_], t_esm[b_]], [t_xdd[b_]])
        for q in range(2, 8):
            x_relu(q)
            x_exp(q - 1)
            m_op(q - 2)
            ce_op(q - 2)
        x_exp(7)
        m_op(6)
        ce_op(6)
        m_op(7)
        ce_op(7)

    def stage_Q(it):
        t = seq[it]
        b_ = it % 2
        b3 = it % 3
        rows = slice(t * 128, (t + 1) * 128)
        for h in range(32):
            yo = pY[h // 8][:, (h % 8) * 64:(h % 8 + 1) * 64]
            S.add("pe", lambda e, h=h, yo=yo: e.matmul(yo, lhsT=Dm[b_][:, h, :], rhs=xdt[b_][:, h, :], start=True, stop=False),
                  [t_Dm[b_][h // 4], t_xdt[b_]], [t_pY[h // 8]])
            S.add("pe", lambda e, h=h, yo=yo: e.matmul(yo, lhsT=CE[b_][:, h, :], rhs=hbf[:, h * 64:(h + 1) * 64], start=False, stop=True),
                  [t_CE[b_][h // 4], t_hbf[h // 8]], [t_pY[h // 8]])
        for gp in range(4):
            for k in range(2):
                g = 2 * gp + k
                S.add("pe", lambda e, g=g, k=k: e.matmul(pSt[:, k * 256:(k + 1) * 256], lhsT=btok[b3][:, g * 128:(g + 1) * 128],
                                                         rhs=xdd[b_][:, 4 * g:4 * g + 4, :], start=True, stop=True),
                      [t_btok[b3], t_xdd[b_]], [t_pSt])
            S.add("dve", lambda e, gp=gp: e.tensor_tensor(out=hst[:, gp * 512:(gp + 1) * 512].rearrange("p (h d) -> p h d", h=8),
                                                           in0=hst[:, gp * 512:(gp + 1) * 512].rearrange("p (h d) -> p h d", h=8),
                                                           in1=esm[b_][:, 64 + gp * 8:64 + gp * 8 + 8].unsqueeze(2).to_broadcast([128, 8, 64]), op=ALU.mult),
                  [t_h[gp], t_esm[b_]], [t_h[gp]])
            S.add("dve", lambda e, gp=gp: e.tensor_tensor(out=hst[:, gp * 512:(gp + 1) * 512], in0=hst[:, gp * 512:(gp + 1) * 512], in1=pSt[:], op=ALU.add),
                  [t_h[gp], t_pSt], [t_h[gp]])
            S.add("act", lambda e, gp=gp: e.copy(out=hbf[:, gp * 512:(gp + 1) * 512], in_=hst[:, gp * 512:(gp + 1) * 512]), [t_h[gp]], [t_hbf[gp]])
        if not bwd:
            for k in range(4):
                S.add("dve", lambda e, k=k: e.tensor_tensor(out=sk[b_][:, k * 512:(k + 1) * 512], in0=sk[b_][:, k * 512:(k + 1) * 512], in1=pY[k][:], op=ALU.add),
                      [t_sk[b_], t_pY[k]], [t_sk[b_]])
            S.dma("pool", yf_d[rows, :], sk[b_][:], reads=[t_sk[b_]])
        else:
            S.dma("sp", yf[b_][:], yf_d[rows, :], writes=[t_yf[b_]])
            for k in range(4):
                S.add("dve", lambda e, k=k: e.tensor_tensor(out=yf[b_][:, k * 512:(k + 1) * 512], in0=yf[b_][:, k * 512:(k + 1) * 512], in1=pY[k][:], op=ALU.add),
                      [t_yf[b_], t_pY[k]], [t_yf[b_]])
            S.dma("pool", yf_d[rows, :], yf[b_][:], reads=[t_yf[b_]])

    for it in range(NT + 1):
        if it < NT:
            stage_P(it)
        if it >= 1:
            stage_Q(it - 1)


def phase_e(nc, S, C, T_, yt_d, zs_d, normw_d, swout_d, xres_d, xout_d, ln_row, ident, t_ident, eps_t, t_eps, lng_d, lnb_d, **_):
    NT = T_ // 128
    w_out = C.sb([128, 16, D], BF16, "e_wout")
    w_tmp = C.sb([128, 16, D], F32, "e_wtmp")
    nw = C.sb([128, 16], F32, "e_nw")
    t_wout = T()
    t_wtmp = T()
    S.dma("sp", w_tmp[:], swout_d.rearrange("(kt p) n -> p kt n", p=128), writes=[t_wtmp])
    S.dma("sp", nw[:], normw_d, writes=[t_wtmp])
    for kt in range(16):
        S.add("dve" if kt % 2 == 0 else "pool", lambda e, kt=kt: e.tensor_scalar(out=w_out[:, kt, :], in0=w_tmp[:, kt, :], scalar1=nw[:, kt:kt + 1], scalar2=None, op0=ALU.mult),
              [t_wtmp], [t_wout])
    g_bc = C.sb([128, D], F32, "e_g")
    b_bc = C.sb([128, D], F32, "e_b")
    t_gb = T()
    S.dma("sp", g_bc[:], lng_d[ln_row:ln_row + 1, :].partition_broadcast(128), writes=[t_gb])
    S.dma("sp", b_bc[:], lnb_d[ln_row:ln_row + 1, :].partition_broadcast(128), writes=[t_gb])
    R2 = range(2)
    yt = [C.sb([128, 2048], F32, "e_yt%d" % i) for i in R2]
    t_yt = [T() for i in R2]
    zst = [C.sb([128, 2048], BF16, "e_zs%d" % i) for i in R2]
    t_zs = [T() for i in R2]
    ss = [C.sb([128, 16], F32, "e_ss%d" % i) for i in R2]
    t_ss = [T() for i in R2]
    junk = C.sb([128, 256], F32, "e_junk")
    t_junk = T()
    ynw = [C.sb([128, 2048], BF16, "e_ynw%d" % i) for i in R2]
    t_ynw = [T() for i in R2]
    yT = [C.sb([128, 16, 128], BF16, "e_yT%d" % i) for i in R2]
    t_yT = [T() for i in R2]
    xres = [C.sb([128, D], F32, "e_xres%d" % i) for i in R2]
    t_xres = [T() for i in R2]
    lnbs = [{"h": (C.sb([128, D], F32, "e_ln_h%d" % i), T()), "st": (C.sb([128, 12], F32, "e_ln_st%d" % i), T()),
             "mv": (C.sb([128, 4], F32, "e_ln_mv%d" % i), T())} for i in R2]
    pT = [C.ps([128, 512], F32, "e_pT%d" % i) for i in range(4)]
    t_pT = [T() for i in range(4)]
    pO = [C.ps([128, 512], F32, "e_pO%d" % i) for i in range(4)]
    t_pO = [T() for i in range(4)]

    def E1(t):
        b_ = t % 2
        rows = slice(t * 128, (t + 1) * 128)
        S.dma("sp", yt[b_][:], yt_d[rows, :], writes=[t_yt[b_]])
        S.dma("sp", zst[b_][:], zs_d[rows, :], writes=[t_zs[b_]])
        S.dma("sp", xres[b_][:], xres_d[rows, :], writes=[t_xres[b_]])
        S.add("dve", lambda e: e.tensor_tensor(out=yt[b_][:], in0=yt[b_][:], in1=zst[b_][:], op=ALU.mult), [t_yt[b_], t_zs[b_]], [t_yt[b_]])
        S.add("pool", lambda e: e.memset(ss[b_][:], 0.0), [], [t_ss[b_]])
        for g in range(8):
            S.add("act", lambda e, g=g: e.activation(out=junk[:], in_=yt[b_][:, g * 256:(g + 1) * 256], func=AF.Square, accum_out=ss[b_][:, g:g + 1]),
                  [t_yt[b_], t_ss[b_]], [t_junk, t_ss[b_]])
        S.add("act", lambda e: e.activation(out=ss[b_][:, 8:16], in_=ss[b_][:, 0:8], func=AF.Ln, bias=eps_t[:], scale=1.0 / 256.0), [t_ss[b_], t_eps], [t_ss[b_]])
        S.add("act", lambda e: e.activation(out=ss[b_][:, 8:16], in_=ss[b_][:, 8:16], func=AF.Exp, scale=-0.5), [t_ss[b_]], [t_ss[b_]])

    def E2(t):
        b_ = t % 2
        S.add("dve", lambda e: e.tensor_tensor(out=ynw[b_][:].rearrange("p (g c) -> p g c", g=8), in0=yt[b_][:].rearrange("p (g c) -> p g c", g=8),
                                               in1=ss[b_][:, 8:16].unsqueeze(2).to_broadcast([128, 8, 256]), op=ALU.mult), [t_yt[b_], t_ss[b_]], [t_ynw[b_]])
        for hf in range(2):
            pp, t_pp = pT[2 * b_ + hf], t_pT[2 * b_ + hf]
            ppb = pp[:].bitcast(BF16)
            for cc in range(8):
                kt = hf * 8 + cc
                S.add("pe", lambda e, kt=kt, cc=cc, ppb=ppb: e.transpose(ppb[:, cc * 128:(cc + 1) * 128], ynw[b_][:, kt * 128:(kt + 1) * 128], ident[:]),
                      [t_ynw[b_], t_ident], [t_pp])
            S.add("act", lambda e, ppb=ppb, hf=hf: e.copy(out=yT[b_][:, hf * 8:(hf + 1) * 8, :], in_=ppb.rearrange("p (k t) -> p k t", k=8)), [t_pp], [t_yT[b_]])
        for hf in range(2):
            po, t_po = pO[2 * b_ + hf], t_pO[2 * b_ + hf]
            for kt in range(16):
                S.add("pe", lambda e, kt=kt, hf=hf, po=po: e.matmul(po[:], lhsT=yT[b_][:, kt, :], rhs=w_out[:, kt, hf * 512:(hf + 1) * 512], start=(kt == 0), stop=(kt == 15)),
                      [t_yT[b_], t_wout], [t_po])

    def E3(t):
        b_ = t % 2
        rows = slice(t * 128, (t + 1) * 128)
        lnb = lnbs[b_]
        h_, t_h_ = lnb["h"]
        for hf in range(2):
            po, t_po = pO[2 * b_ + hf], t_pO[2 * b_ + hf]
            S.add("dve", lambda e, hf=hf, po=po: e.scalar_tensor_tensor(out=h_[:, hf * 512:(hf + 1) * 512], in0=xres[b_][:, hf * 512:(hf + 1) * 512],
                                                                       scalar=ALPHA, in1=po[:], op0=ALU.mult, op1=ALU.add),
                  [t_xres[b_], t_po], [t_h_])
        ln_core(S, lnb, g_bc[:], b_bc[:], t_gb, eps_t, t_eps, h_[:], t_h_)
        S.dma("pool", xout_d[rows, :], h_[:], reads=[t_h_])

    for it in range(NT + 2):
        if 1 <= it <= NT:
            E2(it - 1)
        if it >= 2:
            E3(it - 2)
        if it < NT:
            E1(it)


def phase_mlp(nc, S, C, T_, xin_d, xout_d, w1_d, w2_d, ln_row, ident, t_ident, eps_t, t_eps, lng_d, lnb_d, **_):
    NT = T_ // 128
    NG2 = NT // 2
    w1 = C.sb([128, 8, 4096], BF16, "m_w1")
    w2 = C.sb([128, 32, D], BF16, "m_w2")
    t_w1 = [T("m_w1_%d" % i) for i in range(8)]
    t_w2 = [T("m_w2_%d" % i) for i in range(8)]
    w1v = w1_d.rearrange("(kt p) n -> p kt n", p=128)
    w2v = w2_d.rearrange("(f p) n -> p f n", p=128)
    for i in range(8):
        S.dma("pool", w1[:, :, i * 512:(i + 1) * 512], w1v[:, :, i * 512:(i + 1) * 512], writes=[t_w1[i]])
    for i in range(8):
        S.dma("pool", w2[:, i * 4:(i + 1) * 4, :], w2v[:, i * 4:(i + 1) * 4, :], writes=[t_w2[i]])
    g_bc = C.sb([128, D], F32, "m_g")
    b_bc = C.sb([128, D], F32, "m_b")
    t_gb = T("m_gb")
    S.dma("sp", g_bc[:], lng_d[ln_row:ln_row + 1, :].partition_broadcast(128), writes=[t_gb])
    S.dma("sp", b_bc[:], lnb_d[ln_row:ln_row + 1, :].partition_broadcast(128), writes=[t_gb])
    xf = [C.sb([128, D], F32, "m_xf%d" % i) for i in range(2)]
    t_xf = [T() for i in range(2)]
    xb = [C.sb([128, D], BF16, "m_xb%d" % i) for i in range(2)]
    t_xb = [T() for i in range(2)]
    xT = [C.sb([128, 8, 256], BF16, "m_xT%d" % i) for i in range(2)]
    t_xT = [[T(), T()] for i in range(2)]
    h1T = C.sb([128, 32, 256], BF16, "m_h1T")
    t_h1 = [T() for i in range(32)]
    rl = [C.sb([128, 256], F32, "m_rl%d" % i) for i in range(2)]
    t_rl = [T(), T()]
    xres = [C.sb([128, D], F32, "m_xres%d" % i) for i in range(2)]
    t_xres = [T(), T()]
    lnb = {"h": (C.sb([128, D], F32, "m_ln_h"), T()), "st": (C.sb([128, 12], F32, "m_ln_st"), T()),
           "mv": (C.sb([128, 4], F32, "m_ln_mv"), T())}
    xo = [C.sb([128, D], F32, "m_xo%d" % i) for i in range(2)]
    t_xo = [T(), T()]
    pT = [C.ps([128, 512], F32, "m_pT%d" % i) for i in range(2)]
    t_pT = [T(), T()]
    pH = [C.ps([128, 512], F32, "m_pH%d" % i) for i in range(2)]
    t_pH = [T(), T()]
    pY = [C.ps([128, 512], F32, "m_pY%d" % i) for i in range(4)]
    t_pY = [T() for i in range(4)]
    cnt = [0, 0, 0]
    for g in range(NG2):
        gb = g % 2
        for i in range(2):
            t = 2 * g + i
            s_ = t % 2
            S.dma("sp", xf[s_][:], xin_d[t * 128:(t + 1) * 128, :], writes=[t_xf[s_]])
            S.add("dve", lambda e, s_=s_: e.tensor_copy(out=xb[s_][:], in_=xf[s_][:]), [t_xf[s_]], [t_xb[s_]])
            pi = cnt[0] % 2
            cnt[0] += 1
            ppb = pT[pi][:].bitcast(BF16)
            for kt in range(8):
                S.add("pe", lambda e, s_=s_, kt=kt, ppb=ppb: e.transpose(ppb[:, kt * 128:(kt + 1) * 128], xb[s_][:, kt * 128:(kt + 1) * 128], ident[:]),
                      [t_xb[s_], t_ident], [t_pT[pi]])
            S.add("act", lambda e, i=i, gb=gb, ppb=ppb: e.copy(out=xT[gb][:, :, i * 128:(i + 1) * 128], in_=ppb.rearrange("p (k t) -> p k t", k=8)),
                  [t_pT[pi]], [t_xT[gb][i]])
        for f in range(32):
            hi = cnt[1] % 2
            cnt[1] += 1
            for kt in range(8):
                S.add("pe", lambda e, kt=kt, f=f, hi=hi, gb=gb: e.matmul(pH[hi][:, 0:256], lhsT=w1[:, kt, f * 128:(f + 1) * 128], rhs=xT[gb][:, kt, :],
                                                                        start=(kt == 0), stop=(kt == 7)),
                      [t_w1[f // 4]] + t_xT[gb], [t_pH[hi]])
            S.add("act", lambda e, hi=hi: e.activation(out=rl[hi][:], in_=pH[hi][:, 0:256], func=AF.Relu), [t_pH[hi]], [t_rl[hi]])
            S.add("dve" if f % 2 == 0 else "pool", lambda e, hi=hi, f=f: e.tensor_tensor(out=h1T[:, f, :], in0=rl[hi][:], in1=rl[hi][:], op=ALU.mult),
                  [t_rl[hi]], [t_h1[f]])
        for i in range(2):
            t = 2 * g + i
            xr = t % 2
            S.dma("sp", xres[xr][:], xin_d[t * 128:(t + 1) * 128, :], writes=[t_xres[xr]])
            yb = (cnt[2] % 2) * 2
            cnt[2] += 1
            for hf in range(2):
                for f in range(32):
                    S.add("pe", lambda e, f=f, hf=hf, i=i, yb=yb: e.matmul(pY[yb + hf][:], lhsT=h1T[:, f, i * 128:(i + 1) * 128], rhs=w2[:, f, hf * 512:(hf + 1) * 512],
                                                                          start=(f == 0), stop=(f == 31)),
                          [t_h1[f], t_w2[f // 4]], [t_pY[yb + hf]])
            h, t_h = lnb["h"]
            for hf in range(2):
                S.add("dve", lambda e, hf=hf, xr=xr, yb=yb: e.scalar_tensor_tensor(out=h[:, hf * 512:(hf + 1) * 512], in0=xres[xr][:, hf * 512:(hf + 1) * 512],
                                                                                  scalar=ALPHA, in1=pY[yb + hf][:], op0=ALU.mult, op1=ALU.add),
                      [t_xres[xr], t_pY[yb + hf]], [t_h])
            ln_core(S, lnb, g_bc[:], b_bc[:], t_gb, eps_t, t_eps, xo[xr][:], t_xo[xr])
            S.dma("pool", xout_d[t * 128:(t + 1) * 128, :], xo[xr][:], reads=[t_xo[xr]])


def ln_core(S, bufs, g_bc, b_bc, t_gb, eps_t, t_eps, out_sb, t_out):
    h, t_h = bufs["h"]
    st6, t_st = bufs["st"]
    mv, t_mv = bufs["mv"]
    S.add("dve", lambda e: e.bn_stats(out=st6[:, 0:6], in_=h[:, 0:512]), [t_h], [t_st])
    S.add("dve", lambda e: e.bn_stats(out=st6[:, 6:12], in_=h[:, 512:1024]), [t_h], [t_st])
    S.add("dve", lambda e: e.bn_aggr(out=mv[:, 0:2], in_=st6[:, 0:12]), [t_st], [t_mv])
    S.add("act", lambda e: e.activation(out=mv[:, 2:3], in_=mv[:, 1:2], func=AF.Ln, bias=eps_t[:], scale=1.0), [t_mv, t_eps], [t_mv])
    S.add("act", lambda e: e.activation(out=mv[:, 3:4], in_=mv[:, 2:3], func=AF.Exp, scale=-0.5), [t_mv], [t_mv])
    S.add("dve", lambda e: e.tensor_scalar(out=h[:], in0=h[:], scalar1=mv[:, 0:1], scalar2=mv[:, 3:4],
                                           op0=ALU.subtract, op1=ALU.mult), [t_h, t_mv], [t_h])
    S.add("dve", lambda e: e.tensor_tensor(out=h[:], in0=h[:], in1=g_bc, op=ALU.mult), [t_h, t_gb], [t_h])
    S.add("dve", lambda e: e.tensor_tensor(out=out_sb, in0=h[:], in1=b_bc, op=ALU.add), [t_h, t_gb], [t_out])


def host_inputs(T, x, attn_w_in, attn_sink, attn_rpb, attn_w_out, ln1_g, ln1_b, ln2_g, ln2_b, mlp_w1, mlp_w2,
                ssm_w_in, ssm_conv_w, ssm_conv_b, ssm_dt_bias, ssm_A_log, ssm_D, ssm_norm_w, ssm_w_out, **_):
    NT = T // 128
    cosT, sinS = _rope_tables(T)
    tabs = _na_bias_tables(np.asarray(attn_rpb[0]), NT)
    nbt = np.stack([tabs[k].reshape(128, -1) for k in ("int", "t0", "t1", "tm2", "tm1")], 0)
    kq = np.arange(128)
    maskA = np.concatenate([(kq[:, None] >= kq[None, :]), (kq[:, None] <= kq[None, :])], 1).astype(np.float32)
    common = {
        "awin": _attn_w_layout(np.asarray(attn_w_in[0])),
        "awout": np.ascontiguousarray(attn_w_out[0]),
        "sink": np.ascontiguousarray(attn_sink[0:1]),
        "nbt": np.ascontiguousarray(nbt),
        "cosT": cosT, "sinS": sinS, "maskA": np.ascontiguousarray(maskA),
        "lng": np.ascontiguousarray(np.stack([ln1_g[0], ln2_g[0], ln1_g[1], ln2_g[1]])),
        "lnb": np.ascontiguousarray(np.stack([ln1_b[0], ln2_b[0], ln1_b[1], ln2_b[1]])),
        "ident": np.eye(128, dtype=np.float32),
        "swin": np.ascontiguousarray(ssm_w_in[0]),
        "cw": np.ascontiguousarray(np.asarray(ssm_conv_w[0]).reshape(5, 32, 128).transpose(2, 0, 1).reshape(128, 160)),
        "cb": np.ascontiguousarray(np.asarray(ssm_conv_b[0]).reshape(32, 128).T),
        "dtb": np.ascontiguousarray(np.asarray(ssm_dt_bias[0]).reshape(1, 64)),
        "alog": np.ascontiguousarray(np.asarray(ssm_A_log[0]).reshape(1, 64)),
        "dsk": np.ascontiguousarray(np.asarray(ssm_D[0]).reshape(1, 32)),
        "normwT": np.ascontiguousarray(np.asarray(ssm_norm_w[0]).reshape(16, 128).T),
        "swout": np.ascontiguousarray(ssm_w_out[0]),
        "tri": np.ascontiguousarray(np.concatenate([kq[:, None] <= kq[None, :], kq[:, None] > kq[None, :],
                                                    kq[:, None] >= kq[None, :], kq[:, None] < kq[None, :]], 1).astype(np.float32)),
        "w1_0": np.ascontiguousarray(mlp_w1[0]), "w1_1": np.ascontiguousarray(mlp_w1[1]),
        "w2_0": np.ascontiguousarray(mlp_w2[0]), "w2_1": np.ascontiguousarray(mlp_w2[1]),
    }
    return common


def kernel(**inputs):
    x = np.asarray(inputs["x"])
    B, T, _ = x.shape
    common = host_inputs(T, **inputs)
    nc = build(T)
    in_maps = []
    for c in range(8):
        m = dict(common)
        m["x"] = np.ascontiguousarray(x[c % B])
        in_maps.append(m)
    res = run_bass_kernel_spmd(nc, in_maps, core_ids=list(range(8)))
    out = np.stack([res.results[b]["out"] for b in range(B)], 0)
    return out.astype(np.float32)
```

```python
import contextlib
import numpy as np
import concourse.bass as bass
import concourse.mybir as mybir
from concourse.bass_utils import run_bass_kernel_spmd

F32 = mybir.dt.float32
BF16 = mybir.dt.bfloat16
ALU = mybir.AluOpType
AF = mybir.ActivationFunctionType
AX = mybir.AxisListType

D = 1024
ALPHA = 4.0 ** 0.25
NEGB = -30000.0


class T:
    __slots__ = ("name", "w", "r")

    def __init__(self, name=""):
        self.name = name
        self.w = None
        self.r = []


class Op:
    __slots__ = ("eng", "emit", "deps", "sig", "dma", "cnt", "dsem", "dval")

    def __init__(self, eng, emit, dma):
        self.eng = eng
        self.emit = emit
        self.deps = []
        self.sig = False
        self.dma = dma
        self.cnt = 0
        self.dsem = -1
        self.dval = 0


class Sched:
    ENGS = ("pe", "act", "dve", "pool", "sp")
    ENGOBJ = {"pe": "tensor", "act": "scalar", "dve": "vector", "pool": "gpsimd", "sp": "sync"}

    def __init__(self, nc, st, n_dma_sems=14):
        self.nc = nc
        self.ops = {e: [] for e in self.ENGS}
        self.n_dma_sems = n_dma_sems
        self.csem = {e: st.enter_context(nc.semaphore("c_" + e)) for e in self.ENGS}
        self.dsems = {e: [st.enter_context(nc.semaphore("d_%s%d" % (e, i))) for i in range(n_dma_sems)]
                      for e in ("sp", "pool")}
        self.phsem = st.enter_context(nc.semaphore("phase"))
        self.ccount = {e: 0 for e in self.ENGS}
        self.dcount = {e: 0 for e in ("sp", "pool")}
        self.duses = {e: [0] * n_dma_sems for e in ("sp", "pool")}
        self.nflush = 0

    def add(self, eng, emit, reads=(), writes=(), dma=False):
        op = Op(eng, emit, dma)
        deps = op.deps
        for t in reads:
            w = t.w
            if w is not None and (w.eng != eng or w.dma or eng != "pe"):
                deps.append(w)
        for t in writes:
            w = t.w
            if w is not None and (w.eng != eng or w.dma or dma):
                deps.append(w)
            for r in t.r:
                if r.eng != eng or r.dma or dma:
                    deps.append(r)
        for t in reads:
            t.r.append(op)
        for t in writes:
            t.w = op
            t.r = []
        for d in deps:
            d.sig = True
        self.ops[eng].append(op)
        return op

    def dma(self, eng, out, in_, reads=(), writes=()):
        return self.add(eng, lambda e: e.dma_start(out=out, in_=in_), reads, writes, dma=True)

    def flush(self):
        nc = self.nc
        import os
        if os.environ.get("KDBG"):
            print("flush", self.nflush, "sbuf remaining", nc.sbuf_bytes_remaining, {e: len(v) for e, v in self.ops.items()})
        csem, dsems = self.csem, self.dsems
        last_cnt = {}
        for e in self.ENGS:
            comp = [op for op in self.ops[e] if not op.dma]
            if comp:
                comp[-1].sig = True
            for op in self.ops[e]:
                if op.dma:
                    k = self.dcount[e]
                    self.dcount[e] += 1
                    op.dsem = k % self.n_dma_sems
                    self.duses[e][op.dsem] += 1
                    op.dval = 16 * self.duses[e][op.dsem]
                elif op.sig:
                    self.ccount[e] += 1
                    op.cnt = self.ccount[e]
            last_cnt[e] = self.ccount[e]
        nfl = self.nflush
        ops_all = self.ops

        def run(ename, eobj):
            if nfl > 0:
                eobj.wait_ge(self.phsem, 5 * nfl)
            known = {}
            last_dma = {}
            for op in ops_all[ename]:
                need = {}
                for d in op.deps:
                    if d.dma:
                        key = ("d", d.eng, d.dsem)
                        val = d.dval
                    else:
                        key = ("c", d.eng)
                        val = d.cnt
                    if need.get(key, 0) < val:
                        need[key] = val
                if op.dma:
                    key = ("d", ename, op.dsem)
                    val = op.dval - 16
                    if val > 0 and need.get(key, 0) < val:
                        need[key] = val
                for key, val in need.items():
                    if val <= 0 or known.get(key, 0) >= val:
                        continue
                    known[key] = val
                    sem = csem[key[1]] if key[0] == "c" else dsems[key[1]][key[2]]
                    eobj.wait_ge(sem, val)
                ins = op.emit(eobj)
                if op.dma:
                    ins.then_inc(dsems[ename][op.dsem], 16)
                    last_dma[op.dsem] = op.dval
                elif op.sig:
                    ins.then_inc(csem[ename], 1)
            for k, v in last_dma.items():
                if known.get(("d", ename, k), 0) < v:
                    eobj.wait_ge(dsems[ename][k], v)
            if last_cnt[ename] > 0 and known.get(("c", ename), 0) < last_cnt[ename]:
                eobj.wait_ge(csem[ename], last_cnt[ename])
            eobj.sem_inc(self.phsem, 1)

        with nc.Block() as block:
            for ename in self.ENGS:
                getattr(block, self.ENGOBJ[ename])(lambda eobj, ename=ename: run(ename, eobj))
        self.ops = {e: [] for e in self.ENGS}
        self.nflush += 1


class Ctx:
    _inst = [0]

    def __init__(self, nc, st):
        self.nc = nc
        self.st = st
        self.n = 0
        Ctx._inst[0] += 1
        self.pfx = "c%d_" % Ctx._inst[0]

    def sb(self, shape, dt, name=None):
        self.n += 1
        return self.st.enter_context(self.nc.sbuf_tensor(self.pfx + "s_" + (name or "sb%d" % self.n), list(shape), dt))

    def ps(self, shape, dt, name=None):
        self.n += 1
        return self.st.enter_context(self.nc.psum_tensor(self.pfx + "p_" + (name or "ps%d" % self.n), list(shape), dt))


def _na_types(NT):
    return {"int": [-2, -1, 0, 1, 2], "t0": [0, 1, 2, 3, 3], "t1": [-1, 0, 1, 2, 2],
            "tm2": [-2, -1, 0, 1, 1], "tm1": [-3, -2, -1, 0, 0]}


def _na_type_of(t, NT):
    if t == 0:
        return "t0"
    if t == 1:
        return "t1"
    if t == NT - 2:
        return "tm2"
    if t == NT - 1:
        return "tm1"
    return "int"


def _na_bias_tables(rpb, NT):
    rows = 2 * NT
    types = _na_types(NT)
    rep_t = {"int": min(2, NT - 1), "t0": 0, "t1": 1, "tm2": NT - 2, "tm1": NT - 1}
    out = {}
    q = np.arange(128)
    k = np.arange(128)
    for name, offs in types.items():
        t = rep_t[name]
        tab = np.full((128, 8, 5, 128), NEGB, np.float32)
        r = 2 * t + q // 64
        c = q % 64
        rs = np.clip(r - 4, 0, rows - 8)
        cs = np.clip(c - 8, 0, 64 - 16)
        seen = set()
        for j, off in enumerate(offs):
            if off in seen:
                continue
            seen.add(off)
            kt = t + off
            if kt < 0 or kt >= NT:
                continue
            kr = 2 * kt + k // 64
            kc = k % 64
            valid = ((kr[:, None] >= rs[None, :]) & (kr[:, None] < rs[None, :] + 8) &
                     (kc[:, None] >= cs[None, :]) & (kc[:, None] < cs[None, :] + 16))
            dr = np.clip(kr[:, None] - r[None, :] + 7, 0, 14)
            dc = np.clip(kc[:, None] - c[None, :] + 15, 0, 30)
            g = rpb[:, dr, dc]
            g = np.where(valid[None], g, NEGB)
            tab[:, :, j, :] = g.transpose(1, 0, 2)
        out[name] = tab
    return out


def _rope_tables(T):
    half = 32
    inv = (10000.0 ** (-np.arange(half, dtype=np.float32) / half)).astype(np.float32)
    pos = np.arange(T, dtype=np.float32)
    ang = pos[None, :] * inv[:, None]
    cos = np.cos(ang).astype(np.float32)
    sin = np.sin(ang).astype(np.float32)
    cosT = np.concatenate([cos, cos, cos, cos], 0)
    sinS = np.concatenate([-sin, sin, -sin, sin], 0)
    return np.ascontiguousarray(cosT), np.ascontiguousarray(sinS)


def _attn_w_layout(w_in):
    qa = w_in[:, 0:512].reshape(1024, 8, 64)
    ka = w_in[:, 512:640].reshape(1024, 2, 64)
    va = w_in[:, 640:768]
    qb = w_in[:, 768:1280]
    kb = w_in[:, 1280:1792]
    vb = w_in[:, 1792:2304]
    order = [0, 4, 1, 5, 2, 6, 3, 7]
    qa2 = qa[:, order, :]
    sw = lambda a: np.concatenate([a[..., 32:], a[..., :32]], -1)
    cols = [qa2.reshape(1024, 512), sw(qa2).reshape(1024, 512), ka.reshape(1024, 128),
            sw(ka).reshape(1024, 128), qb, kb, va, vb]
    return np.ascontiguousarray(np.concatenate(cols, 1))


def build(Tlen, upto="all"):
    NT = Tlen // 128
    nc = bass.Bass("TRN2", target_bir_lowering=False)

    def din(name, shape):
        return nc.dram_tensor(name, list(shape), F32, kind="ExternalInput").ap()

    x_d = din("x", [Tlen, D])
    awin_d = din("awin", [D, 2944])
    awout_d = din("awout", [D, D])
    sink_d = din("sink", [1, 8])
    nbt_d = din("nbt", [5, 128, 8 * 5 * 128])
    cos_d = din("cosT", [128, Tlen])
    sin_d = din("sinS", [128, Tlen])
    ma_d = din("maskA", [128, 2 * 128])
    lng_d = din("lng", [4, D])
    lnb_d = din("lnb", [4, D])
    ident_d = din("ident", [128, 128])
    w1_d = [din("w1_%d" % i, [D, 4096]) for i in range(2)]
    w2_d = [din("w2_%d" % i, [4096, D]) for i in range(2)]
    swin_d = din("swin", [D, 6208])
    cw_d = din("cw", [128, 160])
    cb_d = din("cb", [128, 32])
    dtb_d = din("dtb", [1, 64])
    alog_d = din("alog", [1, 64])
    dsk_d = din("dsk", [1, 32])
    normw_d = din("normwT", [128, 16])
    swout_d = din("swout", [2048, D])
    tri_d = din("tri", [128, 4 * 128])
    stages = ["a", "m0", "c1", "c2", "d", "m1"]
    if upto != "all":
        stages = stages[:stages.index(upto) + 1]
    out_d = nc.dram_tensor("out", [Tlen, D], F32, kind="ExternalOutput").ap()

    def scratch(name, shape, dt=F32):
        return nc.dram_tensor(name, list(shape), dt, kind="Internal").ap()

    def act_out(stage, name):
        return out_d if stages[-1] == stage else scratch(name, [Tlen, D])

    x1_d = act_out("a", "x1")
    x2_d = act_out("m0", "x2")
    x3_d = act_out("d", "x3")
    xtok_d = scratch("xtok", [Tlen, 2048], BF16)
    btok_d = scratch("btok", [Tlen, 1024], BF16)
    bT_d = scratch("bT", [1024, Tlen], BF16)
    cT_d = scratch("cT", [1024, Tlen], BF16)
    zs_d = scratch("zs", [Tlen, 2048], BF16)
    dt_d = scratch("dtv", [Tlen, 64], F32)
    yf_d = scratch("yf", [Tlen, 2048], F32)
    acf_d = scratch("acf", [Tlen, 32], F32)
    acb_d = scratch("acb", [Tlen, 32], F32)
    acfT_d = scratch("acfT", [NT, 32 * 128], F32)
    acbT_d = scratch("acbT", [NT, 32 * 128], F32)
    SS = dict(xtok_d=xtok_d, btok_d=btok_d, bT_d=bT_d, cT_d=cT_d, zs_d=zs_d, dt_d=dt_d, yf_d=yf_d, tri_d=tri_d,
              alog_d=alog_d, dsk_d=dsk_d, acf_d=acf_d, acb_d=acb_d, acfT_d=acfT_d, acbT_d=acbT_d)

    with contextlib.ExitStack() as st0:
        S = Sched(nc, st0)
        C0 = Ctx(nc, st0)
        ident = C0.sb([128, 128], BF16, "ident")
        t_ident = T("ident")
        S.dma("pool", ident[:], ident_d, writes=[t_ident])
        eps_t = C0.sb([128, 1], F32, "eps")
        t_eps = T("eps")
        S.add("pool", lambda e: e.memset(eps_t[:], 1e-5), [], [t_eps])
        K = dict(ident=ident, t_ident=t_ident, eps_t=eps_t, t_eps=t_eps, lng_d=lng_d, lnb_d=lnb_d)

        with contextlib.ExitStack() as st:
            phase_a(nc, S, Ctx(nc, st), T_=Tlen, x_d=x_d, awin_d=awin_d, awout_d=awout_d, sink_d=sink_d, nbt_d=nbt_d,
                    cos_d=cos_d, sin_d=sin_d, ma_d=ma_d, x1_d=x1_d, **K)
            S.flush()
        if "m0" in stages:
            with contextlib.ExitStack() as st:
                phase_mlp(nc, S, Ctx(nc, st), T_=Tlen, xin_d=x1_d, xout_d=x2_d, w1_d=w1_d[0], w2_d=w2_d[0], ln_row=1, **K)
                S.flush()
        if "c1" in stages:
            with contextlib.ExitStack() as st:
                phase_c1(nc, S, Ctx(nc, st), T_=Tlen, xin_d=x2_d, swin_d=swin_d, cw_d=cw_d, cb_d=cb_d, dtb_d=dtb_d, **SS, **K)
                S.flush()
        if "c2" in stages:
            with contextlib.ExitStack() as st:
                phase_ssd(nc, S, Ctx(nc, st), T_=Tlen, bwd=False, **SS, **K)
                S.flush()
        if "d" in stages:
            with contextlib.ExitStack() as st:
                phase_ssd(nc, S, Ctx(nc, st), T_=Tlen, bwd=True, **SS, **K)
                S.flush()
            with contextlib.ExitStack() as st:
                phase_e(nc, S, Ctx(nc, st), T_=Tlen, yt_d=yf_d, zs_d=zs_d, normw_d=normw_d, swout_d=swout_d, xres_d=x2_d, xout_d=x3_d, ln_row=2, **K)
                S.flush()
        if "m1" in stages:
            with contextlib.ExitStack() as st:
                phase_mlp(nc, S, Ctx(nc, st), T_=Tlen, xin_d=x3_d, xout_d=out_d, w1_d=w1_d[1], w2_d=w2_d[1], ln_row=3, **K)
                S.flush()
    return nc


def phase_a(nc, S, C, T_, x_d, awin_d, awout_d, sink_d, nbt_d, cos_d, sin_d, ma_d, lng_d, lnb_d, ident, t_ident,
            eps_t, t_eps, x1_d):
    T_tok = T_
    NT = T_tok // 128
    NG = NT // 4
    SCALE = 0.125
    w_in = C.sb([128, 8, 2944], BF16, "a_win")
    t_win = T("a_win")
    wv = awin_d.rearrange("(kt p) n -> p kt n", p=128)
    for kt in range(8):
        S.dma("pool", w_in[:, kt, :], wv[:, kt, :], writes=[t_win])
    w_out = C.sb([128, 8, D], BF16, "a_wout")
    t_wout = T("a_wout")
    S.dma("pool", w_out[:], awout_d.rearrange("(kt p) n -> p kt n", p=128), writes=[t_wout])
    g_bc = C.sb([128, D], F32, "a_g")
    b_bc = C.sb([128, D], F32, "a_b")
    t_gb = T("a_gb")
    S.dma("sp", g_bc[:], lng_d[0:1, :].partition_broadcast(128), writes=[t_gb])
    S.dma("sp", b_bc[:], lnb_d[0:1, :].partition_broadcast(128), writes=[t_gb])
    maskA = C.sb([128, 2, 128], BF16, "maskA")
    t_maskA = T("maskA")
    S.dma("pool", maskA[:], ma_d.rearrange("p (a q) -> p a q", a=2), writes=[t_maskA])
    esink = C.sb([128, 8], F32, "esink")
    t_esink = T("esink")
    S.dma("sp", esink[:], sink_d.partition_broadcast(128), writes=[t_esink])
    S.add("act", lambda e: e.activation(out=esink[:], in_=esink[:], func=AF.Exp), [t_esink], [t_esink])

    EB = C.sb([128, 8, 5, 128], BF16, "EB")
    t_EB = T("EB")
    stage = [C.sb([128, 640], F32, "nbstage%d" % i) for i in range(2)]
    t_stage = [T("nbstage%d" % i) for i in range(2)]
    type_idx = {"int": 0, "t0": 1, "t1": 2, "tm2": 3, "tm1": 4}
    cur_type = [None]

    def load_type(name):
        if cur_type[0] == name:
            return
        cur_type[0] = name
        ti = type_idx[name]
        for hh in range(8):
            sl = hh % 2
            S.dma("sp", stage[sl][:], nbt_d[ti, :, hh * 640:(hh + 1) * 640], writes=[t_stage[sl]])
            S.add("act", lambda e, hh=hh, sl=sl: e.activation(out=EB[:, hh, :, :].rearrange("p j q -> p (j q)"), in_=stage[sl][:],
                                                                func=AF.Exp), [t_stage[sl]], [t_EB])

    xf = [C.sb([128, D], F32, "xf%d" % i) for i in range(2)]
    t_xf = [T("xf%d" % i) for i in range(2)]
    xb = [C.sb([128, D], BF16, "xb%d" % i) for i in range(2)]
    t_xb = [T("xb%d" % i) for i in range(2)]
    xT = [C.sb([128, 8, 512], BF16, "xT%d" % i) for i in range(2)]
    t_xT = [[T("xT%d_%d" % (i, j)) for j in range(4)] for i in range(2)]
    qaT = [C.sb([128, 4, 512], BF16, "qaT%d" % i) for i in range(2)]
    t_qaT = [[T("qaT%d_%d" % (i, c)) for c in range(4)] for i in range(2)]
    qbT = [C.sb([128, 4, 512], BF16, "qbT%d" % i) for i in range(2)]
    t_qbT = [[T("qbT%d_%d" % (i, c)) for c in range(4)] for i in range(2)]
    kaT = C.sb([128, 12 * 128], BF16, "kaT")
    kbT = C.sb([128, 4, 12 * 128], BF16, "kbT")
    t_kaT = [T("kaT%d" % i) for i in range(3)]
    t_kbT = [[T("kbT%d_%d" % (i, c)) for c in range(4)] for i in range(3)]
    va = C.sb([128, 12, 2, 65], BF16, "va")
    vb = C.sb([128, 12, 8, 65], BF16, "vb")
    t_v = [T("v%d" % i) for i in range(12)]
    S.add("pool", lambda e: e.memset(va[:], 1.0), [], t_v)
    S.add("pool", lambda e: e.memset(vb[:], 1.0), [], t_v)
    cs_t = C.sb([128, 2, 512], F32, "cossin")
    t_cs = T("cossin")
    r1 = C.sb([128, 512], F32, "rope1")
    r2 = C.sb([128, 512], F32, "rope2")
    t_r1, t_r2 = T("r1"), T("r2")
    esA = [C.sb([128, 3, 512], BF16, "esA%d" % i) for i in range(2)]
    t_esA = [[T("esA%d_%d" % (i, j)) for j in range(3)] for i in range(2)]
    esB = [C.sb([128, 5, 128], BF16, "esB%d" % i) for i in range(2)]
    t_esB = [T("esB%d" % i) for i in range(2)]
    o_tok = C.sb([128, D], BF16, "o_tok")
    t_otok = T("o_tok")
    den = C.sb([128, 16], F32, "den")
    t_den = T("den")
    oT = C.sb([128, 8, 128], BF16, "oT")
    t_oT = T("oT")
    xres = [C.sb([128, D], F32, "xres%d" % i) for i in range(2)]
    t_xres = [T("xres%d" % i) for i in range(2)]
    lnb = {"h": (C.sb([128, D], F32, "ln_h"), T("ln_h")), "st": (C.sb([128, 12], F32, "ln_st"), T("ln_st")),
           "mv": (C.sb([128, 4], F32, "ln_mv"), T("ln_mv"))}
    xo = [C.sb([128, D], F32, "xo%d" % i) for i in range(2)]
    t_xo = [T("xo%d" % i) for i in range(2)]

    pP = [C.ps([128, 512], F32, "pP%d" % i) for i in range(2)]
    t_pP = [T("pP%d" % i) for i in range(2)]
    pS = [C.ps([128, 512], F32, "pS%d" % i) for i in range(4)]
    t_pS = [T("pS%d" % i) for i in range(4)]
    pO = [C.ps([128, 512], F32, "pO%d" % i) for i in range(2)]
    t_pO = [T("pO%d" % i) for i in range(2)]
    pcount = [0]

    def next_pP():
        i = pcount[0] % 2
        pcount[0] += 1
        return pP[i], t_pP[i]

    COL = {"qa": 0, "qas": 512, "ka": 1024, "kas": 1152, "qb": 1280, "kb": 1792, "va": 2304, "vb": 2432}

    def project_group(g):
        gb = g % 2
        rg = g % 3
        for i in range(4):
            t = 4 * g + i
            s = t % 2
            S.dma("sp", xf[s][:], x_d[t * 128:(t + 1) * 128, :], writes=[t_xf[s]])
            S.add("dve", lambda e, s=s: e.tensor_copy(out=xb[s][:], in_=xf[s][:]), [t_xf[s]], [t_xb[s]])
            pp, t_pp = next_pP()
            ppb = pp[:].bitcast(BF16)
            for kt in range(8):
                S.add("pe", lambda e, s=s, kt=kt, ppb=ppb: e.transpose(ppb[:, kt * 128:(kt + 1) * 128], xb[s][:, kt * 128:(kt + 1) * 128], ident[:]),
                      [t_xb[s], t_ident], [t_pp])
            S.add("act", lambda e, i=i, gb=gb, ppb=ppb: e.copy(out=xT[gb][:, :, i * 128:(i + 1) * 128],
                                                               in_=ppb.rearrange("p (k t) -> p k t", k=8)),
                  [t_pp], [t_xT[gb][i]])
        S.dma("sp", cs_t[:, 0, :], cos_d[:, g * 512:(g + 1) * 512], writes=[t_cs])
        S.dma("sp", cs_t[:, 1, :], sin_d[:, g * 512:(g + 1) * 512], writes=[t_cs])

        def fm_proj(col):
            pp, t_pp = next_pP()
            for kt in range(8):
                S.add("pe", lambda e, kt=kt, col=col, pp=pp: e.matmul(pp[:], lhsT=w_in[:, kt, col:col + 128], rhs=xT[gb][:, kt, :],
                                                                     start=(kt == 0), stop=(kt == 7)),
                      [t_win] + t_xT[gb], [t_pp])
            return pp, t_pp

        def rope_half(col, which, rdst, t_rdst):
            pp, t_pp = fm_proj(col)
            S.add("dve", lambda e, pp=pp: e.tensor_tensor(out=rdst[:], in0=pp[:], in1=cs_t[:, which, :], op=ALU.mult), [t_pp, t_cs], [t_rdst])

        def rope_add(dst, t_dst):
            S.add("pool", lambda e: e.tensor_tensor(out=dst, in0=r1[:], in1=r2[:], op=ALU.add), [t_r1, t_r2], [t_dst])

        rope_half(COL["ka"], 0, r1, t_r1)
        rope_half(COL["kas"], 1, r2, t_r2)
        rope_add(kaT[:, rg * 512:(rg + 1) * 512], t_kaT[rg])
        for c in range(4):
            pp, t_pp = fm_proj(COL["kb"] + c * 128)
            S.add("act", lambda e, pp=pp, c=c: e.copy(out=kbT[:, c, rg * 512:(rg + 1) * 512], in_=pp[:]), [t_pp], [t_kbT[rg][c]])
        for i in range(4):
            t = 4 * g + i
            slot = t % 12
            pp, t_pp = next_pP()
            for kt in range(8):
                S.add("pe", lambda e, kt=kt, pp=pp, i=i: e.matmul(pp[:], lhsT=xT[gb][:, kt, i * 128:(i + 1) * 128],
                                                                 rhs=w_in[:, kt, COL["vb"]:COL["vb"] + 512],
                                                                 start=(kt == 0), stop=(kt == 7)),
                      [t_win, t_xT[gb][i]], [t_pp])
            S.add("act", lambda e, pp=pp, slot=slot: e.copy(out=vb[:, slot, :, 0:64], in_=pp[:].rearrange("p (h d) -> p h d", h=8)),
                  [t_pp], [t_v[slot]])
            pp, t_pp = next_pP()
            for kt in range(8):
                S.add("pe", lambda e, kt=kt, pp=pp, i=i: e.matmul(pp[:, 0:128], lhsT=xT[gb][:, kt, i * 128:(i + 1) * 128],
                                                                 rhs=w_in[:, kt, COL["va"]:COL["va"] + 128],
                                                                 start=(kt == 0), stop=(kt == 7)),
                      [t_win, t_xT[gb][i]], [t_pp])
            S.add("act", lambda e, pp=pp, slot=slot: e.copy(out=va[:, slot, :, 0:64], in_=pp[:, 0:128].rearrange("p (h d) -> p h d", h=2)),
                  [t_pp], [t_v[slot]])
        yield "K"
        for c in range(4):
            rope_half(COL["qa"] + c * 128, 0, r1, t_r1)
            yield None
            rope_half(COL["qas"] + c * 128, 1, r2, t_r2)
            rope_add(qaT[gb][:, c, :], t_qaT[gb][c])
            yield None
        for c in range(4):
            pp, t_pp = fm_proj(COL["qb"] + c * 128)
            S.add("act", lambda e, pp=pp, c=c: e.copy(out=qbT[gb][:, c, :], in_=pp[:]), [t_pp], [t_qbT[gb][c]])
            yield None

    es_cnt = [0, 0]
    ps_rr = [0]

    def attend_tile(t, pump):
        g = t // 4
        gb = g % 2
        i = t % 4
        qsl = slice(i * 128, (i + 1) * 128)
        js = [j for j in (-1, 0, 1) if 0 <= t + j < NT]

        ebs = []
        for gi in range(2):
            eb = es_cnt[0] % 2
            es_cnt[0] += 1
            ebs.append(eb)
            for j in js:
                tk = t + j
                rgk = (tk // 4) % 3
                kof = (tk % 12) * 128
                bi = ps_rr[0] % 4
                ps_rr[0] += 1
                ps, t_ps = pS[bi], t_pS[bi]
                S.add("pe", lambda e, ps=ps, kof=kof, gi=gi: e.matmul(ps[:].rearrange("p (c q) -> p c q", c=4),
                                                                      lhsT=kaT[gi * 64:(gi + 1) * 64, kof:kof + 128],
                                                                      rhs=qaT[gb][gi * 64:(gi + 1) * 64, :, qsl], start=True, stop=True),
                      [t_kaT[rgk]] + t_qaT[gb], [t_ps])
                S.add("act", lambda e, ps=ps, eb=eb, j=j: e.activation(out=esA[eb][:, j + 1, :], in_=ps[:], func=AF.Exp, scale=SCALE),
                      [t_ps], [t_esA[eb][j + 1]])
                if j != 0:
                    mi = 0 if j == -1 else 1
                    S.add("dve", lambda e, eb=eb, j=j, mi=mi: e.tensor_tensor(out=esA[eb][:, j + 1, :].rearrange("p (c q) -> p c q", c=4),
                                                                              in0=esA[eb][:, j + 1, :].rearrange("p (c q) -> p c q", c=4),
                                                                              in1=maskA[:, mi:mi + 1, :].to_broadcast([128, 4, 128]), op=ALU.mult),
                          [t_esA[eb][j + 1], t_maskA], [t_esA[eb][j + 1]])
            pump()
        for gi in range(2):
            eb = ebs[gi]
            for c in range(4):
                for n, j in enumerate(js):
                    slot = (t + j) % 12
                    S.add("pe", lambda e, c=c, j=j, slot=slot, gi=gi, eb=eb, n=n, nj=len(js): e.matmul(pO[gi][:, c * 65:(c + 1) * 65],
                                                                                                    lhsT=esA[eb][:, j + 1, c * 128:(c + 1) * 128],
                                                                                                    rhs=va[:, slot, gi, :], start=(n == 0), stop=(n == nj - 1)),
                          [t_esA[eb][j + 1], t_v[slot]], [t_pO[gi]])
        for gi in range(2):
            ov = pO[gi][:, 0:260].rearrange("p (c e) -> p c e", c=4)
            S.add("dve", lambda e, gi=gi, ov=ov: e.tensor_tensor(out=den[:, gi * 4:(gi + 1) * 4], in0=ov[:, :, 64], in1=esink[:, gi * 4:(gi + 1) * 4], op=ALU.add),
                  [t_pO[gi], t_esink], [t_den])
            S.add("dve", lambda e, gi=gi: e.reciprocal(out=den[:, gi * 4:(gi + 1) * 4], in_=den[:, gi * 4:(gi + 1) * 4]), [t_den], [t_den])
            S.add("dve", lambda e, gi=gi, ov=ov: e.tensor_tensor(out=o_tok[:, gi * 256:(gi + 1) * 256].rearrange("p (c d) -> p c d", c=4), in0=ov[:, :, 0:64],
                                                                 in1=den[:, gi * 4:(gi + 1) * 4].unsqueeze(2).to_broadcast([128, 4, 64]), op=ALU.mult),
                  [t_pO[gi], t_den], [t_otok])

        typ = _na_type_of(t, NT)
        load_type(typ)
        offs = _na_types(NT)[typ]

        def b_scores(hh):
            c, pb = hh // 2, (hh % 2) * 64
            sb_i = es_cnt[1] % 2
            es_cnt[1] += 1
            pa, t_pa = pS[2 * sb_i], t_pS[2 * sb_i]
            pb2, t_pb2 = pS[2 * sb_i + 1], t_pS[2 * sb_i + 1]
            for j, off in enumerate(offs):
                tk = t + off
                rgk = (tk // 4) % 3
                kof = (tk % 12) * 128
                dst, t_dst = (pa[:, j * 128:(j + 1) * 128], t_pa) if j < 4 else (pb2[:, 0:128], t_pb2)
                S.add("pe", lambda e, dst=dst, kof=kof, c=c, pb=pb: e.matmul(dst, lhsT=kbT[pb:pb + 64, c, kof:kof + 128],
                                                                             rhs=qbT[gb][pb:pb + 64, c, qsl], start=True, stop=True),
                      [t_kbT[rgk][c], t_qbT[gb][c]], [t_dst])
            S.add("act", lambda e, pa=pa, sb_i=sb_i: e.activation(out=esB[sb_i][:, 0:4, :].rearrange("p j q -> p (j q)"), in_=pa[:], func=AF.Exp, scale=SCALE),
                  [t_pa], [t_esB[sb_i]])
            S.add("act", lambda e, pb2=pb2, sb_i=sb_i: e.activation(out=esB[sb_i][:, 4, :], in_=pb2[:, 0:128], func=AF.Exp, scale=SCALE),
                  [t_pb2], [t_esB[sb_i]])
            S.add("dve", lambda e, sb_i=sb_i, hh=hh: e.tensor_tensor(out=esB[sb_i][:], in0=esB[sb_i][:], in1=EB[:, hh, :, :], op=ALU.mult),
                  [t_esB[sb_i], t_EB], [t_esB[sb_i]])
            return sb_i

        def b_pv(hh, sb_i):
            ob = hh // 4
            for j, off in enumerate(offs):
                slot = (t + off) % 12
                S.add("pe", lambda e, j=j, slot=slot, hh=hh, sb_i=sb_i, ob=ob: e.matmul(pO[ob][:, (hh % 4) * 65:(hh % 4 + 1) * 65],
                                                                                       lhsT=esB[sb_i][:, j, :], rhs=vb[:, slot, hh, :],
                                                                                       start=(j == 0), stop=(j == 4)),
                      [t_esB[sb_i], t_v[slot]], [t_pO[ob]])

        prev = None
        for hh in range(8):
            sb_i = b_scores(hh)
            pump()
            if prev is not None:
                b_pv(*prev)
            prev = (hh, sb_i)
        b_pv(*prev)
        for ob in range(2):
            ov = pO[ob][:, 0:260].rearrange("p (c e) -> p c e", c=4)
            S.add("dve", lambda e, ob=ob, ov=ov: e.reciprocal(out=den[:, 8 + ob * 4:8 + (ob + 1) * 4], in_=ov[:, :, 64]), [t_pO[ob]], [t_den])
            S.add("dve", lambda e, ob=ob, ov=ov: e.tensor_tensor(out=o_tok[:, 512 + ob * 256:512 + (ob + 1) * 256].rearrange("p (c d) -> p c d", c=4), in0=ov[:, :, 0:64],
                                                                 in1=den[:, 8 + ob * 4:8 + (ob + 1) * 4].unsqueeze(2).to_broadcast([128, 4, 64]), op=ALU.mult),
                  [t_pO[ob], t_den], [t_otok])
        pp, t_pp = next_pP()
        ppb = pp[:].bitcast(BF16)
        for kt in range(8):
            S.add("pe", lambda e, kt=kt, ppb=ppb: e.transpose(ppb[:, kt * 128:(kt + 1) * 128], o_tok[:, kt * 128:(kt + 1) * 128], ident[:]),
                  [t_otok, t_ident], [t_pp])
        S.add("act", lambda e, ppb=ppb: e.copy(out=oT[:], in_=ppb.rearrange("p (k t) -> p k t", k=8)), [t_pp], [t_oT])
        xr = t % 2
        S.dma("sp", xres[xr][:], x_d[t * 128:(t + 1) * 128, :], writes=[t_xres[xr]])
        pump()
        halves = []
        for hf in range(2):
            pp, t_pp = next_pP()
            for kt in range(8):
                S.add("pe", lambda e, kt=kt, pp=pp, hf=hf: e.matmul(pp[:], lhsT=oT[:, kt, :], rhs=w_out[:, kt, hf * 512:(hf + 1) * 512],
                                                                   start=(kt == 0), stop=(kt == 7)),
                      [t_oT, t_wout], [t_pp])
            halves.append((pp, t_pp))
        h, t_h = lnb["h"]
        for hf, (pp, t_pp) in enumerate(halves):
            S.add("dve", lambda e, pp=pp, hf=hf, xr=xr: e.scalar_tensor_tensor(out=h[:, hf * 512:(hf + 1) * 512], in0=xres[xr][:, hf * 512:(hf + 1) * 512],
                                                                              scalar=ALPHA, in1=pp[:], op0=ALU.mult, op1=ALU.add),
                  [t_xres[xr], t_pp], [t_h])
        ln_core(S, lnb, g_bc[:], b_bc[:], t_gb, eps_t, t_eps, xo[xr][:], t_xo[xr])
        S.dma("pool", x1_d[t * 128:(t + 1) * 128, :], xo[xr][:], reads=[t_xo[xr]])

    for g in range(NG + 1):
        gen = project_group(g) if g < NG else iter(())
        for u in gen:
            if u == "K":
                break

        def pump(gen=gen):
            next(gen, None)

        if g >= 1:
            for t in range(4 * (g - 1), 4 * g):
                attend_tile(t, pump)
        for u in gen:
            pass


def phase_c1(nc, S, C, T_, xin_d, swin_d, cw_d, cb_d, dtb_d, xtok_d, btok_d, bT_d, cT_d, zs_d, dt_d,
             ident, t_ident, tri_d=None, alog_d=None, acf_d=None, acb_d=None, acfT_d=None, acbT_d=None, **_):
    NT = T_ // 128
    NG2 = NT // 2
    w_in = C.sb([128, 8, 6208], BF16, "c_win")
    NBLK = 13
    t_w = [T() for i in range(NBLK)]
    wv = swin_d.rearrange("(kt p) n -> p kt n", p=128)
    order = [4, 5, 6, 7, 8, 9, 10, 11, 12, 0, 1, 2, 3]
    for bi in order:
        c0, c1 = bi * 512, min(6208, (bi + 1) * 512)
        S.dma("pool", w_in[:, :, c0:c1], wv[:, :, c0:c1], writes=[t_w[bi]])

    def wdeps(c0, c1):
        return [t_w[b] for b in range(c0 // 512, (c1 - 1) // 512 + 1)]

    cw = C.sb([128, 160], F32, "c_cw")
    cb = C.sb([128, 32], F32, "c_cb")
    t_cw = T()
    S.dma("sp", cw[:], cw_d, writes=[t_cw])
    S.dma("sp", cb[:], cb_d, writes=[t_cw])
    dtb = C.sb([128, 64], F32, "c_dtb")
    S.dma("sp", dtb[:], dtb_d.partition_broadcast(128), writes=[t_cw])
    one_t = C.sb([128, 1], F32, "c_one")
    S.add("pool", lambda e: e.memset(one_t[:], 1.0), [], [t_cw])
    cmask = C.sb([128, 4, 128], BF16, "c_masks")
    S.dma("pool", cmask[:], tri_d.rearrange("p (a q) -> p a q", a=4), writes=[t_cw])
    aneg = C.sb([128, 64], F32, "c_aneg")
    t_an = T()
    S.dma("sp", aneg[:], alog_d.partition_broadcast(128), writes=[t_an])
    S.add("act", lambda e: e.activation(out=aneg[:], in_=aneg[:], func=AF.Exp), [t_an], [t_an])
    S.add("dve", lambda e: e.tensor_scalar(out=aneg[:], in0=aneg[:], scalar1=-1.0, scalar2=None, op0=ALU.mult), [t_an], [t_an])
    av = C.sb([128, 64], F32, "c_av")
    av_hi = C.sb([128, 64], BF16, "c_avhi")
    av_lo = C.sb([128, 64], BF16, "c_avlo")
    acs = C.sb([128, 64], F32, "c_acs")
    t_av = T()
    t_acs = T()
    acsT = C.sb([32, 256], F32, "c_acsT")
    t_acsT = T()
    diag = C.sb([128, 32, 5, 128], BF16, "c_diag")
    t_diag = [T() for i in range(32)]
    for c in range(32):
        for k in range(5):
            S.add("dve" if (c + k) % 2 == 0 else "pool",
                  lambda e, c=c, k=k: e.tensor_scalar(out=diag[:, c, k, :], in0=ident[:], scalar1=cw[:, k * 32 + c:k * 32 + c + 1], scalar2=None, op0=ALU.mult),
                  [t_ident, t_cw], [t_diag[c]])
    xb = [C.sb([128, D], BF16, "c_xb%d" % i) for i in range(2)]
    t_xb = [T(), T()]
    xT = [C.sb([128, 8, 260], BF16, "c_xT%d" % i) for i in range(2)]
    t_xT = [[T(), T(), T()] for i in range(2)]
    u = [C.sb([128, 260], BF16, "c_u%d" % i) for i in range(2)]
    t_u = [T(), T()]
    xsT = C.sb([128, 16, 256], BF16, "c_xsT")
    t_xsT = [T() for i in range(16)]
    BT = C.sb([128, 8, 256], BF16, "c_BT")
    t_BT = [T() for i in range(8)]
    CT = C.sb([128, 8, 256], BF16, "c_CT")
    t_CT = [T() for i in range(8)]
    xtok = C.sb([128, 2048], BF16, "c_xtok")
    t_xtok = T()
    btok = C.sb([128, 1024], BF16, "c_btok")
    t_btok = T()
    zs = C.sb([128, 2048], BF16, "c_zs")
    t_zs = T()
    dv = C.sb([128, 4, 64], F32, "c_dv")
    t_dv = T()
    pT = [C.ps([128, 512], F32, "c_pT%d" % i) for i in range(2)]
    t_pT = [T(), T()]
    pU = [C.ps([128, 512], F32, "c_pU%d" % i) for i in range(2)]
    t_pU = [T(), T()]
    pC = [C.ps([128, 512], F32, "c_pC%d" % i) for i in range(2)]
    t_pC = [T(), T()]
    pZ = [C.ps([128, 512], F32, "c_pZ%d" % i) for i in range(2)]
    t_pZ = [T(), T()]
    cnt = {"T": 0, "U": 0, "C": 0, "Z": 0, "x": 0}

    def nxt(key, arr, tarr):
        i = cnt[key] % 2
        cnt[key] += 1
        return arr[i], tarr[i]

    bTv = bT_d.rearrange("(g n) t -> n g t", n=128)
    cTv = cT_d.rearrange("(g n) t -> n g t", n=128)

    for g2 in range(NG2):
        gb = g2 % 2
        t0 = 256 * g2
        pieces = [(t0 - 2, 128, 0), (t0 + 126, 128, 128), (t0 + 254, 4, 256)]
        for pi_, (r0, nr, c0) in enumerate(pieces):
            s_ = cnt["x"] % 2
            cnt["x"] += 1
            lo, hi = max(r0, 0), min(r0 + nr, T_)
            if lo > r0 or hi < r0 + nr:
                S.add("pool", lambda e, s_=s_: e.memset(xb[s_][:], 0.0), [], [t_xb[s_]])
            S.dma("pool", xb[s_][lo - r0:hi - r0, :], xin_d[lo:hi, :], writes=[t_xb[s_]])
            pp, t_pp = nxt("T", pT, t_pT)
            ppb = pp[:].bitcast(BF16)
            for kt in range(8):
                S.add("pe", lambda e, s_=s_, kt=kt, ppb=ppb, nr=nr: e.transpose(ppb[:, kt * 128:kt * 128 + nr], xb[s_][0:nr, kt * 128:(kt + 1) * 128], ident[0:nr, 0:nr]),
                      [t_xb[s_], t_ident], [t_pp])
            S.add("dve", lambda e, gb=gb, ppb=ppb, nr=nr, c0=c0: e.tensor_copy(out=xT[gb][:, :, c0:c0 + nr], in_=ppb.rearrange("p (k t) -> p k t", k=8)[:, :, 0:nr]),
                  [t_pp], [t_xT[gb][pi_]])
        def proj_chunk(c):
            col = 2048 + c * 128
            pu, t_pu = nxt("U", pU, t_pU)
            for kt in range(8):
                S.add("pe", lambda e, kt=kt, col=col, pu=pu, gb=gb: e.matmul(pu[:, 0:260], lhsT=w_in[:, kt, col:col + 128], rhs=xT[gb][:, kt, :],
                                                                            start=(kt == 0), stop=(kt == 7)),
                      wdeps(col, col + 128) + t_xT[gb], [t_pu])
            ui = c % 2
            S.add("dve", lambda e, ui=ui, pu=pu: e.tensor_copy(out=u[ui][:], in_=pu[:, 0:260]), [t_pu], [t_u[ui]])

        def conv_chunk(c):
            ui = c % 2
            pc, t_pc = nxt("C", pC, t_pC)
            for k in range(5):
                S.add("pe", lambda e, k=k, c=c, ui=ui, pc=pc: e.matmul(pc[:, 0:256], lhsT=diag[:, c, k, :], rhs=u[ui][:, k:k + 256],
                                                                      start=(k == 0), stop=(k == 4)),
                      [t_diag[c], t_u[ui]], [t_pc])
            if c < 16:
                dst, t_dst = xsT[:, c, :], t_xsT[c]
            elif c < 24:
                dst, t_dst = BT[:, c - 16, :], t_BT[c - 16]
            else:
                dst, t_dst = CT[:, c - 24, :], t_CT[c - 24]
            S.add("act", lambda e, pc=pc, dst=dst, c=c: e.activation(out=dst, in_=pc[:, 0:256], func=AF.Silu, bias=cb[:, c:c + 1], scale=1.0),
                  [t_pc, t_cw], [t_dst])

        proj_chunk(0)
        for c in range(32):
            if c + 1 < 32:
                proj_chunk(c + 1)
            conv_chunk(c)
        S.dma("sp", bTv[:, :, t0:t0 + 256], BT[:], reads=t_BT)
        S.dma("sp", cTv[:, :, t0:t0 + 256], CT[:], reads=t_CT)
        for i in range(2):
            t = 2 * g2 + i
            for qz in range(4):
                pz, t_pz = nxt("Z", pZ, t_pZ)
                for kt in range(8):
                    S.add("pe", lambda e, kt=kt, qz=qz, pz=pz, gb=gb, i=i: e.matmul(pz[:], lhsT=xT[gb][:, kt, 2 + i * 128:2 + (i + 1) * 128],
                                                                                   rhs=w_in[:, kt, qz * 512:(qz + 1) * 512], start=(kt == 0), stop=(kt == 7)),
                          wdeps(qz * 512, (qz + 1) * 512) + t_xT[gb], [t_pz])
                S.add("act", lambda e, pz=pz, qz=qz: e.activation(out=zs[:, qz * 512:(qz + 1) * 512], in_=pz[:], func=AF.Silu), [t_pz], [t_zs])
            S.dma("sp", zs_d[t * 128:(t + 1) * 128, :], zs[:], reads=[t_zs])
            for hf in range(2):
                pp, t_pp = nxt("T", pT, t_pT)
                ppb = pp[:].bitcast(BF16)
                for cc in range(8):
                    c = hf * 8 + cc
                    S.add("pe", lambda e, c=c, cc=cc, ppb=ppb, i=i: e.transpose(ppb[:, cc * 128:(cc + 1) * 128], xsT[:, c, i * 128:(i + 1) * 128], ident[:]),
                          [t_xsT[c], t_ident], [t_pp])
                S.add("dve", lambda e, ppb=ppb, hf=hf: e.tensor_copy(out=xtok[:, hf * 1024:(hf + 1) * 1024], in_=ppb), [t_pp], [t_xtok])
            S.dma("sp", xtok_d[t * 128:(t + 1) * 128, :], xtok[:], reads=[t_xtok])
            pp, t_pp = nxt("T", pT, t_pT)
            ppb = pp[:].bitcast(BF16)
            for gg in range(8):
                S.add("pe", lambda e, gg=gg, ppb=ppb, i=i: e.transpose(ppb[:, gg * 128:(gg + 1) * 128], BT[:, gg, i * 128:(i + 1) * 128], ident[:]),
                      [t_BT[gg], t_ident], [t_pp])
            S.add("dve", lambda e, ppb=ppb: e.tensor_copy(out=btok[:], in_=ppb), [t_pp], [t_btok])
            S.dma("sp", btok_d[t * 128:(t + 1) * 128, :], btok[:], reads=[t_btok])
        for i in range(2):
            t = 2 * g2 + i
            pz, t_pz = nxt("Z", pZ, t_pZ)
            for kt in range(8):
                S.add("pe", lambda e, kt=kt, pz=pz, gb=gb, i=i: e.matmul(pz[:, 0:64], lhsT=xT[gb][:, kt, 2 + i * 128:2 + (i + 1) * 128],
                                                                        rhs=w_in[:, kt, 6144:6208], start=(kt == 0), stop=(kt == 7)),
                      wdeps(6144, 6208) + t_xT[gb], [t_pz])
            S.add("dve", lambda e, pz=pz: e.tensor_tensor(out=dv[:, 0, :], in0=pz[:, 0:64], in1=dtb[:], op=ALU.add), [t_pz, t_cw], [t_dv])
            S.add("act", lambda e: e.activation(out=dv[:, 1, :], in_=dv[:, 0, :], func=AF.Abs), [t_dv], [t_dv])
            S.add("act", lambda e: e.activation(out=dv[:, 1, :], in_=dv[:, 1, :], func=AF.Exp, scale=-1.0), [t_dv], [t_dv])
            S.add("act", lambda e: e.activation(out=dv[:, 1, :], in_=dv[:, 1, :], func=AF.Ln, bias=one_t[:], scale=1.0), [t_dv, t_cw], [t_dv])
            S.add("dve", lambda e: e.tensor_scalar(out=dv[:, 2, :], in0=dv[:, 0, :], scalar1=0.0, scalar2=None, op0=ALU.max), [t_dv], [t_dv])
            S.add("dve", lambda e: e.tensor_tensor(out=dv[:, 3, :], in0=dv[:, 2, :], in1=dv[:, 1, :], op=ALU.add), [t_dv], [t_dv])
            S.dma("sp", dt_d[t * 128:(t + 1) * 128, :], dv[:, 3, :], reads=[t_dv])
            S.add("dve", lambda e: e.tensor_tensor(out=av[:], in0=dv[:, 3, :], in1=aneg[:], op=ALU.mult), [t_dv, t_an], [t_av])
            S.add("dve", lambda e: e.tensor_copy(out=av_hi[:], in_=av[:]), [t_av], [t_av])
            S.add("dve", lambda e: e.tensor_tensor(out=av_lo[:], in0=av[:], in1=av_hi[:], op=ALU.subtract), [t_av], [t_av])
            pz, t_pz = nxt("Z", pZ, t_pZ)
            for ci, mi in ((0, 0), (1, 2)):
                S.add("pe", lambda e, ci=ci, mi=mi, pz=pz: e.matmul(pz[:, ci * 32:(ci + 1) * 32], lhsT=cmask[:, mi, :], rhs=av_hi[:, ci * 32:(ci + 1) * 32], start=True, stop=False),
                      [t_av, t_cw], [t_pz])
                S.add("pe", lambda e, ci=ci, mi=mi, pz=pz: e.matmul(pz[:, ci * 32:(ci + 1) * 32], lhsT=cmask[:, mi, :], rhs=av_lo[:, ci * 32:(ci + 1) * 32], start=False, stop=True),
                      [t_av, t_cw], [t_pz])
            S.add("dve", lambda e, pz=pz: e.tensor_copy(out=acs[:], in_=pz[:, 0:64]), [t_pz], [t_acs])
            S.dma("sp", acf_d[t * 128:(t + 1) * 128, :], acs[:, 0:32], reads=[t_acs])
            S.dma("sp", acb_d[t * 128:(t + 1) * 128, :], acs[:, 32:64], reads=[t_acs])
            pz, t_pz = nxt("Z", pZ, t_pZ)
            for ci, mi in ((0, 0), (1, 2)):
                S.add("pe", lambda e, ci=ci, mi=mi, pz=pz: e.matmul(pz[0:32, ci * 128:(ci + 1) * 128], lhsT=av_hi[:, ci * 32:(ci + 1) * 32], rhs=cmask[:, mi, :], start=True, stop=False),
                      [t_av, t_cw], [t_pz])
                S.add("pe", lambda e, ci=ci, mi=mi, pz=pz: e.matmul(pz[0:32, ci * 128:(ci + 1) * 128], lhsT=av_lo[:, ci * 32:(ci + 1) * 32], rhs=cmask[:, mi, :], start=False, stop=True),
                      [t_av, t_cw], [t_pz])
            S.add("dve", lambda e, pz=pz: e.tensor_copy(out=acsT[:], in_=pz[0:32, 0:256]), [t_pz], [t_acsT])
            S.dma("sp", acfT_d[t, :].rearrange("(h l) -> h l", h=32), acsT[:, 0:128], reads=[t_acsT])
            S.dma("sp", acbT_d[t, :].rearrange("(h l) -> h l", h=32), acsT[:, 128:256], reads=[t_acsT])


def phase_ssd(nc, S, C, T_, bwd, xtok_d, btok_d, bT_d, cT_d, dt_d, yf_d, tri_d, alog_d, dsk_d, acf_d, acb_d,
              acfT_d=None, acbT_d=None,
              zs_d=None, normw_d=None, swout_d=None, xres_d=None, xout_d=None, ln_row=2,
              ident=None, t_ident=None, eps_t=None, t_eps=None, lng_d=None, lnb_d=None, **_):
    NT = T_ // 128
    masks = C.sb([128, 4, 128], BF16, "s_masks")
    t_c = T()
    S.dma("pool", masks[:], tri_d.rearrange("p (a q) -> p a q", a=4), writes=[t_c])
    M1 = masks[:, 2, :] if bwd else masks[:, 0, :]
    M2 = masks[:, 3, :] if bwd else masks[:, 1, :]
    ones_m = C.sb([128, 128], BF16, "s_ones")
    S.add("pool", lambda e: e.memset(ones_m[:], 1.0), [], [t_c])
    zero_t = C.sb([128, 1], F32, "s_zero")
    S.add("pool", lambda e: e.memset(zero_t[:], 0.0), [], [t_c])
    aneg = C.sb([128, 32], F32, "s_aneg")
    h0 = 32 if bwd else 0
    S.dma("sp", aneg[:], alog_d[:, h0:h0 + 32].partition_broadcast(128), writes=[t_c])
    S.add("act", lambda e: e.activation(out=aneg[:], in_=aneg[:], func=AF.Exp), [t_c], [t_c])
    S.add("dve", lambda e: e.tensor_scalar(out=aneg[:], in0=aneg[:], scalar1=-1.0, scalar2=None, op0=ALU.mult), [t_c], [t_c])
    if not bwd:
        dsk = C.sb([128, 32], F32, "s_dsk")
        S.dma("sp", dsk[:], dsk_d.partition_broadcast(128), writes=[t_c])
        sk = [C.sb([128, 2048], F32, "s_sk%d" % i) for i in range(2)]
        t_sk = [T(), T()]
    else:
        yf = [C.sb([128, 2048], F32, "s_yf%d" % i) for i in range(2)]
        t_yf = [T(), T()]
    R2 = range(2)
    xtok = [C.sb([128, 32, 64], BF16, "s_xtok%d" % i) for i in R2]
    t_xtok = [T() for i in R2]
    btok = [C.sb([128, 1024], BF16, "s_btok%d" % i) for i in range(3)]
    t_btok = [T() for i in range(3)]
    BT = [C.sb([128, 8, 128], BF16, "s_BT%d" % i) for i in R2]
    t_BT = [T() for i in R2]
    CT = [C.sb([128, 8, 128], BF16, "s_CT%d" % i) for i in R2]
    t_CT = [T() for i in R2]
    dtt = [C.sb([128, 64], F32, "s_dt%d" % i) for i in R2]
    t_dt = [T() for i in R2]
    acT_d = acbT_d if bwd else acfT_d
    ac_d = acb_d if bwd else acf_d
    Rb = [C.sb([128, 32, 128], F32, "s_Rb%d" % i) for i in R2]
    t_Rb = [T() for i in R2]
    act_ = [C.sb([128, 32], F32, "s_act%d" % i) for i in R2]
    t_act = [T() for i in R2]
    a_f = [C.sb([128, 32], F32, "s_a%d" % i) for i in R2]
    a_hi = [C.sb([128, 32], BF16, "s_ahi%d" % i) for i in R2]
    a_lo = [C.sb([128, 32], BF16, "s_alo%d" % i) for i in R2]
    t_a = [T() for i in R2]
    esm = [C.sb([128, 96], F32, "s_esm%d" % i) for i in R2]
    t_esm = [T() for i in R2]
    GTm = [C.sb([128, 8, 128], BF16, "s_GTm%d" % i) for i in R2]
    t_GTm = [T() for i in R2]
    Dm = [C.sb([128, 32, 128], BF16, "s_Dm%d" % i) for i in R2]
    t_Dm = [[T() for q in range(8)] for i in R2]
    xdt = [C.sb([128, 32, 64], BF16, "s_xdt%d" % i) for i in R2]
    t_xdt = [T() for i in R2]
    xdd = [C.sb([128, 32, 64], BF16, "s_xdd%d" % i) for i in R2]
    t_xdd = [T() for i in R2]
    Xc = C.sb([128, 16, 128], F32, "s_Xc")
    t_Xc = [T() for i in range(4)]
    CE = [C.sb([128, 32, 128], BF16, "s_CE%d" % i) for i in R2]
    t_CE = [[T() for q in range(8)] for i in R2]
    hst = C.sb([128, 2048], F32, "s_h")
    t_h = [T() for i in range(4)]
    hbf = C.sb([128, 2048], BF16, "s_hbf")
    t_hbf = [T() for i in range(4)]
    S.add("pool", lambda e: e.memset(hst[:], 0.0), [], t_h)
    S.add("pool", lambda e: e.memset(hbf[:], 0.0), [], t_hbf)
    p_sm = C.ps([128, 512], F32, "s_psm")
    t_psm = T()
    pG = [C.ps([128, 512], F32, "s_pG%d" % i) for i in R2]
    t_pG = [T(), T()]
    pY = [C.ps([128, 512], F32, "s_pY%d" % i) for i in range(4)]
    t_pY = [T() for i in range(4)]
    pSt = C.ps([128, 512], F32, "s_pSt")
    t_pSt = T()
    bTv = bT_d.rearrange("(g n) t -> n g t", n=128)
    cTv = cT_d.rearrange("(g n) t -> n g t", n=128)
    seq = list(range(NT - 1, -1, -1)) if bwd else list(range(NT))

    def stage_P(it):
        t = seq[it]
        b_ = it % 2
        rows = slice(t * 128, (t + 1) * 128)
        b3 = it % 3
        S.dma("sp", dtt[b_][:], dt_d[rows, :], writes=[t_dt[b_]])
        S.dma("sp", act_[b_][:], ac_d[rows, :], writes=[t_act[b_]])
        S.dma("sp", BT[b_][:], bTv[:, :, rows], writes=[t_BT[b_]])
        S.dma("sp", CT[b_][:], cTv[:, :, rows], writes=[t_CT[b_]])
        S.dma("sp", Rb[b_][:].rearrange("p h l -> p (h l)"), acT_d[t, :].partition_broadcast(128), writes=[t_Rb[b_]])
        S.dma("sp", xtok[b_][:].rearrange("p h d -> p (h d)"), xtok_d[rows, :], writes=[t_xtok[b_]])
        S.dma("sp", btok[b3][:], btok_d[rows, :], writes=[t_btok[b3]])
        dth = dtt[b_][:, h0:h0 + 32]
        S.add("dve", lambda e: e.tensor_tensor(out=a_f[b_][:], in0=dth, in1=aneg[:], op=ALU.mult), [t_dt[b_], t_c], [t_a[b_]])
        S.add("dve", lambda e: e.tensor_copy(out=a_hi[b_][:], in_=a_f[b_][:]), [t_a[b_]], [t_a[b_]])
        S.add("dve", lambda e: e.tensor_tensor(out=a_lo[b_][:], in0=a_f[b_][:], in1=a_hi[b_][:], op=ALU.subtract), [t_a[b_]], [t_a[b_]])
        for ci, lm in enumerate((M1, M2, ones_m[:])):
            S.add("pe", lambda e, ci=ci, lm=lm: e.matmul(p_sm[:, ci * 32:(ci + 1) * 32], lhsT=lm, rhs=a_hi[b_][:], start=True, stop=False), [t_a[b_], t_c], [t_psm])
            S.add("pe", lambda e, ci=ci, lm=lm: e.matmul(p_sm[:, ci * 32:(ci + 1) * 32], lhsT=lm, rhs=a_lo[b_][:], start=False, stop=True), [t_a[b_], t_c], [t_psm])
        S.add("act", lambda e: e.activation(out=esm[b_][:], in_=p_sm[:, 0:96], func=AF.Exp), [t_psm], [t_esm[b_]])
        for g in range(8):
            S.add("pe", lambda e, g=g: e.matmul(pG[g // 4][:, (g % 4) * 128:(g % 4 + 1) * 128], lhsT=BT[b_][:, g, :], rhs=CT[b_][:, g, :], start=True, stop=True),
                  [t_BT[b_], t_CT[b_]], [t_pG[g // 4]])
        for k in range(2):
            S.add("dve", lambda e, k=k: e.tensor_tensor(out=GTm[b_][:, k * 4:(k + 1) * 4, :], in0=pG[k][:].rearrange("p (g l) -> p g l", g=4),
                                                        in1=M1.unsqueeze(1).to_broadcast([128, 4, 128]), op=ALU.mult),
                  [t_pG[k], t_c], [t_GTm[b_]])

        def x_relu(q):
            xq = q % 4
            for hh in range(4):
                h = 4 * q + hh
                S.add("act", lambda e, h=h, hh=hh, xq=xq: e.activation(out=Xc[:, 4 * xq + hh, :], in_=Rb[b_][:, h, :], func=AF.Relu,
                                                                      bias=act_[b_][:, h:h + 1], scale=-1.0),
                      [t_Rb[b_], t_act[b_]], [t_Xc[xq]])

        def x_exp(q):
            xq = q % 4
            S.add("act", lambda e, q=q, xq=xq: e.activation(out=Dm[b_][:, 4 * q:4 * q + 4, :].rearrange("p h l -> p (h l)"),
                                                            in_=Xc[:, 4 * xq:4 * xq + 4, :].rearrange("p h l -> p (h l)"), func=AF.Exp, scale=-1.0),
                  [t_Xc[xq]], [t_Dm[b_][q]])

        def ce_op(q):
            S.add("act", lambda e, q=q: e.activation(out=CE[b_][:, 4 * q:4 * q + 4, :].rearrange("p h l -> p (h l)"),
                                                     in_=Rb[b_][:, 4 * q:4 * q + 4, :].rearrange("p h l -> p (h l)"), func=AF.Exp), [t_Rb[b_]], [t_CE[b_][q]])
            S.add("dve", lambda e, q=q: e.tensor_tensor(out=CE[b_][:, 4 * q:4 * q + 4, :], in0=CE[b_][:, 4 * q:4 * q + 4, :],
                                                        in1=CT[b_][:, q:q + 1, :].to_broadcast([128, 4, 128]), op=ALU.mult),
                  [t_CE[b_][q], t_CT[b_]], [t_CE[b_][q]])

        def m_op(q):
            S.add("dve", lambda e, q=q: e.tensor_tensor(out=Dm[b_][:, 4 * q:4 * q + 4, :], in0=Dm[b_][:, 4 * q:4 * q + 4, :],
                                                        in1=GTm[b_][:, q:q + 1, :].to_broadcast([128, 4, 128]), op=ALU.mult),
                  [t_Dm[b_][q], t_GTm[b_]], [t_Dm[b_][q]])

        S.add("dve", lambda e: e.tensor_tensor(out=xdt[b_][:], in0=xtok[b_][:], in1=dth.unsqueeze(2).to_broadcast([128, 32, 64]), op=ALU.mult),
              [t_xtok[b_], t_dt[b_]], [t_xdt[b_]])
        if not bwd:
            S.add("dve", lambda e: e.tensor_tensor(out=sk[b_][:].rearrange("p (h d) -> p h d", h=32), in0=xtok[b_][:],
                                                    in1=dsk[:].unsqueeze(2).to_broadcast([128, 32, 64]), op=ALU.mult),
                  [t_xtok[b_], t_c], [t_sk[b_]])
        x_relu(0)
        x_relu(1)
        x_exp(0)
        S.add("dve", lambda e: e.tensor_tensor(out=xdd[b_][:], in0=xdt[b_][:], in1=esm[b_][:, 32:64].unsqueeze(2).to_broadcast([128, 32, 64]), op=ALU.mult),
              [t_xdt[b_], t_esm[b_]], [t_xdd[b_]])
        for q in range(2, 8):
            x_relu(q)
            x_exp(q - 1)
            m_op(q - 2)
            ce_op(q - 2)
        x_exp(7)
        m_op(6)
        ce_op(6)
        m_op(7)
        ce_op(7)

    def stage_Q(it):
        t = seq[it]
        b_ = it % 2
        b3 = it % 3
        rows = slice(t * 128, (t + 1) * 128)
        for h in range(32):
            yo = pY[h // 8][:, (h % 8) * 64:(h % 8 + 1) * 64]
            S.add("pe", lambda e, h=h, yo=yo: e.matmul(yo, lhsT=Dm[b_][:, h, :], rhs=xdt[b_][:, h, :], start=True, stop=False),
                  [t_Dm[b_][h // 4], t_xdt[b_]], [t_pY[h // 8]])
            S.add("pe", lambda e, h=h, yo=yo: e.matmul(yo, lhsT=CE[b_][:, h, :], rhs=hbf[:, h * 64:(h + 1) * 64], start=False, stop=True),
                  [t_CE[b_][h // 4], t_hbf[h // 8]], [t_pY[h // 8]])
        for gp in range(4):
            for k in range(2):
                g = 2 * gp + k
                S.add("pe", lambda e, g=g, k=k: e.matmul(pSt[:, k * 256:(k + 1) * 256], lhsT=btok[b3][:, g * 128:(g + 1) * 128],
                                                         rhs=xdd[b_][:, 4 * g:4 * g + 4, :], start=True, stop=True),
                      [t_btok[b3], t_xdd[b_]], [t_pSt])
            S.add("dve", lambda e, gp=gp: e.tensor_tensor(out=hst[:, gp * 512:(gp + 1) * 512].rearrange("p (h d) -> p h d", h=8),
                                                           in0=hst[:, gp * 512:(gp + 1) * 512].rearrange("p (h d) -> p h d", h=8),
                                                           in1=esm[b_][:, 64 + gp * 8:64 + gp * 8 + 8].unsqueeze(2).to_broadcast([128, 8, 64]), op=ALU.mult),
                  [t_h[gp], t_esm[b_]], [t_h[gp]])
            S.add("dve", lambda e, gp=gp: e.tensor_tensor(out=hst[:, gp * 512:(gp + 1) * 512], in0=hst[:, gp * 512:(gp + 1) * 512], in1=pSt[:], op=ALU.add),
                  [t_h[gp], t_pSt], [t_h[gp]])
            S.add("act", lambda e, gp=gp: e.copy(out=hbf[:, gp * 512:(gp + 1) * 512], in_=hst[:, gp * 512:(gp + 1) * 512]), [t_h[gp]], [t_hbf[gp]])
        if not bwd:
            for k in range(4):
                S.add("dve", lambda e, k=k: e.tensor_tensor(out=sk[b_][:, k * 512:(k + 1) * 512], in0=sk[b_][:, k * 512:(k + 1) * 512], in1=pY[k][:], op=ALU.add),
                      [t_sk[b_], t_pY[k]], [t_sk[b_]])
            S.dma("pool", yf_d[rows, :], sk[b_][:], reads=[t_sk[b_]])
        else:
            S.dma("sp", yf[b_][:], yf_d[rows, :], writes=[t_yf[b_]])
            for k in range(4):
                S.add("dve", lambda e, k=k: e.tensor_tensor(out=yf[b_][:, k * 512:(k + 1) * 512], in0=yf[b_][:, k * 512:(k + 1) * 512], in1=pY[k][:], op=ALU.add),
                      [t_yf[b_], t_pY[k]], [t_yf[b_]])
            S.dma("pool", yf_d[rows, :], yf[b_][:], reads=[t_yf[b_]])

    for it in range(NT + 1):
        if it < NT:
            stage_P(it)
        if it >= 1:
            stage_Q(it - 1)


def phase_e(nc, S, C, T_, yt_d, zs_d, normw_d, swout_d, xres_d, xout_d, ln_row, ident, t_ident, eps_t, t_eps, lng_d, lnb_d, **_):
    NT = T_ // 128
    w_out = C.sb([128, 16, D], BF16, "e_wout")
    w_tmp = C.sb([128, 16, D], F32, "e_wtmp")
    nw = C.sb([128, 16], F32, "e_nw")
    t_wout = T()
    t_wtmp = T()
    S.dma("sp", w_tmp[:], swout_d.rearrange("(kt p) n -> p kt n", p=128), writes=[t_wtmp])
    S.dma("sp", nw[:], normw_d, writes=[t_wtmp])
    for kt in range(16):
        S.add("dve" if kt % 2 == 0 else "pool", lambda e, kt=kt: e.tensor_scalar(out=w_out[:, kt, :], in0=w_tmp[:, kt, :], scalar1=nw[:, kt:kt + 1], scalar2=None, op0=ALU.mult),
              [t_wtmp], [t_wout])
    g_bc = C.sb([128, D], F32, "e_g")
    b_bc = C.sb([128, D], F32, "e_b")
    t_gb = T()
    S.dma("sp", g_bc[:], lng_d[ln_row:ln_row + 1, :].partition_broadcast(128), writes=[t_gb])
    S.dma("sp", b_bc[:], lnb_d[ln_row:ln_row + 1, :].partition_broadcast(128), writes=[t_gb])
    R2 = range(2)
    yt = [C.sb([128, 2048], F32, "e_yt%d" % i) for i in R2]
    t_yt = [T() for i in R2]
    zst = [C.sb([128, 2048], BF16, "e_zs%d" % i) for i in R2]
    t_zs = [T() for i in R2]
    ss = [C.sb([128, 16], F32, "e_ss%d" % i) for i in R2]
    t_ss = [T() for i in R2]
    junk = C.sb([128, 256], F32, "e_junk")
    t_junk = T()
    ynw = [C.sb([128, 2048], BF16, "e_ynw%d" % i) for i in R2]
    t_ynw = [T() for i in R2]
    yT = [C.sb([128, 16, 128], BF16, "e_yT%d" % i) for i in R2]
    t_yT = [T() for i in R2]
    xres = [C.sb([128, D], F32, "e_xres%d" % i) for i in R2]
    t_xres = [T() for i in R2]
    lnbs = [{"h": (C.sb([128, D], F32, "e_ln_h%d" % i), T()), "st": (C.sb([128, 12], F32, "e_ln_st%d" % i), T()),
             "mv": (C.sb([128, 4], F32, "e_ln_mv%d" % i), T())} for i in R2]
    pT = [C.ps([128, 512], F32, "e_pT%d" % i) for i in range(4)]
    t_pT = [T() for i in range(4)]
    pO = [C.ps([128, 512], F32, "e_pO%d" % i) for i in range(4)]
    t_pO = [T() for i in range(4)]

    def E1(t):
        b_ = t % 2
        rows = slice(t * 128, (t + 1) * 128)
        S.dma("sp", yt[b_][:], yt_d[rows, :], writes=[t_yt[b_]])
        S.dma("sp", zst[b_][:], zs_d[rows, :], writes=[t_zs[b_]])
        S.dma("sp", xres[b_][:], xres_d[rows, :], writes=[t_xres[b_]])
        S.add("dve", lambda e: e.tensor_tensor(out=yt[b_][:], in0=yt[b_][:], in1=zst[b_][:], op=ALU.mult), [t_yt[b_], t_zs[b_]], [t_yt[b_]])
        S.add("pool", lambda e: e.memset(ss[b_][:], 0.0), [], [t_ss[b_]])
        for g in range(8):
            S.add("act", lambda e, g=g: e.activation(out=junk[:], in_=yt[b_][:, g * 256:(g + 1) * 256], func=AF.Square, accum_out=ss[b_][:, g:g + 1]),
                  [t_yt[b_], t_ss[b_]], [t_junk, t_ss[b_]])
        S.add("act", lambda e: e.activation(out=ss[b_][:, 8:16], in_=ss[b_][:, 0:8], func=AF.Ln, bias=eps_t[:], scale=1.0 / 256.0), [t_ss[b_], t_eps], [t_ss[b_]])
        S.add("act", lambda e: e.activation(out=ss[b_][:, 8:16], in_=ss[b_][:, 8:16], func=AF.Exp, scale=-0.5), [t_ss[b_]], [t_ss[b_]])

    def E2(t):
        b_ = t % 2
        S.add("dve", lambda e: e.tensor_tensor(out=ynw[b_][:].rearrange("p (g c) -> p g c", g=8), in0=yt[b_][:].rearrange("p (g c) -> p g c", g=8),
                                               in1=ss[b_][:, 8:16].unsqueeze(2).to_broadcast([128, 8, 256]), op=ALU.mult), [t_yt[b_], t_ss[b_]], [t_ynw[b_]])
        for hf in range(2):
            pp, t_pp = pT[2 * b_ + hf], t_pT[2 * b_ + hf]
            ppb = pp[:].bitcast(BF16)
            for cc in range(8):
                kt = hf * 8 + cc
                S.add("pe", lambda e, kt=kt, cc=cc, ppb=ppb: e.transpose(ppb[:, cc * 128:(cc + 1) * 128], ynw[b_][:, kt * 128:(kt + 1) * 128], ident[:]),
                      [t_ynw[b_], t_ident], [t_pp])
            S.add("act", lambda e, ppb=ppb, hf=hf: e.copy(out=yT[b_][:, hf * 8:(hf + 1) * 8, :], in_=ppb.rearrange("p (k t) -> p k t", k=8)), [t_pp], [t_yT[b_]])
        for hf in range(2):
            po, t_po = pO[2 * b_ + hf], t_pO[2 * b_ + hf]
            for kt in range(16):
                S.add("pe", lambda e, kt=kt, hf=hf, po=po: e.matmul(po[:], lhsT=yT[b_][:, kt, :], rhs=w_out[:, kt, hf * 512:(hf + 1) * 512], start=(kt == 0), stop=(kt == 15)),
                      [t_yT[b_], t_wout], [t_po])

    def E3(t):
        b_ = t % 2
        rows = slice(t * 128, (t + 1) * 128)
        lnb = lnbs[b_]
        h_, t_h_ = lnb["h"]
        for hf in range(2):
            po, t_po = pO[2 * b_ + hf], t_pO[2 * b_ + hf]
            S.add("dve", lambda e, hf=hf, po=po: e.scalar_tensor_tensor(out=h_[:, hf * 512:(hf + 1) * 512], in0=xres[b_][:, hf * 512:(hf + 1) * 512],
                                                                       scalar=ALPHA, in1=po[:], op0=ALU.mult, op1=ALU.add),
                  [t_xres[b_], t_po], [t_h_])
        ln_core(S, lnb, g_bc[:], b_bc[:], t_gb, eps_t, t_eps, h_[:], t_h_)
        S.dma("pool", xout_d[rows, :], h_[:], reads=[t_h_])

    for it in range(NT + 2):
        if 1 <= it <= NT:
            E2(it - 1)
        if it >= 2:
            E3(it - 2)
        if it < NT:
            E1(it)


def phase_mlp(nc, S, C, T_, xin_d, xout_d, w1_d, w2_d, ln_row, ident, t_ident, eps_t, t_eps, lng_d, lnb_d, **_):
    NT = T_ // 128
    NG2 = NT // 2
    w1 = C.sb([128, 8, 4096], BF16, "m_w1")
    w2 = C.sb([128, 32, D], BF16, "m_w2")
    t_w1 = [T("m_w1_%d" % i) for i in range(8)]
    t_w2 = [T("m_w2_%d" % i) for i in range(8)]
    w1v = w1_d.rearrange("(kt p) n -> p kt n", p=128)
    w2v = w2_d.rearrange("(f p) n -> p f n", p=128)
    for i in range(8):
        S.dma("pool", w1[:, :, i * 512:(i + 1) * 512], w1v[:, :, i * 512:(i + 1) * 512], writes=[t_w1[i]])
    for i in range(8):
        S.dma("pool", w2[:, i * 4:(i + 1) * 4, :], w2v[:, i * 4:(i + 1) * 4, :], writes=[t_w2[i]])
    g_bc = C.sb([128, D], F32, "m_g")
    b_bc = C.sb([128, D], F32, "m_b")
    t_gb = T("m_gb")
    S.dma("sp", g_bc[:], lng_d[ln_row:ln_row + 1, :].partition_broadcast(128), writes=[t_gb])
    S.dma("sp", b_bc[:], lnb_d[ln_row:ln_row + 1, :].partition_broadcast(128), writes=[t_gb])
    xf = [C.sb([128, D], F32, "m_xf%d" % i) for i in range(2)]
    t_xf = [T() for i in range(2)]
    xb = [C.sb([128, D], BF16, "m_xb%d" % i) for i in range(2)]
    t_xb = [T() for i in range(2)]
    xT = [C.sb([128, 8, 256], BF16, "m_xT%d" % i) for i in range(2)]
    t_xT = [[T(), T()] for i in range(2)]
    h1T = C.sb([128, 32, 256], BF16, "m_h1T")
    t_h1 = [T() for i in range(32)]
    rl = [C.sb([128, 256], F32, "m_rl%d" % i) for i in range(2)]
    t_rl = [T(), T()]
    xres = [C.sb([128, D], F32, "m_xres%d" % i) for i in range(2)]
    t_xres = [T(), T()]
    lnb = {"h": (C.sb([128, D], F32, "m_ln_h"), T()), "st": (C.sb([128, 12], F32, "m_ln_st"), T()),
           "mv": (C.sb([128, 4], F32, "m_ln_mv"), T())}
    xo = [C.sb([128, D], F32, "m_xo%d" % i) for i in range(2)]
    t_xo = [T(), T()]
    pT = [C.ps([128, 512], F32, "m_pT%d" % i) for i in range(2)]
    t_pT = [T(), T()]
    pH = [C.ps([128, 512], F32, "m_pH%d" % i) for i in range(2)]
    t_pH = [T(), T()]
    pY = [C.ps([128, 512], F32, "m_pY%d" % i) for i in range(4)]
    t_pY = [T() for i in range(4)]
    cnt = [0, 0, 0]
    for g in range(NG2):
        gb = g % 2
        for i in range(2):
            t = 2 * g + i
            s_ = t % 2
            S.dma("sp", xf[s_][:], xin_d[t * 128:(t + 1) * 128, :], writes=[t_xf[s_]])
            S.add("dve", lambda e, s_=s_: e.tensor_copy(out=xb[s_][:], in_=xf[s_][:]), [t_xf[s_]], [t_xb[s_]])
            pi = cnt[0] % 2
            cnt[0] += 1
            ppb = pT[pi][:].bitcast(BF16)
            for kt in range(8):
                S.add("pe", lambda e, s_=s_, kt=kt, ppb=ppb: e.transpose(ppb[:, kt * 128:(kt + 1) * 128], xb[s_][:, kt * 128:(kt + 1) * 128], ident[:]),
                      [t_xb[s_], t_ident], [t_pT[pi]])
            S.add("act", lambda e, i=i, gb=gb, ppb=ppb: e.copy(out=xT[gb][:, :, i * 128:(i + 1) * 128], in_=ppb.rearrange("p (k t) -> p k t", k=8)),
                  [t_pT[pi]], [t_xT[gb][i]])
        for f in range(32):
            hi = cnt[1] % 2
            cnt[1] += 1
            for kt in range(8):
                S.add("pe", lambda e, kt=kt, f=f, hi=hi, gb=gb: e.matmul(pH[hi][:, 0:256], lhsT=w1[:, kt, f * 128:(f + 1) * 128], rhs=xT[gb][:, kt, :],
                                                                        start=(kt == 0), stop=(kt == 7)),
                      [t_w1[f // 4]] + t_xT[gb], [t_pH[hi]])
            S.add("act", lambda e, hi=hi: e.activation(out=rl[hi][:], in_=pH[hi][:, 0:256], func=AF.Relu), [t_pH[hi]], [t_rl[hi]])
            S.add("dve" if f % 2 == 0 else "pool", lambda e, hi=hi, f=f: e.tensor_tensor(out=h1T[:, f, :], in0=rl[hi][:], in1=rl[hi][:], op=ALU.mult),
                  [t_rl[hi]], [t_h1[f]])
        for i in range(2):
            t = 2 * g + i
            xr = t % 2
            S.dma("sp", xres[xr][:], xin_d[t * 128:(t + 1) * 128, :], writes=[t_xres[xr]])
            yb = (cnt[2] % 2) * 2
            cnt[2] += 1
            for hf in range(2):
                for f in range(32):
                    S.add("pe", lambda e, f=f, hf=hf, i=i, yb=yb: e.matmul(pY[yb + hf][:], lhsT=h1T[:, f, i * 128:(i + 1) * 128], rhs=w2[:, f, hf * 512:(hf + 1) * 512],
                                                                          start=(f == 0), stop=(f == 31)),
                          [t_h1[f], t_w2[f // 4]], [t_pY[yb + hf]])
            h, t_h = lnb["h"]
            for hf in range(2):
                S.add("dve", lambda e, hf=hf, xr=xr, yb=yb: e.scalar_tensor_tensor(out=h[:, hf * 512:(hf + 1) * 512], in0=xres[xr][:, hf * 512:(hf + 1) * 512],
                                                                                  scalar=ALPHA, in1=pY[yb + hf][:], op0=ALU.mult, op1=ALU.add),
                      [t_xres[xr], t_pY[yb + hf]], [t_h])
            ln_core(S, lnb, g_bc[:], b_bc[:], t_gb, eps_t, t_eps, xo[xr][:], t_xo[xr])
            S.dma("pool", xout_d[t * 128:(t + 1) * 128, :], xo[xr][:], reads=[t_xo[xr]])


def ln_core(S, bufs, g_bc, b_bc, t_gb, eps_t, t_eps, out_sb, t_out):
    h, t_h = bufs["h"]
    st6, t_st = bufs["st"]
    mv, t_mv = bufs["mv"]
    S.add("dve", lambda e: e.bn_stats(out=st6[:, 0:6], in_=h[:, 0:512]), [t_h], [t_st])
    S.add("dve", lambda e: e.bn_stats(out=st6[:, 6:12], in_=h[:, 512:1024]), [t_h], [t_st])
    S.add("dve", lambda e: e.bn_aggr(out=mv[:, 0:2], in_=st6[:, 0:12]), [t_st], [t_mv])
    S.add("act", lambda e: e.activation(out=mv[:, 2:3], in_=mv[:, 1:2], func=AF.Ln, bias=eps_t[:], scale=1.0), [t_mv, t_eps], [t_mv])
    S.add("act", lambda e: e.activation(out=mv[:, 3:4], in_=mv[:, 2:3], func=AF.Exp, scale=-0.5), [t_mv], [t_mv])
    S.add("dve", lambda e: e.tensor_scalar(out=h[:], in0=h[:], scalar1=mv[:, 0:1], scalar2=mv[:, 3:4],
                                           op0=ALU.subtract, op1=ALU.mult), [t_h, t_mv], [t_h])
    S.add("dve", lambda e: e.tensor_tensor(out=h[:], in0=h[:], in1=g_bc, op=ALU.mult), [t_h, t_gb], [t_h])
    S.add("dve", lambda e: e.tensor_tensor(out=out_sb, in0=h[:], in1=b_bc, op=ALU.add), [t_h, t_gb], [t_out])


def host_inputs(T, x, attn_w_in, attn_sink, attn_rpb, attn_w_out, ln1_g, ln1_b, ln2_g, ln2_b, mlp_w1, mlp_w2,
                ssm_w_in, ssm_conv_w, ssm_conv_b, ssm_dt_bias, ssm_A_log, ssm_D, ssm_norm_w, ssm_w_out, **_):
    NT = T // 128
    cosT, sinS = _rope_tables(T)
    tabs = _na_bias_tables(np.asarray(attn_rpb[0]), NT)
    nbt = np.stack([tabs[k].reshape(128, -1) for k in ("int", "t0", "t1", "tm2", "tm1")], 0)
    kq = np.arange(128)
    maskA = np.concatenate([(kq[:, None] >= kq[None, :]), (kq[:, None] <= kq[None, :])], 1).astype(np.float32)
    common = {
        "awin": _attn_w_layout(np.asarray(attn_w_in[0])),
        "awout": np.ascontiguousarray(attn_w_out[0]),
        "sink": np.ascontiguousarray(attn_sink[0:1]),
        "nbt": np.ascontiguousarray(nbt),
        "cosT": cosT, "sinS": sinS, "maskA": np.ascontiguousarray(maskA),
        "lng": np.ascontiguousarray(np.stack([ln1_g[0], ln2_g[0], ln1_g[1], ln2_g[1]])),
        "lnb": np.ascontiguousarray(np.stack([ln1_b[0], ln2_b[0], ln1_b[1], ln2_b[1]])),
        "ident": np.eye(128, dtype=np.float32),
        "swin": np.ascontiguousarray(ssm_w_in[0]),
        "cw": np.ascontiguousarray(np.asarray(ssm_conv_w[0]).reshape(5, 32, 128).transpose(2, 0, 1).reshape(128, 160)),
        "cb": np.ascontiguousarray(np.asarray(ssm_conv_b[0]).reshape(32, 128).T),
        "dtb": np.ascontiguousarray(np.asarray(ssm_dt_bias[0]).reshape(1, 64)),
        "alog": np.ascontiguousarray(np.asarray(ssm_A_log[0]).reshape(1, 64)),
        "dsk": np.ascontiguousarray(np.asarray(ssm_D[0]).reshape(1, 32)),
        "normwT": np.ascontiguousarray(np.asarray(ssm_norm_w[0]).reshape(16, 128).T),
        "swout": np.ascontiguousarray(ssm_w_out[0]),
        "tri": np.ascontiguousarray(np.concatenate([kq[:, None] <= kq[None, :], kq[:, None] > kq[None, :],
                                                    kq[:, None] >= kq[None, :], kq[:, None] < kq[None, :]], 1).astype(np.float32)),
        "w1_0": np.ascontiguousarray(mlp_w1[0]), "w1_1": np.ascontiguousarray(mlp_w1[1]),
        "w2_0": np.ascontiguousarray(mlp_w2[0]), "w2_1": np.ascontiguousarray(mlp_w2[1]),
    }
    return common


def kernel(**inputs):
    x = np.asarray(inputs["x"])
    B, T, _ = x.shape
    common = host_inputs(T, **inputs)
    nc = build(T)
    in_maps = []
    for c in range(8):
        m = dict(common)
        m["x"] = np.ascontiguousarray(x[c % B])
        in_maps.append(m)
    res = run_bass_kernel_spmd(nc, in_maps, core_ids=list(range(8)))
    out = np.stack([res.results[b]["out"] for b in range(B)], 0)
    return out.astype(np.float32)
```

```python
import contextlib
import numpy as np
import concourse.bass as bass
import concourse.mybir as mybir
from concourse.bass_utils import run_bass_kernel_spmd

F32 = mybir.dt.float32
BF16 = mybir.dt.bfloat16
ALU = mybir.AluOpType
AF = mybir.ActivationFunctionType
AX = mybir.AxisListType

D = 1024
ALPHA = 4.0 ** 0.25
NEGB = -30000.0


class T:
    __slots__ = ("name", "w", "r")

    def __init__(self, name=""):
        self.name = name
        self.w = None
        self.r = []


class Op:
    __slots__ = ("eng", "emit", "deps", "sig", "dma", "cnt", "dsem", "dval")

    def __init__(self, eng, emit, dma):
        self.eng = eng
        self.emit = emit
        self.deps = []
        self.sig = False
        self.dma = dma
        self.cnt = 0
        self.dsem = -1
        self.dval = 0


class Sched:
    ENGS = ("pe", "act", "dve", "pool", "sp")
    ENGOBJ = {"pe": "tensor", "act": "scalar", "dve": "vector", "pool": "gpsimd", "sp": "sync"}

    def __init__(self, nc, st, n_dma_sems=14):
        self.nc = nc
        self.ops = {e: [] for e in self.ENGS}
        self.n_dma_sems = n_dma_sems
        self.csem = {e: st.enter_context(nc.semaphore("c_" + e)) for e in self.ENGS}
        self.dsems = {e: [st.enter_context(nc.semaphore("d_%s%d" % (e, i))) for i in range(n_dma_sems)]
                      for e in ("sp", "pool")}
        self.phsem = st.enter_context(nc.semaphore("phase"))
        self.ccount = {e: 0 for e in self.ENGS}
        self.dcount = {e: 0 for e in ("sp", "pool")}
        self.duses = {e: [0] * n_dma_sems for e in ("sp", "pool")}
        self.nflush = 0

    def add(self, eng, emit, reads=(), writes=(), dma=False):
        op = Op(eng, emit, dma)
        deps = op.deps
        for t in reads:
            w = t.w
            if w is not None and (w.eng != eng or w.dma or eng != "pe"):
                deps.append(w)
        for t in writes:
            w = t.w
            if w is not None and (w.eng != eng or w.dma or dma):
                deps.append(w)
            for r in t.r:
                if r.eng != eng or r.dma or dma:
                    deps.append(r)
        for t in reads:
            t.r.append(op)
        for t in writes:
            t.w = op
            t.r = []
        for d in deps:
            d.sig = True
        self.ops[eng].append(op)
        return op

    def dma(self, eng, out, in_, reads=(), writes=()):
        return self.add(eng, lambda e: e.dma_start(out=out, in_=in_), reads, writes, dma=True)

    def flush(self):
        nc = self.nc
        import os
        if os.environ.get("KDBG"):
            print("flush", self.nflush, "sbuf remaining", nc.sbuf_bytes_remaining, {e: len(v) for e, v in self.ops.items()})
        csem, dsems = self.csem, self.dsems
        last_cnt = {}
        for e in self.ENGS:
            comp = [op for op in self.ops[e] if not op.dma]
            if comp:
                comp[-1].sig = True
            for op in self.ops[e]:
                if op.dma:
                    k = self.dcount[e]
                    self.dcount[e] += 1
                    op.dsem = k % self.n_dma_sems
                    self.duses[e][op.dsem] += 1
                    op.dval = 16 * self.duses[e][op.dsem]
                elif op.sig:
                    self.ccount[e] += 1
                    op.cnt = self.ccount[e]
            last_cnt[e] = self.ccount[e]
        nfl = self.nflush
        ops_all = self.ops

        def run(ename, eobj):
            if nfl > 0:
                eobj.wait_ge(self.phsem, 5 * nfl)
            known = {}
            last_dma = {}
            for op in ops_all[ename]:
                need = {}
                for d in op.deps:
                    if d.dma:
                        key = ("d", d.eng, d.dsem)
                        val = d.dval
                    else:
                        key = ("c", d.eng)
                        val = d.cnt
                    if need.get(key, 0) < val:
                        need[key] = val
                if op.dma:
                    key = ("d", ename, op.dsem)
                    val = op.dval - 16
                    if val > 0 and need.get(key, 0) < val:
                        need[key] = val
                for key, val in need.items():
                    if val <= 0 or known.get(key, 0) >= val:
                        continue
                    known[key] = val
                    sem = csem[key[1]] if key[0] == "c" else dsems[key[1]][key[2]]
                    eobj.wait_ge(sem, val)
                ins = op.emit(eobj)
                if op.dma:
                    ins.then_inc(dsems[ename][op.dsem], 16)
                    last_dma[op.dsem] = op.dval
                elif op.sig:
                    ins.then_inc(csem[ename], 1)
            for k, v in last_dma.items():
                if known.get(("d", ename, k), 0) < v:
                    eobj.wait_ge(dsems[ename][k], v)
            if last_cnt[ename] > 0 and known.get(("c", ename), 0) < last_cnt[ename]:
                eobj.wait_ge(csem[ename], last_cnt[ename])
            eobj.sem_inc(self.phsem, 1)

        with nc.Block() as block:
            for ename in self.ENGS:
                getattr(block, self.ENGOBJ[ename])(lambda eobj, ename=ename: run(ename, eobj))
        self.ops = {e: [] for e in self.ENGS}
        self.nflush += 1


class Ctx:
    _inst = [0]

    def __init__(self, nc, st):
        self.nc = nc
        self.st = st
        self.n = 0
        Ctx._inst[0] += 1
        self.pfx = "c%d_" % Ctx._inst[0]

    def sb(self, shape, dt, name=None):
        self.n += 1
        return self.st.enter_context(self.nc.sbuf_tensor(self.pfx + "s_" + (name or "sb%d" % self.n), list(shape), dt))

    def ps(self, shape, dt, name=None):
        self.n += 1
        return self.st.enter_context(self.nc.psum_tensor(self.pfx + "p_" + (name or "ps%d" % self.n), list(shape), dt))


def _na_types(NT):
    return {"int": [-2, -1, 0, 1, 2], "t0": [0, 1, 2, 3, 3], "t1": [-1, 0, 1, 2, 2],
            "tm2": [-2, -1, 0, 1, 1], "tm1": [-3, -2, -1, 0, 0]}


def _na_type_of(t, NT):
    if t == 0:
        return "t0"
    if t == 1:
        return "t1"
    if t == NT - 2:
        return "tm2"
    if t == NT - 1:
        return "tm1"
    return "int"


def _na_bias_tables(rpb, NT):
    rows = 2 * NT
    types = _na_types(NT)
    rep_t = {"int": min(2, NT - 1), "t0": 0, "t1": 1, "tm2": NT - 2, "tm1": NT - 1}
    out = {}
    q = np.arange(128)
    k = np.arange(128)
    for name, offs in types.items():
        t = rep_t[name]
        tab = np.full((128, 8, 5, 128), NEGB, np.float32)
        r = 2 * t + q // 64
        c = q % 64
        rs = np.clip(r - 4, 0, rows - 8)
        cs = np.clip(c - 8, 0, 64 - 16)
        seen = set()
        for j, off in enumerate(offs):
            if off in seen:
                continue
            seen.add(off)
            kt = t + off
            if kt < 0 or kt >= NT:
                continue
            kr = 2 * kt + k // 64
            kc = k % 64
            valid = ((kr[:, None] >= rs[None, :]) & (kr[:, None] < rs[None, :] + 8) &
                     (kc[:, None] >= cs[None, :]) & (kc[:, None] < cs[None, :] + 16))
            dr = np.clip(kr[:, None] - r[None, :] + 7, 0, 14)
            dc = np.clip(kc[:, None] - c[None, :] + 15, 0, 30)
            g = rpb[:, dr, dc]
            g = np.where(valid[None], g, NEGB)
            tab[:, :, j, :] = g.transpose(1, 0, 2)
        out[name] = tab
    return out


def _rope_tables(T):
    half = 32
    inv = (10000.0 ** (-np.arange(half, dtype=np.float32) / half)).astype(np.float32)
    pos = np.arange(T, dtype=np.float32)
    ang = pos[None, :] * inv[:, None]
    cos = np.cos(ang).astype(np.float32)
    sin = np.sin(ang).astype(np.float32)
    cosT = np.concatenate([cos, cos, cos, cos], 0)
    sinS = np.concatenate([-sin, sin, -sin, sin], 0)
    return np.ascontiguousarray(cosT), np.ascontiguousarray(sinS)


def _attn_w_layout(w_in):
    qa = w_in[:, 0:512].reshape(1024, 8, 64)
    ka = w_in[:, 512:640].reshape(1024, 2, 64)
    va = w_in[:, 640:768]
    qb = w_in[:, 768:1280]
    kb = w_in[:, 1280:1792]
    vb = w_in[:, 1792:2304]
    order = [0, 4, 1, 5, 2, 6, 3, 7]
    qa2 = qa[:, order, :]
    sw = lambda a: np.concatenate([a[..., 32:], a[..., :32]], -1)
    cols = [qa2.reshape(1024, 512), sw(qa2).reshape(1024, 512), ka.reshape(1024, 128),
            sw(ka).reshape(1024, 128), qb, kb, va, vb]
    return np.ascontiguousarray(np.concatenate(cols, 1))


def build(Tlen, upto="all"):
    NT = Tlen // 128
    nc = bass.Bass("TRN2", target_bir_lowering=False)

    def din(name, shape):
        return nc.dram_tensor(name, list(shape), F32, kind="ExternalInput").ap()

    x_d = din("x", [Tlen, D])
    awin_d = din("awin", [D, 2944])
    awout_d = din("awout", [D, D])
    sink_d = din("sink", [1, 8])
    nbt_d = din("nbt", [5, 128, 8 * 5 * 128])
    cos_d = din("cosT", [128, Tlen])
    sin_d = din("sinS", [128, Tlen])
    ma_d = din("maskA", [128, 2 * 128])
    lng_d = din("lng", [4, D])
    lnb_d = din("lnb", [4, D])
    ident_d = din("ident", [128, 128])
    w1_d = [din("w1_%d" % i, [D, 4096]) for i in range(2)]
    w2_d = [din("w2_%d" % i, [4096, D]) for i in range(2)]
    swin_d = din("swin", [D, 6208])
    cw_d = din("cw", [128, 160])
    cb_d = din("cb", [128, 32])
    dtb_d = din("dtb", [1, 64])
    alog_d = din("alog", [1, 64])
    dsk_d = din("dsk", [1, 32])
    normw_d = din("normwT", [128, 16])
    swout_d = din("swout", [2048, D])
    tri_d = din("tri", [128, 4 * 128])
    stages = ["a", "m0", "c1", "c2", "d", "m1"]
    if upto != "all":
        stages = stages[:stages.index(upto) + 1]
    out_d = nc.dram_tensor("out", [Tlen, D], F32, kind="ExternalOutput").ap()

    def scratch(name, shape, dt=F32):
        return nc.dram_tensor(name, list(shape), dt, kind="Internal").ap()

    def act_out(stage, name):
        return out_d if stages[-1] == stage else scratch(name, [Tlen, D])

    x1_d = act_out("a", "x1")
    x2_d = act_out("m0", "x2")
    x3_d = act_out("d", "x3")
    xtok_d = scratch("xtok", [Tlen, 2048], BF16)
    btok_d = scratch("btok", [Tlen, 1024], BF16)
    bT_d = scratch("bT", [1024, Tlen], BF16)
    cT_d = scratch("cT", [1024, Tlen], BF16)
    zs_d = scratch("zs", [Tlen, 2048], BF16)
    dt_d = scratch("dtv", [Tlen, 64], F32)
    yf_d = scratch("yf", [Tlen, 2048], F32)
    acf_d = scratch("acf", [Tlen, 32], F32)
    acb_d = scratch("acb", [Tlen, 32], F32)
    acfT_d = scratch("acfT", [NT, 32 * 128], F32)
    acbT_d = scratch("acbT", [NT, 32 * 128], F32)
    SS = dict(xtok_d=xtok_d, btok_d=btok_d, bT_d=bT_d, cT_d=cT_d, zs_d=zs_d, dt_d=dt_d, yf_d=yf_d, tri_d=tri_d,
              alog_d=alog_d, dsk_d=dsk_d, acf_d=acf_d, acb_d=acb_d, acfT_d=acfT_d, acbT_d=acbT_d)

    with contextlib.ExitStack() as st0:
        S = Sched(nc, st0)
        C0 = Ctx(nc, st0)
        ident = C0.sb([128, 128], BF16, "ident")
        t_ident = T("ident")
        S.dma("pool", ident[:], ident_d, writes=[t_ident])
        eps_t = C0.sb([128, 1], F32, "eps")
        t_eps = T("eps")
        S.add("pool", lambda e: e.memset(eps_t[:], 1e-5), [], [t_eps])
        K = dict(ident=ident, t_ident=t_ident, eps_t=eps_t, t_eps=t_eps, lng_d=lng_d, lnb_d=lnb_d)

        with contextlib.ExitStack() as st:
            phase_a(nc, S, Ctx(nc, st), T_=Tlen, x_d=x_d, awin_d=awin_d, awout_d=awout_d, sink_d=sink_d, nbt_d=nbt_d,
                    cos_d=cos_d, sin_d=sin_d, ma_d=ma_d, x1_d=x1_d, **K)
            S.flush()
        if "m0" in stages:
            with contextlib.ExitStack() as st:
                phase_mlp(nc, S, Ctx(nc, st), T_=Tlen, xin_d=x1_d, xout_d=x2_d, w1_d=w1_d[0], w2_d=w2_d[0], ln_row=1, **K)
                S.flush()
        if "c1" in stages:
            with contextlib.ExitStack() as st:
                phase_c1(nc, S, Ctx(nc, st), T_=Tlen, xin_d=x2_d, swin_d=swin_d, cw_d=cw_d, cb_d=cb_d, dtb_d=dtb_d, **SS, **K)
                S.flush()
        if "c2" in stages:
            with contextlib.ExitStack() as st:
                phase_ssd(nc, S, Ctx(nc, st), T_=Tlen, bwd=False, **SS, **K)
                S.flush()
        if "d" in stages:
            with contextlib.ExitStack() as st:
                phase_ssd(nc, S, Ctx(nc, st), T_=Tlen, bwd=True, **SS, **K)
                S.flush()
            with contextlib.ExitStack() as st:
                phase_e(nc, S, Ctx(nc, st), T_=Tlen, yt_d=yf_d, zs_d=zs_d, normw_d=normw_d, swout_d=swout_d, xres_d=x2_d, xout_d=x3_d, ln_row=2, **K)
                S.flush()
        if "m1" in stages:
            with contextlib.ExitStack() as st:
                phase_mlp(nc, S, Ctx(nc, st), T_=Tlen, xin_d=x3_d, xout_d=out_d, w1_d=w1_d[1], w2_d=w2_d[1], ln_row=3, **K)
                S.flush()
    return nc


def phase_a(nc, S, C, T_, x_d, awin_d, awout_d, sink_d, nbt_d, cos_d, sin_d, ma_d, lng_d, lnb_d, ident, t_ident,
            eps_t, t_eps, x1_d):
    T_tok = T_
    NT = T_tok // 128
    NG = NT // 4
    SCALE = 0.125
    w_in = C.sb([128, 8, 2944], BF16, "a_win")
    t_win = T("a_win")
    wv = awin_d.rearrange("(kt p) n -> p kt n", p=128)
    for kt in range(8):
        S.dma("pool", w_in[:, kt, :], wv[:, kt, :], writes=[t_win])
    w_out = C.sb([128, 8, D], BF16, "a_wout")
    t_wout = T("a_wout")
    S.dma("pool", w_out[:], awout_d.rearrange("(kt p) n -> p kt n", p=128), writes=[t_wout])
    g_bc = C.sb([128, D], F32, "a_g")
    b_bc = C.sb([128, D], F32, "a_b")
    t_gb = T("a_gb")
    S.dma("sp", g_bc[:], lng_d[0:1, :].partition_broadcast(128), writes=[t_gb])
    S.dma("sp", b_bc[:], lnb_d[0:1, :].partition_broadcast(128), writes=[t_gb])
    maskA = C.sb([128, 2, 128], BF16, "maskA")
    t_maskA = T("maskA")
    S.dma("pool", maskA[:], ma_d.rearrange("p (a q) -> p a q", a=2), writes=[t_maskA])
    esink = C.sb([128, 8], F32, "esink")
    t_esink = T("esink")
    S.dma("sp", esink[:], sink_d.partition_broadcast(128), writes=[t_esink])
    S.add("act", lambda e: e.activation(out=esink[:], in_=esink[:], func=AF.Exp), [t_esink], [t_esink])

    EB = C.sb([128, 8, 5, 128], BF16, "EB")
    t_EB = T("EB")
    stage = [C.sb([128, 640], F32, "nbstage%d" % i) for i in range(2)]
    t_stage = [T("nbstage%d" % i) for i in range(2)]
    type_idx = {"int": 0, "t0": 1, "t1": 2, "tm2": 3, "tm1": 4}
    cur_type = [None]

    def load_type(name):
        if cur_type[0] == name:
            return
        cur_type[0] = name
        ti = type_idx[name]
        for hh in range(8):
            sl = hh % 2
            S.dma("sp", stage[sl][:], nbt_d[ti, :, hh * 640:(hh + 1) * 640], writes=[t_stage[sl]])
            S.add("act", lambda e, hh=hh, sl=sl: e.activation(out=EB[:, hh, :, :].rearrange("p j q -> p (j q)"), in_=stage[sl][:],
                                                                func=AF.Exp), [t_stage[sl]], [t_EB])

    xf = [C.sb([128, D], F32, "xf%d" % i) for i in range(2)]
    t_xf = [T("xf%d" % i) for i in range(2)]
    xb = [C.sb([128, D], BF16, "xb%d" % i) for i in range(2)]
    t_xb = [T("xb%d" % i) for i in range(2)]
    xT = [C.sb([128, 8, 512], BF16, "xT%d" % i) for i in range(2)]
    t_xT = [[T("xT%d_%d" % (i, j)) for j in range(4)] for i in range(2)]
    qaT = [C.sb([128, 4, 512], BF16, "qaT%d" % i) for i in range(2)]
    t_qaT = [[T("qaT%d_%d" % (i, c)) for c in range(4)] for i in range(2)]
    qbT = [C.sb([128, 4, 512], BF16, "qbT%d" % i) for i in range(2)]
    t_qbT = [[T("qbT%d_%d" % (i, c)) for c in range(4)] for i in range(2)]
    kaT = C.sb([128, 12 * 128], BF16, "kaT")
    kbT = C.sb([128, 4, 12 * 128], BF16, "kbT")
    t_kaT = [T("kaT%d" % i) for i in range(3)]
    t_kbT = [[T("kbT%d_%d" % (i, c)) for c in range(4)] for i in range(3)]
    va = C.sb([128, 12, 2, 65], BF16, "va")
    vb = C.sb([128, 12, 8, 65], BF16, "vb")
    t_v = [T("v%d" % i) for i in range(12)]
    S.add("pool", lambda e: e.memset(va[:], 1.0), [], t_v)
    S.add("pool", lambda e: e.memset(vb[:], 1.0), [], t_v)
    cs_t = C.sb([128, 2, 512], F32, "cossin")
    t_cs = T("cossin")
    r1 = C.sb([128, 512], F32, "rope1")
    r2 = C.sb([128, 512], F32, "rope2")
    t_r1, t_r2 = T("r1"), T("r2")
    esA = [C.sb([128, 3, 512], BF16, "esA%d" % i) for i in range(2)]
    t_esA = [[T("esA%d_%d" % (i, j)) for j in range(3)] for i in range(2)]
    esB = [C.sb([128, 5, 128], BF16, "esB%d" % i) for i in range(2)]
    t_esB = [T("esB%d" % i) for i in range(2)]
    o_toks = [C.sb([128, D], BF16, "o_tok%d" % i) for i in range(2)]
    t_otoks = [T("o_tok%d" % i) for i in range(2)]
    den = C.sb([128, 16], F32, "den")
    t_den = T("den")
    oT = C.sb([128, 8, 128], BF16, "oT")
    t_oT = T("oT")
    xres = [C.sb([128, D], F32, "xres%d" % i) for i in range(2)]
    t_xres = [T("xres%d" % i) for i in range(2)]
    lnb = {"h": (C.sb([128, D], F32, "ln_h"), T("ln_h")), "st": (C.sb([128, 12], F32, "ln_st"), T("ln_st")),
           "mv": (C.sb([128, 4], F32, "ln_mv"), T("ln_mv"))}
    xo = [C.sb([128, D], F32, "xo%d" % i) for i in range(2)]
    t_xo = [T("xo%d" % i) for i in range(2)]

    pP = [C.ps([128, 512], F32, "pP%d" % i) for i in range(2)]
    t_pP = [T("pP%d" % i) for i in range(2)]
    pS = [C.ps([128, 512], F32, "pS%d" % i) for i in range(4)]
    t_pS = [T("pS%d" % i) for i in range(4)]
    pO = [C.ps([128, 512], F32, "pO%d" % i) for i in range(2)]
    t_pO = [T("pO%d" % i) for i in range(2)]
    pcount = [0]

    def next_pP():
        i = pcount[0] % 2
        pcount[0] += 1
        return pP[i], t_pP[i]

    COL = {"qa": 0, "qas": 512, "ka": 1024, "kas": 1152, "qb": 1280, "kb": 1792, "va": 2304, "vb": 2432}

    def project_group(g):
        gb = g % 2
        rg = g % 3
        for i in range(4):
            t = 4 * g + i
            s = t % 2
            S.dma("sp", xf[s][:], x_d[t * 128:(t + 1) * 128, :], writes=[t_xf[s]])
            S.add("dve", lambda e, s=s: e.tensor_copy(out=xb[s][:], in_=xf[s][:]), [t_xf[s]], [t_xb[s]])
            pp, t_pp = next_pP()
            ppb = pp[:].bitcast(BF16)
            for kt in range(8):
                S.add("pe", lambda e, s=s, kt=kt, ppb=ppb: e.transpose(ppb[:, kt * 128:(kt + 1) * 128], xb[s][:, kt * 128:(kt + 1) * 128], ident[:]),
                      [t_xb[s], t_ident], [t_pp])
            S.add("act", lambda e, i=i, gb=gb, ppb=ppb: e.copy(out=xT[gb][:, :, i * 128:(i + 1) * 128],
                                                               in_=ppb.rearrange("p (k t) -> p k t", k=8)),
                  [t_pp], [t_xT[gb][i]])
        S.dma("sp", cs_t[:, 0, :], cos_d[:, g * 512:(g + 1) * 512], writes=[t_cs])
        S.dma("sp", cs_t[:, 1, :], sin_d[:, g * 512:(g + 1) * 512], writes=[t_cs])

        def fm_proj(col):
            pp, t_pp = next_pP()
            for kt in range(8):
                S.add("pe", lambda e, kt=kt, col=col, pp=pp: e.matmul(pp[:], lhsT=w_in[:, kt, col:col + 128], rhs=xT[gb][:, kt, :],
                                                                     start=(kt == 0), stop=(kt == 7)),
                      [t_win] + t_xT[gb], [t_pp])
            return pp, t_pp

        def rope_half(col, which, rdst, t_rdst):
            pp, t_pp = fm_proj(col)
            S.add("dve", lambda e, pp=pp: e.tensor_tensor(out=rdst[:], in0=pp[:], in1=cs_t[:, which, :], op=ALU.mult), [t_pp, t_cs], [t_rdst])

        def rope_add(dst, t_dst):
            S.add("pool", lambda e: e.tensor_tensor(out=dst, in0=r1[:], in1=r2[:], op=ALU.add), [t_r1, t_r2], [t_dst])

        rope_half(COL["ka"], 0, r1, t_r1)
        rope_half(COL["kas"], 1, r2, t_r2)
        rope_add(kaT[:, rg * 512:(rg + 1) * 512], t_kaT[rg])
        for c in range(4):
            pp, t_pp = fm_proj(COL["kb"] + c * 128)
            S.add("act", lambda e, pp=pp, c=c: e.copy(out=kbT[:, c, rg * 512:(rg + 1) * 512], in_=pp[:]), [t_pp], [t_kbT[rg][c]])
        for i in range(4):
            t = 4 * g + i
            slot = t % 12
            pp, t_pp = next_pP()
            for kt in range(8):
                S.add("pe", lambda e, kt=kt, pp=pp, i=i: e.matmul(pp[:], lhsT=xT[gb][:, kt, i * 128:(i + 1) * 128],
                                                                 rhs=w_in[:, kt, COL["vb"]:COL["vb"] + 512],
                                                                 start=(kt == 0), stop=(kt == 7)),
                      [t_win, t_xT[gb][i]], [t_pp])
            S.add("act", lambda e, pp=pp, slot=slot: e.copy(out=vb[:, slot, :, 0:64], in_=pp[:].rearrange("p (h d) -> p h d", h=8)),
                  [t_pp], [t_v[slot]])
            pp, t_pp = next_pP()
            for kt in range(8):
                S.add("pe", lambda e, kt=kt, pp=pp, i=i: e.matmul(pp[:, 0:128], lhsT=xT[gb][:, kt, i * 128:(i + 1) * 128],
                                                                 rhs=w_in[:, kt, COL["va"]:COL["va"] + 128],
                                                                 start=(kt == 0), stop=(kt == 7)),
                      [t_win, t_xT[gb][i]], [t_pp])
            S.add("act", lambda e, pp=pp, slot=slot: e.copy(out=va[:, slot, :, 0:64], in_=pp[:, 0:128].rearrange("p (h d) -> p h d", h=2)),
                  [t_pp], [t_v[slot]])
        yield "K"
        for c in range(4):
            rope_half(COL["qa"] + c * 128, 0, r1, t_r1)
            yield None
            rope_half(COL["qas"] + c * 128, 1, r2, t_r2)
            rope_add(qaT[gb][:, c, :], t_qaT[gb][c])
            yield None
        for c in range(4):
            pp, t_pp = fm_proj(COL["qb"] + c * 128)
            S.add("act", lambda e, pp=pp, c=c: e.copy(out=qbT[gb][:, c, :], in_=pp[:]), [t_pp], [t_qbT[gb][c]])
            yield None

    es_cnt = [0, 0]
    ps_rr = [0]

    def attend_tile(t, pump):
        g = t // 4
        gb = g % 2
        i = t % 4
        qsl = slice(i * 128, (i + 1) * 128)
        js = [j for j in (-1, 0, 1) if 0 <= t + j < NT]
        o_tok, t_otok = o_toks[t % 2], t_otoks[t % 2]
        if t >= 1:
            tail_1(t - 1)

        ebs = []
        for gi in range(2):
            eb = es_cnt[0] % 2
            es_cnt[0] += 1
            ebs.append(eb)
            for j in js:
                tk = t + j
                rgk = (tk // 4) % 3
                kof = (tk % 12) * 128
                bi = ps_rr[0] % 4
                ps_rr[0] += 1
                ps, t_ps = pS[bi], t_pS[bi]
                S.add("pe", lambda e, ps=ps, kof=kof, gi=gi: e.matmul(ps[:].rearrange("p (c q) -> p c q", c=4),
                                                                      lhsT=kaT[gi * 64:(gi + 1) * 64, kof:kof + 128],
                                                                      rhs=qaT[gb][gi * 64:(gi + 1) * 64, :, qsl], start=True, stop=True),
                      [t_kaT[rgk]] + t_qaT[gb], [t_ps])
                S.add("act", lambda e, ps=ps, eb=eb, j=j: e.activation(out=esA[eb][:, j + 1, :], in_=ps[:], func=AF.Exp, scale=SCALE),
                      [t_ps], [t_esA[eb][j + 1]])
                if j != 0:
                    mi = 0 if j == -1 else 1
                    S.add("dve", lambda e, eb=eb, j=j, mi=mi: e.tensor_tensor(out=esA[eb][:, j + 1, :].rearrange("p (c q) -> p c q", c=4),
                                                                              in0=esA[eb][:, j + 1, :].rearrange("p (c q) -> p c q", c=4),
                                                                              in1=maskA[:, mi:mi + 1, :].to_broadcast([128, 4, 128]), op=ALU.mult),
                          [t_esA[eb][j + 1], t_maskA], [t_esA[eb][j + 1]])
            pump()
        if t >= 1:
            tail_2(t - 1)
        for gi in range(2):
            eb = ebs[gi]
            for c in range(4):
                for n, j in enumerate(js):
                    slot = (t + j) % 12
                    S.add("pe", lambda e, c=c, j=j, slot=slot, gi=gi, eb=eb, n=n, nj=len(js): e.matmul(pO[gi][:, c * 65:(c + 1) * 65],
                                                                                                    lhsT=esA[eb][:, j + 1, c * 128:(c + 1) * 128],
                                                                                                    rhs=va[:, slot, gi, :], start=(n == 0), stop=(n == nj - 1)),
                          [t_esA[eb][j + 1], t_v[slot]], [t_pO[gi]])
        for gi in range(2):
            ov = pO[gi][:, 0:260].rearrange("p (c e) -> p c e", c=4)
            S.add("dve", lambda e, gi=gi, ov=ov: e.tensor_tensor(out=den[:, gi * 4:(gi + 1) * 4], in0=ov[:, :, 64], in1=esink[:, gi * 4:(gi + 1) * 4], op=ALU.add),
                  [t_pO[gi], t_esink], [t_den])
            S.add("dve", lambda e, gi=gi: e.reciprocal(out=den[:, gi * 4:(gi + 1) * 4], in_=den[:, gi * 4:(gi + 1) * 4]), [t_den], [t_den])
            S.add("dve", lambda e, gi=gi, ov=ov: e.tensor_tensor(out=o_tok[:, gi * 256:(gi + 1) * 256].rearrange("p (c d) -> p c d", c=4), in0=ov[:, :, 0:64],
                                                                 in1=den[:, gi * 4:(gi + 1) * 4].unsqueeze(2).to_broadcast([128, 4, 64]), op=ALU.mult),
                  [t_pO[gi], t_den], [t_otok])

        typ = _na_type_of(t, NT)
        load_type(typ)
        offs = _na_types(NT)[typ]

        def b_scores(hh):
            c, pb = hh // 2, (hh % 2) * 64
            sb_i = es_cnt[1] % 2
            es_cnt[1] += 1
            pa, t_pa = pS[2 * sb_i], t_pS[2 * sb_i]
            pb2, t_pb2 = pS[2 * sb_i + 1], t_pS[2 * sb_i + 1]
            for j, off in enumerate(offs):
                tk = t + off
                rgk = (tk // 4) % 3
                kof = (tk % 12) * 128
                dst, t_dst = (pa[:, j * 128:(j + 1) * 128], t_pa) if j < 4 else (pb2[:, 0:128], t_pb2)
                S.add("pe", lambda e, dst=dst, kof=kof, c=c, pb=pb: e.matmul(dst, lhsT=kbT[pb:pb + 64, c, kof:kof + 128],
                                                                             rhs=qbT[gb][pb:pb + 64, c, qsl], start=True, stop=True),
                      [t_kbT[rgk][c], t_qbT[gb][c]], [t_dst])
            S.add("act", lambda e, pa=pa, sb_i=sb_i: e.activation(out=esB[sb_i][:, 0:4, :].rearrange("p j q -> p (j q)"), in_=pa[:], func=AF.Exp, scale=SCALE),
                  [t_pa], [t_esB[sb_i]])
            S.add("act", lambda e, pb2=pb2, sb_i=sb_i: e.activation(out=esB[sb_i][:, 4, :], in_=pb2[:, 0:128], func=AF.Exp, scale=SCALE),
                  [t_pb2], [t_esB[sb_i]])
            S.add("dve", lambda e, sb_i=sb_i, hh=hh: e.tensor_tensor(out=esB[sb_i][:], in0=esB[sb_i][:], in1=EB[:, hh, :, :], op=ALU.mult),
                  [t_esB[sb_i], t_EB], [t_esB[sb_i]])
            return sb_i

        def b_pv(hh, sb_i):
            ob = hh // 4
            for j, off in enumerate(offs):
                slot = (t + off) % 12
                S.add("pe", lambda e, j=j, slot=slot, hh=hh, sb_i=sb_i, ob=ob: e.matmul(pO[ob][:, (hh % 4) * 65:(hh % 4 + 1) * 65],
                                                                                       lhsT=esB[sb_i][:, j, :], rhs=vb[:, slot, hh, :],
                                                                                       start=(j == 0), stop=(j == 4)),
                      [t_esB[sb_i], t_v[slot]], [t_pO[ob]])

        prev = None
        for hh in range(8):
            sb_i = b_scores(hh)
            pump()
            if prev is not None:
                b_pv(*prev)
            prev = (hh, sb_i)
        b_pv(*prev)
        for ob in range(2):
            ov = pO[ob][:, 0:260].rearrange("p (c e) -> p c e", c=4)
            S.add("dve", lambda e, ob=ob, ov=ov: e.reciprocal(out=den[:, 8 + ob * 4:8 + (ob + 1) * 4], in_=ov[:, :, 64]), [t_pO[ob]], [t_den])
            S.add("dve", lambda e, ob=ob, ov=ov: e.tensor_tensor(out=o_tok[:, 512 + ob * 256:512 + (ob + 1) * 256].rearrange("p (c d) -> p c d", c=4), in0=ov[:, :, 0:64],
                                                                 in1=den[:, 8 + ob * 4:8 + (ob + 1) * 4].unsqueeze(2).to_broadcast([128, 4, 64]), op=ALU.mult),
                  [t_pO[ob], t_den], [t_otok])

    def tail_1(t):
        o_tok, t_otok = o_toks[t % 2], t_otoks[t % 2]
        pp, t_pp = next_pP()
        ppb = pp[:].bitcast(BF16)
        for kt in range(8):
            S.add("pe", lambda e, kt=kt, ppb=ppb: e.transpose(ppb[:, kt * 128:(kt + 1) * 128], o_tok[:, kt * 128:(kt + 1) * 128], ident[:]),
                  [t_otok, t_ident], [t_pp])
        S.add("act", lambda e, ppb=ppb: e.copy(out=oT[:], in_=ppb.rearrange("p (k t) -> p k t", k=8)), [t_pp], [t_oT])
        xr = t % 2
        S.dma("sp", xres[xr][:], x_d[t * 128:(t + 1) * 128, :], writes=[t_xres[xr]])

    def tail_2(t):
        xr = t % 2
        halves = []
        for hf in range(2):
            pp, t_pp = next_pP()
            for kt in range(8):
                S.add("pe", lambda e, kt=kt, pp=pp, hf=hf: e.matmul(pp[:], lhsT=oT[:, kt, :], rhs=w_out[:, kt, hf * 512:(hf + 1) * 512],
                                                                   start=(kt == 0), stop=(kt == 7)),
                      [t_oT, t_wout], [t_pp])
            halves.append((pp, t_pp))
        h, t_h = lnb["h"]
        for hf, (pp, t_pp) in enumerate(halves):
            S.add("dve", lambda e, pp=pp, hf=hf, xr=xr: e.scalar_tensor_tensor(out=h[:, hf * 512:(hf + 1) * 512], in0=xres[xr][:, hf * 512:(hf + 1) * 512],
                                                                              scalar=ALPHA, in1=pp[:], op0=ALU.mult, op1=ALU.add),
                  [t_xres[xr], t_pp], [t_h])
        ln_core(S, lnb, g_bc[:], b_bc[:], t_gb, eps_t, t_eps, xo[xr][:], t_xo[xr])
        S.dma("pool", x1_d[t * 128:(t + 1) * 128, :], xo[xr][:], reads=[t_xo[xr]])

    for g in range(NG + 1):
        gen = project_group(g) if g < NG else iter(())
        for u in gen:
            if u == "K":
                break

        def pump(gen=gen):
            next(gen, None)

        if g >= 1:
            for t in range(4 * (g - 1), 4 * g):
                attend_tile(t, pump)
        for u in gen:
            pass
    tail_1(NT - 1)
    tail_2(NT - 1)


def phase_c1(nc, S, C, T_, xin_d, swin_d, cw_d, cb_d, dtb_d, xtok_d, btok_d, bT_d, cT_d, zs_d, dt_d,
             ident, t_ident, tri_d=None, alog_d=None, acf_d=None, acb_d=None, acfT_d=None, acbT_d=None, **_):
    NT = T_ // 128
    NG2 = NT // 2
    w_in = C.sb([128, 8, 6208], BF16, "c_win")
    NBLK = 13
    t_w = [T() for i in range(NBLK)]
    wv = swin_d.rearrange("(kt p) n -> p kt n", p=128)
    order = [4, 5, 6, 7, 8, 9, 10, 11, 12, 0, 1, 2, 3]
    for bi in order:
        c0, c1 = bi * 512, min(6208, (bi + 1) * 512)
        S.dma("pool", w_in[:, :, c0:c1], wv[:, :, c0:c1], writes=[t_w[bi]])

    def wdeps(c0, c1):
        return [t_w[b] for b in range(c0 // 512, (c1 - 1) // 512 + 1)]

    cw = C.sb([128, 160], F32, "c_cw")
    cb = C.sb([128, 32], F32, "c_cb")
    t_cw = T()
    S.dma("sp", cw[:], cw_d, writes=[t_cw])
    S.dma("sp", cb[:], cb_d, writes=[t_cw])
    dtb = C.sb([128, 64], F32, "c_dtb")
    S.dma("sp", dtb[:], dtb_d.partition_broadcast(128), writes=[t_cw])
    one_t = C.sb([128, 1], F32, "c_one")
    S.add("pool", lambda e: e.memset(one_t[:], 1.0), [], [t_cw])
    cmask = C.sb([128, 4, 128], BF16, "c_masks")
    S.dma("pool", cmask[:], tri_d.rearrange("p (a q) -> p a q", a=4), writes=[t_cw])
    aneg = C.sb([128, 64], F32, "c_aneg")
    t_an = T()
    S.dma("sp", aneg[:], alog_d.partition_broadcast(128), writes=[t_an])
    S.add("act", lambda e: e.activation(out=aneg[:], in_=aneg[:], func=AF.Exp), [t_an], [t_an])
    S.add("dve", lambda e: e.tensor_scalar(out=aneg[:], in0=aneg[:], scalar1=-1.0, scalar2=None, op0=ALU.mult), [t_an], [t_an])
    av = C.sb([128, 64], F32, "c_av")
    av_hi = C.sb([128, 64], BF16, "c_avhi")
    av_lo = C.sb([128, 64], BF16, "c_avlo")
    acs = C.sb([128, 64], F32, "c_acs")
    t_av = T()
    t_acs = T()
    acsT = C.sb([32, 256], F32, "c_acsT")
    t_acsT = T()
    diag = C.sb([128, 32, 5, 128], BF16, "c_diag")
    t_diag = [T() for i in range(32)]
    for c in range(32):
        for k in range(5):
            S.add("dve" if (c + k) % 2 == 0 else "pool",
                  lambda e, c=c, k=k: e.tensor_scalar(out=diag[:, c, k, :], in0=ident[:], scalar1=cw[:, k * 32 + c:k * 32 + c + 1], scalar2=None, op0=ALU.mult),
                  [t_ident, t_cw], [t_diag[c]])
    xb = [C.sb([128, D], BF16, "c_xb%d" % i) for i in range(2)]
    t_xb = [T(), T()]
    xT = [C.sb([128, 8, 260], BF16, "c_xT%d" % i) for i in range(2)]
    t_xT = [[T(), T(), T()] for i in range(2)]
    u = [C.sb([128, 260], BF16, "c_u%d" % i) for i in range(2)]
    t_u = [T(), T()]
    xsT = C.sb([128, 16, 256], BF16, "c_xsT")
    t_xsT = [T() for i in range(16)]
    BT = C.sb([128, 8, 256], BF16, "c_BT")
    t_BT = [T() for i in range(8)]
    CT = C.sb([128, 8, 256], BF16, "c_CT")
    t_CT = [T() for i in range(8)]
    xtok = C.sb([128, 2048], BF16, "c_xtok")
    t_xtok = T()
    btok = C.sb([128, 1024], BF16, "c_btok")
    t_btok = T()
    zs = C.sb([128, 2048], BF16, "c_zs")
    t_zs = T()
    dv = C.sb([128, 4, 64], F32, "c_dv")
    t_dv = T()
    pT = [C.ps([128, 512], F32, "c_pT%d" % i) for i in range(2)]
    t_pT = [T(), T()]
    pU = [C.ps([128, 512], F32, "c_pU%d" % i) for i in range(2)]
    t_pU = [T(), T()]
    pC = [C.ps([128, 512], F32, "c_pC%d" % i) for i in range(2)]
    t_pC = [T(), T()]
    pZ = [C.ps([128, 512], F32, "c_pZ%d" % i) for i in range(2)]
    t_pZ = [T(), T()]
    cnt = {"T": 0, "U": 0, "C": 0, "Z": 0, "x": 0}

    def nxt(key, arr, tarr):
        i = cnt[key] % 2
        cnt[key] += 1
        return arr[i], tarr[i]

    bTv = bT_d.rearrange("(g n) t -> n g t", n=128)
    cTv = cT_d.rearrange("(g n) t -> n g t", n=128)

    for g2 in range(NG2):
        gb = g2 % 2
        t0 = 256 * g2
        pieces = [(t0 - 2, 128, 0), (t0 + 126, 128, 128), (t0 + 254, 4, 256)]
        for pi_, (r0, nr, c0) in enumerate(pieces):
            s_ = cnt["x"] % 2
            cnt["x"] += 1
            lo, hi = max(r0, 0), min(r0 + nr, T_)
            if lo > r0 or hi < r0 + nr:
                S.add("pool", lambda e, s_=s_: e.memset(xb[s_][:], 0.0), [], [t_xb[s_]])
            S.dma("pool", xb[s_][lo - r0:hi - r0, :], xin_d[lo:hi, :], writes=[t_xb[s_]])
            pp, t_pp = nxt("T", pT, t_pT)
            ppb = pp[:].bitcast(BF16)
            for kt in range(8):
                S.add("pe", lambda e, s_=s_, kt=kt, ppb=ppb, nr=nr: e.transpose(ppb[:, kt * 128:kt * 128 + nr], xb[s_][0:nr, kt * 128:(kt + 1) * 128], ident[0:nr, 0:nr]),
                      [t_xb[s_], t_ident], [t_pp])
            S.add("dve", lambda e, gb=gb, ppb=ppb, nr=nr, c0=c0: e.tensor_copy(out=xT[gb][:, :, c0:c0 + nr], in_=ppb.rearrange("p (k t) -> p k t", k=8)[:, :, 0:nr]),
                  [t_pp], [t_xT[gb][pi_]])
        def proj_chunk(c):
            col = 2048 + c * 128
            pu, t_pu = nxt("U", pU, t_pU)
            for kt in range(8):
                S.add("pe", lambda e, kt=kt, col=col, pu=pu, gb=gb: e.matmul(pu[:, 0:260], lhsT=w_in[:, kt, col:col + 128], rhs=xT[gb][:, kt, :],
                                                                            start=(kt == 0), stop=(kt == 7)),
                      wdeps(col, col + 128) + t_xT[gb], [t_pu])
            ui = c % 2
            S.add("dve", lambda e, ui=ui, pu=pu: e.tensor_copy(out=u[ui][:], in_=pu[:, 0:260]), [t_pu], [t_u[ui]])

        def conv_chunk(c):
            ui = c % 2
            pc, t_pc = nxt("C", pC, t_pC)
            for k in range(5):
                S.add("pe", lambda e, k=k, c=c, ui=ui, pc=pc: e.matmul(pc[:, 0:256], lhsT=diag[:, c, k, :], rhs=u[ui][:, k:k + 256],
                                                                      start=(k == 0), stop=(k == 4)),
                      [t_diag[c], t_u[ui]], [t_pc])
            if c < 16:
                dst, t_dst = xsT[:, c, :], t_xsT[c]
            elif c < 24:
                dst, t_dst = BT[:, c - 16, :], t_BT[c - 16]
            else:
                dst, t_dst = CT[:, c - 24, :], t_CT[c - 24]
            S.add("act", lambda e, pc=pc, dst=dst, c=c: e.activation(out=dst, in_=pc[:, 0:256], func=AF.Silu, bias=cb[:, c:c + 1], scale=1.0),
                  [t_pc, t_cw], [t_dst])

        proj_chunk(0)
        for c in range(32):
            if c + 1 < 32:
                proj_chunk(c + 1)
            conv_chunk(c)
        S.dma("sp", bTv[:, :, t0:t0 + 256], BT[:], reads=t_BT)
        S.dma("sp", cTv[:, :, t0:t0 + 256], CT[:], reads=t_CT)
        for i in range(2):
            t = 2 * g2 + i
            for qz in range(4):
                pz, t_pz = nxt("Z", pZ, t_pZ)
                for kt in range(8):
                    S.add("pe", lambda e, kt=kt, qz=qz, pz=pz, gb=gb, i=i: e.matmul(pz[:], lhsT=xT[gb][:, kt, 2 + i * 128:2 + (i + 1) * 128],
                                                                                   rhs=w_in[:, kt, qz * 512:(qz + 1) * 512], start=(kt == 0), stop=(kt == 7)),
                          wdeps(qz * 512, (qz + 1) * 512) + t_xT[gb], [t_pz])
                S.add("act", lambda e, pz=pz, qz=qz: e.activation(out=zs[:, qz * 512:(qz + 1) * 512], in_=pz[:], func=AF.Silu), [t_pz], [t_zs])
            S.dma("sp", zs_d[t * 128:(t + 1) * 128, :], zs[:], reads=[t_zs])
            for hf in range(2):
                pp, t_pp = nxt("T", pT, t_pT)
                ppb = pp[:].bitcast(BF16)
                for cc in range(8):
                    c = hf * 8 + cc
                    S.add("pe", lambda e, c=c, cc=cc, ppb=ppb, i=i: e.transpose(ppb[:, cc * 128:(cc + 1) * 128], xsT[:, c, i * 128:(i + 1) * 128], ident[:]),
                          [t_xsT[c], t_ident], [t_pp])
                S.add("dve", lambda e, ppb=ppb, hf=hf: e.tensor_copy(out=xtok[:, hf * 1024:(hf + 1) * 1024], in_=ppb), [t_pp], [t_xtok])
            S.dma("sp", xtok_d[t * 128:(t + 1) * 128, :], xtok[:], reads=[t_xtok])
            pp, t_pp = nxt("T", pT, t_pT)
            ppb = pp[:].bitcast(BF16)
            for gg in range(8):
                S.add("pe", lambda e, gg=gg, ppb=ppb, i=i: e.transpose(ppb[:, gg * 128:(gg + 1) * 128], BT[:, gg, i * 128:(i + 1) * 128], ident[:]),
                      [t_BT[gg], t_ident], [t_pp])
            S.add("dve", lambda e, ppb=ppb: e.tensor_copy(out=btok[:], in_=ppb), [t_pp], [t_btok])
            S.dma("sp", btok_d[t * 128:(t + 1) * 128, :], btok[:], reads=[t_btok])
        for i in range(2):
            t = 2 * g2 + i
            pz, t_pz = nxt("Z", pZ, t_pZ)
            for kt in range(8):
                S.add("pe", lambda e, kt=kt, pz=pz, gb=gb, i=i: e.matmul(pz[:, 0:64], lhsT=xT[gb][:, kt, 2 + i * 128:2 + (i + 1) * 128],
                                                                        rhs=w_in[:, kt, 6144:6208], start=(kt == 0), stop=(kt == 7)),
                      wdeps(6144, 6208) + t_xT[gb], [t_pz])
            S.add("dve", lambda e, pz=pz: e.tensor_tensor(out=dv[:, 0, :], in0=pz[:, 0:64], in1=dtb[:], op=ALU.add), [t_pz, t_cw], [t_dv])
            S.add("act", lambda e: e.activation(out=dv[:, 1, :], in_=dv[:, 0, :], func=AF.Abs), [t_dv], [t_dv])
            S.add("act", lambda e: e.activation(out=dv[:, 1, :], in_=dv[:, 1, :], func=AF.Exp, scale=-1.0), [t_dv], [t_dv])
            S.add("act", lambda e: e.activation(out=dv[:, 1, :], in_=dv[:, 1, :], func=AF.Ln, bias=one_t[:], scale=1.0), [t_dv, t_cw], [t_dv])
            S.add("dve", lambda e: e.tensor_scalar(out=dv[:, 2, :], in0=dv[:, 0, :], scalar1=0.0, scalar2=None, op0=ALU.max), [t_dv], [t_dv])
            S.add("dve", lambda e: e.tensor_tensor(out=dv[:, 3, :], in0=dv[:, 2, :], in1=dv[:, 1, :], op=ALU.add), [t_dv], [t_dv])
            S.dma("sp", dt_d[t * 128:(t + 1) * 128, :], dv[:, 3, :], reads=[t_dv])
            S.add("dve", lambda e: e.tensor_tensor(out=av[:], in0=dv[:, 3, :], in1=aneg[:], op=ALU.mult), [t_dv, t_an], [t_av])
            S.add("dve", lambda e: e.tensor_copy(out=av_hi[:], in_=av[:]), [t_av], [t_av])
            S.add("dve", lambda e: e.tensor_tensor(out=av_lo[:], in0=av[:], in1=av_hi[:], op=ALU.subtract), [t_av], [t_av])
            pz, t_pz = nxt("Z", pZ, t_pZ)
            for ci, mi in ((0, 0), (1, 2)):
                S.add("pe", lambda e, ci=ci, mi=mi, pz=pz: e.matmul(pz[:, ci * 32:(ci + 1) * 32], lhsT=cmask[:, mi, :], rhs=av_hi[:, ci * 32:(ci + 1) * 32], start=True, stop=False),
                      [t_av, t_cw], [t_pz])
                S.add("pe", lambda e, ci=ci, mi=mi, pz=pz: e.matmul(pz[:, ci * 32:(ci + 1) * 32], lhsT=cmask[:, mi, :], rhs=av_lo[:, ci * 32:(ci + 1) * 32], start=False, stop=True),
                      [t_av, t_cw], [t_pz])
            S.add("dve", lambda e, pz=pz: e.tensor_copy(out=acs[:], in_=pz[:, 0:64]), [t_pz], [t_acs])
            S.dma("sp", acf_d[t * 128:(t + 1) * 128, :], acs[:, 0:32], reads=[t_acs])
            S.dma("sp", acb_d[t * 128:(t + 1) * 128, :], acs[:, 32:64], reads=[t_acs])
            pz, t_pz = nxt("Z", pZ, t_pZ)
            for ci, mi in ((0, 0), (1, 2)):
                S.add("pe", lambda e, ci=ci, mi=mi, pz=pz: e.matmul(pz[0:32, ci * 128:(ci + 1) * 128], lhsT=av_hi[:, ci * 32:(ci + 1) * 32], rhs=cmask[:, mi, :], start=True, stop=False),
                      [t_av, t_cw], [t_pz])
                S.add("pe", lambda e, ci=ci, mi=mi, pz=pz: e.matmul(pz[0:32, ci * 128:(ci + 1) * 128], lhsT=av_lo[:, ci * 32:(ci + 1) * 32], rhs=cmask[:, mi, :], start=False, stop=True),
                      [t_av, t_cw], [t_pz])
            S.add("dve", lambda e, pz=pz: e.tensor_copy(out=acsT[:], in_=pz[0:32, 0:256]), [t_pz], [t_acsT])
            S.dma("sp", acfT_d[t, :].rearrange("(h l) -> h l", h=32), acsT[:, 0:128], reads=[t_acsT])
            S.dma("sp", acbT_d[t, :].rearrange("(h l) -> h l", h=32), acsT[:, 128:256], reads=[t_acsT])


def phase_ssd(nc, S, C, T_, bwd, xtok_d, btok_d, bT_d, cT_d, dt_d, yf_d, tri_d, alog_d, dsk_d, acf_d, acb_d,
              acfT_d=None, acbT_d=None,
              zs_d=None, normw_d=None, swout_d=None, xres_d=None, xout_d=None, ln_row=2,
              ident=None, t_ident=None, eps_t=None, t_eps=None, lng_d=None, lnb_d=None, **_):
    NT = T_ // 128
    masks = C.sb([128, 4, 128], BF16, "s_masks")
    t_c = T()
    S.dma("pool", masks[:], tri_d.rearrange("p (a q) -> p a q", a=4), writes=[t_c])
    M1 = masks[:, 2, :] if bwd else masks[:, 0, :]
    M2 = masks[:, 3, :] if bwd else masks[:, 1, :]
    ones_m = C.sb([128, 128], BF16, "s_ones")
    S.add("pool", lambda e: e.memset(ones_m[:], 1.0), [], [t_c])
    zero_t = C.sb([128, 1], F32, "s_zero")
    S.add("pool", lambda e: e.memset(zero_t[:], 0.0), [], [t_c])
    aneg = C.sb([128, 32], F32, "s_aneg")
    h0 = 32 if bwd else 0
    S.dma("sp", aneg[:], alog_d[:, h0:h0 + 32].partition_broadcast(128), writes=[t_c])
    S.add("act", lambda e: e.activation(out=aneg[:], in_=aneg[:], func=AF.Exp), [t_c], [t_c])
    S.add("dve", lambda e: e.tensor_scalar(out=aneg[:], in0=aneg[:], scalar1=-1.0, scalar2=None, op0=ALU.mult), [t_c], [t_c])
    if not bwd:
        dsk = C.sb([128, 32], F32, "s_dsk")
        S.dma("sp", dsk[:], dsk_d.partition_broadcast(128), writes=[t_c])
        sk = [C.sb([128, 2048], F32, "s_sk%d" % i) for i in range(2)]
        t_sk = [T(), T()]
    else:
        yf = [C.sb([128, 2048], F32, "s_yf%d" % i) for i in range(2)]
        t_yf = [T(), T()]
    R2 = range(2)
    xtok = [C.sb([128, 32, 64], BF16, "s_xtok%d" % i) for i in R2]
    t_xtok = [T() for i in R2]
    btok = [C.sb([128, 1024], BF16, "s_btok%d" % i) for i in range(3)]
    t_btok = [T() for i in range(3)]
    BT = [C.sb([128, 8, 128], BF16, "s_BT%d" % i) for i in R2]
    t_BT = [T() for i in R2]
    CT = [C.sb([128, 8, 128], BF16, "s_CT%d" % i) for i in R2]
    t_CT = [T() for i in R2]
    dtt = [C.sb([128, 64], F32, "s_dt%d" % i) for i in R2]
    t_dt = [T() for i in R2]
    acT_d = acbT_d if bwd else acfT_d
    ac_d = acb_d if bwd else acf_d
    Rb = [C.sb([128, 32, 128], F32, "s_Rb%d" % i) for i in R2]
    t_Rb = [T() for i in R2]
    act_ = [C.sb([128, 32], F32, "s_act%d" % i) for i in R2]
    t_act = [T() for i in R2]
    a_f = [C.sb([128, 32], F32, "s_a%d" % i) for i in R2]
    a_hi = [C.sb([128, 32], BF16, "s_ahi%d" % i) for i in R2]
    a_lo = [C.sb([128, 32], BF16, "s_alo%d" % i) for i in R2]
    t_a = [T() for i in R2]
    esm = [C.sb([128, 96], F32, "s_esm%d" % i) for i in R2]
    t_esm = [T() for i in R2]
    GTm = [C.sb([128, 8, 128], BF16, "s_GTm%d" % i) for i in R2]
    t_GTm = [T() for i in R2]
    Dm = [C.sb([128, 32, 128], BF16, "s_Dm%d" % i) for i in R2]
    t_Dm = [[T() for q in range(8)] for i in R2]
    xdt = [C.sb([128, 32, 64], BF16, "s_xdt%d" % i) for i in R2]
    t_xdt = [T() for i in R2]
    xdd = [C.sb([128, 32, 64], BF16, "s_xdd%d" % i) for i in R2]
    t_xdd = [T() for i in R2]
    Xc = C.sb([128, 16, 128], F32, "s_Xc")
    t_Xc = [T() for i in range(4)]
    CE = [C.sb([128, 32, 128], BF16, "s_CE%d" % i) for i in R2]
    t_CE = [[T() for q in range(8)] for i in R2]
    hst = C.sb([128, 2048], F32, "s_h")
    t_h = [T() for i in range(4)]
    hbf = C.sb([128, 2048], BF16, "s_hbf")
    t_hbf = [T() for i in range(4)]
    S.add("pool", lambda e: e.memset(hst[:], 0.0), [], t_h)
    S.add("pool", lambda e: e.memset(hbf[:], 0.0), [], t_hbf)
    p_sm = C.ps([128, 512], F32, "s_psm")
    t_psm = T()
    pG = [C.ps([128, 512], F32, "s_pG%d" % i) for i in R2]
    t_pG = [T(), T()]
    pY = [C.ps([128, 512], F32, "s_pY%d" % i) for i in range(4)]
    t_pY = [T() for i in range(4)]
    pSt = C.ps([128, 512], F32, "s_pSt")
    t_pSt = T()
    bTv = bT_d.rearrange("(g n) t -> n g t", n=128)
    cTv = cT_d.rearrange("(g n) t -> n g t", n=128)
    seq = list(range(NT - 1, -1, -1)) if bwd else list(range(NT))

    def stage_P(it):
        t = seq[it]
        b_ = it % 2
        rows = slice(t * 128, (t + 1) * 128)
        b3 = it % 3
        S.dma("sp", dtt[b_][:], dt_d[rows, :], writes=[t_dt[b_]])
        S.dma("sp", act_[b_][:], ac_d[rows, :], writes=[t_act[b_]])
        S.dma("sp", BT[b_][:], bTv[:, :, rows], writes=[t_BT[b_]])
        S.dma("sp", CT[b_][:], cTv[:, :, rows], writes=[t_CT[b_]])
        S.dma("sp", Rb[b_][:].rearrange("p h l -> p (h l)"), acT_d[t, :].partition_broadcast(128), writes=[t_Rb[b_]])
        S.dma("sp", xtok[b_][:].rearrange("p h d -> p (h d)"), xtok_d[rows, :], writes=[t_xtok[b_]])
        S.dma("sp", btok[b3][:], btok_d[rows, :], writes=[t_btok[b3]])
        dth = dtt[b_][:, h0:h0 + 32]
        S.add("dve", lambda e: e.tensor_tensor(out=a_f[b_][:], in0=dth, in1=aneg[:], op=ALU.mult), [t_dt[b_], t_c], [t_a[b_]])
        S.add("dve", lambda e: e.tensor_copy(out=a_hi[b_][:], in_=a_f[b_][:]), [t_a[b_]], [t_a[b_]])
        S.add("dve", lambda e: e.tensor_tensor(out=a_lo[b_][:], in0=a_f[b_][:], in1=a_hi[b_][:], op=ALU.subtract), [t_a[b_]], [t_a[b_]])
        for ci, lm in enumerate((M1, M2, ones_m[:])):
            S.add("pe", lambda e, ci=ci, lm=lm: e.matmul(p_sm[:, ci * 32:(ci + 1) * 32], lhsT=lm, rhs=a_hi[b_][:], start=True, stop=False), [t_a[b_], t_c], [t_psm])
            S.add("pe", lambda e, ci=ci, lm=lm: e.matmul(p_sm[:, ci * 32:(ci + 1) * 32], lhsT=lm, rhs=a_lo[b_][:], start=False, stop=True), [t_a[b_], t_c], [t_psm])
        S.add("act", lambda e: e.activation(out=esm[b_][:], in_=p_sm[:, 0:96], func=AF.Exp), [t_psm], [t_esm[b_]])
        for g in range(8):
            S.add("pe", lambda e, g=g: e.matmul(pG[g // 4][:, (g % 4) * 128:(g % 4 + 1) * 128], lhsT=BT[b_][:, g, :], rhs=CT[b_][:, g, :], start=True, stop=True),
                  [t_BT[b_], t_CT[b_]], [t_pG[g // 4]])
        for k in range(2):
            S.add("dve", lambda e, k=k: e.tensor_tensor(out=GTm[b_][:, k * 4:(k + 1) * 4, :], in0=pG[k][:].rearrange("p (g l) -> p g l", g=4),
                                                        in1=M1.unsqueeze(1).to_broadcast([128, 4, 128]), op=ALU.mult),
                  [t_pG[k], t_c], [t_GTm[b_]])

        def x_relu(q):
            xq = q % 4
            for hh in range(4):
                h = 4 * q + hh
                S.add("act", lambda e, h=h, hh=hh, xq=xq: e.activation(out=Xc[:, 4 * xq + hh, :], in_=Rb[b_][:, h, :], func=AF.Relu,
                                                                      bias=act_[b_][:, h:h + 1], scale=-1.0),
                      [t_Rb[b_], t_act[b_]], [t_Xc[xq]])

        def x_exp(q):
            xq = q % 4
            S.add("act", lambda e, q=q, xq=xq: e.activation(out=Dm[b_][:, 4 * q:4 * q + 4, :].rearrange("p h l -> p (h l)"),
                                                            in_=Xc[:, 4 * xq:4 * xq + 4, :].rearrange("p h l -> p (h l)"), func=AF.Exp, scale=-1.0),
                  [t_Xc[xq]], [t_Dm[b_][q]])

        def ce_op(q):
            S.add("act", lambda e, q=q: e.activation(out=CE[b_][:, 4 * q:4 * q + 4, :].rearrange("p h l -> p (h l)"),
                                                     in_=Rb[b_][:, 4 * q:4 * q + 4, :].rearrange("p h l -> p (h l)"), func=AF.Exp), [t_Rb[b_]], [t_CE[b_][q]])
            S.add("dve", lambda e, q=q: e.tensor_tensor(out=CE[b_][:, 4 * q:4 * q + 4, :], in0=CE[b_][:, 4 * q:4 * q + 4, :],
                                                        in1=CT[b_][:, q:q + 1, :].to_broadcast([128, 4, 128]), op=ALU.mult),
                  [t_CE[b_][q], t_CT[b_]], [t_CE[b_][q]])

        def m_op(q):
            S.add("dve", lambda e, q=q: e.tensor_tensor(out=Dm[b_][:, 4 * q:4 * q + 4, :], in0=Dm[b_][:, 4 * q:4 * q + 4, :],
                                                        in1=GTm[b_][:, q:q + 1, :].to_broadcast([128, 4, 128]), op=ALU.mult),
                  [t_Dm[b_][q], t_GTm[b_]], [t_Dm[b_][q]])

        S.add("dve", lambda e: e.tensor_tensor(out=xdt[b_][:], in0=xtok[b_][:], in1=dth.unsqueeze(2).to_broadcast([128, 32, 64]), op=ALU.mult),
              [t_xtok[b_], t_dt[b_]], [t_xdt[b_]])
        if not bwd:
            S.add("dve", lambda e: e.tensor_tensor(out=sk[b_][:].rearrange("p (h d) -> p h d", h=32), in0=xtok[b_][:],
                                                    in1=dsk[:].unsqueeze(2).to_broadcast([128, 32, 64]), op=ALU.mult),
                  [t_xtok[b_], t_c], [t_sk[b_]])
        x_relu(0)
        x_relu(1)
        x_exp(0)
        S.add("dve", lambda e: e.tensor_tensor(out=xdd[b_][:], in0=xdt[b_][:], in1=esm[b_][:, 32:64].unsqueeze(2).to_broadcast([128, 32, 64]), op=ALU.mult),
              [t_xdt[b_], t_esm[b_]], [t_xdd[b_]])
        for q in range(2, 8):
            x_relu(q)
            x_exp(q - 1)
            m_op(q - 2)
            ce_op(q - 2)
        x_exp(7)
        m_op(6)
        ce_op(6)
        m_op(7)
        ce_op(7)

    def stage_Q(it):
        t = seq[it]
        b_ = it % 2
        b3 = it % 3
        rows = slice(t * 128, (t + 1) * 128)
        for h in range(32):
            yo = pY[h // 8][:, (h % 8) * 64:(h % 8 + 1) * 64]
            S.add("pe", lambda e, h=h, yo=yo: e.matmul(yo, lhsT=Dm[b_][:, h, :], rhs=xdt[b_][:, h, :], start=True, stop=False),
                  [t_Dm[b_][h // 4], t_xdt[b_]], [t_pY[h // 8]])
            S.add("pe", lambda e, h=h, yo=yo: e.matmul(yo, lhsT=CE[b_][:, h, :], rhs=hbf[:, h * 64:(h + 1) * 64], start=False, stop=True),
                  [t_CE[b_][h // 4], t_hbf[h // 8]], [t_pY[h // 8]])
        for gp in range(4):
            for k in range(2):
                g = 2 * gp + k
                S.add("pe", lambda e, g=g, k=k: e.matmul(pSt[:, k * 256:(k + 1) * 256], lhsT=btok[b3][:, g * 128:(g + 1) * 128],
                                                         rhs=xdd[b_][:, 4 * g:4 * g + 4, :], start=True, stop=True),
                      [t_btok[b3], t_xdd[b_]], [t_pSt])
            S.add("dve", lambda e, gp=gp: e.tensor_tensor(out=hst[:, gp * 512:(gp + 1) * 512].rearrange("p (h d) -> p h d", h=8),
                                                           in0=hst[:, gp * 512:(gp + 1) * 512].rearrange("p (h d) -> p h d", h=8),
                                                           in1=esm[b_][:, 64 + gp * 8:64 + gp * 8 + 8].unsqueeze(2).to_broadcast([128, 8, 64]), op=ALU.mult),
                  [t_h[gp], t_esm[b_]], [t_h[gp]])
            S.add("dve", lambda e, gp=gp: e.tensor_tensor(out=hst[:, gp * 512:(gp + 1) * 512], in0=hst[:, gp * 512:(gp + 1) * 512], in1=pSt[:], op=ALU.add),
                  [t_h[gp], t_pSt], [t_h[gp]])
            S.add("act", lambda e, gp=gp: e.copy(out=hbf[:, gp * 512:(gp + 1) * 512], in_=hst[:, gp * 512:(gp + 1) * 512]), [t_h[gp]], [t_hbf[gp]])
        if not bwd:
            for k in range(4):
                S.add("dve", lambda e, k=k: e.tensor_tensor(out=sk[b_][:, k * 512:(k + 1) * 512], in0=sk[b_][:, k * 512:(k + 1) * 512], in1=pY[k][:], op=ALU.add),
                      [t_sk[b_], t_pY[k]], [t_sk[b_]])
            S.dma("pool", yf_d[rows, :], sk[b_][:], reads=[t_sk[b_]])
        else:
            S.dma("sp", yf[b_][:], yf_d[rows, :], writes=[t_yf[b_]])
            for k in range(4):
                S.add("dve", lambda e, k=k: e.tensor_tensor(out=yf[b_][:, k * 512:(k + 1) * 512], in0=yf[b_][:, k * 512:(k + 1) * 512], in1=pY[k][:], op=ALU.add),
                      [t_yf[b_], t_pY[k]], [t_yf[b_]])
            S.dma("pool", yf_d[rows, :], yf[b_][:], reads=[t_yf[b_]])

    for it in range(NT + 1):
        if it < NT:
            stage_P(it)
        if it >= 1:
            stage_Q(it - 1)


def phase_e(nc, S, C, T_, yt_d, zs_d, normw_d, swout_d, xres_d, xout_d, ln_row, ident, t_ident, eps_t, t_eps, lng_d, lnb_d, **_):
    NT = T_ // 128
    w_out = C.sb([128, 16, D], BF16, "e_wout")
    w_tmp = C.sb([128, 16, D], F32, "e_wtmp")
    nw = C.sb([128, 16], F32, "e_nw")
    t_wout = T()
    t_wtmp = T()
    S.dma("sp", w_tmp[:], swout_d.rearrange("(kt p) n -> p kt n", p=128), writes=[t_wtmp])
    S.dma("sp", nw[:], normw_d, writes=[t_wtmp])
    for kt in range(16):
        S.add("dve" if kt % 2 == 0 else "pool", lambda e, kt=kt: e.tensor_scalar(out=w_out[:, kt, :], in0=w_tmp[:, kt, :], scalar1=nw[:, kt:kt + 1], scalar2=None, op0=ALU.mult),
              [t_wtmp], [t_wout])
    g_bc = C.sb([128, D], F32, "e_g")
    b_bc = C.sb([128, D], F32, "e_b")
    t_gb = T()
    S.dma("sp", g_bc[:], lng_d[ln_row:ln_row + 1, :].partition_broadcast(128), writes=[t_gb])
    S.dma("sp", b_bc[:], lnb_d[ln_row:ln_row + 1, :].partition_broadcast(128), writes=[t_gb])
    R2 = range(2)
    yt = [C.sb([128, 2048], F32, "e_yt%d" % i) for i in R2]
    t_yt = [T() for i in R2]
    zst = [C.sb([128, 2048], BF16, "e_zs%d" % i) for i in R2]
    t_zs = [T() for i in R2]
    ss = [C.sb([128, 16], F32, "e_ss%d" % i) for i in R2]
    t_ss = [T() for i in R2]
    junk = C.sb([128, 256], F32, "e_junk")
    t_junk = T()
    ynw = [C.sb([128, 2048], BF16, "e_ynw%d" % i) for i in R2]
    t_ynw = [T() for i in R2]
    yT = [C.sb([128, 16, 128], BF16, "e_yT%d" % i) for i in R2]
    t_yT = [T() for i in R2]
    xres = [C.sb([128, D], F32, "e_xres%d" % i) for i in R2]
    t_xres = [T() for i in R2]
    lnbs = [{"h": (C.sb([128, D], F32, "e_ln_h%d" % i), T()), "st": (C.sb([128, 12], F32, "e_ln_st%d" % i), T()),
             "mv": (C.sb([128, 4], F32, "e_ln_mv%d" % i), T())} for i in R2]
    pT = [C.ps([128, 512], F32, "e_pT%d" % i) for i in range(4)]
    t_pT = [T() for i in range(4)]
    pO = [C.ps([128, 512], F32, "e_pO%d" % i) for i in range(4)]
    t_pO = [T() for i in range(4)]

    def E1(t):
        b_ = t % 2
        rows = slice(t * 128, (t + 1) * 128)
        S.dma("sp", yt[b_][:], yt_d[rows, :], writes=[t_yt[b_]])
        S.dma("sp", zst[b_][:], zs_d[rows, :], writes=[t_zs[b_]])
        S.dma("sp", xres[b_][:], xres_d[rows, :], writes=[t_xres[b_]])
        S.add("dve", lambda e: e.tensor_tensor(out=yt[b_][:], in0=yt[b_][:], in1=zst[b_][:], op=ALU.mult), [t_yt[b_], t_zs[b_]], [t_yt[b_]])
        S.add("pool", lambda e: e.memset(ss[b_][:], 0.0), [], [t_ss[b_]])
        for g in range(8):
            S.add("act", lambda e, g=g: e.activation(out=junk[:], in_=yt[b_][:, g * 256:(g + 1) * 256], func=AF.Square, accum_out=ss[b_][:, g:g + 1]),
                  [t_yt[b_], t_ss[b_]], [t_junk, t_ss[b_]])
        S.add("act", lambda e: e.activation(out=ss[b_][:, 8:16], in_=ss[b_][:, 0:8], func=AF.Ln, bias=eps_t[:], scale=1.0 / 256.0), [t_ss[b_], t_eps], [t_ss[b_]])
        S.add("act", lambda e: e.activation(out=ss[b_][:, 8:16], in_=ss[b_][:, 8:16], func=AF.Exp, scale=-0.5), [t_ss[b_]], [t_ss[b_]])

    def E2(t):
        b_ = t % 2
        S.add("dve", lambda e: e.tensor_tensor(out=ynw[b_][:].rearrange("p (g c) -> p g c", g=8), in0=yt[b_][:].rearrange("p (g c) -> p g c", g=8),
                                               in1=ss[b_][:, 8:16].unsqueeze(2).to_broadcast([128, 8, 256]), op=ALU.mult), [t_yt[b_], t_ss[b_]], [t_ynw[b_]])
        for hf in range(2):
            pp, t_pp = pT[2 * b_ + hf], t_pT[2 * b_ + hf]
            ppb = pp[:].bitcast(BF16)
            for cc in range(8):
                kt = hf * 8 + cc
                S.add("pe", lambda e, kt=kt, cc=cc, ppb=ppb: e.transpose(ppb[:, cc * 128:(cc + 1) * 128], ynw[b_][:, kt * 128:(kt + 1) * 128], ident[:]),
                      [t_ynw[b_], t_ident], [t_pp])
            S.add("act", lambda e, ppb=ppb, hf=hf: e.copy(out=yT[b_][:, hf * 8:(hf + 1) * 8, :], in_=ppb.rearrange("p (k t) -> p k t", k=8)), [t_pp], [t_yT[b_]])
        for hf in range(2):
            po, t_po = pO[2 * b_ + hf], t_pO[2 * b_ + hf]
            for kt in range(16):
                S.add("pe", lambda e, kt=kt, hf=hf, po=po: e.matmul(po[:], lhsT=yT[b_][:, kt, :], rhs=w_out[:, kt, hf * 512:(hf + 1) * 512], start=(kt == 0), stop=(kt == 15)),
                      [t_yT[b_], t_wout], [t_po])

    def E3(t):
        b_ = t % 2
        rows = slice(t * 128, (t + 1) * 128)
        lnb = lnbs[b_]
        h_, t_h_ = lnb["h"]
        for hf in range(2):
            po, t_po = pO[2 * b_ + hf], t_pO[2 * b_ + hf]
            S.add("dve", lambda e, hf=hf, po=po: e.scalar_tensor_tensor(out=h_[:, hf * 512:(hf + 1) * 512], in0=xres[b_][:, hf * 512:(hf + 1) * 512],
                                                                       scalar=ALPHA, in1=po[:], op0=ALU.mult, op1=ALU.add),
                  [t_xres[b_], t_po], [t_h_])
        ln_core(S, lnb, g_bc[:], b_bc[:], t_gb, eps_t, t_eps, h_[:], t_h_)
        S.dma("pool", xout_d[rows, :], h_[:], reads=[t_h_])

    for it in range(NT + 2):
        if 1 <= it <= NT:
            E2(it - 1)
        if it >= 2:
            E3(it - 2)
        if it < NT:
            E1(it)


def phase_mlp(nc, S, C, T_, xin_d, xout_d, w1_d, w2_d, ln_row, ident, t_ident, eps_t, t_eps, lng_d, lnb_d, **_):
    NT = T_ // 128
    NG2 = NT // 2
    w1 = C.sb([128, 8, 4096], BF16, "m_w1")
    w2 = C.sb([128, 32, D], BF16, "m_w2")
    t_w1 = [T("m_w1_%d" % i) for i in range(8)]
    t_w2 = [T("m_w2_%d" % i) for i in range(8)]
    w1v = w1_d.rearrange("(kt p) n -> p kt n", p=128)
    w2v = w2_d.rearrange("(f p) n -> p f n", p=128)
    for i in range(8):
        S.dma("pool", w1[:, :, i * 512:(i + 1) * 512], w1v[:, :, i * 512:(i + 1) * 512], writes=[t_w1[i]])
    for i in range(8):
        S.dma("pool", w2[:, i * 4:(i + 1) * 4, :], w2v[:, i * 4:(i + 1) * 4, :], writes=[t_w2[i]])
    g_bc = C.sb([128, D], F32, "m_g")
    b_bc = C.sb([128, D], F32, "m_b")
    t_gb = T("m_gb")
    S.dma("sp", g_bc[:], lng_d[ln_row:ln_row + 1, :].partition_broadcast(128), writes=[t_gb])
    S.dma("sp", b_bc[:], lnb_d[ln_row:ln_row + 1, :].partition_broadcast(128), writes=[t_gb])
    xf = [C.sb([128, D], F32, "m_xf%d" % i) for i in range(2)]
    t_xf = [T() for i in range(2)]
    xb = [C.sb([128, D], BF16, "m_xb%d" % i) for i in range(2)]
    t_xb = [T() for i in range(2)]
    xT = [C.sb([128, 8, 256], BF16, "m_xT%d" % i) for i in range(2)]
    t_xT = [[T(), T()] for i in range(2)]
    h1T = C.sb([128, 32, 256], BF16, "m_h1T")
    t_h1 = [T() for i in range(32)]
    rl = [C.sb([128, 256], F32, "m_rl%d" % i) for i in range(2)]
    t_rl = [T(), T()]
    xres = [C.sb([128, D], F32, "m_xres%d" % i) for i in range(2)]
    t_xres = [T(), T()]
    lnb = {"h": (C.sb([128, D], F32, "m_ln_h"), T()), "st": (C.sb([128, 12], F32, "m_ln_st"), T()),
           "mv": (C.sb([128, 4], F32, "m_ln_mv"), T())}
    xo = [C.sb([128, D], F32, "m_xo%d" % i) for i in range(2)]
    t_xo = [T(), T()]
    pT = [C.ps([128, 512], F32, "m_pT%d" % i) for i in range(2)]
    t_pT = [T(), T()]
    pH = [C.ps([128, 512], F32, "m_pH%d" % i) for i in range(2)]
    t_pH = [T(), T()]
    pY = [C.ps([128, 512], F32, "m_pY%d" % i) for i in range(4)]
    t_pY = [T() for i in range(4)]
    cnt = [0, 0, 0]
    for g in range(NG2):
        gb = g % 2
        for i in range(2):
            t = 2 * g + i
            s_ = t % 2
            S.dma("sp", xf[s_][:], xin_d[t * 128:(t + 1) * 128, :], writes=[t_xf[s_]])
            S.add("dve", lambda e, s_=s_: e.tensor_copy(out=xb[s_][:], in_=xf[s_][:]), [t_xf[s_]], [t_xb[s_]])
            pi = cnt[0] % 2
            cnt[0] += 1
            ppb = pT[pi][:].bitcast(BF16)
            for kt in range(8):
                S.add("pe", lambda e, s_=s_, kt=kt, ppb=ppb: e.transpose(ppb[:, kt * 128:(kt + 1) * 128], xb[s_][:, kt * 128:(kt + 1) * 128], ident[:]),
                      [t_xb[s_], t_ident], [t_pT[pi]])
            S.add("act", lambda e, i=i, gb=gb, ppb=ppb: e.copy(out=xT[gb][:, :, i * 128:(i + 1) * 128], in_=ppb.rearrange("p (k t) -> p k t", k=8)),
                  [t_pT[pi]], [t_xT[gb][i]])
        for f in range(32):
            hi = cnt[1] % 2
            cnt[1] += 1
            for kt in range(8):
                S.add("pe", lambda e, kt=kt, f=f, hi=hi, gb=gb: e.matmul(pH[hi][:, 0:256], lhsT=w1[:, kt, f * 128:(f + 1) * 128], rhs=xT[gb][:, kt, :],
                                                                        start=(kt == 0), stop=(kt == 7)),
                      [t_w1[f // 4]] + t_xT[gb], [t_pH[hi]])
            S.add("act", lambda e, hi=hi: e.activation(out=rl[hi][:], in_=pH[hi][:, 0:256], func=AF.Relu), [t_pH[hi]], [t_rl[hi]])
            S.add("dve" if f % 2 == 0 else "pool", lambda e, hi=hi, f=f: e.tensor_tensor(out=h1T[:, f, :], in0=rl[hi][:], in1=rl[hi][:], op=ALU.mult),
                  [t_rl[hi]], [t_h1[f]])
        for i in range(2):
            t = 2 * g + i
            xr = t % 2
            S.dma("sp", xres[xr][:], xin_d[t * 128:(t + 1) * 128, :], writes=[t_xres[xr]])
            yb = (cnt[2] % 2) * 2
            cnt[2] += 1
            for hf in range(2):
                for f in range(32):
                    S.add("pe", lambda e, f=f, hf=hf, i=i, yb=yb: e.matmul(pY[yb + hf][:], lhsT=h1T[:, f, i * 128:(i + 1) * 128], rhs=w2[:, f, hf * 512:(hf + 1) * 512],
                                                                          start=(f == 0), stop=(f == 31)),
                          [t_h1[f], t_w2[f // 4]], [t_pY[yb + hf]])
            h, t_h = lnb["h"]
            for hf in range(2):
                S.add("dve", lambda e, hf=hf, xr=xr, yb=yb: e.scalar_tensor_tensor(out=h[:, hf * 512:(hf + 1) * 512], in0=xres[xr][:, hf * 512:(hf + 1) * 512],
                                                                                  scalar=ALPHA, in1=pY[yb + hf][:], op0=ALU.mult, op1=ALU.add),
                      [t_xres[xr], t_pY[yb + hf]], [t_h])
            ln_core(S, lnb, g_bc[:], b_bc[:], t_gb, eps_t, t_eps, xo[xr][:], t_xo[xr])
            S.dma("pool", xout_d[t * 128:(t + 1) * 128, :], xo[xr][:], reads=[t_xo[xr]])


def ln_core(S, bufs, g_bc, b_bc, t_gb, eps_t, t_eps, out_sb, t_out):
    h, t_h = bufs["h"]
    st6, t_st = bufs["st"]
    mv, t_mv = bufs["mv"]
    S.add("dve", lambda e: e.bn_stats(out=st6[:, 0:6], in_=h[:, 0:512]), [t_h], [t_st])
    S.add("dve", lambda e: e.bn_stats(out=st6[:, 6:12], in_=h[:, 512:1024]), [t_h], [t_st])
    S.add("dve", lambda e: e.bn_aggr(out=mv[:, 0:2], in_=st6[:, 0:12]), [t_st], [t_mv])
    S.add("act", lambda e: e.activation(out=mv[:, 2:3], in_=mv[:, 1:2], func=AF.Ln, bias=eps_t[:], scale=1.0), [t_mv, t_eps], [t_mv])
    S.add("act", lambda e: e.activation(out=mv[:, 3:4], in_=mv[:, 2:3], func=AF.Exp, scale=-0.5), [t_mv], [t_mv])
    S.add("dve", lambda e: e.tensor_scalar(out=h[:], in0=h[:], scalar1=mv[:, 0:1], scalar2=mv[:, 3:4],
                                           op0=ALU.subtract, op1=ALU.mult), [t_h, t_mv], [t_h])
    S.add("dve", lambda e: e.tensor_tensor(out=h[:], in0=h[:], in1=g_bc, op=ALU.mult), [t_h, t_gb], [t_h])
    S.add("dve", lambda e: e.tensor_tensor(out=out_sb, in0=h[:], in1=b_bc, op=ALU.add), [t_h, t_gb], [t_out])


def host_inputs(T, x, attn_w_in, attn_sink, attn_rpb, attn_w_out, ln1_g, ln1_b, ln2_g, ln2_b, mlp_w1, mlp_w2,
                ssm_w_in, ssm_conv_w, ssm_conv_b, ssm_dt_bias, ssm_A_log, ssm_D, ssm_norm_w, ssm_w_out, **_):
    NT = T // 128
    cosT, sinS = _rope_tables(T)
    tabs = _na_bias_tables(np.asarray(attn_rpb[0]), NT)
    nbt = np.stack([tabs[k].reshape(128, -1) for k in ("int", "t0", "t1", "tm2", "tm1")], 0)
    kq = np.arange(128)
    maskA = np.concatenate([(kq[:, None] >= kq[None, :]), (kq[:, None] <= kq[None, :])], 1).astype(np.float32)
    common = {
        "awin": _attn_w_layout(np.asarray(attn_w_in[0])),
        "awout": np.ascontiguousarray(attn_w_out[0]),
        "sink": np.ascontiguousarray(attn_sink[0:1]),
        "nbt": np.ascontiguousarray(nbt),
        "cosT": cosT, "sinS": sinS, "maskA": np.ascontiguousarray(maskA),
        "lng": np.ascontiguousarray(np.stack([ln1_g[0], ln2_g[0], ln1_g[1], ln2_g[1]])),
        "lnb": np.ascontiguousarray(np.stack([ln1_b[0], ln2_b[0], ln1_b[1], ln2_b[1]])),
        "ident": np.eye(128, dtype=np.float32),
        "swin": np.ascontiguousarray(ssm_w_in[0]),
        "cw": np.ascontiguousarray(np.asarray(ssm_conv_w[0]).reshape(5, 32, 128).transpose(2, 0, 1).reshape(128, 160)),
        "cb": np.ascontiguousarray(np.asarray(ssm_conv_b[0]).reshape(32, 128).T),
        "dtb": np.ascontiguousarray(np.asarray(ssm_dt_bias[0]).reshape(1, 64)),
        "alog": np.ascontiguousarray(np.asarray(ssm_A_log[0]).reshape(1, 64)),
        "dsk": np.ascontiguousarray(np.asarray(ssm_D[0]).reshape(1, 32)),
        "normwT": np.ascontiguousarray(np.asarray(ssm_norm_w[0]).reshape(16, 128).T),
        "swout": np.ascontiguousarray(ssm_w_out[0]),
        "tri": np.ascontiguousarray(np.concatenate([kq[:, None] <= kq[None, :], kq[:, None] > kq[None, :],
                                                    kq[:, None] >= kq[None, :], kq[:, None] < kq[None, :]], 1).astype(np.float32)),
        "w1_0": np.ascontiguousarray(mlp_w1[0]), "w1_1": np.ascontiguousarray(mlp_w1[1]),
        "w2_0": np.ascontiguousarray(mlp_w2[0]), "w2_1": np.ascontiguousarray(mlp_w2[1]),
    }
    return common


def kernel(**inputs):
    x = np.asarray(inputs["x"])
    B, T, _ = x.shape
    common = host_inputs(T, **inputs)
    nc = build(T)
    in_maps = []
    for c in range(8):
        m = dict(common)
        m["x"] = np.ascontiguousarray(x[c % B])
        in_maps.append(m)
    res = run_bass_kernel_spmd(nc, in_maps, core_ids=list(range(8)))
    out = np.stack([res.results[b]["out"] for b in range(B)], 0)
    return out.astype(np.float32)
```
